# Optimizing a Trainium2 kernel written in Bass

```python
import math
import jax, jax.numpy as jnp
from jax import lax
import numpy as np

D_MODEL = 1024
BATCH = 8
SEQ = 4096
DEPTH = 4

GRID_W = 64
CTX_LEN = 256
HEAD_DIM = 64
N_MIXERS = 4
GROUP_HEADS = D_MODEL // (N_MIXERS * HEAD_DIM)
GROUP_W = GROUP_HEADS * HEAD_DIM
MIX_W = N_MIXERS * GROUP_W
ROPE_BASE = 10000.0
RET_CHUNK = 128
DIFF_QK_DIM = HEAD_DIM // 2
Q_BLOCK = 128
GDN_CHUNK = 64
SHORT_CONV = 3
WINDOW = 128
SWA_KV_HEADS = GROUP_HEADS // 2
N_EXPERTS = 16
CAPACITY_FACTOR = 2
EXPERT_FF = 2 * D_MODEL
LN_EPS = 1e-5
DEEPNORM_ALPHA = (2 * DEPTH) ** 0.25
DEEPNORM_BETA = (8 * DEPTH) ** -0.25

IN_COLS = (
    ('ret_q', GROUP_W), ('ret_k', GROUP_W), ('ret_v', GROUP_W), ('ret_g', GROUP_W),
    ('diff_q', GROUP_W), ('diff_k', GROUP_W), ('diff_v', GROUP_W),
    ('gdn_qkv', 3 * GROUP_W), ('gdn_g', GROUP_W), ('gdn_a', 2 * GROUP_HEADS), ('gdn_b', 2 * GROUP_HEADS),
    ('swa_q', GROUP_W), ('swa_k', SWA_KV_HEADS * HEAD_DIM), ('swa_v', SWA_KV_HEADS * HEAD_DIM),
)
IN_W = sum(w for _, w in IN_COLS)

kernel_name = 'hybrid_parallel_heads_diffusion_block'

F32 = jnp.float32


def split_cols(p):
    out, off = {}, 0
    for name, w in IN_COLS:
        out[name] = p[..., off:off + w]
        off += w
    return out


def layer_norm(x, g, b):
    xf = x.astype(F32)
    mu = jnp.mean(xf, -1, keepdims=True)
    var = jnp.mean(jnp.square(xf - mu), -1, keepdims=True)
    return ((xf - mu) * lax.rsqrt(var + LN_EPS) * g.astype(F32) + b.astype(F32)).astype(x.dtype)


def rms_norm(x, g=None, eps=1e-6):
    xf = x.astype(F32)
    y = xf * lax.rsqrt(jnp.mean(jnp.square(xf), -1, keepdims=True) + eps)
    if g is not None:
        y = y * g.astype(F32)
    return y


def l2_normalize(x, eps=1e-6):
    return x * lax.rsqrt(jnp.sum(jnp.square(x), -1, keepdims=True) + eps)


def axial_rope_angles(n, rot_dim):
    rows = n // GRID_W
    row = jnp.repeat(jnp.arange(rows, dtype=F32), GRID_W)
    col = jnp.tile(jnp.arange(GRID_W, dtype=F32), rows)
    n_freq = rot_dim // 4
    inv = ROPE_BASE ** (-jnp.arange(n_freq, dtype=F32) / n_freq)
    return jnp.concatenate([row[:, None] * inv, col[:, None] * inv], -1)


def retention_angles(n, dim):
    theta = 1.0 / (ROPE_BASE ** jnp.linspace(0.0, 1.0, dim // 2, dtype=F32))
    return jnp.arange(n, dtype=F32)[:, None] * theta


def rotate(x, ang):
    half = x.shape[-1] // 2
    cos, sin = jnp.cos(ang).astype(x.dtype), jnp.sin(ang).astype(x.dtype)
    x1, x2 = x[..., :half], x[..., half:]
    return jnp.concatenate([x1 * cos - x2 * sin, x1 * sin + x2 * cos], -1)


def flip_seq(t):
    return jnp.flip(t, axis=2)


def sink_softmax(s, sink):
    full = jnp.concatenate([s, jnp.broadcast_to(sink, s.shape[:-1] + (1,)).astype(s.dtype)], -1)
    return jax.nn.softmax(full, -1)[..., :-1]


def retention_scan(q, k, v, log_g, s0, with_out):
    b, h, n, dk = q.shape
    dv = v.shape[-1]
    c = RET_CHUNK
    nc = n // c
    qc, kc, vc = (t.reshape(b, h, nc, c, t.shape[-1]) for t in (q, k, v))
    pos = jnp.arange(c, dtype=F32)
    lg = log_g[:, None]
    k_decay = jnp.exp(lg * (c - 1 - pos))
    kv = jnp.einsum('bhncd,hc,bhnce->nbhde', kc, k_decay, vc)
    chunk_decay = jnp.exp(log_g * c)[None, :, None, None]

    def step(s, kv_n):
        return s * chunk_decay + kv_n, (s if with_out else None)

    s_last, s_prev = lax.scan(step, s0, kv)
    if not with_out:
        return None, s_last
    rel = pos[:, None] - pos[None, :]
    d_mat = jnp.where(rel >= 0, jnp.exp(lg[:, :, None] * jnp.maximum(rel, 0.0)), 0.0)
    q_decay = jnp.exp(lg * (pos + 1.0))
    scores = jnp.einsum('bhnid,bhnjd->bhnij', qc, kc) * d_mat[:, None]
    o = (jnp.einsum('bhnij,bhnje->bhnie', scores, vc)
         + jnp.einsum('bhnid,hi,nbhde->bhnie', qc, q_decay, s_prev))
    return o.reshape(b, h, n, dv), s_last


def retention_group(pl, pc, ret_decay, with_ctx_out):
    def heads(t):
        return t.reshape(t.shape[0], t.shape[1], GROUP_HEADS, HEAD_DIM).transpose(0, 2, 1, 3).astype(F32)
    scale = HEAD_DIM ** -0.5
    ang = retention_angles(pl['ret_q'].shape[1], HEAD_DIM)
    ql = rotate(heads(pl['ret_q']) * scale, ang)
    kl = rotate(heads(pl['ret_k']), ang)
    vl = heads(pl['ret_v'])
    qc, kc, vc = heads(pc['ret_q']) * scale, heads(pc['ret_k']), heads(pc['ret_v'])
    log_g = jax.nn.log_sigmoid(ret_decay.astype(F32))
    s0 = jnp.zeros((ql.shape[0], GROUP_HEADS, HEAD_DIM, HEAD_DIM), F32)
    o_l, o_c = 0.0, 0.0
    for dr in range(2):
        f = flip_seq if dr else (lambda t: t)
        oc, s_ctx = retention_scan(f(qc), f(kc), f(vc), log_g[dr], s0, with_ctx_out)
        ol, _ = retention_scan(f(ql), f(kl), f(vl), log_g[dr], s_ctx, True)
        o_l = o_l + f(ol)
        if with_ctx_out:
            o_c = o_c + f(oc)

    def finish(o, gate):
        y = rms_norm(o).transpose(0, 2, 1, 3).reshape(gate.shape)
        return (jax.nn.silu(gate.astype(F32)) * y).astype(gate.dtype)

    return finish(o_l, pl['ret_g']), (finish(o_c, pc['ret_g']) if with_ctx_out else None)


def diff_attention_group(pl, pc, lam_params, norm_g, layer_idx, ang_axial, with_ctx_out):
    h, dq = GROUP_HEADS, DIFF_QK_DIM
    scale = dq ** -0.5
    qk = lambda t: t.reshape(t.shape[0], t.shape[1], h, 2, dq)
    vv = lambda t: t.reshape(t.shape[0], t.shape[1], h, HEAD_DIM)
    ang = ang_axial[:, None, None, :]
    ql, kl = rotate(qk(pl['diff_q']), ang), rotate(qk(pl['diff_k']), ang)
    qc, kc = qk(pc['diff_q']), qk(pc['diff_k'])
    vl, vc = vv(pl['diff_v']), vv(pc['diff_v'])
    lam_init = 0.8 - 0.6 * math.exp(-0.3 * layer_idx)
    lp = lam_params.astype(F32)
    lam = jnp.exp(jnp.sum(lp[0] * lp[1])) - jnp.exp(jnp.sum(lp[2] * lp[3])) + lam_init

    def attend(q, k, v):
        s = jnp.einsum('bqhcd,bkhcd->bhcqk', q, k, preferred_element_type=F32) * scale
        p = jax.nn.softmax(s, -1)
        a = p[:, :, 0] - lam * p[:, :, 1]
        return jnp.einsum('bhqk,bkhe->bqhe', a.astype(v.dtype), v)

    k_all = jnp.concatenate([kl, kc], 1)
    v_all = jnp.concatenate([vl, vc], 1)
    b, n = ql.shape[:2]
    nb = n // Q_BLOCK
    q_blocks = ql.reshape(b, nb, Q_BLOCK, h, 2, dq).swapaxes(0, 1)
    o_l = lax.map(lambda qb: attend(qb, k_all, v_all), q_blocks)
    o_l = o_l.swapaxes(0, 1).reshape(b, n, h, HEAD_DIM)

    def finish(o):
        y = rms_norm(o, norm_g) * (1.0 - lam_init)
        return y.reshape(o.shape[0], o.shape[1], GROUP_W).astype(pl['diff_v'].dtype)

    return finish(o_l), (finish(attend(qc, kc, vc)) if with_ctx_out else None)


def short_conv(x, w):
    return lax.conv_general_dilated(
        x, w[:, None, :].astype(x.dtype), window_strides=(1,),
        padding=[(SHORT_CONV // 2, SHORT_CONV // 2)],
        dimension_numbers=('NWC', 'WIO', 'NWC'), feature_group_count=x.shape[-1])


def gdn_scan(q, k, v, log_a, beta, s0, with_out):
    b, h, n, dk = q.shape
    dv = v.shape[-1]
    c = GDN_CHUNK
    nc = n // c
    q, k, v = (t.reshape(b, h, nc, c, t.shape[-1]) for t in (q, k, v))
    log_a, beta = (t.reshape(b, h, nc, c) for t in (log_a, beta))
    g = jnp.cumsum(log_a, -1)
    idx = jnp.arange(c)
    incl = idx[:, None] >= idx[None, :]
    strict = idx[:, None] > idx[None, :]
    decay = jnp.exp(jnp.where(incl, g[..., :, None] - g[..., None, :], -jnp.inf))
    a_mat = jnp.where(strict, beta[..., :, None] * jnp.einsum('bhnid,bhnjd->bhnij', k, k) * decay, 0.0)
    rhs = jnp.concatenate([beta[..., None] * v, (beta * jnp.exp(g))[..., None] * k], -1)
    sol = lax.linalg.triangular_solve(a_mat + jnp.eye(c, dtype=a_mat.dtype), rhs, left_side=True, lower=True)
    u_base, w = sol[..., :dv], sol[..., dv:]
    k_tail = k * jnp.exp(g[..., -1:] - g)[..., None]
    chunk_decay = jnp.exp(g[..., -1])
    mv = lambda t: jnp.moveaxis(t, 2, 0)
    xs = (mv(u_base), mv(w), mv(k_tail), mv(chunk_decay))
    if with_out:
        qk = jnp.einsum('bhnid,bhnjd->bhnij', q, k) * decay
        xs = xs + (mv(qk), mv(q * jnp.exp(g)[..., None]))

    def step(s, xc):
        u = xc[0] - jnp.einsum('bhck,bhkv->bhcv', xc[1], s)
        s_new = s * xc[3][..., None, None] + jnp.einsum('bhck,bhcv->bhkv', xc[2], u)
        if not with_out:
            return s_new, None
        o = jnp.einsum('bhck,bhkv->bhcv', xc[5], s) + jnp.einsum('bhcj,bhjv->bhcv', xc[4], u)
        return s_new, o

    s_last, o = lax.scan(step, s0, xs)
    if not with_out:
        return None, s_last
    return jnp.moveaxis(o, 0, 2).reshape(b, h, n, dv), s_last


def gdn_group(pl, pc, conv_w, a_log, dt_bias, norm_g, with_ctx_out):
    def prep(p):
        b, n, _ = p['gdn_qkv'].shape
        qkv = jax.nn.silu(short_conv(p['gdn_qkv'], conv_w)).astype(F32)
        qkv = qkv.reshape(b, n, 3, GROUP_HEADS, HEAD_DIM).transpose(2, 0, 3, 1, 4)
        q = l2_normalize(qkv[0]) * HEAD_DIM ** -0.5
        k = l2_normalize(qkv[1])
        a = p['gdn_a'].astype(F32).reshape(b, n, 2, GROUP_HEADS).transpose(2, 0, 3, 1)
        log_a = -jnp.exp(a_log.astype(F32))[:, None, :, None] * jax.nn.softplus(
            a + dt_bias.astype(F32)[:, None, :, None])
        beta = jax.nn.sigmoid(p['gdn_b'].astype(F32).reshape(b, n, 2, GROUP_HEADS).transpose(2, 0, 3, 1))
        return q, k, qkv[2], log_a, beta

    ql, kl, vl, la_l, bt_l = prep(pl)
    qc, kc, vc, la_c, bt_c = prep(pc)
    s0 = jnp.zeros((ql.shape[0], GROUP_HEADS, HEAD_DIM, HEAD_DIM), F32)
    o_l, o_c = 0.0, 0.0
    for dr in range(2):
        f = flip_seq if dr else (lambda t: t)
        oc, s_ctx = gdn_scan(f(qc), f(kc), f(vc), f(la_c[dr]), f(bt_c[dr]), s0, with_ctx_out)
        ol, _ = gdn_scan(f(ql), f(kl), f(vl), f(la_l[dr]), f(bt_l[dr]), s_ctx, True)
        o_l = o_l + f(ol)
        if with_ctx_out:
            o_c = o_c + f(oc)

    def finish(o, gate):
        o = o.transpose(0, 2, 1, 3)
        gt = gate.astype(F32).reshape(o.shape)
        return (rms_norm(o, norm_g) * jax.nn.silu(gt)).reshape(gate.shape).astype(gate.dtype)

    return finish(o_l, pl['gdn_g']), (finish(o_c, pc['gdn_g']) if with_ctx_out else None)


def swa_group(pl, pc, sink, ang_axial, with_ctx_out):
    hkv = SWA_KV_HEADS
    grp = GROUP_HEADS // hkv
    scale = HEAD_DIM ** -0.5
    b, n, _ = pl['swa_q'].shape
    lc = pc['swa_q'].shape[1]
    ql = rotate(pl['swa_q'].reshape(b, n, hkv, grp, HEAD_DIM), ang_axial[:, None, None, :])
    kl = rotate(pl['swa_k'].reshape(b, n, hkv, HEAD_DIM), ang_axial[:, None, :])
    vl = pl['swa_v'].reshape(b, n, hkv, HEAD_DIM)
    qc = pc['swa_q'].reshape(b, lc, hkv, grp, HEAD_DIM)
    kc = pc['swa_k'].reshape(b, lc, hkv, HEAD_DIM)
    vc = pc['swa_v'].reshape(b, lc, hkv, HEAD_DIM)
    sink = sink.astype(F32).reshape(hkv, grp)[:, :, None, None]
    blk = WINDOW
    nb = n // blk

    def band(t):
        tp = jnp.pad(t, ((0, 0), (blk, blk), (0, 0), (0, 0))).reshape(b, nb + 2, blk, hkv, HEAD_DIM)
        return jnp.concatenate([tp[:, :-2], tp[:, 1:-1], tp[:, 2:]], axis=2)

    kb, vb = band(kl), band(vl)
    qb = ql.reshape(b, nb, blk, hkv, grp, HEAD_DIM)
    qpos = jnp.arange(nb)[:, None] * blk + jnp.arange(blk)[None]
    kpos = jnp.arange(nb)[:, None] * blk - blk + jnp.arange(3 * blk)[None]
    valid = ((jnp.abs(qpos[:, :, None] - kpos[:, None, :]) <= WINDOW)
             & (kpos[:, None, :] >= 0) & (kpos[:, None, :] < n))
    s_band = jnp.einsum('bnqhgd,bnkhd->bnhgqk', qb, kb, preferred_element_type=F32) * scale
    s_band = jnp.where(valid[None, :, None, None], s_band, -jnp.inf)
    s_ctx = jnp.einsum('bnqhgd,bkhd->bnhgqk', qb, kc, preferred_element_type=F32) * scale
    p = sink_softmax(jnp.concatenate([s_band, s_ctx], -1), sink)
    o = (jnp.einsum('bnhgqk,bnkhd->bnqhgd', p[..., :3 * blk].astype(vb.dtype), vb)
         + jnp.einsum('bnhgqk,bkhd->bnqhgd', p[..., 3 * blk:].astype(vc.dtype), vc))
    y_l = o.reshape(b, n, GROUP_W)
    y_c = None
    if with_ctx_out:
        sc = jnp.einsum('bqhgd,bkhd->bhgqk', qc, kc, preferred_element_type=F32) * scale
        pcx = sink_softmax(sc, sink)
        y_c = jnp.einsum('bhgqk,bkhd->bqhgd', pcx.astype(vc.dtype), vc).reshape(b, lc, GROUP_W)
    return y_l, y_c


def expert_choice_ffn(u, router_w, w_gate, w_up, w_down):
    b, n, d = u.shape
    cap = CAPACITY_FACTOR * n // N_EXPERTS
    aff = jax.nn.softmax(jnp.einsum('bnd,de->bne', u, router_w, preferred_element_type=F32), -1)
    weight, idx = lax.top_k(jnp.swapaxes(aff, 1, 2), cap)
    xin = jax.vmap(lambda ub, ib: ub[ib])(u, idx)
    hid = jax.nn.silu(jnp.einsum('becd,edf->becf', xin, w_gate)) * jnp.einsum('becd,edf->becf', xin, w_up)
    y = jnp.einsum('becf,efd->becd', hid, w_down) * weight[..., None].astype(u.dtype)
    flat = (jnp.arange(b)[:, None, None] * n + idx).reshape(-1)
    return jnp.zeros((b * n, d), u.dtype).at[flat].add(y.reshape(-1, d)).reshape(b, n, d)


def setup_inputs(seed: int = 0) -> dict:
    key = jax.random.key(seed)
    ks = jax.random.split(key, 24)
    D = D_MODEL
    nrm = lambda k, shape, s: jax.random.normal(k, shape, F32) * s
    x = nrm(ks[0], (BATCH, SEQ, D), 1.0)
    c = nrm(ks[1], (BATCH, D), 1.0)
    ctx = nrm(ks[2], (BATCH, CTX_LEN, D), 1.0)
    c_ctx = nrm(ks[3], (D,), 1.0)
    ada_w = nrm(ks[4], (DEPTH, D, 6 * D), 0.5 * D ** -0.5)
    ada_b = nrm(ks[5], (DEPTH, 6 * D), 0.02)
    w_in = nrm(ks[6], (DEPTH, D, IN_W), D ** -0.5)
    w_out = nrm(ks[7], (DEPTH, MIX_W, D), DEEPNORM_BETA * MIX_W ** -0.5)
    gamma_logit = jnp.log(2.0 ** (5.0 + jnp.arange(GROUP_HEADS, dtype=F32)) - 1.0)
    ret_decay = gamma_logit + nrm(ks[8], (DEPTH, 2, GROUP_HEADS), 0.1)
    diff_lambda = nrm(ks[9], (DEPTH, 4, DIFF_QK_DIM), 0.1)
    diff_norm = 1.0 + nrm(ks[10], (DEPTH, HEAD_DIM), 0.02)
    gdn_conv = nrm(ks[11], (DEPTH, SHORT_CONV, 3 * GROUP_W), SHORT_CONV ** -0.5)
    gdn_a_log = jnp.log(jax.random.uniform(ks[12], (DEPTH, 2, GROUP_HEADS), F32, 1.0, 16.0))
    dt = jnp.exp(jax.random.uniform(ks[13], (DEPTH, 2, GROUP_HEADS), F32, math.log(1e-3), math.log(1e-1)))
    gdn_dt_bias = dt + jnp.log(-jnp.expm1(-dt))
    gdn_norm = 1.0 + nrm(ks[14], (DEPTH, HEAD_DIM), 0.02)
    swa_sink = nrm(ks[15], (DEPTH, GROUP_HEADS), 0.5)
    ln_g = 1.0 + nrm(ks[16], (DEPTH, 2, D), 0.02)
    ln_b = nrm(ks[17], (DEPTH, 2, D), 0.02)
    router_w = nrm(ks[18], (DEPTH, D, N_EXPERTS), D ** -0.5)
    w_gate = nrm(ks[19], (DEPTH, N_EXPERTS, D, EXPERT_FF), D ** -0.5)
    w_up = nrm(ks[20], (DEPTH, N_EXPERTS, D, EXPERT_FF), D ** -0.5)
    w_down = nrm(ks[21], (DEPTH, N_EXPERTS, EXPERT_FF, D), DEEPNORM_BETA * EXPERT_FF ** -0.5)
    return {'x': x, 'c': c, 'ctx': ctx, 'c_ctx': c_ctx, 'ada_w': ada_w, 'ada_b': ada_b,
            'w_in': w_in, 'w_out': w_out, 'ret_decay': ret_decay, 'diff_lambda': diff_lambda,
            'diff_norm': diff_norm, 'gdn_conv': gdn_conv, 'gdn_a_log': gdn_a_log,
            'gdn_dt_bias': gdn_dt_bias, 'gdn_norm': gdn_norm, 'swa_sink': swa_sink,
            'ln_g': ln_g, 'ln_b': ln_b, 'router_w': router_w, 'w_gate': w_gate,
            'w_up': w_up, 'w_down': w_down}


def reference(x, c, ctx, c_ctx, ada_w, ada_b, w_in, w_out, ret_decay, diff_lambda, diff_norm,
              gdn_conv, gdn_a_log, gdn_dt_bias, gdn_norm, swa_sink, ln_g, ln_b,
              router_w, w_gate, w_up, w_down):
    b, n, d = x.shape
    ang_diff = axial_rope_angles(n, DIFF_QK_DIM)
    ang_swa = axial_rope_angles(n, HEAD_DIM)
    cond_l = jax.nn.silu(c)
    cond_c = jax.nn.silu(c_ctx)
    h, hc = x, ctx
    for layer in range(DEPTH):
        full_ctx = layer < DEPTH - 1
        mod_l = (cond_l @ ada_w[layer] + ada_b[layer]).reshape(b, 1, 6, d)
        mod_c = (cond_c @ ada_w[layer] + ada_b[layer]).reshape(1, 1, 6, d)
        u_l = h * (1.0 + mod_l[:, :, 1]) + mod_l[:, :, 0]
        u_c = hc * (1.0 + mod_c[:, :, 1]) + mod_c[:, :, 0]
        pl = split_cols(u_l @ w_in[layer])
        pc = split_cols(u_c @ w_in[layer])
        ya_l, ya_c = retention_group(pl, pc, ret_decay[layer], full_ctx)
        yb_l, yb_c = diff_attention_group(pl, pc, diff_lambda[layer], diff_norm[layer], layer, ang_diff, full_ctx)
        yc_l, yc_c = gdn_group(pl, pc, gdn_conv[layer], gdn_a_log[layer], gdn_dt_bias[layer], gdn_norm[layer], full_ctx)
        yd_l, yd_c = swa_group(pl, pc, swa_sink[layer], ang_swa, full_ctx)
        mix_l = jnp.concatenate([ya_l, yb_l, yc_l, yd_l], -1) @ w_out[layer]
        h = layer_norm(DEEPNORM_ALPHA * h + mod_l[:, :, 2] * mix_l, ln_g[layer, 0], ln_b[layer, 0])
        if full_ctx:
            mix_c = jnp.concatenate([ya_c, yb_c, yc_c, yd_c], -1) @ w_out[layer]
            hc = layer_norm(DEEPNORM_ALPHA * hc + mod_c[:, :, 2] * mix_c, ln_g[layer, 0], ln_b[layer, 0])
        f_l = expert_choice_ffn(h * (1.0 + mod_l[:, :, 4]) + mod_l[:, :, 3],
                                router_w[layer], w_gate[layer], w_up[layer], w_down[layer])
        h = layer_norm(DEEPNORM_ALPHA * h + mod_l[:, :, 5] * f_l, ln_g[layer, 1], ln_b[layer, 1])
        if full_ctx:
            f_c = expert_choice_ffn(hc * (1.0 + mod_c[:, :, 4]) + mod_c[:, :, 3],
                                    router_w[layer], w_gate[layer], w_up[layer], w_down[layer])
            hc = layer_norm(DEEPNORM_ALPHA * hc + mod_c[:, :, 5] * f_c, ln_g[layer, 1], ln_b[layer, 1])
    return h
```

```python
import math
import numpy as np
import ml_dtypes
from contextlib import ExitStack
import concourse.bass as bass
import concourse.mybir as mybir
from concourse.bass_utils import run_bass_kernel_spmd

F32 = mybir.dt.float32
BF16 = mybir.dt.bfloat16
AF = mybir.ActivationFunctionType
ALU = mybir.AluOpType
AX = mybir.AxisListType

ENGS = ["pe", "dve", "act", "pool", "sp"]
DMA_RING = 6
SEM_EPOCH = 20000

D = 1024
DEPTH = 4
ALPHA = (2 * DEPTH) ** 0.25
LN_EPS = 1e-5
IN_W = 3344
NEG = -30000.0


class Sched:
    def __init__(self, nc, es, same_engine_sync=True):
        self.nc = nc
        self.es = es
        self.q = {e: [] for e in ENGS}
        self.cnt = {e: 0 for e in ENGS}
        self.epoch = {e: 0 for e in ENGS}
        self.sem = {e: es.enter_context(nc.semaphore(f"s_{e}_0")) for e in ENGS}
        self.dq = ["sp", "act", "pool"]
        self.dsem = {e: [es.enter_context(nc.semaphore(f"d_{e}_{i}")) for i in range(DMA_RING)]
                     for e in self.dq}
        self.dcnt = {e: 0 for e in self.dq}
        self.lastw = {}
        self.readers = {}
        self.seen = {}
        self.same = same_engine_sync
        self.ninst = 0

    def _key(self, k):
        return k if isinstance(k, str) else k.k

    def _need(self, eng, tok, waits):
        if tok is None:
            return
        sem, val, prod = tok
        if prod == eng and (eng == "pe" or not self.same):
            return
        kk = (eng, id(sem))
        if self.seen.get(kk, 0) >= val:
            return
        self.seen[kk] = val
        waits.append((sem, val))

    def _deps(self, eng, reads, writes):
        waits = []
        for k in reads:
            k = self._key(k)
            self._need(eng, self.lastw.get(k), waits)
            if k.startswith("ps"):
                for t in self.readers.get(k, ()):
                    if t[2] != eng:
                        self._need(eng, t, waits)
        for k in writes:
            k = self._key(k)
            self._need(eng, self.lastw.get(k), waits)
            for t in self.readers.get(k, ()):
                self._need(eng, t, waits)
        return waits

    def _commit(self, tok, reads, writes):
        for k in reads:
            self.readers.setdefault(self._key(k), []).append(tok)
        for k in writes:
            k = self._key(k)
            self.lastw[k] = tok
            self.readers[k] = []

    def op(self, eng, fn, reads=(), writes=()):
        waits = self._deps(eng, reads, writes)
        if self.cnt[eng] >= SEM_EPOCH:
            self.epoch[eng] += 1
            self.sem[eng] = self.es.enter_context(self.nc.semaphore(f"s_{eng}_{self.epoch[eng]}"))
            self.cnt[eng] = 0
        self.cnt[eng] += 1
        tok = (self.sem[eng], self.cnt[eng], eng)
        self.q[eng].append((waits, fn, self.sem[eng], 1))
        self._commit(tok, reads, writes)
        self.ninst += 1

    def dma(self, qn, fn, reads=(), writes=()):
        waits = self._deps(qn, reads, writes)
        i = self.dcnt[qn]
        self.dcnt[qn] += 1
        slot, rnd = i % DMA_RING, i // DMA_RING
        sem = self.dsem[qn][slot]
        if rnd > 0:
            self._need(qn, (sem, 16 * rnd, None), waits)
        tok = (sem, 16 * (rnd + 1), None)
        self.q[qn].append((waits, fn, sem, 16))
        self._commit(tok, reads, writes)
        self.ninst += 1

    def barrier(self):
        toks = []
        for e in ENGS:
            if self.cnt[e] > 0:
                toks.append((self.sem[e], self.cnt[e], e))
        for qn in self.dq:
            n = self.dcnt[qn]
            for slot in range(DMA_RING):
                if n > slot:
                    rounds = (n - 1 - slot) // DMA_RING + 1
                    toks.append((self.dsem[qn][slot], 16 * rounds, None))
        for e in ENGS:
            waits = []
            for t in toks:
                if t[2] == e:
                    continue
                self._need(e, t, waits)
            self.q[e].append((waits, None, None, 0))
        self.lastw = {}
        self.readers = {}

    def emit(self):
        nc = self.nc
        q = self.q

        def run(e, name):
            for waits, fn, sem, inc in q[name]:
                for (s, v) in waits:
                    e.wait_ge(s, v)
                if fn is not None:
                    fn(e).then_inc(sem, inc)

        with nc.Block() as block:
            @block.tensor
            def _(e):
                run(e, "pe")

            @block.vector
            def _(e):
                run(e, "dve")

            @block.scalar
            def _(e):
                run(e, "act")

            @block.gpsimd
            def _(e):
                run(e, "pool")

            @block.sync
            def _(e):
                run(e, "sp")
        self.q = {e: [] for e in ENGS}


class Buf:
    def __init__(self, t, k):
        self.t = t
        self.k = k

    def sub(self, *idx):
        return self.k + "/" + "/".join(str(i) for i in idx)

    def __getitem__(self, key):
        return self.t[key]


class Ctx:
    def __init__(self, nc, es, same_engine_sync=True):
        self.nc = nc
        self.es = es
        self.S = Sched(nc, es, same_engine_sync)
        self.ps = [self.psum(f"ps{i}") for i in range(8)]
        self.psi = 0
        self.uid = 0

    def sb(self, name, shape, dtype=F32, es=None):
        self.uid += 1
        nm = f"{name}_{self.uid}"
        t = (es or self.es).enter_context(self.nc.sbuf_tensor(nm, list(shape), dtype))
        return Buf(t, nm)

    def psum(self, name, shape=(128, 512), dtype=F32):
        t = self.es.enter_context(self.nc.psum_tensor(name, list(shape), dtype))
        return Buf(t, name)

    def dram(self, name, shape, dtype=F32, kind="Internal"):
        t = self.nc.dram_tensor(name, list(shape), dtype, kind=kind)
        return Buf(t, name)

    def nextps(self):
        r = getattr(self, "rot", None) or list(range(8))
        p = self.ps[r[self.psi % len(r)]]
        self.psi += 1
        return p


def fm_groups():
    def swp(cols, blk):
        cols = np.asarray(cols)
        half = blk // 2
        c = cols.reshape(-1, blk)
        return np.concatenate([c[:, half:], c[:, :half]], 1).reshape(-1)
    g = []
    def add_rot(name, start, width, blk, kind):
        for i in range(width // 128):
            cols = np.arange(start + i * 128, start + (i + 1) * 128)
            g.append((f"{name}{i}", cols, kind, swp(cols, blk)))
    add_rot("ret_q", 0, 256, 64, 0)
    add_rot("ret_k", 256, 256, 64, 0)
    add_rot("diff_q", 1024, 256, 32, 1)
    add_rot("diff_k", 1280, 256, 32, 1)
    for i in range(6):
        g.append((f"gdn{i}", np.arange(1792 + i * 128, 1792 + (i + 1) * 128), None, None))
    add_rot("swa_q", 2832, 256, 64, 2)
    add_rot("swa_k", 3088, 128, 64, 2)
    return g


TM_GROUPS = [("ret_vg", 512, 512), ("diff_v", 1536, 256), ("gdn_gab", 2560, 272), ("swa_v", 3216, 128)]


def build_w_in_ext(w_in_l):
    cols = []
    for name, c, kind, sw in fm_groups():
        cols.append(c)
        if kind is not None:
            cols.append(sw)
    for name, st, w in TM_GROUPS:
        cols.append(np.arange(st, st + w))
    cols = np.concatenate(cols)
    return np.ascontiguousarray(w_in_l[:, cols])


def rope_tables(NLT):
    n = NLT * 128
    T = n + 256
    tabs = np.zeros((3, 2, 128, T), np.float32)
    tabs[:, 0] = 1.0
    f32 = np.float32
    theta = (1.0 / (f32(10000.0) ** np.linspace(0.0, 1.0, 32, dtype=np.float32))).astype(np.float32)
    ang_ret = (np.arange(n, dtype=np.float32)[:, None] * theta).astype(np.float32)
    def axial(rot_dim):
        rows = n // 64
        row = np.repeat(np.arange(rows, dtype=np.float32), 64)
        col = np.tile(np.arange(64, dtype=np.float32), rows)
        nf = rot_dim // 4
        inv = (f32(10000.0) ** (-np.arange(nf, dtype=np.float32) / f32(nf))).astype(np.float32)
        return np.concatenate([row[:, None] * inv, col[:, None] * inv], -1).astype(np.float32)
    ang_diff = axial(32)
    ang_swa = axial(64)
    for kind, (ang, blk) in enumerate([(ang_ret, 64), (ang_diff, 32), (ang_swa, 64)]):
        half = blk // 2
        for f in range(128):
            d = f % blk
            j = d % half
            c = np.cos(ang[:, j]).astype(np.float32)
            s = np.sin(ang[:, j]).astype(np.float32)
            tabs[kind, 0, f, :n] = c
            tabs[kind, 1, f, :n] = -s if d < half else s
    return tabs


def build(NLT=32, NL=4, dbg=(), phases="A,BC,ret,diff,swa,gdn,E,F,G"):
    nc = bass.Bass("TRN2", target_bir_lowering=False)
    NT = NLT + 2
    T = NT * 128
    NLAT = NLT * 128
    FMG = fm_groups()
    NFM = len(FMG)
    NWE = sum(128 * (2 if g[2] is not None else 1) for g in FMG) + sum(w for _, _, w in TM_GROUPS)

    def din(name, shape, dt=F32):
        return nc.dram_tensor(name, list(shape), dt, kind="ExternalInput")

    h0 = din("h0", [T, D])
    cc = din("cc", [128, 8, 2])
    ada_w = din("ada_w", [NL, D, 6 * D])
    ada_b = din("ada_b", [NL, 6 * D])
    ada_bT = din("ada_bT", [NL, 128, 48])
    w_in = din("w_in", [NL, D, NWE])
    w_out = din("w_out", [NL, D, D])
    rope = din("rope", [3, 2, 128, T])
    ret_decay = din("ret_decay", [NL, 8])
    ln_g = din("ln_g", [NL, 2 * D])
    ln_b = din("ln_b", [NL, 2 * D])
    cmask = din("cmask", [128, 128])
    diff_lambda = din("diff_lambda", [NL, 128])
    diff_norm = din("diff_norm", [NL, 64])
    swa_sink = din("swa_sink", [NL, 4])
    router_wp = din("router_wp", [NL, D, 128])
    w_gate = din("w_gate", [NL, 16, D, 2 * D])
    w_up = din("w_up", [NL, 16, D, 2 * D])
    w_down = din("w_down", [NL, 16, 2 * D, D])
    iota512 = din("iota512", [128, 512])
    jcol_in = din("jcol_in", [128, 4])
    selE_in = din("selE_in", [16, 16, 128])
    gdn_convT = din("gdn_convT", [NL, 128, 18])
    gdn_a_log = din("gdn_a_log", [NL, 8])
    gdn_dt_bias = din("gdn_dt_bias", [NL, 8])
    gdn_norm = din("gdn_norm", [NL, 64])
    out = nc.dram_tensor("out", [NLAT, D], F32, kind="ExternalOutput")

    dbg_t = {}

    with ExitStack() as es:
        C = Ctx(nc, es)
        S = C.S
        H = C.dram("H", [T, D], kind=("ExternalOutput" if "H" in dbg else "Internal"))
        FMO = C.dram("FMO", [NFM, 128, T], BF16)
        TMA = C.dram("TMA", [T, 512], BF16)
        TMB = C.dram("TMB", [T, 256], BF16)
        TMC = C.dram("TMC", [T, 256], BF16)
        GAB = C.dram("GAB", [128, NT * 16], F32)
        TMD = C.dram("TMD", [T, 128], BF16)
        GFM = C.dram("GFM", [4, 128, T], BF16)
        GTM = C.dram("GTM", [T, 512], BF16)
        U2 = C.dram("U2", [T, D], BF16)
        LGD = C.dram("LGD", [16, T], F32)
        YE = C.dram("YE", [16, 640, D], BF16)
        KTD = C.dram("KTD", [128, NT * 16], F32)
        KEYD = C.dram("KEYD", [16, T], F32)
        AFFD = C.dram("AFFD", [16, T], F32)
        Y = C.dram("Y", [T, D], BF16, kind=("ExternalOutput" if "Y" in dbg else "Internal"))
        if "FMO" in dbg:
            dbg_t["FMO"] = FMO
        fm_index = {g[0]: i for i, g in enumerate(FMG)}

        ident_f = C.sb("ident_f", [128, 128])
        ident_b = C.sb("ident_b", [128, 128], BF16)
        condT = C.sb("condT", [128, 8, 2])
        condB = [C.sb(f"condB{i}", [128, 8, 128]) for i in range(2)]
        modT = C.sb("modT", [128, 8, 4, 2])
        gateB = C.sb("gateB", [128, 4, 2, D])
        relpos = C.sb("relpos", [128, 128])

        S.op("pool", lambda e: e.memset(ident_f[:], 0.0), [], [ident_f])
        S.op("pool", lambda e: e.affine_select(out=ident_f[:], in_=ident_f[:], pattern=[[-1, 128]],
                                               compare_op=ALU.not_equal, fill=1.0, base=0, channel_multiplier=1),
             [ident_f], [ident_f])
        S.op("dve", lambda e: e.tensor_copy(out=ident_b[:], in_=ident_f[:]), [ident_f], [ident_b])
        S.dma("sp", lambda e: e.dma_start(out=condT[:], in_=cc[:, :, :]), [], [condT])
        S.dma("sp", lambda e: e.dma_start(out=relpos[:], in_=cmask[:, :]), [], [relpos])
        S.op("act", lambda e: e.activation(out=condT[:], in_=condT[:], func=AF.Silu), [condT], [condT])
        for lc in range(2):
            for k in range(8):
                S.op("dve", lambda e, lc=lc, k=k: e.tensor_copy(
                    out=condB[lc][:, k, :], in_=condT[:, k, lc:lc + 1].broadcast_to([128, 128])),
                    [condT], [condB[lc]])
        S.barrier()
        S.emit()

        def phase_A(l):
            with ExitStack() as pes:
                wa = [C.sb(f"wa{i}", [128, 8, 512], es=pes) for i in range(2)]
                bbc = [C.sb(f"bbc{i}", [128, 512], es=pes) for i in range(2)]
                bT = C.sb("bT", [128, 48], es=pes)
                S.dma("sp", lambda e: e.dma_start(out=bT[:], in_=ada_bT[l, :, :]), [], [bT])
                import os
                ADBG = int(os.environ.get("ADBG", 9))
                for g in range(12):
                    m, half = g // 2, g % 2
                    w = wa[g % 2]
                    S.dma("sp", lambda e, w=w, g=g: e.dma_start(
                        out=w[:], in_=ada_w[l, :, g * 512:(g + 1) * 512].rearrange("(k p) n -> p k n", p=128)),
                        [], [w])
                    if m in (2, 5, 3, 4):
                        mi = {2: 0, 5: 1, 3: 2, 4: 3}[m]
                        bb = bbc[g % 2]
                        S.dma("act", lambda e, bb=bb, g=g: e.dma_start(
                            out=bb[:], in_=ada_b[l:l + 1, g * 512:(g + 1) * 512].broadcast_to([128, 512])), [], [bb])
                        if m == 4:
                            S.op("dve", lambda e, bb=bb: e.tensor_scalar_add(out=bb[:], in0=bb[:], scalar1=1.0), [bb], [bb])
                        for lc in range(2):
                            ps = C.nextps()
                            for k in range(8):
                                S.op("pe", lambda e, ps=ps, w=w, lc=lc, k=k: e.matmul(
                                    ps[:, :], lhsT=condB[lc][:, k, :], rhs=w[:, k, :], start=(k == 0), stop=(k == 7)),
                                    [condB[lc], w], [ps])
                            S.op("dve", lambda e, ps=ps, bb=bb, mi=mi, lc=lc, half=half: e.tensor_tensor(
                                out=gateB[:, mi, lc, half * 512:(half + 1) * 512], in0=ps[:, :], in1=bb[:], op=ALU.add),
                                [ps, bb], [gateB])
                    if m in (0, 1, 3, 4):
                        mi = {0: 0, 1: 1, 3: 2, 4: 3}[m]
                        for j in range(4):
                            kk = half * 4 + j
                            ps = C.nextps()
                            for lc in range(2):
                                for k in range(8):
                                    S.op("pe", lambda e, ps=ps, w=w, j=j, k=k, lc=lc: e.matmul(
                                        ps[:, lc * 128:(lc + 1) * 128], lhsT=w[:, k, j * 128:(j + 1) * 128],
                                        rhs=condB[lc][:, k, :], start=(k == 0), stop=(k == 7)), [condB[lc], w], [ps])
                            addone = 1.0 if m in (1, 4) else 0.0
                            for lc in range(2):
                                S.op("dve", lambda e, ps=ps, kk=kk, mi=mi, m=m, addone=addone, lc=lc: e.tensor_scalar(
                                    out=modT[:, kk, mi, lc:lc + 1], in0=ps[:, lc * 128:lc * 128 + 1],
                                    scalar1=bT[:, m * 8 + kk:m * 8 + kk + 1],
                                    scalar2=addone, op0=ALU.add, op1=ALU.add), [ps, bT], [modT])
                S.barrier()
                S.emit()

        def phase_BC(l):
            src = h0 if l == 0 else H.t
            with ExitStack() as pes:
                uT = C.sb("uT", [128, 8, T], BF16, es=pes)
                wsb = C.sb("wsb", [128, 8, NWE], BF16, es=pes)
                bes = ExitStack()
                hts = [C.sb(f"ht{i}", [128, D], es=bes) for i in range(2)]
                for k in range(8):
                    S.dma("pool", lambda e, k=k: e.dma_start(out=wsb[:, k, :], in_=w_in[l, k * 128:(k + 1) * 128, :]),
                          [], [wsb.sub(k)])
                wkeys = [wsb.sub(k) for k in range(8)]
                for t in range(NT):
                    lc = 0 if t < NLT else 1
                    ht = hts[t % 2]
                    S.dma("sp", lambda e, ht=ht, t=t: e.dma_start(out=ht[:], in_=src[t * 128:(t + 1) * 128, :]),
                          ["H/%d" % t], [ht])
                    for half in range(2):
                        ps = C.nextps()
                        for j in range(4):
                            k = half * 4 + j
                            S.op("pe", lambda e, ps=ps, ht=ht, j=j, k=k: e.transpose(
                                ps[:, j * 128:(j + 1) * 128], ht[:, k * 128:(k + 1) * 128], ident_f[:]),
                                [ht, ident_f], [ps])
                        for j in range(4):
                            k = half * 4 + j
                            S.op("act", lambda e, ps=ps, j=j, k=k, t=t, lc=lc: e.activation(
                                out=uT[:, k, t * 128:(t + 1) * 128], in_=ps[:, j * 128:(j + 1) * 128], func=AF.Identity,
                                bias=modT[:, k, 0, lc:lc + 1], scale=modT[:, k, 1, lc:lc + 1]),
                                [ps, modT], [uT.sub(t)])
                S.barrier()
                S.emit()
                bes.close()
                import os
                BDBG = int(os.environ.get("BDBG", 9))
                tabs1 = [C.sb(f"tab_{kd}", [128, 2, 512], es=pes) for kd in range(3)]
                tabs = [tabs1, tabs1]
                evs = [C.sb(f"ev{i}", [128, 512], BF16, es=pes) for i in range(3)]
                tmp = [C.sb(f"tmp{i}", [128, 512], es=pes) for i in range(2)]
                tmo = [C.sb(f"tmo{i}", [128, 512], BF16, es=pes) for i in range(2)]
                gabAll = C.sb("gabAll", [128, NT * 16], es=pes)
                offs = {}
                o = 0
                for name, c, kind, sw in FMG:
                    offs[name] = o
                    o += 128 * (2 if kind is not None else 1)
                for name, st, w in TM_GROUPS:
                    offs[name] = o
                    o += w
                nblk = (T + 511) // 512
                evc = 0
                for b in range(nblk if BDBG >= 2 else 0):
                    t0 = b * 512
                    tw = min(512, T - t0)
                    tiles = list(range(t0 // 128, (t0 + tw) // 128))
                    ukeys = [uT.sub(t) for t in tiles]
                    tb = tabs[b % 2]
                    for kd in range(3):
                        S.dma("act", lambda e, tb=tb, kd=kd, t0=t0, tw=tw: e.dma_start(
                            out=tb[kd][:, :, 0:tw], in_=rope[kd, :, :, t0:t0 + tw].rearrange("c p t -> p c t")),
                            [], [tb[kd]])
                    for gi, (name, c, kind, sw) in enumerate(FMG):
                        co = offs[name]
                        psa = C.nextps()
                        for k in range(8):
                            S.op("pe", lambda e, psa=psa, k=k, co=co, t0=t0, tw=tw: e.matmul(
                                psa[:, 0:tw], lhsT=wsb[:, k, co:co + 128], rhs=uT[:, k, t0:t0 + tw],
                                start=(k == 0), stop=(k == 7)), wkeys + ukeys, [psa])
                        ev = evs[evc % 3]
                        evc += 1
                        if kind is None:
                            S.op("act", lambda e, ev=ev, psa=psa, tw=tw: e.copy(out=ev[:, 0:tw], in_=psa[:, 0:tw]),
                                 [psa], [ev])
                        else:
                            psb = C.nextps()
                            for k in range(8):
                                S.op("pe", lambda e, psb=psb, k=k, co=co, t0=t0, tw=tw: e.matmul(
                                    psb[:, 0:tw], lhsT=wsb[:, k, co + 128:co + 256], rhs=uT[:, k, t0:t0 + tw],
                                    start=(k == 0), stop=(k == 7)), wkeys + ukeys, [psb])
                            ta, tbb = tmp
                            S.op("dve", lambda e, ta=ta, psa=psa, tb=tb, kind=kind, tw=tw: e.tensor_tensor(
                                out=ta[:, 0:tw], in0=psa[:, 0:tw], in1=tb[kind][:, 0, 0:tw], op=ALU.mult),
                                [psa, tb[kind]], [ta])
                            S.op("dve", lambda e, tbb=tbb, psb=psb, tb=tb, kind=kind, tw=tw: e.tensor_tensor(
                                out=tbb[:, 0:tw], in0=psb[:, 0:tw], in1=tb[kind][:, 1, 0:tw], op=ALU.mult),
                                [psb, tb[kind]], [tbb])
                            S.op("pool", lambda e, ev=ev, ta=ta, tbb=tbb, tw=tw: e.tensor_tensor(
                                out=ev[:, 0:tw], in0=ta[:, 0:tw], in1=tbb[:, 0:tw], op=ALU.add), [ta, tbb], [ev])
                        S.dma("sp", lambda e, ev=ev, gi=gi, t0=t0, tw=tw: e.dma_start(
                            out=FMO.t[gi, :, t0:t0 + tw], in_=ev[:, 0:tw]), [ev], ["FMO/%d/%d" % (gi, b)])
                    for t in (tiles if BDBG >= 3 else []):
                        for (name, st, w), dst in zip(TM_GROUPS, [TMA, TMB, TMC, TMD]):
                            co = offs[name]
                            ps = C.nextps()
                            for k in range(8):
                                S.op("pe", lambda e, ps=ps, k=k, co=co, w=w, t=t: e.matmul(
                                    ps[:, 0:w], lhsT=uT[:, k, t * 128:(t + 1) * 128], rhs=wsb[:, k, co:co + w],
                                    start=(k == 0), stop=(k == 7)), wkeys + [uT.sub(t)], [ps])
                            ob = tmo[evc % 2]
                            evc += 1
                            wd = min(w, 256) if name == "gdn_gab" else w
                            S.op("act", lambda e, ob=ob, ps=ps, wd=wd: e.copy(out=ob[:, 0:wd], in_=ps[:, 0:wd]),
                                 [ps], [ob])
                            S.dma("sp", lambda e, ob=ob, dst=dst, t=t, wd=wd: e.dma_start(
                                out=dst.t[t * 128:(t + 1) * 128, 0:wd], in_=ob[:, 0:wd]), [ob],
                                [dst.sub(t)])
                            if name == "gdn_gab" and BDBG >= 4:
                                S.op("dve", lambda e, ps=ps, t=t: e.tensor_copy(out=gabAll[:, t * 16:(t + 1) * 16], in_=ps[:, 256:272]),
                                     [ps], [gabAll])
                if BDBG >= 4:
                    S.dma("sp", lambda e: e.dma_start(out=GAB.t[:, :], in_=gabAll[:]), [gabAll], [GAB])
                S.barrier()
                S.emit()

        def phase_ret(l, full_ctx):
            gq = fm_index["ret_q0"]
            gk = fm_index["ret_k0"]
            seq_f = [NLT, NLT + 1] + list(range(NLT))
            with ExitStack() as pes:
                rd = C.sb("rd", [128, 8], es=pes)
                lg = C.sb("lg", [128, 8], es=pes)
                MT = C.sb("MT", [128, 4, 128], es=pes)
                mtmp = C.sb("mtmp", [128, 128], es=pes)
                mtmp2 = C.sb("mtmp2", [128, 128], es=pes)
                pcol = C.sb("pcol", [128, 4], es=pes)
                kdec = C.sb("kdec", [128, 2, 256], es=pes)
                qdec = C.sb("qdec", [128, 2, 256], es=pes)
                cdec = C.sb("cdec", [128, 2, 256], es=pes)
                dcol = C.sb("dcol", [128, 8], es=pes)
                S.dma("sp", lambda e: e.dma_start(out=rd[:], in_=ret_decay[l:l + 1, :].broadcast_to([128, 8])), [], [rd])
                S.op("act", lambda e: e.activation(out=lg[:], in_=rd[:], func=AF.Exp, scale=-1.0), [rd], [lg])
                S.op("act", lambda e: e.activation(out=lg[:], in_=lg[:], func=AF.Ln, bias=1.0), [lg], [lg])
                S.op("dve", lambda e: e.tensor_scalar_mul(out=lg[:], in0=lg[:], scalar1=-1.0), [lg], [lg])
                S.op("dve", lambda e: e.tensor_scalar(out=pcol[:, 3:4], in0=relpos[:, 0:1], scalar1=-1.0, scalar2=None,
                                                      op0=ALU.mult), [relpos], [pcol])
                S.op("dve", lambda e: e.tensor_scalar_add(out=pcol[:, 0:1], in0=pcol[:, 3:4], scalar1=1.0), [pcol], [pcol])
                S.op("dve", lambda e: e.tensor_scalar(out=pcol[:, 1:2], in0=pcol[:, 3:4], scalar1=-1.0, scalar2=128.0,
                                                      op0=ALU.mult, op1=ALU.add), [pcol], [pcol])
                S.op("dve", lambda e: e.tensor_scalar(out=pcol[:, 2:3], in0=pcol[:, 3:4], scalar1=-1.0, scalar2=127.0,
                                                      op0=ALU.mult, op1=ALU.add), [pcol], [pcol])
                for dr in range(2):
                    for h in range(4):
                        c = dr * 4 + h
                        if dr == 0:
                            S.op("dve", lambda e: e.tensor_scalar_max(out=mtmp[:], in0=relpos[:], scalar1=0.0), [relpos], [mtmp])
                        else:
                            S.op("dve", lambda e: e.tensor_scalar(out=mtmp[:], in0=relpos[:], scalar1=-1.0, scalar2=0.0,
                                                                  op0=ALU.mult, op1=ALU.max), [relpos], [mtmp])
                        S.op("act", lambda e, c=c: e.activation(out=mtmp[:], in_=mtmp[:], func=AF.Exp, scale=lg[:, c:c + 1]),
                             [mtmp, lg], [mtmp])
                        if dr == 0:
                            S.op("dve", lambda e: e.tensor_scalar(out=mtmp2[:], in0=relpos[:], scalar1=0.0, scalar2=0.125,
                                                                  op0=ALU.is_ge, op1=ALU.mult), [relpos], [mtmp2])
                            S.op("dve", lambda e, h=h: e.tensor_tensor(out=MT[:, h, :], in0=mtmp[:], in1=mtmp2[:], op=ALU.mult),
                                 [mtmp, mtmp2], [MT])
                        else:
                            S.op("dve", lambda e: e.tensor_scalar(out=mtmp2[:], in0=relpos[:], scalar1=0.0, scalar2=0.125,
                                                                  op0=ALU.is_le, op1=ALU.mult), [relpos], [mtmp2])
                            S.op("dve", lambda e: e.tensor_tensor(out=mtmp[:], in0=mtmp[:], in1=mtmp2[:], op=ALU.mult),
                                 [mtmp, mtmp2], [mtmp])
                            S.op("dve", lambda e, h=h: e.tensor_tensor(out=MT[:, h, :], in0=MT[:, h, :], in1=mtmp[:], op=ALU.add),
                                 [mtmp, MT], [MT])
                        S.op("act", lambda e, c=c, dr=dr: e.activation(out=dcol[:, 0:1], in_=pcol[:, (0 if dr == 0 else 1):(1 if dr == 0 else 2)],
                                                                        func=AF.Exp, scale=lg[:, c:c + 1]), [pcol, lg], [dcol])
                        S.op("dve", lambda e, dr=dr, h=h: e.tensor_scalar_mul(
                            out=qdec[:, dr, h * 64:(h + 1) * 64], in0=dcol[:, 0:1].broadcast_to([128, 64]), scalar1=0.125),
                            [dcol], [qdec])
                        S.op("act", lambda e, c=c, dr=dr: e.activation(out=dcol[:, 1:2], in_=pcol[:, (2 if dr == 0 else 3):(3 if dr == 0 else 4)],
                                                                        func=AF.Exp, scale=lg[:, c:c + 1]), [pcol, lg], [dcol])
                        S.op("dve", lambda e, dr=dr, h=h: e.tensor_copy(
                            out=kdec[:, dr, h * 64:(h + 1) * 64], in_=dcol[:, 1:2].broadcast_to([128, 64])), [dcol], [kdec])
                        S.op("act", lambda e, c=c: e.activation(out=dcol[:, 2:3], in_=lg[:, c:c + 1], func=AF.Exp, scale=128.0),
                             [lg], [dcol])
                        S.op("dve", lambda e, dr=dr, h=h: e.tensor_copy(
                            out=cdec[:, dr, h * 64:(h + 1) * 64], in_=dcol[:, 2:3].broadcast_to([128, 64])), [dcol], [cdec])
                import os
                RDBG = int(os.environ.get("RDBG", 9))
                Sprev = C.sb("Sprev", [64, 2, NT, 256], BF16, es=pes)
                dSb = C.sb("dSb", [64, NT, 256], es=pes)
                Srun = C.sb("Srun", [64, 2, 256], es=pes)
                kts = [C.sb(f"kT{i}", [64, 4, 128], BF16, es=pes) for i in range(2)]
                qts = [C.sb(f"qT{i}", [64, 4, 128], BF16, es=pes) for i in range(2)]
                vgs = [C.sb(f"vg{i}", [128, 512], BF16, es=pes) for i in range(2)]
                ktok = [C.sb(f"ktok{i}", [128, 256], BF16, es=pes) for i in range(2)]
                kd = [C.sb(f"kd{i}", [128, 2, 256], BF16, es=pes) for i in range(2)]
                S.op("pool", lambda e: e.memset(Srun[:], 0.0), [], [Srun])

                def load_k(c, i):
                    S.dma("sp", lambda e: e.dma_start(
                        out=kts[i][:], in_=FMO.t[gk:gk + 2, :, c * 128:(c + 1) * 128].rearrange("g (h d) t -> d (g h) t", h=2)),
                        [f"FMO/{gk}/{c // 4}", f"FMO/{gk + 1}/{c // 4}"], [kts[i]])

                def make_ktok(c, i):
                    ps = C.nextps()
                    pv = ps.t[:, 0:128].bitcast(BF16)
                    for h in range(4):
                        S.op("pe", lambda e, pv=pv, h=h: e.transpose(pv[:, h * 64:(h + 1) * 64], kts[i][:, h, :], ident_b[0:64, 0:64]),
                             [kts[i], ident_b], [ps])
                    S.op("act", lambda e, pv=pv: e.copy(out=ktok[i][:], in_=pv[:, :]), [ps], [ktok[i]])

                R2 = int(os.environ.get("R2", 9))
                for n_, c in enumerate(seq_f if RDBG >= 2 else []):
                    i = n_ % 2
                    load_k(c, i)
                    S.dma("act", lambda e, c=c, i=i: e.dma_start(out=vgs[i][:], in_=TMA.t[c * 128:(c + 1) * 128, :]),
                          [TMA.sub(c)], [vgs[i]])
                    if R2 < 2:
                        continue
                    make_ktok(c, i)
                    if R2 < 3:
                        continue
                    S.op("dve", lambda e, i=i: e.tensor_tensor(
                        out=kd[i][:], in0=ktok[i][:].rearrange("p (o c) -> p o c", o=1).broadcast_to([128, 2, 256]),
                        in1=kdec[:], op=ALU.mult), [ktok[i], kdec], [kd[i]])
                    if R2 < 4:
                        continue
                    ps = C.nextps()
                    for dr in range(2):
                        for h in range(4):
                            S.op("pe", lambda e, ps=ps, dr=dr, h=h, i=i: e.matmul(
                                ps[0:64, dr * 256 + h * 64:dr * 256 + (h + 1) * 64], lhsT=kd[i][:, dr, h * 64:(h + 1) * 64],
                                rhs=vgs[i][:, h * 64:(h + 1) * 64], start=True, stop=True), [kd[i], vgs[i]], [ps])
                    if R2 < 5:
                        continue
                    S.op("act", lambda e, c=c: e.copy(out=Sprev[:, 0, c, :], in_=Srun[:, 0, :]), [Srun], [Sprev.sub(0, c)])
                    if R2 < 6:
                        continue
                    S.op("dve", lambda e: e.tensor_tensor(out=Srun[:, 0, :], in0=Srun[:, 0, :], in1=cdec[0:64, 0, :], op=ALU.mult),
                         [Srun, cdec], [Srun])
                    if R2 < 7:
                        continue
                    S.op("dve", lambda e, ps=ps: e.tensor_tensor(out=Srun[:, 0, :], in0=Srun[:, 0, :], in1=ps[0:64, 0:256], op=ALU.add),
                         [Srun, ps], [Srun])
                    if R2 < 8:
                        continue
                    S.op("dve", lambda e, ps=ps, c=c: e.tensor_copy(out=dSb[:, c, :], in_=ps[0:64, 256:512]), [ps], [dSb.sub(c)])
                seq_b = [NLT + 1, NLT] + list(range(NLT - 1, -1, -1))
                for c in (seq_b if RDBG >= 3 else []):
                    S.op("act", lambda e, c=c: e.copy(out=Sprev[:, 1, c, :], in_=Srun[:, 1, :]), [Srun], [Sprev.sub(1, c)])
                    S.op("dve", lambda e: e.tensor_tensor(out=Srun[:, 1, :], in0=Srun[:, 1, :], in1=cdec[0:64, 1, :], op=ALU.mult),
                         [Srun, cdec], [Srun])
                    S.op("dve", lambda e, c=c: e.tensor_tensor(out=Srun[:, 1, :], in0=Srun[:, 1, :], in1=dSb[:, c, :], op=ALU.add),
                         [Srun, dSb.sub(c)], [Srun])
                AMs = [C.sb(f"AM{i}", [128, 4, 128], BF16, es=pes) for i in range(2)]
                osum = C.sb("osum", [128, 256], es=pes)
                t1 = C.sb("t1", [128, 256], es=pes)
                sq = C.sb("sq", [128, 256], es=pes)
                ss = C.sb("ss", [128, 4], es=pes)
                sg = C.sb("sg", [128, 256], es=pes)
                ys = [C.sb(f"y{i}", [128, 256], BF16, es=pes) for i in range(2)]
                chunks = list(range(NT)) if full_ctx else list(range(NLT))
                if RDBG < 4:
                    chunks = []
                for n_, c in enumerate(chunks):
                    i = n_ % 2
                    load_k(c, i)
                    S.dma("sp", lambda e, c=c, i=i: e.dma_start(
                        out=qts[i][:], in_=FMO.t[gq:gq + 2, :, c * 128:(c + 1) * 128].rearrange("g (h d) t -> d (g h) t", h=2)),
                        [f"FMO/{gq}/{c // 4}", f"FMO/{gq + 1}/{c // 4}"], [qts[i]])
                    S.dma("act", lambda e, c=c, i=i: e.dma_start(out=vgs[i][:], in_=TMA.t[c * 128:(c + 1) * 128, :]),
                          [TMA.sub(c)], [vgs[i]])
                    psA = C.nextps()
                    for h in range(4):
                        S.op("pe", lambda e, psA=psA, h=h, i=i: e.matmul(
                            psA[:, h * 128:(h + 1) * 128], lhsT=kts[i][:, h, :], rhs=qts[i][:, h, :], start=True, stop=True),
                            [kts[i], qts[i]], [psA])
                    S.op("dve", lambda e, psA=psA, i=i: e.tensor_tensor(
                        out=AMs[i][:].rearrange("p h t -> p (h t)"), in0=psA[:, :], in1=MT[:].rearrange("p h t -> p (h t)"),
                        op=ALU.mult), [psA, MT], [AMs[i]])
                    psO = C.nextps()
                    psX = C.nextps()
                    for h in range(4):
                        S.op("pe", lambda e, psO=psO, h=h, i=i: e.matmul(
                            psO[:, h * 64:(h + 1) * 64], lhsT=AMs[i][:, h, :], rhs=vgs[i][:, h * 64:(h + 1) * 64],
                            start=True, stop=True), [AMs[i], vgs[i]], [psO])
                        S.op("pe", lambda e, psO=psO, h=h, i=i, c=c: e.matmul(
                            psO[:, 256 + h * 64:256 + (h + 1) * 64], lhsT=qts[i][:, h, :], rhs=Sprev[:, 0, c, h * 64:(h + 1) * 64],
                            start=True, stop=True), [qts[i], Sprev.sub(0, c)], [psO])
                        S.op("pe", lambda e, psX=psX, h=h, i=i, c=c: e.matmul(
                            psX[:, h * 64:(h + 1) * 64], lhsT=qts[i][:, h, :], rhs=Sprev[:, 1, c, h * 64:(h + 1) * 64],
                            start=True, stop=True), [qts[i], Sprev.sub(1, c)], [psX])
                    S.op("dve", lambda e, psO=psO: e.tensor_tensor(out=t1[:], in0=psO[:, 256:512], in1=qdec[:, 0, :], op=ALU.mult),
                         [psO, qdec], [t1])
                    S.op("dve", lambda e, psO=psO: e.tensor_tensor(out=osum[:], in0=psO[:, 0:256], in1=t1[:], op=ALU.add),
                         [psO, t1], [osum])
                    S.op("dve", lambda e, psX=psX: e.tensor_tensor(out=t1[:], in0=psX[:, 0:256], in1=qdec[:, 1, :], op=ALU.mult),
                         [psX, qdec], [t1])
                    S.op("dve", lambda e: e.tensor_tensor(out=osum[:], in0=osum[:], in1=t1[:], op=ALU.add), [osum, t1], [osum])
                    S.op("act", lambda e: e.activation(out=sq[:], in_=osum[:], func=AF.Square), [osum], [sq])
                    S.op("dve", lambda e: e.tensor_reduce(out=ss[:], in_=sq[:].rearrange("p (h d) -> p h d", h=4), axis=AX.X, op=ALU.add),
                         [sq], [ss])
                    S.op("dve", lambda e: e.tensor_scalar(out=ss[:], in0=ss[:], scalar1=1.0 / 64, scalar2=1e-6, op0=ALU.mult, op1=ALU.add),
                         [ss], [ss])
                    S.op("act", lambda e: e.activation(out=ss[:], in_=ss[:], func=AF.Sqrt), [ss], [ss])
                    S.op("dve", lambda e: e.reciprocal(out=ss[:], in_=ss[:]), [ss], [ss])
                    S.op("act", lambda e, i=i: e.activation(out=sg[:], in_=vgs[i][:, 256:512], func=AF.Silu), [vgs[i]], [sg])
                    S.op("dve", lambda e: e.tensor_tensor(
                        out=osum[:].rearrange("p (h d) -> p h d", h=4), in0=osum[:].rearrange("p (h d) -> p h d", h=4),
                        in1=ss[:].rearrange("p (h o) -> p h o", o=1).broadcast_to([128, 4, 64]), op=ALU.mult), [osum, ss], [osum])
                    S.op("dve", lambda e, i=i: e.tensor_tensor(out=ys[i][:], in0=osum[:], in1=sg[:], op=ALU.mult), [osum, sg], [ys[i]])
                    S.dma("sp", lambda e, i=i, c=c: e.dma_start(out=Y.t[c * 128:(c + 1) * 128, 0:256], in_=ys[i][:]),
                          [ys[i]], [Y.sub(c, 0)])
                S.barrier()
                S.emit()

        def phase_diff(l, full_ctx):
            gq = fm_index["diff_q0"]
            gk = fm_index["diff_k0"]
            lam_init = 0.8 - 0.6 * math.exp(-0.3 * l)
            scale = 32 ** -0.5
            with ExitStack() as pes:
                C.rot = [0, 1, 2, 3]
                kT8 = C.sb("kT8", [32, 8, T], BF16, es=pes)
                V1 = C.sb("V1", [128, NT, 4, 65], BF16, es=pes)
                qT8s = [C.sb(f"qT8_{i}", [32, 8, 512], BF16, es=pes) for i in range(2)]
                Es = [C.sb(f"E{i}", [128, 512], BF16, es=pes) for i in range(3)]
                dl = C.sb("dl", [128, 128], es=pes)
                dl2 = C.sb("dl2", [128, 2], es=pes)
                nlam = C.sb("nlam", [128, 1], es=pes)
                gn = C.sb("gn", [128, 64], es=pes)
                od = C.sb("od", [128, 4, 256], es=pes)
                oT = [C.sb(f"oT{i}", [65, 512], es=pes) for i in range(2)]
                rr = C.sb("rr", [128, 4], es=pes)
                a_ = C.sb("a_", [128, 64], es=pes)
                sq = C.sb("dsq", [128, 4, 256], es=pes)
                ss = C.sb("dss", [128, 16], es=pes)
                yo = [C.sb(f"dy{i}", [128, 4, 256], BF16, es=pes) for i in range(2)]
                S.dma("sp", lambda e: e.dma_start(out=dl[:], in_=diff_lambda[l:l + 1, :].broadcast_to([128, 128])), [], [dl])
                S.dma("sp", lambda e: e.dma_start(out=gn[:], in_=diff_norm[l:l + 1, :].broadcast_to([128, 64])), [], [gn])
                dl4 = dl[:].rearrange("p (a b d) -> p a b d", a=2, b=2)
                S.op("dve", lambda e: e.tensor_tensor(out=dl4[:, :, 0, :], in0=dl4[:, :, 0, :], in1=dl4[:, :, 1, :], op=ALU.mult), [dl], [dl])
                S.op("dve", lambda e: e.tensor_reduce(out=dl2[:], in_=dl4[:, :, 0, :], axis=AX.X, op=ALU.add), [dl], [dl2])
                S.op("act", lambda e: e.activation(out=dl2[:], in_=dl2[:], func=AF.Exp), [dl2], [dl2])
                S.op("dve", lambda e: e.tensor_tensor(out=nlam[:], in0=dl2[:, 1:2], in1=dl2[:, 0:1], op=ALU.subtract), [dl2], [nlam])
                S.op("dve", lambda e: e.tensor_scalar_add(out=nlam[:], in0=nlam[:], scalar1=-lam_init), [nlam], [nlam])
                S.op("dve", lambda e: e.tensor_scalar_mul(out=gn[:], in0=gn[:], scalar1=(1.0 - lam_init)), [gn], [gn])
                S.dma("sp", lambda e: e.dma_start(out=kT8[:], in_=FMO.t[gk:gk + 2, :, :].rearrange("g (x d) t -> d (g x) t", d=32)),
                      [], [kT8])
                S.op("pool", lambda e: e.memset(V1[:], 1.0), [], [V1])
                for h in range(4):
                    for n0 in range(0, NT, 8):
                        n1 = min(NT, n0 + 8)
                        S.dma("act", lambda e, h=h, n0=n0, n1=n1: e.dma_start(
                            out=V1[:, n0:n1, h, 0:64],
                            in_=TMB.t[n0 * 128:n1 * 128, h * 64:(h + 1) * 64].rearrange("(n p) e -> p n e", p=128)), [V1], [V1])
                blocks = [(b * 512, 512, list(range(NT))) for b in range(NLAT // 512)]
                if full_ctx:
                    blocks.append((NLAT, 256, [NLT, NLT + 1]))
                ec = 0
                for bi, (q0, qw, ktiles) in enumerate(blocks):
                    qT8 = qT8s[bi % 2]
                    nqs = qw // 128
                    S.dma("sp", lambda e, qT8=qT8, q0=q0, qw=qw: e.dma_start(
                        out=qT8[:, :, 0:qw], in_=FMO.t[gq:gq + 2, :, q0:q0 + qw].rearrange("g (x d) t -> d (g x) t", d=32)),
                        [], [qT8])
                    for h in range(4):
                        acc = [C.ps[4 + (h % 2) * 2], C.ps[5 + (h % 2) * 2]]
                        for ki, kt in enumerate(ktiles):
                            for c in range(2):
                                hc = h * 2 + c
                                pss = C.nextps()
                                S.op("pe", lambda e, pss=pss, hc=hc, kt=kt, qT8=qT8, qw=qw: e.matmul(
                                    pss[:, 0:qw], lhsT=kT8[:, hc, kt * 128:(kt + 1) * 128], rhs=qT8[:, hc, 0:qw],
                                    start=True, stop=True), [kT8, qT8], [pss])
                                E = Es[ec % 3]
                                ec += 1
                                S.op("act", lambda e, E=E, pss=pss, qw=qw: e.activation(
                                    out=E[:, 0:qw], in_=pss[:, 0:qw], func=AF.Exp, scale=scale), [pss], [E])
                                S.op("pe", lambda e, E=E, c=c, kt=kt, h=h, ki=ki, qw=qw, nk=len(ktiles), ac=acc[c]: e.matmul(
                                    ac[0:65, 0:qw], lhsT=V1[:, kt, h, :], rhs=E[:, 0:qw],
                                    start=(ki == 0), stop=(ki == nk - 1)), [E, V1], [acc[c]])
                        accs = []
                        for c in range(2):
                            S.op("act" if c == 0 else "dve", (lambda e, c=c, qw=qw, ac=acc[c]: e.copy(out=oT[c][:, 0:qw], in_=ac[0:65, 0:qw])) if c == 0 else
                                 (lambda e, c=c, qw=qw, ac=acc[c]: e.tensor_copy(out=oT[c][:, 0:qw], in_=ac[0:65, 0:qw])), [acc[c]], [oT[c]])
                            pt = C.nextps()
                            for qs in range(nqs):
                                S.op("pe", lambda e, pt=pt, c=c, qs=qs: e.transpose(
                                    pt[:, qs * 65:(qs + 1) * 65], oT[c][:, qs * 128:(qs + 1) * 128], ident_f[0:65, 0:65]),
                                    [oT[c], ident_f], [pt])
                            accs.append(pt)
                        acc = accs
                        S.op("dve", lambda e, a0=acc[0], nqs=nqs: e.reciprocal(out=rr[:, 0:nqs], in_=a0[:, 0:nqs * 65].rearrange("p (q e) -> p q e", e=65)[:, :, 64]),
                             [acc[0]], [rr])
                        for qs in range(nqs):
                            S.op("dve", lambda e, qs=qs, h=h, a0=acc[0]: e.tensor_scalar(
                                out=od[:, qs, h * 64:(h + 1) * 64], in0=a0[:, qs * 65:qs * 65 + 64], scalar1=rr[:, qs:qs + 1],
                                scalar2=None, op0=ALU.mult), [acc[0], rr], [od])
                        S.op("dve", lambda e, a1=acc[1], nqs=nqs: e.reciprocal(out=rr[:, 0:nqs], in_=a1[:, 0:nqs * 65].rearrange("p (q e) -> p q e", e=65)[:, :, 64]),
                             [acc[1], od], [rr])
                        S.op("dve", lambda e, nqs=nqs: e.tensor_scalar(out=rr[:, 0:nqs], in0=rr[:, 0:nqs], scalar1=nlam[:, 0:1], scalar2=None,
                                                              op0=ALU.mult), [rr, nlam], [rr])
                        for qs in range(nqs):
                            S.op("dve", lambda e, qs=qs, h=h, a1=acc[1]: e.scalar_tensor_tensor(
                                out=od[:, qs, h * 64:(h + 1) * 64], in0=a1[:, qs * 65:qs * 65 + 64], scalar=rr[:, qs:qs + 1],
                                in1=od[:, qs, h * 64:(h + 1) * 64], op0=ALU.mult, op1=ALU.add), [acc[1], rr, od], [od])
                    y = yo[bi % 2]
                    S.op("act", lambda e, nqs=nqs: e.activation(out=sq[:, 0:nqs, :], in_=od[:, 0:nqs, :], func=AF.Square), [od], [sq])
                    S.op("dve", lambda e, nqs=nqs: e.tensor_reduce(out=ss[:, 0:nqs * 4], in_=sq[:, 0:nqs, :].rearrange("p q (h d) -> p (q h) d", h=4),
                                                          axis=AX.X, op=ALU.add), [sq], [ss])
                    S.op("dve", lambda e, nqs=nqs: e.tensor_scalar(out=ss[:], in0=ss[:], scalar1=1.0 / 64, scalar2=1e-6, op0=ALU.mult, op1=ALU.add),
                         [ss], [ss])
                    S.op("act", lambda e, nqs=nqs: e.activation(out=ss[:], in_=ss[:], func=AF.Sqrt), [ss], [ss])
                    S.op("dve", lambda e, nqs=nqs: e.reciprocal(out=ss[:], in_=ss[:]), [ss], [ss])
                    S.op("dve", lambda e, nqs=nqs: e.tensor_tensor(
                        out=od[:, 0:nqs, :].rearrange("p q (h d) -> p (q h) d", h=4), in0=od[:, 0:nqs, :].rearrange("p q (h d) -> p (q h) d", h=4),
                        in1=ss[:, 0:nqs * 4].rearrange("p (x o) -> p x o", o=1).broadcast_to([128, nqs * 4, 64]), op=ALU.mult), [od, ss], [od])
                    S.op("dve", lambda e, y=y, nqs=nqs: e.tensor_tensor(
                        out=y[:, 0:nqs, :].rearrange("p q (h d) -> p (q h) d", h=4), in0=od[:, 0:nqs, :].rearrange("p q (h d) -> p (q h) d", h=4),
                        in1=gn[:].rearrange("p (o d) -> p o d", o=1).broadcast_to([128, nqs * 4, 64]), op=ALU.mult), [od, gn], [y])
                    S.dma("sp", lambda e, y=y, q0=q0, qw=qw, nqs=nqs: e.dma_start(
                        out=Y.t[q0:q0 + qw, 256:512].rearrange("(q p) c -> p q c", p=128), in_=y[:, 0:nqs, :]), [y], [Y.sub(bi, 1)])
                C.rot = None
                S.barrier()
                S.emit()

        def phase_swa(l, full_ctx):
            gq = fm_index["swa_q0"]
            gk = fm_index["swa_k0"]
            scale = 0.125
            with ExitStack() as pes:
                C.rot = [0, 1, 2, 3]
                skT = C.sb("skT", [64, 2, T], BF16, es=pes)
                sqT = C.sb("sqT", [64, 4, T], BF16, es=pes)
                V1 = C.sb("sV1", [128, NT, 2, 65], BF16, es=pes)
                nm = C.sb("nm", [128, 2, 2, 128], BF16, es=pes)
                Es = [C.sb(f"sE{i}", [128, 256], BF16, es=pes) for i in range(3)]
                oT = C.sb("soT", [65, 512], es=pes)
                es_ = C.sb("esink", [128, 4], es=pes)
                den = C.sb("den", [128, 4], es=pes)
                ys = [C.sb(f"sy{i}", [128, 256], BF16, es=pes) for i in range(2)]
                S.dma("sp", lambda e: e.dma_start(out=es_[:], in_=swa_sink[l:l + 1, :].broadcast_to([128, 4])), [], [es_])
                S.op("act", lambda e: e.activation(out=es_[:], in_=es_[:], func=AF.Exp), [es_], [es_])
                for r in range(2):
                    S.op("dve", lambda e, r=r: e.tensor_scalar(out=nm[:, 0, r, :], in0=relpos[:], scalar1=0.0, scalar2=NEG,
                                                               op0=ALU.is_gt, op1=ALU.mult), [relpos], [nm])
                    S.op("dve", lambda e, r=r: e.tensor_scalar(out=nm[:, 1, r, :], in0=relpos[:], scalar1=0.0, scalar2=NEG,
                                                               op0=ALU.is_lt, op1=ALU.mult), [relpos], [nm])
                S.dma("sp", lambda e: e.dma_start(out=skT[:], in_=FMO.t[gk, :, :].rearrange("(h d) t -> d h t", h=2)), [], [skT])
                S.dma("sp", lambda e: e.dma_start(out=sqT[:], in_=FMO.t[gq:gq + 2, :, :].rearrange("g (h d) t -> d (g h) t", h=2)), [], [sqT])
                S.op("pool", lambda e: e.memset(V1[:], 1.0), [], [V1])
                for g in range(2):
                    for n0 in range(0, NT, 8):
                        n1 = min(NT, n0 + 8)
                        S.dma("act", lambda e, g=g, n0=n0, n1=n1: e.dma_start(
                            out=V1[:, n0:n1, g, 0:64],
                            in_=TMD.t[n0 * 128:n1 * 128, g * 64:(g + 1) * 64].rearrange("(n p) e -> p n e", p=128)), [V1], [V1])
                tiles = list(range(NT)) if full_ctx else list(range(NLT))
                ec = 0
                for ti, t in enumerate(tiles):
                    if t < NLT:
                        keys = []
                        if t > 0:
                            keys.append((t - 1, 0))
                        keys.append((t, None))
                        if t < NLT - 1:
                            keys.append((t + 1, 1))
                        keys += [(NLT, None), (NLT + 1, None)]
                    else:
                        keys = [(NLT, None), (NLT + 1, None)]
                    accs = [C.ps[4 + (ti % 2) * 2], C.ps[5 + (ti % 2) * 2]]
                    for g in range(2):
                        for ki, (kt, mk) in enumerate(keys):
                            pss = C.nextps()
                            S.op("pe", lambda e, pss=pss, g=g, kt=kt, t=t, mk=mk: e.matmul(
                                pss[:, 0:256], lhsT=skT[:, g, kt * 128:(kt + 1) * 128], rhs=sqT[:, 2 * g:2 * g + 2, t * 128:(t + 1) * 128],
                                start=True, stop=(mk is None)), [skT, sqT], [pss])
                            if mk is not None:
                                S.op("pe", lambda e, pss=pss, mk=mk: e.matmul(
                                    pss[:, 0:256], lhsT=ident_b[:], rhs=nm[:, mk, :, :], start=False, stop=True), [ident_b, nm], [pss])
                            E = Es[ec % 3]
                            ec += 1
                            S.op("act", lambda e, E=E, pss=pss: e.activation(out=E[:], in_=pss[:, 0:256], func=AF.Exp, scale=scale),
                                 [pss], [E])
                            S.op("pe", lambda e, E=E, g=g, kt=kt, ki=ki, nk=len(keys), ac=accs[g]: e.matmul(
                                ac[0:65, 0:256], lhsT=V1[:, kt, g, :], rhs=E[:], start=(ki == 0), stop=(ki == nk - 1)),
                                [E, V1], [accs[g]])
                        S.op("dve", lambda e, g=g, ac=accs[g]: e.tensor_copy(out=oT[:, g * 256:(g + 1) * 256], in_=ac[0:65, 0:256]),
                             [accs[g]], [oT])
                    pt = C.nextps()
                    for h in range(4):
                        S.op("pe", lambda e, pt=pt, h=h: e.transpose(pt[:, h * 65:(h + 1) * 65], oT[:, h * 128:(h + 1) * 128],
                                                                     ident_f[0:65, 0:65]), [oT, ident_f], [pt])
                    S.op("dve", lambda e, pt=pt: e.tensor_tensor(
                        out=den[:], in0=pt[:, 0:260].rearrange("p (h e) -> p h e", e=65)[:, :, 64], in1=es_[:], op=ALU.add),
                        [pt, es_], [den])
                    S.op("dve", lambda e: e.reciprocal(out=den[:], in_=den[:]), [den], [den])
                    y = ys[ti % 2]
                    S.op("dve", lambda e, pt=pt, y=y: e.tensor_tensor(
                        out=y[:].rearrange("p (h d) -> p h d", h=4), in0=pt[:, 0:260].rearrange("p (h e) -> p h e", e=65)[:, :, 0:64],
                        in1=den[:].rearrange("p (h o) -> p h o", o=1).broadcast_to([128, 4, 64]), op=ALU.mult), [pt, den], [y])
                    S.dma("sp", lambda e, y=y, t=t: e.dma_start(out=Y.t[t * 128:(t + 1) * 128, 768:1024], in_=y[:]), [y], [Y.sub(t, 3)])
                C.rot = None
                S.barrier()
                S.emit()

        def phase_gdn(l, full_ctx):
            g0 = fm_index["gdn0"]
            BIG = 30000.0
            with ExitStack() as pes:
                cw = C.sb("cw", [128, 18], es=pes)
                bones = C.sb("bones", [128, 128], es=pes)
                xb = [C.sb(f"xb{i}", [128, 514], BF16, es=pes) for i in range(2)]
                yc = C.sb("yc", [128, 512], es=pes)
                ysl = C.sb("ysl", [128, 512], es=pes)
                sq = C.sb("gsq", [128, 512], es=pes)
                rs = C.sb("grs", [128, 512], es=pes)
                ynb = [C.sb(f"ynb{i}", [128, 512], BF16, es=pes) for i in range(2)]
                ytm = [C.sb(f"ytm{i}", [128, 4, 128], BF16, es=pes) for i in range(2)]
                S.dma("sp", lambda e: e.dma_start(out=cw[:], in_=gdn_convT[l, :, :]), [], [cw])
                S.op("pool", lambda e: e.memset(bones[:], 0.0), [], [bones])
                S.op("pool", lambda e: e.memset(bones[0:64, 0:64], 1.0), [bones], [bones])
                S.op("pool", lambda e: e.memset(bones[64:128, 64:128], 1.0), [bones], [bones])
                seqs = [(0, NLAT)] + [(NLAT, T)]
                bi = 0
                for gi in range(6):
                    for (s0, s1) in seqs:
                        for t0 in range(s0, s1, 512):
                            tw = min(512, s1 - t0)
                            x = xb[bi % 2]
                            lo = max(t0 - 1, s0)
                            hi = min(t0 + tw + 1, s1)
                            if lo == t0 or hi == t0 + tw:
                                S.op("pool", lambda e, x=x: e.memset(x[:], 0.0), [], [x])
                            S.dma("sp", lambda e, x=x, gi=gi, lo=lo, hi=hi, t0=t0: e.dma_start(
                                out=x[:, lo - (t0 - 1):hi - (t0 - 1)], in_=FMO.t[g0 + gi, :, lo:hi]), [x], [x])
                            S.op("dve", lambda e, x=x, gi=gi, tw=tw: e.tensor_scalar(
                                out=yc[:, 0:tw], in0=x[:, 1:1 + tw], scalar1=cw[:, gi * 3 + 1:gi * 3 + 2], scalar2=None, op0=ALU.mult),
                                [x, cw], [yc])
                            S.op("dve", lambda e, x=x, gi=gi, tw=tw: e.scalar_tensor_tensor(
                                out=yc[:, 0:tw], in0=x[:, 0:tw], scalar=cw[:, gi * 3:gi * 3 + 1], in1=yc[:, 0:tw],
                                op0=ALU.mult, op1=ALU.add), [x, cw, yc], [yc])
                            S.op("dve", lambda e, x=x, gi=gi, tw=tw: e.scalar_tensor_tensor(
                                out=yc[:, 0:tw], in0=x[:, 2:2 + tw], scalar=cw[:, gi * 3 + 2:gi * 3 + 3], in1=yc[:, 0:tw],
                                op0=ALU.mult, op1=ALU.add), [x, cw, yc], [yc])
                            S.op("act", lambda e, tw=tw: e.activation(out=ysl[:, 0:tw], in_=yc[:, 0:tw], func=AF.Silu), [yc], [ysl])
                            yn = ynb[bi % 2]
                            if gi < 4:
                                S.op("act", lambda e, tw=tw: e.activation(out=sq[:, 0:tw], in_=ysl[:, 0:tw], func=AF.Square), [ysl], [sq])
                                ps = C.nextps()
                                S.op("pe", lambda e, ps=ps, tw=tw: e.matmul(ps[:, 0:tw], lhsT=bones[:], rhs=sq[:, 0:tw], start=True, stop=True),
                                     [bones, sq], [ps])
                                S.op("dve", lambda e, ps=ps, tw=tw: e.tensor_scalar_add(out=rs[:, 0:tw], in0=ps[:, 0:tw], scalar1=1e-6), [ps], [rs])
                                S.op("act", lambda e, tw=tw: e.activation(out=rs[:, 0:tw], in_=rs[:, 0:tw], func=AF.Sqrt), [rs], [rs])
                                S.op("dve", lambda e, tw=tw: e.reciprocal(out=rs[:, 0:tw], in_=rs[:, 0:tw]), [rs], [rs])
                                if gi < 2:
                                    S.op("dve", lambda e, tw=tw, yn=yn: e.scalar_tensor_tensor(
                                        out=yn[:, 0:tw], in0=ysl[:, 0:tw], scalar=0.125, in1=rs[:, 0:tw], op0=ALU.mult, op1=ALU.mult),
                                        [ysl, rs], [yn])
                                else:
                                    S.op("dve", lambda e, tw=tw, yn=yn: e.tensor_tensor(out=yn[:, 0:tw], in0=ysl[:, 0:tw], in1=rs[:, 0:tw], op=ALU.mult),
                                         [ysl, rs], [yn])
                                S.dma("act", lambda e, yn=yn, gi=gi, t0=t0, tw=tw: e.dma_start(out=GFM.t[gi, :, t0:t0 + tw], in_=yn[:, 0:tw]),
                                      [yn], [GFM.sub(gi, t0)])
                            else:
                                S.op("dve", lambda e, tw=tw, yn=yn: e.tensor_copy(out=yn[:, 0:tw], in_=ysl[:, 0:tw]), [ysl], [yn])
                            if gi >= 2:
                                ps = C.nextps()
                                pv = ps.t[:, 0:256].bitcast(BF16)
                                nq = tw // 128
                                for q in range(nq):
                                    S.op("pe", lambda e, pv=pv, q=q, yn=yn: e.transpose(pv[:, q * 128:(q + 1) * 128], yn[:, q * 128:(q + 1) * 128], ident_b[:]),
                                         [yn, ident_b], [ps])
                                yt = ytm[bi % 2]
                                S.op("act", lambda e, pv=pv, yt=yt, nq=nq: e.copy(out=yt[:, 0:nq, :].rearrange("p q c -> p (q c)"), in_=pv[:, 0:nq * 128]),
                                     [ps], [yt])
                                S.dma("act", lambda e, yt=yt, gi=gi, t0=t0, tw=tw, nq=nq: e.dma_start(
                                    out=GTM.t[t0:t0 + tw, (gi - 2) * 128:(gi - 1) * 128].rearrange("(q p) c -> p q c", p=128), in_=yt[:, 0:nq, :]),
                                    [yt], [GTM.sub(gi, t0)])
                            bi += 1
                S.barrier()
                S.emit()
            import os
            GD = int(os.environ.get("GD", 9))
            if GD < 2:
                return
            with ExitStack() as pes:
                gab = C.sb("gab", [128, NT, 16], es=pes)
                par = C.sb("gpar", [128, 16], es=pes)
                la = C.sb("la", [128, NT, 8], es=pes)
                nbeta = C.sb("nbeta", [128, NT, 8], es=pes)
                gg = C.sb("gg", [128, NT, 8], es=pes)
                gt = C.sb("gt", [128, NT, 8], es=pes)
                eg = C.sb("eg", [128, NT, 8], es=pes)
                ekt = C.sb("ekt", [128, NT, 8], es=pes)
                cd = C.sb("cd", [128, NT, 8], es=pes)
                beg = C.sb("beg", [128, NT, 8], es=pes)
                tri = C.sb("tri", [128, 2, 128], es=pes)
                onesf = C.sb("onesf", [128, 128], es=pes)
                nonesf = C.sb("nonesf", [128, 128], es=pes)
                mD = C.sb("mD", [128, 2, 4, 128], es=pes)
                mDT = C.sb("mDT", [128, 2, 4, 128], es=pes)
                gnb = C.sb("gnb", [128, 64], es=pes)
                S.dma("sp", lambda e: e.dma_start(out=gab[:].rearrange("p n c -> p (n c)"), in_=GAB.t[:, :]), [], [gab])
                S.dma("sp", lambda e: e.dma_start(out=par[:, 0:8], in_=gdn_a_log[l:l + 1, :].broadcast_to([128, 8])), [], [par])
                S.dma("sp", lambda e: e.dma_start(out=par[:, 8:16], in_=gdn_dt_bias[l:l + 1, :].broadcast_to([128, 8])), [par], [par])
                S.dma("sp", lambda e: e.dma_start(out=gnb[:], in_=gdn_norm[l:l + 1, :].broadcast_to([128, 64])), [], [gnb])
                S.op("pool", lambda e: e.memset(onesf[:], 1.0), [], [onesf])
                S.op("pool", lambda e: e.memset(nonesf[:], -1.0), [], [nonesf])
                S.op("dve", lambda e: e.tensor_single_scalar(out=tri[:, 0, :], in_=relpos[:], scalar=0.0, op=ALU.is_ge), [relpos], [tri])
                S.op("dve", lambda e: e.tensor_single_scalar(out=tri[:, 1, :], in_=relpos[:], scalar=0.0, op=ALU.is_le), [relpos], [tri])
                for h in range(4):
                    S.op("dve", lambda e, h=h: e.tensor_scalar(out=mD[:, 0, h, :], in0=relpos[:], scalar1=0.0, scalar2=BIG, op0=ALU.is_ge, op1=ALU.mult), [relpos], [mD])
                    S.op("dve", lambda e, h=h: e.tensor_scalar(out=mD[:, 1, h, :], in0=relpos[:], scalar1=0.0, scalar2=BIG, op0=ALU.is_le, op1=ALU.mult), [relpos], [mD])
                    S.op("dve", lambda e, h=h: e.tensor_scalar(out=mDT[:, 0, h, :], in0=relpos[:], scalar1=0.0, scalar2=-BIG, op0=ALU.is_lt, op1=ALU.mult), [relpos], [mDT])
                    S.op("dve", lambda e, h=h: e.tensor_scalar(out=mDT[:, 1, h, :], in0=relpos[:], scalar1=0.0, scalar2=-BIG, op0=ALU.is_gt, op1=ALU.mult), [relpos], [mDT])
                S.op("act", lambda e: e.activation(out=par[:, 0:8], in_=par[:, 0:8], func=AF.Exp), [par], [par])
                S.op("dve", lambda e: e.tensor_tensor(out=la[:], in0=gab[:, :, 0:8],
                                                      in1=par[:, 8:16].rearrange("p (o c) -> p o c", o=1).broadcast_to([128, NT, 8]), op=ALU.add),
                     [gab, par], [la])
                S.op("act", lambda e: e.activation(out=la[:], in_=la[:], func=AF.Exp), [la], [la])
                S.op("act", lambda e: e.activation(out=la[:], in_=la[:], func=AF.Ln, bias=1.0), [la], [la])
                S.op("dve", lambda e: e.scalar_tensor_tensor(
                    out=la[:], in0=la[:], scalar=-1.0, in1=par[:, 0:8].rearrange("p (o c) -> p o c", o=1).broadcast_to([128, NT, 8]),
                    op0=ALU.mult, op1=ALU.mult), [la, par], [la])
                S.op("act", lambda e: e.activation(out=nbeta[:], in_=gab[:, :, 8:16], func=AF.Sigmoid), [gab], [nbeta])
                for r in range(2):
                    ps = C.nextps()
                    S.op("pe", lambda e, ps=ps, r=r: e.matmul(ps[:, 0:NT * 4], lhsT=tri[:, r, :], rhs=la[:, :, r * 4:(r + 1) * 4], start=True, stop=True),
                         [tri, la], [ps])
                    S.op("dve", lambda e, ps=ps, r=r: e.tensor_copy(out=gg[:, :, r * 4:(r + 1) * 4], in_=ps[:, 0:NT * 4].rearrange("p (n c) -> p n c", c=4)),
                         [ps], [gg])
                ps = C.nextps()
                S.op("pe", lambda e, ps=ps: e.matmul(ps[:, 0:NT * 8], lhsT=onesf[:], rhs=la[:].rearrange("p n c -> p (n c)"), start=True, stop=True),
                     [onesf, la], [ps])
                S.op("dve", lambda e, ps=ps: e.tensor_copy(out=gt[:].rearrange("p n c -> p (n c)"), in_=ps[:, 0:NT * 8]), [ps], [gt])
                S.op("act", lambda e: e.activation(out=eg[:], in_=gg[:], func=AF.Exp), [gg], [eg])
                S.op("act", lambda e: e.activation(out=cd[:], in_=gt[:], func=AF.Exp), [gt], [cd])
                S.op("dve", lambda e: e.tensor_tensor(out=ekt[:], in0=gt[:], in1=gg[:], op=ALU.subtract), [gt, gg], [ekt])
                S.op("act", lambda e: e.activation(out=ekt[:], in_=ekt[:], func=AF.Exp), [ekt], [ekt])
                S.op("dve", lambda e: e.tensor_tensor(out=beg[:], in0=nbeta[:], in1=eg[:], op=ALU.mult), [nbeta, eg], [beg])
                S.op("dve", lambda e: e.tensor_scalar_mul(out=nbeta[:], in0=nbeta[:], scalar1=-1.0), [nbeta], [nbeta])
                qT = [C.sb(f"gqT{i}", [64, 4, 128], BF16, es=pes) for i in range(2)]
                kT = [C.sb(f"gkT{i}", [64, 4, 128], BF16, es=pes) for i in range(2)]
                kv = [C.sb(f"gkv{i}", [128, 512], BF16, es=pes) for i in range(2)]
                Rm = C.sb("Rm", [128, 4, 128], es=pes)
                Ds = C.sb("Ds", [128, 4, 128], es=pes)
                DT = C.sb("DT", [128, 4, 128], es=pes)
                X = [C.sb(f"X{i}", [128, 4, 2, 128], es=pes) for i in range(2)]
                P = C.sb("P", [128, 4, 128], es=pes)
                Pb = C.sb("Pb", [128, 4, 128], BF16, es=pes)
                qkT = C.sb("qkT", [128, 4, 128], BF16, es=pes)
                kg = C.sb("kg", [128, 256], BF16, es=pes)
                vb = C.sb("vb", [128, 256], BF16, es=pes)
                ktl = C.sb("ktl", [128, 256], BF16, es=pes)
                wT = C.sb("wT", [64, 4, 128], BF16, es=pes)
                ub = C.sb("ub", [128, 256], es=pes)
                u = C.sb("u", [128, 256], BF16, es=pes)
                Sf = C.sb("Sf", [64, 256], es=pes)
                Sb16 = C.sb("Sb16", [64, 256], BF16, es=pes)
                Oacc = C.sb("Oacc", [128, NT, 256], es=pes)
                ocr = C.sb("ocr", [128, 256], es=pes)
                gsq = C.sb("gosq", [128, 256], es=pes)
                gss = C.sb("goss", [128, 4], es=pes)
                gsg = C.sb("gosg", [128, 256], es=pes)
                ggate = [C.sb(f"ggate{i}", [128, 256], BF16, es=pes) for i in range(2)]
                gy = [C.sb(f"gy{i}", [128, 256], BF16, es=pes) for i in range(2)]
                seq = {0: [NLT, NLT + 1] + list(range(NLT)), 1: [NLT + 1, NLT] + list(range(NLT - 1, -1, -1))}
                step = 0
                for r in range(2 if GD >= 3 else 0):
                    S.op("pool", lambda e: e.memset(Sf[:], 0.0), [], [Sf])
                    S.op("pool", lambda e: e.memset(Sb16[:], 0.0), [], [Sb16])
                    for c in seq[r]:
                        i = step % 2
                        step += 1
                        rc = slice(r * 4, r * 4 + 4)
                        S.dma("sp", lambda e, i=i, c=c: e.dma_start(
                            out=qT[i][:], in_=GFM.t[0:2, :, c * 128:(c + 1) * 128].rearrange("g (h d) t -> d (g h) t", h=2)), [], [qT[i]])
                        S.dma("sp", lambda e, i=i, c=c: e.dma_start(
                            out=kT[i][:], in_=GFM.t[2:4, :, c * 128:(c + 1) * 128].rearrange("g (h d) t -> d (g h) t", h=2)), [], [kT[i]])
                        S.dma("act", lambda e, i=i, c=c: e.dma_start(out=kv[i][:], in_=GTM.t[c * 128:(c + 1) * 128, :]), [], [kv[i]])
                        pKK = C.nextps()
                        pQK = C.nextps()
                        for h in range(4):
                            S.op("pe", lambda e, pKK=pKK, h=h, i=i: e.matmul(pKK[:, h * 128:(h + 1) * 128], lhsT=kT[i][:, h, :], rhs=kT[i][:, h, :],
                                                                              start=True, stop=True), [kT[i]], [pKK])
                        for h in range(4):
                            S.op("pe", lambda e, pQK=pQK, h=h, i=i: e.matmul(pQK[:, h * 128:(h + 1) * 128], lhsT=kT[i][:, h, :], rhs=qT[i][:, h, :],
                                                                              start=True, stop=True), [kT[i], qT[i]], [pQK])
                        for h in range(4):
                            S.op("dve", lambda e, h=h, c=c, r=r: e.tensor_scalar(
                                out=Rm[:, h, :], in0=tri[:, r, :], scalar1=la[:, c, r * 4 + h:r * 4 + h + 1], scalar2=None, op0=ALU.mult),
                                [tri, la], [Rm])
                        pD = C.nextps()
                        pDT = C.nextps()
                        for (pp, mk) in ((pD, mD), (pDT, mDT)):
                            S.op("pe", lambda e, pp=pp, mk=mk, r=r: e.matmul(pp[:, :], lhsT=ident_f[:], rhs=mk[:, r, :, :].rearrange("p h t -> p (h t)"),
                                                                             start=True, stop=False), [ident_f, mk], [pp])
                            for h in range(4):
                                S.op("pe", lambda e, pp=pp, h=h: e.matmul(pp[:, h * 128:(h + 1) * 128], lhsT=onesf[:], rhs=Rm[:, h, :],
                                                                          start=False, stop=False), [onesf, Rm], [pp])
                                S.op("pe", lambda e, pp=pp, h=h: e.matmul(pp[:, h * 128:(h + 1) * 128], lhsT=Rm[:, h, :], rhs=nonesf[:],
                                                                          start=False, stop=(h == 3)), [nonesf, Rm], [pp])
                        S.op("act", lambda e, pD=pD: e.activation(out=Ds[:].rearrange("p h t -> p (h t)"), in_=pD[:, :], func=AF.Exp, scale=-1.0),
                             [pD], [Ds])
                        S.op("act", lambda e, pDT=pDT: e.activation(out=DT[:].rearrange("p h t -> p (h t)"), in_=pDT[:, :], func=AF.Exp), [pDT], [DT])
                        if GD < 4:
                            continue
                        Xc = X[0]
                        S.op("dve", lambda e, pKK=pKK, Xc=Xc: e.tensor_tensor(out=Xc[:, :, 0, :], in0=pKK[:, :].rearrange("p (h t) -> p h t", h=4), in1=Ds[:], op=ALU.mult),
                             [pKK, Ds], [Xc])
                        S.op("dve", lambda e, Xc=Xc, c=c, rc=rc: e.tensor_tensor(
                            out=Xc[:, :, 0, :], in0=Xc[:, :, 0, :], in1=nbeta[:, c, rc].rearrange("p (h o) -> p h o", o=1).broadcast_to([128, 4, 128]), op=ALU.mult),
                            [Xc, nbeta], [Xc])
                        S.op("dve", lambda e, pQK=pQK: e.tensor_tensor(out=qkT[:], in0=pQK[:, :].rearrange("p (h t) -> p h t", h=4), in1=DT[:], op=ALU.mult),
                             [pQK, DT], [qkT])
                        pZ = C.nextps()
                        for h in range(4):
                            S.op("pe", lambda e, pZ=pZ, h=h, Xc=Xc: e.transpose(pZ[:, h * 128:(h + 1) * 128], Xc[:, h, 0, :], ident_f[:]), [Xc, ident_f], [pZ])
                        S.op("act", lambda e, pZ=pZ, Xc=Xc: e.copy(out=Xc[:, :, 1, :], in_=pZ[:, :].rearrange("p (h t) -> p h t", h=4)), [pZ], [Xc])
                        S.op("dve", lambda e, Xc=Xc: e.tensor_tensor(out=P[:], in0=Xc[:, :, 1, :], in1=ident_f[:].rearrange("p (o t) -> p o t", o=1).broadcast_to([128, 4, 128]), op=ALU.add),
                             [Xc, ident_f], [P])
                        for lev in range(6):
                            Xo = X[lev % 2]
                            Xn = X[(lev + 1) % 2]
                            last = lev == 5
                            pxa = C.nextps()
                            pxb = C.nextps()
                            for h in range(4):
                                pp = pxa if h < 2 else pxb
                                o = (h % 2) * 256
                                S.op("pe", lambda e, pp=pp, o=o, h=h, Xo=Xo: e.matmul(pp[:, o:o + 128], lhsT=Xo[:, h, 1, :], rhs=Xo[:, h, 0, :], start=True, stop=True),
                                     [Xo], [pp])
                                if not last:
                                    S.op("pe", lambda e, pp=pp, o=o, h=h, Xo=Xo: e.matmul(pp[:, o + 128:o + 256], lhsT=Xo[:, h, 0, :], rhs=Xo[:, h, 1, :], start=True, stop=True),
                                         [Xo], [pp])
                            S.op("act", lambda e, pxa=pxa, Xn=Xn: e.copy(out=Xn[:, 0:2, :, :].rearrange("p h x t -> p (h x t)"), in_=pxa[:, :]), [pxa], [Xn])
                            S.op("dve", lambda e, pxb=pxb, Xn=Xn: e.tensor_copy(out=Xn[:, 2:4, :, :].rearrange("p h x t -> p (h x t)"), in_=pxb[:, :]), [pxb], [Xn])
                            pP = C.nextps()
                            for h in range(4):
                                S.op("pe", lambda e, pP=pP, h=h, Xn=Xn: e.matmul(pP[:, h * 128:(h + 1) * 128], lhsT=Xn[:, h, 0, :], rhs=P[:, h, :], start=True, stop=True),
                                     [Xn, P], [pP])
                            S.op("dve", lambda e, pP=pP: e.tensor_tensor(out=P[:].rearrange("p h t -> p (h t)"), in0=P[:].rearrange("p h t -> p (h t)"), in1=pP[:, :], op=ALU.add),
                                 [pP, P], [P])
                        if GD < 5:
                            continue
                        S.op("act", lambda e: e.copy(out=Pb[:], in_=P[:]), [P], [Pb])
                        S.op("dve", lambda e, i=i, c=c, rc=rc: e.tensor_tensor(
                            out=kg[:].rearrange("p (h d) -> p h d", h=4), in0=kv[i][:, 0:256].rearrange("p (h d) -> p h d", h=4),
                            in1=beg[:, c, rc].rearrange("p (h o) -> p h o", o=1).broadcast_to([128, 4, 64]), op=ALU.mult), [kv[i], beg], [kg])
                        S.op("dve", lambda e, i=i, c=c, rc=rc: e.tensor_tensor(
                            out=vb[:].rearrange("p (h d) -> p h d", h=4), in0=kv[i][:, 256:512].rearrange("p (h d) -> p h d", h=4),
                            in1=nbeta[:, c, rc].rearrange("p (h o) -> p h o", o=1).broadcast_to([128, 4, 64]), op=ALU.mult), [kv[i], nbeta], [vb])
                        S.op("dve", lambda e, i=i, c=c, rc=rc: e.tensor_tensor(
                            out=ktl[:].rearrange("p (h d) -> p h d", h=4), in0=kv[i][:, 0:256].rearrange("p (h d) -> p h d", h=4),
                            in1=ekt[:, c, rc].rearrange("p (h o) -> p h o", o=1).broadcast_to([128, 4, 64]), op=ALU.mult), [kv[i], ekt], [ktl])
                        pw = C.nextps()
                        pu = C.nextps()
                        for h in range(4):
                            S.op("pe", lambda e, pw=pw, h=h: e.matmul(pw[0:64, h * 128:(h + 1) * 128], lhsT=kg[:, h * 64:(h + 1) * 64], rhs=Pb[:, h, :], start=True, stop=True),
                                 [kg, Pb], [pw])
                            S.op("pe", lambda e, pu=pu, h=h: e.matmul(pu[:, h * 64:(h + 1) * 64], lhsT=Pb[:, h, :], rhs=vb[:, h * 64:(h + 1) * 64], start=True, stop=True),
                                 [vb, Pb], [pu])
                        S.op("dve", lambda e, pw=pw: e.tensor_copy(out=wT[:].rearrange("p h t -> p (h t)"), in_=pw[0:64, :]), [pw], [wT])
                        S.op("dve", lambda e, pu=pu: e.tensor_scalar_mul(out=ub[:], in0=pu[:, 0:256], scalar1=-1.0), [pu], [ub])
                        pws = C.nextps()
                        for h in range(4):
                            S.op("pe", lambda e, pws=pws, h=h: e.matmul(pws[:, h * 64:(h + 1) * 64], lhsT=wT[:, h, :], rhs=Sb16[:, h * 64:(h + 1) * 64], start=True, stop=True),
                                 [wT, Sb16], [pws])
                        S.op("dve", lambda e, pws=pws: e.tensor_tensor(out=u[:], in0=ub[:], in1=pws[:, 0:256], op=ALU.subtract), [ub, pws], [u])
                        pcr = C.nextps()
                        pin = C.nextps()
                        for h in range(4):
                            S.op("pe", lambda e, pcr=pcr, h=h, i=i: e.matmul(pcr[:, h * 64:(h + 1) * 64], lhsT=qT[i][:, h, :], rhs=Sb16[:, h * 64:(h + 1) * 64], start=True, stop=True),
                                 [qT[i], Sb16], [pcr])
                            S.op("pe", lambda e, pin=pin, h=h: e.matmul(pin[:, h * 64:(h + 1) * 64], lhsT=qkT[:, h, :], rhs=u[:, h * 64:(h + 1) * 64], start=True, stop=True),
                                 [qkT, u], [pin])
                        pS = C.nextps()
                        for h in range(4):
                            S.op("pe", lambda e, pS=pS, h=h: e.matmul(pS[0:64, h * 64:(h + 1) * 64], lhsT=ktl[:, h * 64:(h + 1) * 64], rhs=u[:, h * 64:(h + 1) * 64], start=True, stop=True),
                                 [ktl, u], [pS])
                        S.op("dve", lambda e, pcr=pcr, c=c, rc=rc: e.tensor_tensor(
                            out=ocr[:].rearrange("p (h d) -> p h d", h=4), in0=pcr[:, 0:256].rearrange("p (h d) -> p h d", h=4),
                            in1=eg[:, c, rc].rearrange("p (h o) -> p h o", o=1).broadcast_to([128, 4, 64]), op=ALU.mult), [pcr, eg], [ocr])
                        if r == 0:
                            S.op("dve", lambda e, pin=pin, c=c: e.tensor_tensor(out=Oacc[:, c, :], in0=ocr[:], in1=pin[:, 0:256], op=ALU.add), [ocr, pin], [Oacc.sub(c)])
                        else:
                            S.op("dve", lambda e, pin=pin: e.tensor_tensor(out=ocr[:], in0=ocr[:], in1=pin[:, 0:256], op=ALU.add), [ocr, pin], [ocr])
                            S.op("dve", lambda e, c=c: e.tensor_tensor(out=ocr[:], in0=ocr[:], in1=Oacc[:, c, :], op=ALU.add), [ocr, Oacc.sub(c)], [ocr])
                        S.op("dve", lambda e, c=c, rc=rc: e.tensor_tensor(
                            out=Sf[:].rearrange("p (h d) -> p h d", h=4), in0=Sf[:].rearrange("p (h d) -> p h d", h=4),
                            in1=cd[0:64, c, rc].rearrange("p (h o) -> p h o", o=1).broadcast_to([64, 4, 64]), op=ALU.mult), [Sf, cd], [Sf])
                        S.op("dve", lambda e, pS=pS: e.tensor_tensor(out=Sf[:], in0=Sf[:], in1=pS[0:64, 0:256], op=ALU.add), [Sf, pS], [Sf])
                        S.op("act", lambda e: e.copy(out=Sb16[:], in_=Sf[:]), [Sf], [Sb16])
                        if r == 1 and (full_ctx or c < NLT):
                            gt_ = ggate[i]
                            S.dma("act", lambda e, gt_=gt_, c=c: e.dma_start(out=gt_[:], in_=TMC.t[c * 128:(c + 1) * 128, :]), [], [gt_])
                            S.op("act", lambda e: e.activation(out=gsq[:], in_=ocr[:], func=AF.Square), [ocr], [gsq])
                            S.op("dve", lambda e: e.tensor_reduce(out=gss[:], in_=gsq[:].rearrange("p (h d) -> p h d", h=4), axis=AX.X, op=ALU.add), [gsq], [gss])
                            S.op("dve", lambda e: e.tensor_scalar(out=gss[:], in0=gss[:], scalar1=1.0 / 64, scalar2=1e-6, op0=ALU.mult, op1=ALU.add), [gss], [gss])
                            S.op("act", lambda e: e.activation(out=gss[:], in_=gss[:], func=AF.Sqrt), [gss], [gss])
                            S.op("dve", lambda e: e.reciprocal(out=gss[:], in_=gss[:]), [gss], [gss])
                            S.op("act", lambda e, gt_=gt_: e.activation(out=gsg[:], in_=gt_[:], func=AF.Silu), [gt_], [gsg])
                            S.op("dve", lambda e: e.tensor_tensor(
                                out=ocr[:].rearrange("p (h d) -> p h d", h=4), in0=ocr[:].rearrange("p (h d) -> p h d", h=4),
                                in1=gss[:].rearrange("p (h o) -> p h o", o=1).broadcast_to([128, 4, 64]), op=ALU.mult), [ocr, gss], [ocr])
                            S.op("dve", lambda e: e.tensor_tensor(
                                out=ocr[:].rearrange("p (h d) -> p h d", h=4), in0=ocr[:].rearrange("p (h d) -> p h d", h=4),
                                in1=gnb[:].rearrange("p (o d) -> p o d", o=1).broadcast_to([128, 4, 64]), op=ALU.mult), [ocr, gnb], [ocr])
                            yy = gy[i]
                            S.op("dve", lambda e, yy=yy: e.tensor_tensor(out=yy[:], in0=ocr[:], in1=gsg[:], op=ALU.mult), [ocr, gsg], [yy])
                            S.dma("sp", lambda e, yy=yy, c=c: e.dma_start(out=Y.t[c * 128:(c + 1) * 128, 512:768], in_=yy[:]), [yy], [Y.sub(c, 2)])
                S.barrier()
                S.emit()

        def layer_norm_tile(z, sqt, st, lnG, lnB, gi, outt, eng2="dve"):
            S.op("dve", lambda e: e.tensor_reduce(out=st[:, 0:1], in_=z[:], axis=AX.X, op=ALU.add), [z], [st])
            S.op("dve", lambda e: e.tensor_tensor(out=sqt[:], in0=z[:], in1=z[:], op=ALU.mult), [z], [sqt])
            S.op("dve", lambda e: e.tensor_reduce(out=st[:, 1:2], in_=sqt[:], axis=AX.X, op=ALU.add), [sqt], [st])
            S.op("dve", lambda e: e.tensor_scalar_mul(out=st[:, 0:2], in0=st[:, 0:2], scalar1=1.0 / D), [st], [st])
            S.op("dve", lambda e: e.tensor_tensor(out=st[:, 2:3], in0=st[:, 0:1], in1=st[:, 0:1], op=ALU.mult), [st], [st])
            S.op("dve", lambda e: e.tensor_tensor(out=st[:, 2:3], in0=st[:, 1:2], in1=st[:, 2:3], op=ALU.subtract), [st], [st])
            S.op("dve", lambda e: e.tensor_scalar_add(out=st[:, 2:3], in0=st[:, 2:3], scalar1=LN_EPS), [st], [st])
            S.op("act", lambda e: e.activation(out=st[:, 2:3], in_=st[:, 2:3], func=AF.Sqrt), [st], [st])
            S.op("dve", lambda e: e.reciprocal(out=st[:, 2:3], in_=st[:, 2:3]), [st], [st])
            S.op("dve", lambda e: e.tensor_scalar(out=sqt[:], in0=z[:], scalar1=st[:, 0:1], scalar2=st[:, 2:3], op0=ALU.subtract, op1=ALU.mult),
                 [z, st], [sqt])
            S.op("pool", lambda e: e.tensor_tensor(out=sqt[:], in0=sqt[:], in1=lnG[:, gi, :], op=ALU.mult), [sqt, lnG], [sqt])
            S.op("pool", lambda e: e.tensor_tensor(out=outt[:], in0=sqt[:], in1=lnB[:, gi, :], op=ALU.add), [sqt, lnB], [outt])

        def phase_E(l, full_ctx):
            src = h0 if l == 0 else H.t
            with ExitStack() as pes:
                wo = C.sb("wo", [128, 8, D], BF16, es=pes)
                lnG = C.sb("lnG", [128, 2, D], es=pes)
                lnB = C.sb("lnB", [128, 2, D], es=pes)
                rw = C.sb("rw", [128, 8, 128], es=pes)
                LG = C.sb("LG", [16, T], es=pes)
                yts = [C.sb(f"yt{i}", [128, D], BF16, es=pes) for i in range(2)]
                hts = [C.sb(f"eht{i}", [128, D], es=pes) for i in range(2)]
                YT = C.sb("YT", [128, 8, 128], BF16, es=pes)
                z = C.sb("z", [128, D], es=pes)
                sqt = C.sb("sqt", [128, D], es=pes)
                st = C.sb("st", [128, 4], es=pes)
                hn = [C.sb(f"hn{i}", [128, D], es=pes) for i in range(2)]
                u2b = [C.sb(f"u2b{i}", [128, D], BF16, es=pes) for i in range(2)]
                u2t = C.sb("u2t", [128, D], es=pes)
                u2T = C.sb("u2T", [128, 8, 128], es=pes)
                for k in range(8):
                    S.dma("pool", lambda e, k=k: e.dma_start(out=wo[:, k, :], in_=w_out[l, k * 128:(k + 1) * 128, :]), [], [wo])
                S.dma("sp", lambda e: e.dma_start(out=lnG[:].rearrange("p g d -> p (g d)"), in_=ln_g[l:l + 1, :].broadcast_to([128, 2 * D])), [], [lnG])
                S.dma("sp", lambda e: e.dma_start(out=lnB[:].rearrange("p g d -> p (g d)"), in_=ln_b[l:l + 1, :].broadcast_to([128, 2 * D])), [], [lnB])
                S.dma("sp", lambda e: e.dma_start(out=rw[:], in_=router_wp[l, :, :].rearrange("(k p) e -> p k e", p=128)), [], [rw])
                tiles = list(range(NT)) if full_ctx else list(range(NLT))
                for ti, t in enumerate(tiles):
                    lc = 0 if t < NLT else 1
                    yt = yts[ti % 2]
                    ht = hts[ti % 2]
                    S.dma("sp", lambda e, yt=yt, t=t: e.dma_start(out=yt[:], in_=Y.t[t * 128:(t + 1) * 128, :]), [], [yt])
                    S.dma("act", lambda e, ht=ht, t=t: e.dma_start(out=ht[:], in_=src[t * 128:(t + 1) * 128, :]), ["H/%d" % t], [ht])
                    ps = C.nextps()
                    pv = ps.t[:, :].bitcast(BF16)
                    for k in range(8):
                        S.op("pe", lambda e, pv=pv, yt=yt, k=k: e.transpose(pv[:, k * 128:(k + 1) * 128], yt[:, k * 128:(k + 1) * 128], ident_b[:]),
                             [yt, ident_b], [ps])
                    S.op("act", lambda e, pv=pv: e.copy(out=YT[:].rearrange("p k t -> p (k t)"), in_=pv[:, :]), [ps], [YT])
                    for half in range(2):
                        pm = C.nextps()
                        for k in range(8):
                            S.op("pe", lambda e, pm=pm, k=k, half=half: e.matmul(pm[:, :], lhsT=YT[:, k, :], rhs=wo[:, k, half * 512:(half + 1) * 512],
                                                                                 start=(k == 0), stop=(k == 7)), [YT, wo], [pm])
                        S.op("dve", lambda e, pm=pm, half=half, lc=lc: e.tensor_tensor(
                            out=z[:, half * 512:(half + 1) * 512], in0=pm[:, :], in1=gateB[:, 0, lc, half * 512:(half + 1) * 512], op=ALU.mult),
                            [pm, gateB], [z])
                    S.op("dve", lambda e, ht=ht: e.scalar_tensor_tensor(out=z[:], in0=ht[:], scalar=ALPHA, in1=z[:], op0=ALU.mult, op1=ALU.add),
                         [ht, z], [z])
                    h_ = hn[ti % 2]
                    layer_norm_tile(z, sqt, st, lnG, lnB, 0, h_)
                    S.dma("sp", lambda e, h_=h_, t=t: e.dma_start(out=H.t[t * 128:(t + 1) * 128, :], in_=h_[:]), [h_], ["H/%d" % t])
                    ub_ = u2b[ti % 2]
                    S.op("pool", lambda e, h_=h_, lc=lc: e.tensor_tensor(out=u2t[:], in0=h_[:], in1=gateB[:, 3, lc, :], op=ALU.mult), [h_, gateB], [u2t])
                    S.op("pool", lambda e, ub_=ub_, lc=lc: e.tensor_tensor(out=ub_[:], in0=u2t[:], in1=gateB[:, 2, lc, :], op=ALU.add), [u2t, gateB], [ub_])
                    S.dma("act", lambda e, ub_=ub_, t=t: e.dma_start(out=U2.t[t * 128:(t + 1) * 128, :], in_=ub_[:]), [ub_], [U2.sub(t)])
                    for half in range(2):
                        pt = C.nextps()
                        for j in range(4):
                            k = half * 4 + j
                            S.op("pe", lambda e, pt=pt, h_=h_, j=j, k=k: e.transpose(pt[:, j * 128:(j + 1) * 128], h_[:, k * 128:(k + 1) * 128], ident_f[:]),
                                 [h_, ident_f], [pt])
                        for j in range(4):
                            k = half * 4 + j
                            S.op("act", lambda e, pt=pt, j=j, k=k, lc=lc: e.activation(
                                out=u2T[:, k, :], in_=pt[:, j * 128:(j + 1) * 128], func=AF.Identity,
                                bias=modT[:, k, 2, lc:lc + 1], scale=modT[:, k, 3, lc:lc + 1]), [pt, modT], [u2T])
                    pl_ = C.nextps()
                    for k in range(8):
                        S.op("pe", lambda e, pl_=pl_, k=k: e.matmul(pl_[0:16, 0:128], lhsT=rw[:, k, 0:16], rhs=u2T[:, k, :], start=(k == 0), stop=(k == 7)),
                             [rw, u2T], [pl_])
                    S.op("dve", lambda e, pl_=pl_, t=t: e.tensor_copy(out=LG[:, t * 128:(t + 1) * 128], in_=pl_[0:16, 0:128]), [pl_], [LG])
                S.dma("sp", lambda e: e.dma_start(out=LGD.t[:, :], in_=LG[:]), [LG], [LGD])
                S.barrier()
                S.emit()

        def moe_segs(full_ctx):
            segs = [(0, NLT, NLAT // 8, 0, 0)]
            if full_ctx:
                segs.append((NLT, 2, 32, NLAT // 8, 512))
            return segs

        def phase_F(l, full_ctx):
            segs = moe_segs(full_ctx)
            NTOT = sum(sg[2] for sg in segs)
            with ExitStack() as pes:
                LG = C.sb("fLG", [16, T], es=pes)
                aff = C.sb("aff", [16, T], es=pes)
                work = C.sb("work", [16, NLAT], es=pes)
                m8 = C.sb("m8", [16, 8], es=pes)
                key = C.sb("key", [16, T], es=pes)
                ones16 = C.sb("ones16", [16, 16], es=pes)
                onesr = C.sb("onesr", [16, NLAT], es=pes)
                id16p = C.sb("id16p", [16, 128], es=pes)
                keyTok = C.sb("keyTok", [128, NT, 16], es=pes)
                iot = C.sb("iot", [128, 512], es=pes)
                S.dma("sp", lambda e: e.dma_start(out=LG[:], in_=LGD.t[:, :]), [], [LG])
                S.dma("sp", lambda e: e.dma_start(out=iot[:], in_=iota512[:, :]), [], [iot])
                S.op("pool", lambda e: e.memset(ones16[:], 1.0), [], [ones16])
                S.op("pool", lambda e: e.memset(onesr[:], 1.0), [], [onesr])
                S.op("pool", lambda e: e.memset(id16p[:], 0.0), [], [id16p])
                S.op("dve", lambda e: e.tensor_copy(out=id16p[:, 0:16], in_=ident_f[0:16, 0:16]), [id16p, ident_f], [id16p])
                S.op("act", lambda e: e.activation(out=aff[:], in_=LG[:], func=AF.Exp), [LG], [aff])
                for c0 in range(0, T, 512):
                    cw = min(512, T - c0)
                    ps = C.nextps()
                    S.op("pe", lambda e, ps=ps, c0=c0, cw=cw: e.matmul(ps[0:16, 0:cw], lhsT=ones16[:], rhs=aff[:, c0:c0 + cw], start=True, stop=True),
                         [ones16, aff], [ps])
                    S.op("dve", lambda e, ps=ps, c0=c0, cw=cw: e.reciprocal(out=LG[:, c0:c0 + cw], in_=ps[0:16, 0:cw]), [ps], [LG])
                S.op("dve", lambda e: e.tensor_tensor(out=aff[:], in0=aff[:], in1=LG[:], op=ALU.mult), [aff, LG], [aff])
                for (t0, ntl, cap, c0, r0) in segs:
                    n_ = ntl * 128
                    a0 = t0 * 128
                    S.op("dve", lambda e, a0=a0, n_=n_: e.tensor_copy(out=work[:, 0:n_], in_=aff[:, a0:a0 + n_]), [aff], [work])
                    for rnd in range(cap // 8):
                        S.op("dve", lambda e, n_=n_: e.max(out=m8[:], in_=work[:, 0:n_]), [work], [m8])
                        S.op("dve", lambda e, n_=n_: e.match_replace(out=work[:, 0:n_], in_to_replace=m8[:], in_values=work[:, 0:n_], imm_value=-1.0),
                             [work, m8], [work])
                    S.op("dve", lambda e, n_=n_: e.tensor_single_scalar(out=work[:, 0:n_], in_=work[:, 0:n_], scalar=0.0, op=ALU.is_lt), [work], [work])
                    S.op("dve", lambda e, a0=a0, n_=n_: e.tensor_tensor_scan(out=key[:, a0:a0 + n_], data0=onesr[:, 0:n_], data1=work[:, 0:n_], initial=0.0,
                                                                            op0=ALU.mult, op1=ALU.add), [work, onesr], [key])
                    S.op("dve", lambda e, a0=a0, n_=n_: e.tensor_tensor(out=key[:, a0:a0 + n_], in0=key[:, a0:a0 + n_], in1=work[:, 0:n_], op=ALU.mult), [key, work], [key])
                    S.op("dve", lambda e, a0=a0, n_=n_: e.tensor_scalar_add(out=key[:, a0:a0 + n_], in0=key[:, a0:a0 + n_], scalar1=-1.0), [key], [key])
                S.dma("sp", lambda e: e.dma_start(out=KEYD.t[:, :], in_=key[:]), [key], [KEYD])
                S.dma("sp", lambda e: e.dma_start(out=AFFD.t[:, :], in_=aff[:]), [aff], [AFFD])
                tl_all = [t for (t0, ntl, cap, c0, r0) in segs for t in range(t0, t0 + ntl)]
                for q0 in range(0, len(tl_all), 4):
                    grp = tl_all[q0:q0 + 4]
                    ps = C.nextps()
                    for qi, t in enumerate(grp):
                        S.op("pe", lambda e, ps=ps, qi=qi, t=t: e.matmul(ps[:, qi * 128:(qi + 1) * 128], lhsT=key[:, t * 128:(t + 1) * 128], rhs=id16p[:],
                                                                          start=True, stop=True), [key, id16p], [ps])
                    for qi, t in enumerate(grp):
                        S.op("dve", lambda e, ps=ps, qi=qi, t=t: e.tensor_copy(out=keyTok[:, t, :], in_=ps[:, qi * 128:qi * 128 + 16]), [ps], [keyTok])
                S.dma("sp", lambda e: e.dma_start(out=KTD.t[:, :], in_=keyTok[:].rearrange("p n c -> p (n c)")), [keyTok], [KTD])
                S.barrier()
                S.emit()
            with ExitStack() as pes:
                keyTok = C.sb("keyTok2", [128, NT, 16], es=pes)
                iot = C.sb("iot2", [128, 512], es=pes)
                Se = C.sb("Se", [128, NT, 512], BF16, es=pes)
                xinT = C.sb("xinT", [128, 8, NTOT], BF16, es=pes)
                hT = C.sb("hT", [128, 16, NTOT], BF16, es=pes)
                yacc = C.sb("yacc", [128, 5, D], es=pes)
                yw = C.sb("yw", [128, 5, D], BF16, es=pes)
                u2s = [C.sb(f"u2s{i}", [128, D], BF16, es=pes) for i in range(2)]
                wg = [C.sb(f"wg{i}", [128, 8, 512], BF16, es=pes) for i in range(2)]
                wu = [C.sb(f"wu{i}", [128, 8, 512], BF16, es=pes) for i in range(2)]
                wd = [C.sb(f"wd{i}", [128, 4, D], BF16, es=pes) for i in range(2)]
                sg_ = [C.sb(f"sgt{i}", [128, 512], es=pes) for i in range(2)]
                S.dma("sp", lambda e: e.dma_start(out=keyTok[:].rearrange("p n c -> p (n c)"), in_=KTD.t[:, :]), [], [keyTok])
                S.dma("sp", lambda e: e.dma_start(out=iot[:], in_=iota512[:, :]), [], [iot])
                C.rot = [4, 5, 6, 7]
                wcnt = 0
                ucnt = 0
                for ex in range(16):
                    for (t0, ntl, cap, c0, r0) in segs:
                        for t in range(t0, t0 + ntl):
                            S.op("dve", lambda e, t=t, cap=cap, ex=ex: e.tensor_scalar(
                                out=Se[:, t, 0:cap], in0=iot[:, 0:cap], scalar1=keyTok[:, t, ex:ex + 1], scalar2=None, op0=ALU.is_equal),
                                [iot, keyTok], [Se.sub(t)])
                        for kh in range(2):
                            for ti, t in enumerate(range(t0, t0 + ntl)):
                                ut = u2s[ucnt % 2]
                                ucnt += 1
                                S.dma("sp", lambda e, ut=ut, t=t: e.dma_start(out=ut[:], in_=U2.t[t * 128:(t + 1) * 128, :]), [], [ut])
                                for kk in range(4):
                                    k = kh * 4 + kk
                                    S.op("pe", lambda e, ut=ut, t=t, k=k, kk=kk, cap=cap, ti=ti, ntl=ntl: e.matmul(
                                        C.ps[kk][:, 0:cap], lhsT=ut[:, k * 128:(k + 1) * 128], rhs=Se[:, t, 0:cap],
                                        start=(ti == 0), stop=(ti == ntl - 1)), [ut, Se.sub(t)], [C.ps[kk]])
                            for kk in range(4):
                                k = kh * 4 + kk
                                S.op("act", lambda e, k=k, kk=kk, c0=c0, cap=cap: e.copy(out=xinT[:, k, c0:c0 + cap], in_=C.ps[kk][:, 0:cap]),
                                     [C.ps[kk]], [xinT])
                    jts = []
                    for (t0, ntl, cap, c0, r0) in segs:
                        for j0 in range(0, cap, 128):
                            jts.append((r0 + j0, min(128, cap - j0), c0 + j0))
                    for fg in range(4):
                        g_, u_, d_ = wg[wcnt % 2], wu[wcnt % 2], wd[wcnt % 2]
                        wcnt += 1
                        for k in range(8):
                            S.dma("pool", lambda e, g_=g_, ex=ex, fg=fg, k=k: e.dma_start(
                                out=g_[:, k, :], in_=w_gate[l, ex, k * 128:(k + 1) * 128, fg * 512:(fg + 1) * 512]), [], [g_])
                            S.dma("pool", lambda e, u_=u_, ex=ex, fg=fg, k=k: e.dma_start(
                                out=u_[:, k, :], in_=w_up[l, ex, k * 128:(k + 1) * 128, fg * 512:(fg + 1) * 512]), [], [u_])
                        for cc in range(4):
                            S.dma("pool", lambda e, d_=d_, ex=ex, fg=fg, cc=cc: e.dma_start(
                                out=d_[:, cc, :], in_=w_down[l, ex, fg * 512 + cc * 128:fg * 512 + (cc + 1) * 128, :]), [], [d_])
                        for fc in range(4):
                            f = fg * 4 + fc
                            for (t0, ntl, cap, c0, r0) in segs:
                                pg = C.nextps()
                                pu_ = C.nextps()
                                for k in range(8):
                                    S.op("pe", lambda e, pg=pg, g_=g_, k=k, fc=fc, c0=c0, cap=cap: e.matmul(
                                        pg[:, 0:cap], lhsT=g_[:, k, fc * 128:(fc + 1) * 128], rhs=xinT[:, k, c0:c0 + cap], start=(k == 0), stop=(k == 7)),
                                        [g_, xinT], [pg])
                                for k in range(8):
                                    S.op("pe", lambda e, pu_=pu_, u_=u_, k=k, fc=fc, c0=c0, cap=cap: e.matmul(
                                        pu_[:, 0:cap], lhsT=u_[:, k, fc * 128:(fc + 1) * 128], rhs=xinT[:, k, c0:c0 + cap], start=(k == 0), stop=(k == 7)),
                                        [u_, xinT], [pu_])
                                sgt = sg_[f % 2]
                                S.op("act", lambda e, pg=pg, sgt=sgt, cap=cap: e.activation(out=sgt[:, 0:cap], in_=pg[:, 0:cap], func=AF.Silu), [pg], [sgt])
                                S.op("dve", lambda e, pu_=pu_, sgt=sgt, f=f, c0=c0, cap=cap: e.tensor_tensor(
                                    out=hT[:, f, c0:c0 + cap], in0=pu_[:, 0:cap], in1=sgt[:, 0:cap], op=ALU.mult), [pu_, sgt], [hT])
                        for ji, (ro, rows, co) in enumerate(jts):
                            for half in range(2):
                                py = C.nextps()
                                for fc in range(4):
                                    S.op("pe", lambda e, py=py, d_=d_, fc=fc, fg=fg, co=co, rows=rows, half=half: e.matmul(
                                        py[0:rows, :], lhsT=hT[:, fg * 4 + fc, co:co + rows], rhs=d_[:, fc, half * 512:(half + 1) * 512],
                                        start=(fc == 0), stop=(fc == 3)), [hT, d_], [py])
                                if fg == 0:
                                    S.op("dve", lambda e, py=py, ji=ji, rows=rows, half=half: e.tensor_copy(
                                        out=yacc[0:rows, ji, half * 512:(half + 1) * 512], in_=py[0:rows, :]), [py], [yacc])
                                elif fg < 3:
                                    S.op("dve", lambda e, py=py, ji=ji, rows=rows, half=half: e.tensor_tensor(
                                        out=yacc[0:rows, ji, half * 512:(half + 1) * 512], in0=yacc[0:rows, ji, half * 512:(half + 1) * 512],
                                        in1=py[0:rows, :], op=ALU.add), [py, yacc], [yacc])
                                else:
                                    S.op("dve", lambda e, py=py, ji=ji, rows=rows, half=half: e.tensor_tensor(
                                        out=yw[0:rows, ji, half * 512:(half + 1) * 512], in0=yacc[0:rows, ji, half * 512:(half + 1) * 512],
                                        in1=py[0:rows, :], op=ALU.add), [py, yacc], [yw])
                    for ji, (ro, rows, co) in enumerate(jts):
                        S.dma("act", lambda e, ji=ji, ro=ro, rows=rows, ex=ex: e.dma_start(out=YE.t[ex, ro:ro + rows, :], in_=yw[0:rows, ji, :]),
                              [yw], [YE.sub(ex, ji)])
                C.rot = None
                S.barrier()
                S.emit()

        def phase_G(l, full_ctx, last):
            segs = moe_segs(full_ctx)
            with ExitStack() as pes:
                key = C.sb("gkey", [16, T], es=pes)
                aff = C.sb("gaff", [16, T], es=pes)
                selE = C.sb("selE", [16, 16, 128], es=pes)
                jcol = C.sb("jcol", [128, 4], es=pes)
                lnG = C.sb("glnG", [128, 2, D], es=pes)
                lnB = C.sb("glnB", [128, 2, D], es=pes)
                acc = C.sb("acc", [128, 8, D], es=pes)
                yes = [C.sb(f"ye{i}", [128, 4, D], BF16, es=pes) for i in range(2)]
                Ws = [C.sb(f"W{i}", [128, 4, 512], BF16, es=pes) for i in range(2)]
                affB = [C.sb(f"affB{i}", [128, 512], es=pes) for i in range(2)]
                hts = [C.sb(f"ght{i}", [128, D], es=pes) for i in range(2)]
                z = C.sb("gz", [128, D], es=pes)
                sqt = C.sb("gsqt", [128, D], es=pes)
                st = C.sb("gst", [128, 4], es=pes)
                hn = [C.sb(f"ghn{i}", [128, D], es=pes) for i in range(2)]
                S.dma("sp", lambda e: e.dma_start(out=key[:], in_=KEYD.t[:, :]), [], [key])
                S.dma("sp", lambda e: e.dma_start(out=aff[:], in_=AFFD.t[:, :]), [], [aff])
                S.dma("sp", lambda e: e.dma_start(out=selE[:], in_=selE_in[:, :, :]), [], [selE])
                S.dma("sp", lambda e: e.dma_start(out=jcol[:], in_=jcol_in[:, :]), [], [jcol])
                S.dma("sp", lambda e: e.dma_start(out=lnG[:].rearrange("p g d -> p (g d)"), in_=ln_g[l:l + 1, :].broadcast_to([128, 2 * D])), [], [lnG])
                S.dma("sp", lambda e: e.dma_start(out=lnB[:].rearrange("p g d -> p (g d)"), in_=ln_b[l:l + 1, :].broadcast_to([128, 2 * D])), [], [lnB])
                cnt = 0
                for (t0, ntl, cap, c0, r0) in segs:
                    lc = 0 if t0 < NLT else 1
                    njt = (cap + 127) // 128
                    for gq in range(t0, t0 + ntl, 8):
                        gtiles = list(range(gq, min(gq + 8, t0 + ntl)))
                        for ex in range(16):
                            ye = yes[ex % 2]
                            for jt in range(njt):
                                rows = min(128, cap - jt * 128)
                                S.dma("sp", lambda e, ye=ye, jt=jt, rows=rows, ex=ex, r0=r0: e.dma_start(
                                    out=ye[0:rows, jt, :], in_=YE.t[ex, r0 + jt * 128:r0 + jt * 128 + rows, :]), [ye], [ye])
                            for q0 in range(0, len(gtiles), 4):
                                ch = gtiles[q0:q0 + 4]
                                a0 = ch[0] * 128
                                cw = len(ch) * 128
                                pk = C.nextps()
                                pa = C.nextps()
                                S.op("pe", lambda e, pk=pk, ex=ex, a0=a0, cw=cw: e.matmul(pk[:, 0:cw], lhsT=selE[:, ex, :], rhs=key[:, a0:a0 + cw], start=True, stop=True),
                                     [selE, key], [pk])
                                S.op("pe", lambda e, pa=pa, ex=ex, a0=a0, cw=cw: e.matmul(pa[:, 0:cw], lhsT=selE[:, ex, :], rhs=aff[:, a0:a0 + cw], start=True, stop=True),
                                     [selE, aff], [pa])
                                ab = affB[cnt % 2]
                                W = Ws[cnt % 2]
                                cnt += 1
                                S.op("act", lambda e, pa=pa, ab=ab, cw=cw: e.copy(out=ab[:, 0:cw], in_=pa[:, 0:cw]), [pa], [ab])
                                for jt in range(njt):
                                    S.op("dve", lambda e, pk=pk, ab=ab, W=W, jt=jt, cw=cw: e.scalar_tensor_tensor(
                                        out=W[:, jt, 0:cw], in0=pk[:, 0:cw], scalar=jcol[:, jt:jt + 1], in1=ab[:, 0:cw], op0=ALU.is_equal, op1=ALU.mult),
                                        [pk, ab, jcol], [W])
                                for qi, t in enumerate(ch):
                                    ti = t - gq
                                    for half in range(2):
                                        py = C.nextps()
                                        for jt in range(njt):
                                            rows = min(128, cap - jt * 128)
                                            S.op("pe", lambda e, py=py, W=W, ye=ye, jt=jt, rows=rows, qi=qi, half=half, njt=njt: e.matmul(
                                                py[:, :], lhsT=W[0:rows, jt, qi * 128:(qi + 1) * 128], rhs=ye[0:rows, jt, half * 512:(half + 1) * 512],
                                                start=(jt == 0), stop=(jt == njt - 1)), [W, ye], [py])
                                        if ex == 0:
                                            S.op("dve", lambda e, py=py, ti=ti, half=half: e.tensor_copy(out=acc[:, ti, half * 512:(half + 1) * 512], in_=py[:, :]),
                                                 [py], [acc.sub(ti)])
                                        else:
                                            S.op("dve", lambda e, py=py, ti=ti, half=half: e.tensor_tensor(
                                                out=acc[:, ti, half * 512:(half + 1) * 512], in0=acc[:, ti, half * 512:(half + 1) * 512], in1=py[:, :], op=ALU.add),
                                                [py, acc.sub(ti)], [acc.sub(ti)])
                        for t in gtiles:
                            ti = t - gq
                            ht = hts[t % 2]
                            S.dma("act", lambda e, ht=ht, t=t: e.dma_start(out=ht[:], in_=H.t[t * 128:(t + 1) * 128, :]), ["H/%d" % t], [ht])
                            S.op("pool", lambda e, ti=ti, lc=lc: e.tensor_tensor(out=z[:], in0=acc[:, ti, :], in1=gateB[:, 1, lc, :], op=ALU.mult),
                                 [acc.sub(ti), gateB], [z])
                            S.op("dve", lambda e, ht=ht: e.scalar_tensor_tensor(out=z[:], in0=ht[:], scalar=ALPHA, in1=z[:], op0=ALU.mult, op1=ALU.add),
                                 [ht, z], [z])
                            h_ = hn[t % 2]
                            layer_norm_tile(z, sqt, st, lnG, lnB, 1, h_)
                            if last:
                                S.dma("sp", lambda e, h_=h_, t=t: e.dma_start(out=out[t * 128:(t + 1) * 128, :], in_=h_[:]), [h_], ["out/%d" % t])
                            else:
                                S.dma("sp", lambda e, h_=h_, t=t: e.dma_start(out=H.t[t * 128:(t + 1) * 128, :], in_=h_[:]), [h_], ["H/%d" % t])
                S.barrier()
                S.emit()

        for l in range(NL):
            full_ctx = l < DEPTH - 1
            if "A" in phases.split(","):
                phase_A(l)
            if "BC" in phases.split(","):
                phase_BC(l)
            if "ret" in phases.split(","):
                phase_ret(l, full_ctx)
            if "diff" in phases.split(","):
                phase_diff(l, full_ctx)
            if "swa" in phases.split(","):
                phase_swa(l, full_ctx)
            if "gdn" in phases.split(","):
                phase_gdn(l, full_ctx)
            if "E" in phases.split(","):
                phase_E(l, full_ctx)
            if "F" in phases.split(","):
                phase_F(l, full_ctx)
            if "G" in phases.split(","):
                phase_G(l, full_ctx, last=(l == NL - 1 and "keepH" not in dbg))
    dbg_t["ninst"] = S.ninst
    return nc, dbg_t


def prep_shared(inputs, NLT, NL):
    sh = {}
    ada_b = np.asarray(inputs["ada_b"], np.float32)[:NL]
    sh["ada_w"] = np.ascontiguousarray(np.asarray(inputs["ada_w"], np.float32)[:NL])
    sh["ada_b"] = np.ascontiguousarray(ada_b)
    sh["ada_bT"] = np.ascontiguousarray(ada_b.reshape(NL, 48, 128).transpose(0, 2, 1))
    w_in = np.asarray(inputs["w_in"], np.float32)
    sh["w_in"] = np.stack([build_w_in_ext(w_in[l]) for l in range(NL)])
    sh["w_out"] = np.ascontiguousarray(np.asarray(inputs["w_out"], np.float32)[:NL])
    sh["rope"] = rope_tables(NLT)
    sh["ret_decay"] = np.ascontiguousarray(np.asarray(inputs["ret_decay"], np.float32)[:NL].reshape(NL, 8))
    sh["ln_g"] = np.ascontiguousarray(np.asarray(inputs["ln_g"], np.float32)[:NL].reshape(NL, 2 * D))
    sh["ln_b"] = np.ascontiguousarray(np.asarray(inputs["ln_b"], np.float32)[:NL].reshape(NL, 2 * D))
    sh["diff_lambda"] = np.ascontiguousarray(np.asarray(inputs["diff_lambda"], np.float32)[:NL].reshape(NL, 128))
    sh["diff_norm"] = np.ascontiguousarray(np.asarray(inputs["diff_norm"], np.float32)[:NL])
    sh["swa_sink"] = np.ascontiguousarray(np.asarray(inputs["swa_sink"], np.float32)[:NL])
    rw = np.asarray(inputs["router_w"], np.float32)[:NL]
    rwp = np.zeros((NL, D, 128), np.float32)
    rwp[:, :, :16] = rw
    sh["router_wp"] = rwp
    sh["w_gate"] = np.ascontiguousarray(np.asarray(inputs["w_gate"], np.float32)[:NL])
    sh["w_up"] = np.ascontiguousarray(np.asarray(inputs["w_up"], np.float32)[:NL])
    sh["w_down"] = np.ascontiguousarray(np.asarray(inputs["w_down"], np.float32)[:NL])
    sh["iota512"] = np.ascontiguousarray(np.broadcast_to(np.arange(512, dtype=np.float32)[None, :], (128, 512)))
    sh["jcol_in"] = np.ascontiguousarray(np.arange(128, dtype=np.float32)[:, None] + 128.0 * np.arange(4, dtype=np.float32)[None, :])
    se = np.zeros((16, 16, 128), np.float32)
    for e_ in range(16):
        se[e_, e_, :] = 1.0
    sh["selE_in"] = se
    gc = np.asarray(inputs["gdn_conv"], np.float32)[:NL]
    sh["gdn_convT"] = np.ascontiguousarray(gc.reshape(NL, 3, 6, 128).transpose(0, 3, 2, 1).reshape(NL, 128, 18))
    sh["gdn_a_log"] = np.ascontiguousarray(np.asarray(inputs["gdn_a_log"], np.float32)[:NL].reshape(NL, 8))
    sh["gdn_dt_bias"] = np.ascontiguousarray(np.asarray(inputs["gdn_dt_bias"], np.float32)[:NL].reshape(NL, 8))
    sh["gdn_norm"] = np.ascontiguousarray(np.asarray(inputs["gdn_norm"], np.float32)[:NL])
    i = np.arange(128, dtype=np.float32)
    sh["cmask"] = np.ascontiguousarray(i[None, :] - i[:, None])
    return sh


def prep_core(inputs, b, NLT):
    m = {}
    x = np.asarray(inputs["x"], np.float32)[b, :NLT * 128]
    ctx = np.asarray(inputs["ctx"], np.float32)[b]
    m["h0"] = np.ascontiguousarray(np.concatenate([x, ctx], 0))
    c = np.asarray(inputs["c"], np.float32)[b]
    cx = np.asarray(inputs["c_ctx"], np.float32)
    m["cc"] = np.ascontiguousarray(np.stack([c, cx], -1).reshape(8, 128, 2).transpose(1, 0, 2))
    return m


def kernel(**inputs):
    NLT, NL = 32, 4
    nc, _ = build(NLT, NL)
    sh = prep_shared(inputs, NLT, NL)
    in_maps = []
    for b in range(8):
        m = dict(sh)
        m.update(prep_core(inputs, b, NLT))
        in_maps.append(m)
    res = run_bass_kernel_spmd(nc, in_maps, core_ids=list(range(8)))
    return np.stack([np.asarray(r["out"], np.float32) for r in res.results], 0)
```

```python
import math
import numpy as np
import ml_dtypes
from contextlib import ExitStack
import concourse.bass as bass
import concourse.mybir as mybir
from concourse.bass_utils import run_bass_kernel_spmd

F32 = mybir.dt.float32
BF16 = mybir.dt.bfloat16
F32R = mybir.dt.float32r
AF = mybir.ActivationFunctionType
ALU = mybir.AluOpType
AX = mybir.AxisListType

ENGS = ["pe", "dve", "act", "pool", "sp"]
DMA_RING = 6
SEM_EPOCH = 20000

D = 1024
DEPTH = 4
ALPHA = (2 * DEPTH) ** 0.25
LN_EPS = 1e-5
IN_W = 3344
NEG = -30000.0


class Sched:
    def __init__(self, nc, es, same_engine_sync=True):
        self.nc = nc
        self.es = es
        self.q = {e: [] for e in ENGS}
        self.cnt = {e: 0 for e in ENGS}
        self.epoch = {e: 0 for e in ENGS}
        self.sem = {e: es.enter_context(nc.semaphore(f"s_{e}_0")) for e in ENGS}
        self.dq = ["sp", "act", "pool"]
        self.dsem = {e: [es.enter_context(nc.semaphore(f"d_{e}_{i}")) for i in range(DMA_RING)]
                     for e in self.dq}
        self.dcnt = {e: 0 for e in self.dq}
        self.lastw = {}
        self.readers = {}
        self.seen = {}
        self.same = same_engine_sync
        self.ninst = 0

    def _key(self, k):
        return k if isinstance(k, str) else k.k

    def _need(self, eng, tok, waits):
        if tok is None:
            return
        sem, val, prod = tok
        if prod == eng and (eng == "pe" or not self.same):
            return
        kk = (eng, id(sem))
        if self.seen.get(kk, 0) >= val:
            return
        self.seen[kk] = val
        waits.append((sem, val))

    def _deps(self, eng, reads, writes):
        waits = []
        for k in reads:
            k = self._key(k)
            self._need(eng, self.lastw.get(k), waits)
            if k.startswith("ps"):
                for t in self.readers.get(k, ()):
                    if t[2] != eng:
                        self._need(eng, t, waits)
        for k in writes:
            k = self._key(k)
            self._need(eng, self.lastw.get(k), waits)
            for t in self.readers.get(k, ()):
                self._need(eng, t, waits)
        return waits

    def _commit(self, tok, reads, writes):
        for k in reads:
            self.readers.setdefault(self._key(k), []).append(tok)
        for k in writes:
            k = self._key(k)
            self.lastw[k] = tok
            self.readers[k] = []

    def op(self, eng, fn, reads=(), writes=()):
        waits = self._deps(eng, reads, writes)
        if self.cnt[eng] >= SEM_EPOCH:
            self.epoch[eng] += 1
            self.sem[eng] = self.es.enter_context(self.nc.semaphore(f"s_{eng}_{self.epoch[eng]}"))
            self.cnt[eng] = 0
        self.cnt[eng] += 1
        tok = (self.sem[eng], self.cnt[eng], eng)
        self.q[eng].append((waits, fn, self.sem[eng], 1))
        self._commit(tok, reads, writes)
        self.ninst += 1

    def dma(self, qn, fn, reads=(), writes=()):
        waits = self._deps(qn, reads, writes)
        i = self.dcnt[qn]
        self.dcnt[qn] += 1
        slot, rnd = i % DMA_RING, i // DMA_RING
        sem = self.dsem[qn][slot]
        if rnd > 0:
            self._need(qn, (sem, 16 * rnd, None), waits)
        tok = (sem, 16 * (rnd + 1), None)
        self.q[qn].append((waits, fn, sem, 16))
        self._commit(tok, reads, writes)
        self.ninst += 1

    def barrier(self):
        toks = []
        for e in ENGS:
            if self.cnt[e] > 0:
                toks.append((self.sem[e], self.cnt[e], e))
        for qn in self.dq:
            n = self.dcnt[qn]
            for slot in range(DMA_RING):
                if n > slot:
                    rounds = (n - 1 - slot) // DMA_RING + 1
                    toks.append((self.dsem[qn][slot], 16 * rounds, None))
        for e in ENGS:
            waits = []
            for t in toks:
                if t[2] == e:
                    continue
                self._need(e, t, waits)
            self.q[e].append((waits, None, None, 0))
        self.lastw = {}
        self.readers = {}

    def emit(self):
        nc = self.nc
        q = self.q

        def run(e, name):
            for waits, fn, sem, inc in q[name]:
                for (s, v) in waits:
                    e.wait_ge(s, v)
                if fn is not None:
                    fn(e).then_inc(sem, inc)

        with nc.Block() as block:
            @block.tensor
            def _(e):
                run(e, "pe")

            @block.vector
            def _(e):
                run(e, "dve")

            @block.scalar
            def _(e):
                run(e, "act")

            @block.gpsimd
            def _(e):
                run(e, "pool")

            @block.sync
            def _(e):
                run(e, "sp")
        self.q = {e: [] for e in ENGS}


class Buf:
    def __init__(self, t, k):
        self.t = t
        self.k = k

    def sub(self, *idx):
        return self.k + "/" + "/".join(str(i) for i in idx)

    def __getitem__(self, key):
        return self.t[key]


class Ctx:
    def __init__(self, nc, es, same_engine_sync=True):
        self.nc = nc
        self.es = es
        self.S = Sched(nc, es, same_engine_sync)
        self.ps = [self.psum(f"ps{i}") for i in range(8)]
        self.psi = 0
        self.uid = 0

    def sb(self, name, shape, dtype=F32, es=None):
        self.uid += 1
        nm = f"{name}_{self.uid}"
        t = (es or self.es).enter_context(self.nc.sbuf_tensor(nm, list(shape), dtype))
        return Buf(t, nm)

    def psum(self, name, shape=(128, 512), dtype=F32):
        t = self.es.enter_context(self.nc.psum_tensor(name, list(shape), dtype))
        return Buf(t, name)

    def dram(self, name, shape, dtype=F32, kind="Internal"):
        t = self.nc.dram_tensor(name, list(shape), dtype, kind=kind)
        return Buf(t, name)

    def nextps(self):
        r = getattr(self, "rot", None) or list(range(8))
        p = self.ps[r[self.psi % len(r)]]
        self.psi += 1
        return p


def fm_groups():
    def swp(cols, blk):
        cols = np.asarray(cols)
        half = blk // 2
        c = cols.reshape(-1, blk)
        return np.concatenate([c[:, half:], c[:, :half]], 1).reshape(-1)
    g = []
    def add_rot(name, start, width, blk, kind):
        for i in range(width // 128):
            cols = np.arange(start + i * 128, start + (i + 1) * 128)
            g.append((f"{name}{i}", cols, kind, swp(cols, blk)))
    add_rot("ret_q", 0, 256, 64, 0)
    add_rot("ret_k", 256, 256, 64, 0)
    add_rot("diff_q", 1024, 256, 32, 1)
    add_rot("diff_k", 1280, 256, 32, 1)
    for i in range(6):
        g.append((f"gdn{i}", np.arange(1792 + i * 128, 1792 + (i + 1) * 128), None, None))
    add_rot("swa_q", 2832, 256, 64, 2)
    add_rot("swa_k", 3088, 128, 64, 2)
    return g


TM_GROUPS = [("ret_vg", 512, 512), ("diff_v", 1536, 256), ("gdn_gab", 2560, 272), ("swa_v", 3216, 128)]


def build_w_in_ext(w_in_l):
    cols = []
    for name, c, kind, sw in fm_groups():
        cols.append(c)
        if kind is not None:
            cols.append(sw)
    for name, st, w in TM_GROUPS:
        cols.append(np.arange(st, st + w))
    cols = np.concatenate(cols)
    return np.ascontiguousarray(w_in_l[:, cols])


def rope_tables(NLT):
    n = NLT * 128
    T = n + 256
    tabs = np.zeros((3, 2, 128, T), np.float32)
    tabs[:, 0] = 1.0
    f32 = np.float32
    theta = (1.0 / (f32(10000.0) ** np.linspace(0.0, 1.0, 32, dtype=np.float32))).astype(np.float32)
    ang_ret = (np.arange(n, dtype=np.float32)[:, None] * theta).astype(np.float32)
    def axial(rot_dim):
        rows = n // 64
        row = np.repeat(np.arange(rows, dtype=np.float32), 64)
        col = np.tile(np.arange(64, dtype=np.float32), rows)
        nf = rot_dim // 4
        inv = (f32(10000.0) ** (-np.arange(nf, dtype=np.float32) / f32(nf))).astype(np.float32)
        return np.concatenate([row[:, None] * inv, col[:, None] * inv], -1).astype(np.float32)
    ang_diff = axial(32)
    ang_swa = axial(64)
    for kind, (ang, blk) in enumerate([(ang_ret, 64), (ang_diff, 32), (ang_swa, 64)]):
        half = blk // 2
        for f in range(128):
            d = f % blk
            j = d % half
            c = np.cos(ang[:, j]).astype(np.float32)
            s = np.sin(ang[:, j]).astype(np.float32)
            tabs[kind, 0, f, :n] = c
            tabs[kind, 1, f, :n] = -s if d < half else s
    return tabs


def build(NLT=32, NL=4, dbg=(), phases="A,BC,ret,diff,swa,gdn,E,F,G"):
    nc = bass.Bass("TRN2", target_bir_lowering=False)
    NT = NLT + 2
    T = NT * 128
    NLAT = NLT * 128
    FMG = fm_groups()
    NFM = len(FMG)
    NWE = sum(128 * (2 if g[2] is not None else 1) for g in FMG) + sum(w for _, _, w in TM_GROUPS)

    def din(name, shape, dt=F32):
        return nc.dram_tensor(name, list(shape), dt, kind="ExternalInput")

    h0 = din("h0", [T, D])
    cc = din("cc", [128, 8, 2])
    ada_w = din("ada_w", [NL, D, 6 * D])
    ada_b = din("ada_b", [NL, 6 * D])
    ada_bT = din("ada_bT", [NL, 128, 48])
    w_in = din("w_in", [NL, D, NWE])
    w_out = din("w_out", [NL, D, D])
    rope = din("rope", [3, 2, 128, T])
    ret_decay = din("ret_decay", [NL, 8])
    ln_g = din("ln_g", [NL, 2 * D])
    ln_b = din("ln_b", [NL, 2 * D])
    cmask = din("cmask", [128, 128])
    diff_lambda = din("diff_lambda", [NL, 128])
    diff_norm = din("diff_norm", [NL, 64])
    swa_sink = din("swa_sink", [NL, 4])
    router_wp = din("router_wp", [NL, D, 128])
    w_gate = din("w_gate", [NL, 16, D, 2 * D])
    w_up = din("w_up", [NL, 16, D, 2 * D])
    w_down = din("w_down", [NL, 16, 2 * D, D])
    iota512 = din("iota512", [128, 512])
    jcol_in = din("jcol_in", [128, 4])
    selE_in = din("selE_in", [16, 16, 128])
    gdn_convT = din("gdn_convT", [NL, 128, 18])
    gdn_a_log = din("gdn_a_log", [NL, 8])
    gdn_dt_bias = din("gdn_dt_bias", [NL, 8])
    gdn_norm = din("gdn_norm", [NL, 64])
    out = nc.dram_tensor("out", [NLAT, D], F32, kind="ExternalOutput")

    dbg_t = {}

    with ExitStack() as es:
        C = Ctx(nc, es)
        S = C.S
        H = C.dram("H", [T, D], kind=("ExternalOutput" if "H" in dbg else "Internal"))
        FMO = C.dram("FMO", [NFM, 128, T], BF16)
        TMA = C.dram("TMA", [T, 512], BF16)
        TMB = C.dram("TMB", [T, 256], BF16)
        TMC = C.dram("TMC", [T, 256], BF16)
        GAB = C.dram("GAB", [128, NT * 16], F32)
        TMD = C.dram("TMD", [T, 128], BF16)
        GFM = C.dram("GFM", [4, 128, T], BF16)
        GTM = C.dram("GTM", [T, 512], BF16)
        U2 = C.dram("U2", [T, D], BF16)
        LGD = C.dram("LGD", [16, T], F32)
        YE = C.dram("YE", [16, 640, D], BF16)
        KTD = C.dram("KTD", [128, NT * 16], F32)
        KEYD = C.dram("KEYD", [16, T], F32)
        AFFD = C.dram("AFFD", [16, T], F32)
        Y = C.dram("Y", [T, D], BF16, kind=("ExternalOutput" if "Y" in dbg else "Internal"))
        if "FMO" in dbg:
            dbg_t["FMO"] = FMO
        fm_index = {g[0]: i for i, g in enumerate(FMG)}

        ident_f = C.sb("ident_f", [128, 128])
        ident_b = C.sb("ident_b", [128, 128], BF16)
        condT = C.sb("condT", [128, 8, 2])
        condB = [C.sb(f"condB{i}", [128, 8, 128]) for i in range(2)]
        modT = C.sb("modT", [128, 8, 4, 2])
        gateB = C.sb("gateB", [128, 4, 2, D])
        relpos = C.sb("relpos", [128, 128])

        S.op("pool", lambda e: e.memset(ident_f[:], 0.0), [], [ident_f])
        S.op("pool", lambda e: e.affine_select(out=ident_f[:], in_=ident_f[:], pattern=[[-1, 128]],
                                               compare_op=ALU.not_equal, fill=1.0, base=0, channel_multiplier=1),
             [ident_f], [ident_f])
        S.op("dve", lambda e: e.tensor_copy(out=ident_b[:], in_=ident_f[:]), [ident_f], [ident_b])
        S.dma("sp", lambda e: e.dma_start(out=condT[:], in_=cc[:, :, :]), [], [condT])
        S.dma("sp", lambda e: e.dma_start(out=relpos[:], in_=cmask[:, :]), [], [relpos])
        S.op("act", lambda e: e.activation(out=condT[:], in_=condT[:], func=AF.Silu), [condT], [condT])
        for lc in range(2):
            for k in range(8):
                S.op("dve", lambda e, lc=lc, k=k: e.tensor_copy(
                    out=condB[lc][:, k, :], in_=condT[:, k, lc:lc + 1].broadcast_to([128, 128])),
                    [condT], [condB[lc]])
        S.barrier()
        S.emit()

        def phase_A(l):
            with ExitStack() as pes:
                wa = [C.sb(f"wa{i}", [128, 8, 512], es=pes) for i in range(2)]
                bbc = [C.sb(f"bbc{i}", [128, 512], es=pes) for i in range(2)]
                bT = C.sb("bT", [128, 48], es=pes)
                S.dma("sp", lambda e: e.dma_start(out=bT[:], in_=ada_bT[l, :, :]), [], [bT])
                import os
                ADBG = int(os.environ.get("ADBG", 9))
                for g in range(12):
                    m, half = g // 2, g % 2
                    w = wa[g % 2]
                    S.dma("sp", lambda e, w=w, g=g: e.dma_start(
                        out=w[:], in_=ada_w[l, :, g * 512:(g + 1) * 512].rearrange("(k p) n -> p k n", p=128)),
                        [], [w])
                    if m in (2, 5, 3, 4):
                        mi = {2: 0, 5: 1, 3: 2, 4: 3}[m]
                        bb = bbc[g % 2]
                        S.dma("act", lambda e, bb=bb, g=g: e.dma_start(
                            out=bb[:], in_=ada_b[l:l + 1, g * 512:(g + 1) * 512].broadcast_to([128, 512])), [], [bb])
                        if m == 4:
                            S.op("dve", lambda e, bb=bb: e.tensor_scalar_add(out=bb[:], in0=bb[:], scalar1=1.0), [bb], [bb])
                        for lc in range(2):
                            ps = C.nextps()
                            for k in range(8):
                                S.op("pe", lambda e, ps=ps, w=w, lc=lc, k=k: e.matmul(
                                    ps[:, :], lhsT=condB[lc][:, k, :], rhs=w[:, k, :], start=(k == 0), stop=(k == 7)),
                                    [condB[lc], w], [ps])
                            S.op("dve", lambda e, ps=ps, bb=bb, mi=mi, lc=lc, half=half: e.tensor_tensor(
                                out=gateB[:, mi, lc, half * 512:(half + 1) * 512], in0=ps[:, :], in1=bb[:], op=ALU.add),
                                [ps, bb], [gateB])
                    if m in (0, 1, 3, 4):
                        mi = {0: 0, 1: 1, 3: 2, 4: 3}[m]
                        for j in range(4):
                            kk = half * 4 + j
                            ps = C.nextps()
                            for lc in range(2):
                                for k in range(8):
                                    S.op("pe", lambda e, ps=ps, w=w, j=j, k=k, lc=lc: e.matmul(
                                        ps[:, lc * 128:(lc + 1) * 128], lhsT=w[:, k, j * 128:(j + 1) * 128],
                                        rhs=condB[lc][:, k, :], start=(k == 0), stop=(k == 7)), [condB[lc], w], [ps])
                            addone = 1.0 if m in (1, 4) else 0.0
                            for lc in range(2):
                                S.op("dve", lambda e, ps=ps, kk=kk, mi=mi, m=m, addone=addone, lc=lc: e.tensor_scalar(
                                    out=modT[:, kk, mi, lc:lc + 1], in0=ps[:, lc * 128:lc * 128 + 1],
                                    scalar1=bT[:, m * 8 + kk:m * 8 + kk + 1],
                                    scalar2=addone, op0=ALU.add, op1=ALU.add), [ps, bT], [modT])
                S.barrier()
                S.emit()

        def phase_BC(l):
            src = h0 if l == 0 else H.t
            with ExitStack() as pes:
                uT = C.sb("uT", [128, 8, T], BF16, es=pes)
                wsb = C.sb("wsb", [128, 8, NWE], BF16, es=pes)
                bes = ExitStack()
                hts = [C.sb(f"ht{i}", [128, D], es=bes) for i in range(2)]
                for k in range(8):
                    S.dma("pool", lambda e, k=k: e.dma_start(out=wsb[:, k, :], in_=w_in[l, k * 128:(k + 1) * 128, :]),
                          [], [wsb.sub(k)])
                wkeys = [wsb.sub(k) for k in range(8)]
                for t in range(NT):
                    lc = 0 if t < NLT else 1
                    ht = hts[t % 2]
                    S.dma("sp", lambda e, ht=ht, t=t: e.dma_start(out=ht[:], in_=src[t * 128:(t + 1) * 128, :]),
                          ["H/%d" % t], [ht])
                    for half in range(2):
                        ps = C.nextps()
                        for j in range(4):
                            k = half * 4 + j
                            S.op("pe", lambda e, ps=ps, ht=ht, j=j, k=k: e.transpose(
                                ps[:, j * 128:(j + 1) * 128], ht[:, k * 128:(k + 1) * 128], ident_f[:]),
                                [ht, ident_f], [ps])
                        for j in range(4):
                            k = half * 4 + j
                            S.op("act", lambda e, ps=ps, j=j, k=k, t=t, lc=lc: e.activation(
                                out=uT[:, k, t * 128:(t + 1) * 128], in_=ps[:, j * 128:(j + 1) * 128], func=AF.Identity,
                                bias=modT[:, k, 0, lc:lc + 1], scale=modT[:, k, 1, lc:lc + 1]),
                                [ps, modT], [uT.sub(t)])
                S.barrier()
                S.emit()
                bes.close()
                import os
                BDBG = int(os.environ.get("BDBG", 9))
                tabs1 = [C.sb(f"tab_{kd}", [128, 2, 512], es=pes) for kd in range(3)]
                tabs = [tabs1, tabs1]
                evs = [C.sb(f"ev{i}", [128, 512], BF16, es=pes) for i in range(3)]
                tmp = [C.sb(f"tmp{i}", [128, 512], es=pes) for i in range(2)]
                tmo = [C.sb(f"tmo{i}", [128, 512], BF16, es=pes) for i in range(2)]
                gabAll = C.sb("gabAll", [128, NT * 16], es=pes)
                offs = {}
                o = 0
                for name, c, kind, sw in FMG:
                    offs[name] = o
                    o += 128 * (2 if kind is not None else 1)
                for name, st, w in TM_GROUPS:
                    offs[name] = o
                    o += w
                nblk = (T + 511) // 512
                evc = 0
                for b in range(nblk if BDBG >= 2 else 0):
                    t0 = b * 512
                    tw = min(512, T - t0)
                    tiles = list(range(t0 // 128, (t0 + tw) // 128))
                    ukeys = [uT.sub(t) for t in tiles]
                    tb = tabs[b % 2]
                    for kd in range(3):
                        S.dma("act", lambda e, tb=tb, kd=kd, t0=t0, tw=tw: e.dma_start(
                            out=tb[kd][:, :, 0:tw], in_=rope[kd, :, :, t0:t0 + tw].rearrange("c p t -> p c t")),
                            [], [tb[kd]])
                    for gi, (name, c, kind, sw) in enumerate(FMG):
                        co = offs[name]
                        psa = C.nextps()
                        for k in range(8):
                            S.op("pe", lambda e, psa=psa, k=k, co=co, t0=t0, tw=tw: e.matmul(
                                psa[:, 0:tw], lhsT=wsb[:, k, co:co + 128], rhs=uT[:, k, t0:t0 + tw],
                                start=(k == 0), stop=(k == 7)), wkeys + ukeys, [psa])
                        ev = evs[evc % 3]
                        evc += 1
                        if kind is None:
                            S.op("act", lambda e, ev=ev, psa=psa, tw=tw: e.copy(out=ev[:, 0:tw], in_=psa[:, 0:tw]),
                                 [psa], [ev])
                        else:
                            psb = C.nextps()
                            for k in range(8):
                                S.op("pe", lambda e, psb=psb, k=k, co=co, t0=t0, tw=tw: e.matmul(
                                    psb[:, 0:tw], lhsT=wsb[:, k, co + 128:co + 256], rhs=uT[:, k, t0:t0 + tw],
                                    start=(k == 0), stop=(k == 7)), wkeys + ukeys, [psb])
                            ta, tbb = tmp
                            S.op("dve", lambda e, ta=ta, psa=psa, tb=tb, kind=kind, tw=tw: e.tensor_tensor(
                                out=ta[:, 0:tw], in0=psa[:, 0:tw], in1=tb[kind][:, 0, 0:tw], op=ALU.mult),
                                [psa, tb[kind]], [ta])
                            S.op("dve", lambda e, tbb=tbb, psb=psb, tb=tb, kind=kind, tw=tw: e.tensor_tensor(
                                out=tbb[:, 0:tw], in0=psb[:, 0:tw], in1=tb[kind][:, 1, 0:tw], op=ALU.mult),
                                [psb, tb[kind]], [tbb])
                            S.op("pool", lambda e, ev=ev, ta=ta, tbb=tbb, tw=tw: e.tensor_tensor(
                                out=ev[:, 0:tw], in0=ta[:, 0:tw], in1=tbb[:, 0:tw], op=ALU.add), [ta, tbb], [ev])
                        S.dma("sp", lambda e, ev=ev, gi=gi, t0=t0, tw=tw: e.dma_start(
                            out=FMO.t[gi, :, t0:t0 + tw], in_=ev[:, 0:tw]), [ev], ["FMO/%d/%d" % (gi, b)])
                    for t in (tiles if BDBG >= 3 else []):
                        for (name, st, w), dst in zip(TM_GROUPS, [TMA, TMB, TMC, TMD]):
                            co = offs[name]
                            ps = C.nextps()
                            for k in range(8):
                                S.op("pe", lambda e, ps=ps, k=k, co=co, w=w, t=t: e.matmul(
                                    ps[:, 0:w], lhsT=uT[:, k, t * 128:(t + 1) * 128], rhs=wsb[:, k, co:co + w],
                                    start=(k == 0), stop=(k == 7)), wkeys + [uT.sub(t)], [ps])
                            ob = tmo[evc % 2]
                            evc += 1
                            wd = min(w, 256) if name == "gdn_gab" else w
                            S.op("act", lambda e, ob=ob, ps=ps, wd=wd: e.copy(out=ob[:, 0:wd], in_=ps[:, 0:wd]),
                                 [ps], [ob])
                            S.dma("sp", lambda e, ob=ob, dst=dst, t=t, wd=wd: e.dma_start(
                                out=dst.t[t * 128:(t + 1) * 128, 0:wd], in_=ob[:, 0:wd]), [ob],
                                [dst.sub(t)])
                            if name == "gdn_gab" and BDBG >= 4:
                                S.op("dve", lambda e, ps=ps, t=t: e.tensor_copy(out=gabAll[:, t * 16:(t + 1) * 16], in_=ps[:, 256:272]),
                                     [ps], [gabAll])
                if BDBG >= 4:
                    S.dma("sp", lambda e: e.dma_start(out=GAB.t[:, :], in_=gabAll[:]), [gabAll], [GAB])
                S.barrier()
                S.emit()

        def phase_ret(l, full_ctx):
            gq = fm_index["ret_q0"]
            gk = fm_index["ret_k0"]
            seq_f = [NLT, NLT + 1] + list(range(NLT))
            with ExitStack() as pes:
                rd = C.sb("rd", [128, 8], es=pes)
                lg = C.sb("lg", [128, 8], es=pes)
                MT = C.sb("MT", [128, 4, 128], es=pes)
                mtmp = C.sb("mtmp", [128, 128], es=pes)
                mtmp2 = C.sb("mtmp2", [128, 128], es=pes)
                pcol = C.sb("pcol", [128, 4], es=pes)
                kdec = C.sb("kdec", [128, 2, 256], es=pes)
                qdec = C.sb("qdec", [128, 2, 256], es=pes)
                cdec = C.sb("cdec", [128, 2, 256], es=pes)
                dcol = C.sb("dcol", [128, 8], es=pes)
                S.dma("sp", lambda e: e.dma_start(out=rd[:], in_=ret_decay[l:l + 1, :].broadcast_to([128, 8])), [], [rd])
                S.op("act", lambda e: e.activation(out=lg[:], in_=rd[:], func=AF.Exp, scale=-1.0), [rd], [lg])
                S.op("act", lambda e: e.activation(out=lg[:], in_=lg[:], func=AF.Ln, bias=1.0), [lg], [lg])
                S.op("dve", lambda e: e.tensor_scalar_mul(out=lg[:], in0=lg[:], scalar1=-1.0), [lg], [lg])
                S.op("dve", lambda e: e.tensor_scalar(out=pcol[:, 3:4], in0=relpos[:, 0:1], scalar1=-1.0, scalar2=None,
                                                      op0=ALU.mult), [relpos], [pcol])
                S.op("dve", lambda e: e.tensor_scalar_add(out=pcol[:, 0:1], in0=pcol[:, 3:4], scalar1=1.0), [pcol], [pcol])
                S.op("dve", lambda e: e.tensor_scalar(out=pcol[:, 1:2], in0=pcol[:, 3:4], scalar1=-1.0, scalar2=128.0,
                                                      op0=ALU.mult, op1=ALU.add), [pcol], [pcol])
                S.op("dve", lambda e: e.tensor_scalar(out=pcol[:, 2:3], in0=pcol[:, 3:4], scalar1=-1.0, scalar2=127.0,
                                                      op0=ALU.mult, op1=ALU.add), [pcol], [pcol])
                for dr in range(2):
                    for h in range(4):
                        c = dr * 4 + h
                        if dr == 0:
                            S.op("dve", lambda e: e.tensor_scalar_max(out=mtmp[:], in0=relpos[:], scalar1=0.0), [relpos], [mtmp])
                        else:
                            S.op("dve", lambda e: e.tensor_scalar(out=mtmp[:], in0=relpos[:], scalar1=-1.0, scalar2=0.0,
                                                                  op0=ALU.mult, op1=ALU.max), [relpos], [mtmp])
                        S.op("act", lambda e, c=c: e.activation(out=mtmp[:], in_=mtmp[:], func=AF.Exp, scale=lg[:, c:c + 1]),
                             [mtmp, lg], [mtmp])
                        if dr == 0:
                            S.op("dve", lambda e: e.tensor_scalar(out=mtmp2[:], in0=relpos[:], scalar1=0.0, scalar2=0.125,
                                                                  op0=ALU.is_ge, op1=ALU.mult), [relpos], [mtmp2])
                            S.op("dve", lambda e, h=h: e.tensor_tensor(out=MT[:, h, :], in0=mtmp[:], in1=mtmp2[:], op=ALU.mult),
                                 [mtmp, mtmp2], [MT])
                        else:
                            S.op("dve", lambda e: e.tensor_scalar(out=mtmp2[:], in0=relpos[:], scalar1=0.0, scalar2=0.125,
                                                                  op0=ALU.is_le, op1=ALU.mult), [relpos], [mtmp2])
                            S.op("dve", lambda e: e.tensor_tensor(out=mtmp[:], in0=mtmp[:], in1=mtmp2[:], op=ALU.mult),
                                 [mtmp, mtmp2], [mtmp])
                            S.op("dve", lambda e, h=h: e.tensor_tensor(out=MT[:, h, :], in0=MT[:, h, :], in1=mtmp[:], op=ALU.add),
                                 [mtmp, MT], [MT])
                        S.op("act", lambda e, c=c, dr=dr: e.activation(out=dcol[:, 0:1], in_=pcol[:, (0 if dr == 0 else 1):(1 if dr == 0 else 2)],
                                                                        func=AF.Exp, scale=lg[:, c:c + 1]), [pcol, lg], [dcol])
                        S.op("dve", lambda e, dr=dr, h=h: e.tensor_scalar_mul(
                            out=qdec[:, dr, h * 64:(h + 1) * 64], in0=dcol[:, 0:1].broadcast_to([128, 64]), scalar1=0.125),
                            [dcol], [qdec])
                        S.op("act", lambda e, c=c, dr=dr: e.activation(out=dcol[:, 1:2], in_=pcol[:, (2 if dr == 0 else 3):(3 if dr == 0 else 4)],
                                                                        func=AF.Exp, scale=lg[:, c:c + 1]), [pcol, lg], [dcol])
                        S.op("dve", lambda e, dr=dr, h=h: e.tensor_copy(
                            out=kdec[:, dr, h * 64:(h + 1) * 64], in_=dcol[:, 1:2].broadcast_to([128, 64])), [dcol], [kdec])
                        S.op("act", lambda e, c=c: e.activation(out=dcol[:, 2:3], in_=lg[:, c:c + 1], func=AF.Exp, scale=128.0),
                             [lg], [dcol])
                        S.op("dve", lambda e, dr=dr, h=h: e.tensor_copy(
                            out=cdec[:, dr, h * 64:(h + 1) * 64], in_=dcol[:, 2:3].broadcast_to([128, 64])), [dcol], [cdec])
                import os
                RDBG = int(os.environ.get("RDBG", 9))
                Sprev = C.sb("Sprev", [64, 2, NT, 256], BF16, es=pes)
                dSb = C.sb("dSb", [64, NT, 256], es=pes)
                Srun = C.sb("Srun", [64, 2, 256], es=pes)
                kts = [C.sb(f"kT{i}", [64, 4, 128], BF16, es=pes) for i in range(2)]
                qts = [C.sb(f"qT{i}", [64, 4, 128], BF16, es=pes) for i in range(2)]
                vgs = [C.sb(f"vg{i}", [128, 512], BF16, es=pes) for i in range(2)]
                ktok = [C.sb(f"ktok{i}", [128, 256], BF16, es=pes) for i in range(2)]
                kd = [C.sb(f"kd{i}", [128, 2, 256], BF16, es=pes) for i in range(2)]
                S.op("pool", lambda e: e.memset(Srun[:], 0.0), [], [Srun])

                def load_k(c, i):
                    S.dma("sp", lambda e: e.dma_start(
                        out=kts[i][:], in_=FMO.t[gk:gk + 2, :, c * 128:(c + 1) * 128].rearrange("g (h d) t -> d (g h) t", h=2)),
                        [f"FMO/{gk}/{c // 4}", f"FMO/{gk + 1}/{c // 4}"], [kts[i]])

                def make_ktok(c, i):
                    ps = C.nextps()
                    pv = ps.t[:, 0:128].bitcast(BF16)
                    for h in range(4):
                        S.op("pe", lambda e, pv=pv, h=h: e.transpose(pv[:, h * 64:(h + 1) * 64], kts[i][:, h, :], ident_b[0:64, 0:64]),
                             [kts[i], ident_b], [ps])
                    S.op("act", lambda e, pv=pv: e.copy(out=ktok[i][:], in_=pv[:, :]), [ps], [ktok[i]])

                R2 = int(os.environ.get("R2", 9))
                for n_, c in enumerate(seq_f if RDBG >= 2 else []):
                    i = n_ % 2
                    load_k(c, i)
                    S.dma("act", lambda e, c=c, i=i: e.dma_start(out=vgs[i][:], in_=TMA.t[c * 128:(c + 1) * 128, :]),
                          [TMA.sub(c)], [vgs[i]])
                    if R2 < 2:
                        continue
                    make_ktok(c, i)
                    if R2 < 3:
                        continue
                    S.op("dve", lambda e, i=i: e.tensor_tensor(
                        out=kd[i][:], in0=ktok[i][:].rearrange("p (o c) -> p o c", o=1).broadcast_to([128, 2, 256]),
                        in1=kdec[:], op=ALU.mult), [ktok[i], kdec], [kd[i]])
                    if R2 < 4:
                        continue
                    ps = C.nextps()
                    for dr in range(2):
                        for h in range(4):
                            S.op("pe", lambda e, ps=ps, dr=dr, h=h, i=i: e.matmul(
                                ps[0:64, dr * 256 + h * 64:dr * 256 + (h + 1) * 64], lhsT=kd[i][:, dr, h * 64:(h + 1) * 64],
                                rhs=vgs[i][:, h * 64:(h + 1) * 64], start=True, stop=True), [kd[i], vgs[i]], [ps])
                    if R2 < 5:
                        continue
                    S.op("act", lambda e, c=c: e.copy(out=Sprev[:, 0, c, :], in_=Srun[:, 0, :]), [Srun], [Sprev.sub(0, c)])
                    if R2 < 6:
                        continue
                    S.op("dve", lambda e: e.tensor_tensor(out=Srun[:, 0, :], in0=Srun[:, 0, :], in1=cdec[0:64, 0, :], op=ALU.mult),
                         [Srun, cdec], [Srun])
                    if R2 < 7:
                        continue
                    S.op("dve", lambda e, ps=ps: e.tensor_tensor(out=Srun[:, 0, :], in0=Srun[:, 0, :], in1=ps[0:64, 0:256], op=ALU.add),
                         [Srun, ps], [Srun])
                    if R2 < 8:
                        continue
                    S.op("dve", lambda e, ps=ps, c=c: e.tensor_copy(out=dSb[:, c, :], in_=ps[0:64, 256:512]), [ps], [dSb.sub(c)])
                seq_b = [NLT + 1, NLT] + list(range(NLT - 1, -1, -1))
                for c in (seq_b if RDBG >= 3 else []):
                    S.op("act", lambda e, c=c: e.copy(out=Sprev[:, 1, c, :], in_=Srun[:, 1, :]), [Srun], [Sprev.sub(1, c)])
                    S.op("dve", lambda e: e.tensor_tensor(out=Srun[:, 1, :], in0=Srun[:, 1, :], in1=cdec[0:64, 1, :], op=ALU.mult),
                         [Srun, cdec], [Srun])
                    S.op("dve", lambda e, c=c: e.tensor_tensor(out=Srun[:, 1, :], in0=Srun[:, 1, :], in1=dSb[:, c, :], op=ALU.add),
                         [Srun, dSb.sub(c)], [Srun])
                AMs = [C.sb(f"AM{i}", [128, 4, 128], BF16, es=pes) for i in range(2)]
                osum = C.sb("osum", [128, 256], es=pes)
                t1 = C.sb("t1", [128, 256], es=pes)
                sq = C.sb("sq", [128, 256], es=pes)
                ss = C.sb("ss", [128, 4], es=pes)
                sg = C.sb("sg", [128, 256], es=pes)
                ys = [C.sb(f"y{i}", [128, 256], BF16, es=pes) for i in range(2)]
                chunks = list(range(NT)) if full_ctx else list(range(NLT))
                if RDBG < 4:
                    chunks = []
                for n_, c in enumerate(chunks):
                    i = n_ % 2
                    load_k(c, i)
                    S.dma("sp", lambda e, c=c, i=i: e.dma_start(
                        out=qts[i][:], in_=FMO.t[gq:gq + 2, :, c * 128:(c + 1) * 128].rearrange("g (h d) t -> d (g h) t", h=2)),
                        [f"FMO/{gq}/{c // 4}", f"FMO/{gq + 1}/{c // 4}"], [qts[i]])
                    S.dma("act", lambda e, c=c, i=i: e.dma_start(out=vgs[i][:], in_=TMA.t[c * 128:(c + 1) * 128, :]),
                          [TMA.sub(c)], [vgs[i]])
                    psA = C.nextps()
                    for h in range(4):
                        S.op("pe", lambda e, psA=psA, h=h, i=i: e.matmul(
                            psA[:, h * 128:(h + 1) * 128], lhsT=kts[i][:, h, :], rhs=qts[i][:, h, :], start=True, stop=True),
                            [kts[i], qts[i]], [psA])
                    S.op("dve", lambda e, psA=psA, i=i: e.tensor_tensor(
                        out=AMs[i][:].rearrange("p h t -> p (h t)"), in0=psA[:, :], in1=MT[:].rearrange("p h t -> p (h t)"),
                        op=ALU.mult), [psA, MT], [AMs[i]])
                    psO = C.nextps()
                    psX = C.nextps()
                    for h in range(4):
                        S.op("pe", lambda e, psO=psO, h=h, i=i: e.matmul(
                            psO[:, h * 64:(h + 1) * 64], lhsT=AMs[i][:, h, :], rhs=vgs[i][:, h * 64:(h + 1) * 64],
                            start=True, stop=True), [AMs[i], vgs[i]], [psO])
                        S.op("pe", lambda e, psO=psO, h=h, i=i, c=c: e.matmul(
                            psO[:, 256 + h * 64:256 + (h + 1) * 64], lhsT=qts[i][:, h, :], rhs=Sprev[:, 0, c, h * 64:(h + 1) * 64],
                            start=True, stop=True), [qts[i], Sprev.sub(0, c)], [psO])
                        S.op("pe", lambda e, psX=psX, h=h, i=i, c=c: e.matmul(
                            psX[:, h * 64:(h + 1) * 64], lhsT=qts[i][:, h, :], rhs=Sprev[:, 1, c, h * 64:(h + 1) * 64],
                            start=True, stop=True), [qts[i], Sprev.sub(1, c)], [psX])
                    S.op("dve", lambda e, psO=psO: e.tensor_tensor(out=t1[:], in0=psO[:, 256:512], in1=qdec[:, 0, :], op=ALU.mult),
                         [psO, qdec], [t1])
                    S.op("dve", lambda e, psO=psO: e.tensor_tensor(out=osum[:], in0=psO[:, 0:256], in1=t1[:], op=ALU.add),
                         [psO, t1], [osum])
                    S.op("dve", lambda e, psX=psX: e.tensor_tensor(out=t1[:], in0=psX[:, 0:256], in1=qdec[:, 1, :], op=ALU.mult),
                         [psX, qdec], [t1])
                    S.op("dve", lambda e: e.tensor_tensor(out=osum[:], in0=osum[:], in1=t1[:], op=ALU.add), [osum, t1], [osum])
                    S.op("act", lambda e: e.activation(out=sq[:], in_=osum[:], func=AF.Square), [osum], [sq])
                    S.op("dve", lambda e: e.tensor_reduce(out=ss[:], in_=sq[:].rearrange("p (h d) -> p h d", h=4), axis=AX.X, op=ALU.add),
                         [sq], [ss])
                    S.op("dve", lambda e: e.tensor_scalar(out=ss[:], in0=ss[:], scalar1=1.0 / 64, scalar2=1e-6, op0=ALU.mult, op1=ALU.add),
                         [ss], [ss])
                    S.op("act", lambda e: e.activation(out=ss[:], in_=ss[:], func=AF.Sqrt), [ss], [ss])
                    S.op("dve", lambda e: e.reciprocal(out=ss[:], in_=ss[:]), [ss], [ss])
                    S.op("act", lambda e, i=i: e.activation(out=sg[:], in_=vgs[i][:, 256:512], func=AF.Silu), [vgs[i]], [sg])
                    S.op("dve", lambda e: e.tensor_tensor(
                        out=osum[:].rearrange("p (h d) -> p h d", h=4), in0=osum[:].rearrange("p (h d) -> p h d", h=4),
                        in1=ss[:].rearrange("p (h o) -> p h o", o=1).broadcast_to([128, 4, 64]), op=ALU.mult), [osum, ss], [osum])
                    S.op("dve", lambda e, i=i: e.tensor_tensor(out=ys[i][:], in0=osum[:], in1=sg[:], op=ALU.mult), [osum, sg], [ys[i]])
                    S.dma("sp", lambda e, i=i, c=c: e.dma_start(out=Y.t[c * 128:(c + 1) * 128, 0:256], in_=ys[i][:]),
                          [ys[i]], [Y.sub(c, 0)])
                S.barrier()
                S.emit()

        def phase_diff(l, full_ctx):
            gq = fm_index["diff_q0"]
            gk = fm_index["diff_k0"]
            lam_init = 0.8 - 0.6 * math.exp(-0.3 * l)
            scale = 32 ** -0.5
            with ExitStack() as pes:
                C.rot = [0, 1, 2]
                kT8 = C.sb("kT8", [32, 8, T], BF16, es=pes)
                V1 = C.sb("V1", [128, NT, 4, 65], BF16, es=pes)
                qT8s = [C.sb(f"qT8_{i}", [32, 8, 512], BF16, es=pes) for i in range(2)]
                Es = [C.sb(f"E{i}", [128, 512], BF16, es=pes) for i in range(3)]
                dl = C.sb("dl", [128, 128], es=pes)
                dl2 = C.sb("dl2", [128, 2], es=pes)
                nlam = C.sb("nlam", [128, 1], es=pes)
                gn = C.sb("gn", [128, 64], es=pes)
                od = C.sb("od", [128, 4, 256], es=pes)
                oT = [C.sb(f"oT{i}", [65, 512], es=pes) for i in range(2)]
                rr = C.sb("rr", [128, 4], es=pes)
                a_ = C.sb("a_", [128, 64], es=pes)
                sq = C.sb("dsq", [128, 4, 256], es=pes)
                ss = C.sb("dss", [128, 16], es=pes)
                yo = [C.sb(f"dy{i}", [128, 4, 256], BF16, es=pes) for i in range(2)]
                S.dma("sp", lambda e: e.dma_start(out=dl[:], in_=diff_lambda[l:l + 1, :].broadcast_to([128, 128])), [], [dl])
                S.dma("sp", lambda e: e.dma_start(out=gn[:], in_=diff_norm[l:l + 1, :].broadcast_to([128, 64])), [], [gn])
                dl4 = dl[:].rearrange("p (a b d) -> p a b d", a=2, b=2)
                S.op("dve", lambda e: e.tensor_tensor(out=dl4[:, :, 0, :], in0=dl4[:, :, 0, :], in1=dl4[:, :, 1, :], op=ALU.mult), [dl], [dl])
                S.op("dve", lambda e: e.tensor_reduce(out=dl2[:], in_=dl4[:, :, 0, :], axis=AX.X, op=ALU.add), [dl], [dl2])
                S.op("act", lambda e: e.activation(out=dl2[:], in_=dl2[:], func=AF.Exp), [dl2], [dl2])
                S.op("dve", lambda e: e.tensor_tensor(out=nlam[:], in0=dl2[:, 1:2], in1=dl2[:, 0:1], op=ALU.subtract), [dl2], [nlam])
                S.op("dve", lambda e: e.tensor_scalar_add(out=nlam[:], in0=nlam[:], scalar1=-lam_init), [nlam], [nlam])
                S.op("dve", lambda e: e.tensor_scalar_mul(out=gn[:], in0=gn[:], scalar1=(1.0 - lam_init)), [gn], [gn])
                S.dma("sp", lambda e: e.dma_start(out=kT8[:], in_=FMO.t[gk:gk + 2, :, :].rearrange("g (x d) t -> d (g x) t", d=32)),
                      [], [kT8])
                S.op("pool", lambda e: e.memset(V1[:], 1.0), [], [V1])
                for h in range(4):
                    for n0 in range(0, NT, 8):
                        n1 = min(NT, n0 + 8)
                        S.dma("act", lambda e, h=h, n0=n0, n1=n1: e.dma_start(
                            out=V1[:, n0:n1, h, 0:64],
                            in_=TMB.t[n0 * 128:n1 * 128, h * 64:(h + 1) * 64].rearrange("(n p) e -> p n e", p=128)), [V1], [V1])
                blocks = [(b * 512, 512, list(range(NT))) for b in range(NLAT // 512)]
                if full_ctx:
                    blocks.append((NLAT, 256, [NLT, NLT + 1]))
                ec = 0
                for bi, (q0, qw, ktiles) in enumerate(blocks):
                    qT8 = qT8s[bi % 2]
                    nqs = qw // 128
                    S.dma("sp", lambda e, qT8=qT8, q0=q0, qw=qw: e.dma_start(
                        out=qT8[:, :, 0:qw], in_=FMO.t[gq:gq + 2, :, q0:q0 + qw].rearrange("g (x d) t -> d (g x) t", d=32)),
                        [], [qT8])
                    LA = 2
                    steps = [(h, ki, kt, c) for h in range(4) for ki, kt in enumerate(ktiles) for c in range(2)]
                    pend = {}

                    def emit_S(si, qT8=qT8, qw=qw):
                        h, ki, kt, c = steps[si]
                        hc = h * 2 + c
                        pss = C.nextps()
                        S.op("pe", lambda e, pss=pss, hc=hc, kt=kt, qT8=qT8, qw=qw: e.matmul(
                            pss[:, 0:qw], lhsT=kT8[:, hc, kt * 128:(kt + 1) * 128], rhs=qT8[:, hc, 0:qw],
                            start=True, stop=True), [kT8, qT8], [pss])
                        pend[si] = pss

                    def emit_rest(si, qw=qw, nqs=nqs, nk=len(ktiles)):
                        nonlocal ec
                        h, ki, kt, c = steps[si]
                        acc = [C.ps[4 + (h % 2) * 2], C.ps[5 + (h % 2) * 2]]
                        pss = pend.pop(si)
                        E = Es[ec % 3]
                        ec += 1
                        S.op("act", lambda e, E=E, pss=pss, qw=qw: e.activation(
                            out=E[:, 0:qw], in_=pss[:, 0:qw], func=AF.Exp, scale=scale), [pss], [E])
                        S.op("pe", lambda e, E=E, c=c, kt=kt, h=h, ki=ki, qw=qw, nk=nk, ac=acc[c]: e.matmul(
                            ac[0:65, 0:qw], lhsT=V1[:, kt, h, :], rhs=E[:, 0:qw],
                            start=(ki == 0), stop=(ki == nk - 1)), [E, V1], [acc[c]])
                        if not (ki == nk - 1 and c == 1):
                            return
                        pt = C.ps[3]
                        for c_ in range(2):
                            S.op("dve", lambda e, c_=c_, qw=qw, ac=acc[c_]: e.tensor_copy(out=oT[c_][:, 0:qw], in_=ac[0:65, 0:qw]), [acc[c_]], [oT[c_]])
                            for qs in range(nqs):
                                S.op("pe", lambda e, pt=pt, c_=c_, qs=qs: e.transpose(
                                    pt[:, qs * 65:(qs + 1) * 65], oT[c_][:, qs * 128:(qs + 1) * 128], ident_f[0:65, 0:65]),
                                    [oT[c_], ident_f], [pt])
                            S.op("dve", lambda e, pt=pt, nqs=nqs: e.reciprocal(out=rr[:, 0:nqs], in_=pt[:, 0:nqs * 65].rearrange("p (q e) -> p q e", e=65)[:, :, 64]),
                                 [pt, od], [rr])
                            if c_ == 0:
                                for qs in range(nqs):
                                    S.op("dve", lambda e, qs=qs, h=h, pt=pt: e.tensor_scalar(
                                        out=od[:, qs, h * 64:(h + 1) * 64], in0=pt[:, qs * 65:qs * 65 + 64], scalar1=rr[:, qs:qs + 1],
                                        scalar2=None, op0=ALU.mult), [pt, rr], [od])
                            else:
                                S.op("dve", lambda e, nqs=nqs: e.tensor_scalar(out=rr[:, 0:nqs], in0=rr[:, 0:nqs], scalar1=nlam[:, 0:1], scalar2=None,
                                                                      op0=ALU.mult), [rr, nlam], [rr])
                                for qs in range(nqs):
                                    S.op("dve", lambda e, qs=qs, h=h, pt=pt: e.scalar_tensor_tensor(
                                        out=od[:, qs, h * 64:(h + 1) * 64], in0=pt[:, qs * 65:qs * 65 + 64], scalar=rr[:, qs:qs + 1],
                                        in1=od[:, qs, h * 64:(h + 1) * 64], op0=ALU.mult, op1=ALU.add), [pt, rr, od], [od])

                    for si in range(len(steps) + LA):
                        if si < len(steps):
                            emit_S(si)
                        if si >= LA:
                            emit_rest(si - LA)
                    y = yo[bi % 2]
                    S.op("act", lambda e, nqs=nqs: e.activation(out=sq[:, 0:nqs, :], in_=od[:, 0:nqs, :], func=AF.Square), [od], [sq])
                    S.op("dve", lambda e, nqs=nqs: e.tensor_reduce(out=ss[:, 0:nqs * 4], in_=sq[:, 0:nqs, :].rearrange("p q (h d) -> p (q h) d", h=4),
                                                          axis=AX.X, op=ALU.add), [sq], [ss])
                    S.op("dve", lambda e, nqs=nqs: e.tensor_scalar(out=ss[:], in0=ss[:], scalar1=1.0 / 64, scalar2=1e-6, op0=ALU.mult, op1=ALU.add),
                         [ss], [ss])
                    S.op("act", lambda e, nqs=nqs: e.activation(out=ss[:], in_=ss[:], func=AF.Sqrt), [ss], [ss])
                    S.op("dve", lambda e, nqs=nqs: e.reciprocal(out=ss[:], in_=ss[:]), [ss], [ss])
                    S.op("dve", lambda e, nqs=nqs: e.tensor_tensor(
                        out=od[:, 0:nqs, :].rearrange("p q (h d) -> p (q h) d", h=4), in0=od[:, 0:nqs, :].rearrange("p q (h d) -> p (q h) d", h=4),
                        in1=ss[:, 0:nqs * 4].rearrange("p (x o) -> p x o", o=1).broadcast_to([128, nqs * 4, 64]), op=ALU.mult), [od, ss], [od])
                    S.op("dve", lambda e, y=y, nqs=nqs: e.tensor_tensor(
                        out=y[:, 0:nqs, :].rearrange("p q (h d) -> p (q h) d", h=4), in0=od[:, 0:nqs, :].rearrange("p q (h d) -> p (q h) d", h=4),
                        in1=gn[:].rearrange("p (o d) -> p o d", o=1).broadcast_to([128, nqs * 4, 64]), op=ALU.mult), [od, gn], [y])
                    S.dma("sp", lambda e, y=y, q0=q0, qw=qw, nqs=nqs: e.dma_start(
                        out=Y.t[q0:q0 + qw, 256:512].rearrange("(q p) c -> p q c", p=128), in_=y[:, 0:nqs, :]), [y], [Y.sub(bi, 1)])
                C.rot = None
                S.barrier()
                S.emit()

        def phase_swa(l, full_ctx):
            gq = fm_index["swa_q0"]
            gk = fm_index["swa_k0"]
            scale = 0.125
            with ExitStack() as pes:
                C.rot = [0, 1, 2, 3]
                skT = C.sb("skT", [64, 2, T], BF16, es=pes)
                sqT = C.sb("sqT", [64, 4, T], BF16, es=pes)
                V1 = C.sb("sV1", [128, NT, 2, 65], BF16, es=pes)
                nm = C.sb("nm", [128, 2, 2, 128], BF16, es=pes)
                Es = [C.sb(f"sE{i}", [128, 256], BF16, es=pes) for i in range(3)]
                oT = C.sb("soT", [65, 512], es=pes)
                es_ = C.sb("esink", [128, 4], es=pes)
                den = C.sb("den", [128, 4], es=pes)
                ys = [C.sb(f"sy{i}", [128, 256], BF16, es=pes) for i in range(2)]
                S.dma("sp", lambda e: e.dma_start(out=es_[:], in_=swa_sink[l:l + 1, :].broadcast_to([128, 4])), [], [es_])
                S.op("act", lambda e: e.activation(out=es_[:], in_=es_[:], func=AF.Exp), [es_], [es_])
                for r in range(2):
                    S.op("dve", lambda e, r=r: e.tensor_scalar(out=nm[:, 0, r, :], in0=relpos[:], scalar1=0.0, scalar2=NEG,
                                                               op0=ALU.is_gt, op1=ALU.mult), [relpos], [nm])
                    S.op("dve", lambda e, r=r: e.tensor_scalar(out=nm[:, 1, r, :], in0=relpos[:], scalar1=0.0, scalar2=NEG,
                                                               op0=ALU.is_lt, op1=ALU.mult), [relpos], [nm])
                S.dma("sp", lambda e: e.dma_start(out=skT[:], in_=FMO.t[gk, :, :].rearrange("(h d) t -> d h t", h=2)), [], [skT])
                S.dma("sp", lambda e: e.dma_start(out=sqT[:], in_=FMO.t[gq:gq + 2, :, :].rearrange("g (h d) t -> d (g h) t", h=2)), [], [sqT])
                S.op("pool", lambda e: e.memset(V1[:], 1.0), [], [V1])
                for g in range(2):
                    for n0 in range(0, NT, 8):
                        n1 = min(NT, n0 + 8)
                        S.dma("act", lambda e, g=g, n0=n0, n1=n1: e.dma_start(
                            out=V1[:, n0:n1, g, 0:64],
                            in_=TMD.t[n0 * 128:n1 * 128, g * 64:(g + 1) * 64].rearrange("(n p) e -> p n e", p=128)), [V1], [V1])
                tiles = list(range(NT)) if full_ctx else list(range(NLT))
                ec = 0
                for ti, t in enumerate(tiles):
                    if t < NLT:
                        keys = []
                        if t > 0:
                            keys.append((t - 1, 0))
                        keys.append((t, None))
                        if t < NLT - 1:
                            keys.append((t + 1, 1))
                        keys += [(NLT, None), (NLT + 1, None)]
                    else:
                        keys = [(NLT, None), (NLT + 1, None)]
                    accs = [C.ps[4 + (ti % 2) * 2], C.ps[5 + (ti % 2) * 2]]
                    for g in range(2):
                        for ki, (kt, mk) in enumerate(keys):
                            pss = C.nextps()
                            S.op("pe", lambda e, pss=pss, g=g, kt=kt, t=t, mk=mk: e.matmul(
                                pss[:, 0:256], lhsT=skT[:, g, kt * 128:(kt + 1) * 128], rhs=sqT[:, 2 * g:2 * g + 2, t * 128:(t + 1) * 128],
                                start=True, stop=(mk is None)), [skT, sqT], [pss])
                            if mk is not None:
                                S.op("pe", lambda e, pss=pss, mk=mk: e.matmul(
                                    pss[:, 0:256], lhsT=ident_b[:], rhs=nm[:, mk, :, :], start=False, stop=True), [ident_b, nm], [pss])
                            E = Es[ec % 3]
                            ec += 1
                            S.op("act", lambda e, E=E, pss=pss: e.activation(out=E[:], in_=pss[:, 0:256], func=AF.Exp, scale=scale),
                                 [pss], [E])
                            S.op("pe", lambda e, E=E, g=g, kt=kt, ki=ki, nk=len(keys), ac=accs[g]: e.matmul(
                                ac[0:65, 0:256], lhsT=V1[:, kt, g, :], rhs=E[:], start=(ki == 0), stop=(ki == nk - 1)),
                                [E, V1], [accs[g]])
                        S.op("dve", lambda e, g=g, ac=accs[g]: e.tensor_copy(out=oT[:, g * 256:(g + 1) * 256], in_=ac[0:65, 0:256]),
                             [accs[g]], [oT])
                    pt = C.nextps()
                    for h in range(4):
                        S.op("pe", lambda e, pt=pt, h=h: e.transpose(pt[:, h * 65:(h + 1) * 65], oT[:, h * 128:(h + 1) * 128],
                                                                     ident_f[0:65, 0:65]), [oT, ident_f], [pt])
                    S.op("dve", lambda e, pt=pt: e.tensor_tensor(
                        out=den[:], in0=pt[:, 0:260].rearrange("p (h e) -> p h e", e=65)[:, :, 64], in1=es_[:], op=ALU.add),
                        [pt, es_], [den])
                    S.op("dve", lambda e: e.reciprocal(out=den[:], in_=den[:]), [den], [den])
                    y = ys[ti % 2]
                    S.op("dve", lambda e, pt=pt, y=y: e.tensor_tensor(
                        out=y[:].rearrange("p (h d) -> p h d", h=4), in0=pt[:, 0:260].rearrange("p (h e) -> p h e", e=65)[:, :, 0:64],
                        in1=den[:].rearrange("p (h o) -> p h o", o=1).broadcast_to([128, 4, 64]), op=ALU.mult), [pt, den], [y])
                    S.dma("sp", lambda e, y=y, t=t: e.dma_start(out=Y.t[t * 128:(t + 1) * 128, 768:1024], in_=y[:]), [y], [Y.sub(t, 3)])
                C.rot = None
                S.barrier()
                S.emit()

        def phase_gdn(l, full_ctx):
            g0 = fm_index["gdn0"]
            BIG = 30000.0
            with ExitStack() as pes:
                cw = C.sb("cw", [128, 18], es=pes)
                bones = C.sb("bones", [128, 128], es=pes)
                xb = [C.sb(f"xb{i}", [128, 514], BF16, es=pes) for i in range(2)]
                yc = C.sb("yc", [128, 512], es=pes)
                ysl = C.sb("ysl", [128, 512], es=pes)
                sq = C.sb("gsq", [128, 512], es=pes)
                rs = C.sb("grs", [128, 512], es=pes)
                ynb = [C.sb(f"ynb{i}", [128, 512], BF16, es=pes) for i in range(2)]
                ytm = [C.sb(f"ytm{i}", [128, 4, 128], BF16, es=pes) for i in range(2)]
                S.dma("sp", lambda e: e.dma_start(out=cw[:], in_=gdn_convT[l, :, :]), [], [cw])
                S.op("pool", lambda e: e.memset(bones[:], 0.0), [], [bones])
                S.op("pool", lambda e: e.memset(bones[0:64, 0:64], 1.0), [bones], [bones])
                S.op("pool", lambda e: e.memset(bones[64:128, 64:128], 1.0), [bones], [bones])
                seqs = [(0, NLAT)] + [(NLAT, T)]
                bi = 0
                for gi in range(6):
                    for (s0, s1) in seqs:
                        for t0 in range(s0, s1, 512):
                            tw = min(512, s1 - t0)
                            x = xb[bi % 2]
                            lo = max(t0 - 1, s0)
                            hi = min(t0 + tw + 1, s1)
                            if lo == t0 or hi == t0 + tw:
                                S.op("pool", lambda e, x=x: e.memset(x[:], 0.0), [], [x])
                            S.dma("sp", lambda e, x=x, gi=gi, lo=lo, hi=hi, t0=t0: e.dma_start(
                                out=x[:, lo - (t0 - 1):hi - (t0 - 1)], in_=FMO.t[g0 + gi, :, lo:hi]), [x], [x])
                            S.op("dve", lambda e, x=x, gi=gi, tw=tw: e.tensor_scalar(
                                out=yc[:, 0:tw], in0=x[:, 1:1 + tw], scalar1=cw[:, gi * 3 + 1:gi * 3 + 2], scalar2=None, op0=ALU.mult),
                                [x, cw], [yc])
                            S.op("dve", lambda e, x=x, gi=gi, tw=tw: e.scalar_tensor_tensor(
                                out=yc[:, 0:tw], in0=x[:, 0:tw], scalar=cw[:, gi * 3:gi * 3 + 1], in1=yc[:, 0:tw],
                                op0=ALU.mult, op1=ALU.add), [x, cw, yc], [yc])
                            S.op("dve", lambda e, x=x, gi=gi, tw=tw: e.scalar_tensor_tensor(
                                out=yc[:, 0:tw], in0=x[:, 2:2 + tw], scalar=cw[:, gi * 3 + 2:gi * 3 + 3], in1=yc[:, 0:tw],
                                op0=ALU.mult, op1=ALU.add), [x, cw, yc], [yc])
                            S.op("act", lambda e, tw=tw: e.activation(out=ysl[:, 0:tw], in_=yc[:, 0:tw], func=AF.Silu), [yc], [ysl])
                            yn = ynb[bi % 2]
                            if gi < 4:
                                S.op("act", lambda e, tw=tw: e.activation(out=sq[:, 0:tw], in_=ysl[:, 0:tw], func=AF.Square), [ysl], [sq])
                                ps = C.nextps()
                                S.op("pe", lambda e, ps=ps, tw=tw: e.matmul(ps[:, 0:tw], lhsT=bones[:], rhs=sq[:, 0:tw], start=True, stop=True),
                                     [bones, sq], [ps])
                                S.op("dve", lambda e, ps=ps, tw=tw: e.tensor_scalar_add(out=rs[:, 0:tw], in0=ps[:, 0:tw], scalar1=1e-6), [ps], [rs])
                                S.op("act", lambda e, tw=tw: e.activation(out=rs[:, 0:tw], in_=rs[:, 0:tw], func=AF.Sqrt), [rs], [rs])
                                S.op("dve", lambda e, tw=tw: e.reciprocal(out=rs[:, 0:tw], in_=rs[:, 0:tw]), [rs], [rs])
                                if gi < 2:
                                    S.op("dve", lambda e, tw=tw, yn=yn: e.scalar_tensor_tensor(
                                        out=yn[:, 0:tw], in0=ysl[:, 0:tw], scalar=0.125, in1=rs[:, 0:tw], op0=ALU.mult, op1=ALU.mult),
                                        [ysl, rs], [yn])
                                else:
                                    S.op("dve", lambda e, tw=tw, yn=yn: e.tensor_tensor(out=yn[:, 0:tw], in0=ysl[:, 0:tw], in1=rs[:, 0:tw], op=ALU.mult),
                                         [ysl, rs], [yn])
                                S.dma("act", lambda e, yn=yn, gi=gi, t0=t0, tw=tw: e.dma_start(out=GFM.t[gi, :, t0:t0 + tw], in_=yn[:, 0:tw]),
                                      [yn], [GFM.sub(gi, t0)])
                            else:
                                S.op("dve", lambda e, tw=tw, yn=yn: e.tensor_copy(out=yn[:, 0:tw], in_=ysl[:, 0:tw]), [ysl], [yn])
                            if gi >= 2:
                                ps = C.nextps()
                                pv = ps.t[:, 0:256].bitcast(BF16)
                                nq = tw // 128
                                for q in range(nq):
                                    S.op("pe", lambda e, pv=pv, q=q, yn=yn: e.transpose(pv[:, q * 128:(q + 1) * 128], yn[:, q * 128:(q + 1) * 128], ident_b[:]),
                                         [yn, ident_b], [ps])
                                yt = ytm[bi % 2]
                                S.op("act", lambda e, pv=pv, yt=yt, nq=nq: e.copy(out=yt[:, 0:nq, :].rearrange("p q c -> p (q c)"), in_=pv[:, 0:nq * 128]),
                                     [ps], [yt])
                                S.dma("act", lambda e, yt=yt, gi=gi, t0=t0, tw=tw, nq=nq: e.dma_start(
                                    out=GTM.t[t0:t0 + tw, (gi - 2) * 128:(gi - 1) * 128].rearrange("(q p) c -> p q c", p=128), in_=yt[:, 0:nq, :]),
                                    [yt], [GTM.sub(gi, t0)])
                            bi += 1
                S.barrier()
                S.emit()
            import os
            GD = int(os.environ.get("GD", 9))
            if GD < 2:
                return
            with ExitStack() as pes:
                gab = C.sb("gab", [128, NT, 16], es=pes)
                par = C.sb("gpar", [128, 16], es=pes)
                la = C.sb("la", [128, NT, 8], es=pes)
                nbeta = C.sb("nbeta", [128, NT, 8], es=pes)
                gg = C.sb("gg", [128, NT, 8], es=pes)
                gt = C.sb("gt", [128, NT, 8], es=pes)
                eg = C.sb("eg", [128, NT, 8], es=pes)
                ekt = C.sb("ekt", [128, NT, 8], es=pes)
                cd = C.sb("cd", [128, NT, 8], es=pes)
                beg = C.sb("beg", [128, NT, 8], es=pes)
                tri = C.sb("tri", [128, 2, 128], es=pes)
                onesf = C.sb("onesf", [128, 128], es=pes)
                nonesf = C.sb("nonesf", [128, 128], es=pes)
                mD = C.sb("mD", [128, 2, 4, 128], es=pes)
                mDT = C.sb("mDT", [128, 2, 4, 128], es=pes)
                gnb = C.sb("gnb", [128, 64], es=pes)
                S.dma("sp", lambda e: e.dma_start(out=gab[:].rearrange("p n c -> p (n c)"), in_=GAB.t[:, :]), [], [gab])
                S.dma("sp", lambda e: e.dma_start(out=par[:, 0:8], in_=gdn_a_log[l:l + 1, :].broadcast_to([128, 8])), [], [par])
                S.dma("sp", lambda e: e.dma_start(out=par[:, 8:16], in_=gdn_dt_bias[l:l + 1, :].broadcast_to([128, 8])), [par], [par])
                S.dma("sp", lambda e: e.dma_start(out=gnb[:], in_=gdn_norm[l:l + 1, :].broadcast_to([128, 64])), [], [gnb])
                S.op("pool", lambda e: e.memset(onesf[:], 1.0), [], [onesf])
                S.op("pool", lambda e: e.memset(nonesf[:], -1.0), [], [nonesf])
                S.op("dve", lambda e: e.tensor_single_scalar(out=tri[:, 0, :], in_=relpos[:], scalar=0.0, op=ALU.is_ge), [relpos], [tri])
                S.op("dve", lambda e: e.tensor_single_scalar(out=tri[:, 1, :], in_=relpos[:], scalar=0.0, op=ALU.is_le), [relpos], [tri])
                for h in range(4):
                    S.op("dve", lambda e, h=h: e.tensor_scalar(out=mD[:, 0, h, :], in0=relpos[:], scalar1=0.0, scalar2=BIG, op0=ALU.is_ge, op1=ALU.mult), [relpos], [mD])
                    S.op("dve", lambda e, h=h: e.tensor_scalar(out=mD[:, 1, h, :], in0=relpos[:], scalar1=0.0, scalar2=BIG, op0=ALU.is_le, op1=ALU.mult), [relpos], [mD])
                    S.op("dve", lambda e, h=h: e.tensor_scalar(out=mDT[:, 0, h, :], in0=relpos[:], scalar1=0.0, scalar2=-BIG, op0=ALU.is_lt, op1=ALU.mult), [relpos], [mDT])
                    S.op("dve", lambda e, h=h: e.tensor_scalar(out=mDT[:, 1, h, :], in0=relpos[:], scalar1=0.0, scalar2=-BIG, op0=ALU.is_gt, op1=ALU.mult), [relpos], [mDT])
                S.op("act", lambda e: e.activation(out=par[:, 0:8], in_=par[:, 0:8], func=AF.Exp), [par], [par])
                S.op("dve", lambda e: e.tensor_tensor(out=la[:], in0=gab[:, :, 0:8],
                                                      in1=par[:, 8:16].rearrange("p (o c) -> p o c", o=1).broadcast_to([128, NT, 8]), op=ALU.add),
                     [gab, par], [la])
                S.op("act", lambda e: e.activation(out=la[:], in_=la[:], func=AF.Exp), [la], [la])
                S.op("act", lambda e: e.activation(out=la[:], in_=la[:], func=AF.Ln, bias=1.0), [la], [la])
                S.op("dve", lambda e: e.scalar_tensor_tensor(
                    out=la[:], in0=la[:], scalar=-1.0, in1=par[:, 0:8].rearrange("p (o c) -> p o c", o=1).broadcast_to([128, NT, 8]),
                    op0=ALU.mult, op1=ALU.mult), [la, par], [la])
                S.op("act", lambda e: e.activation(out=nbeta[:], in_=gab[:, :, 8:16], func=AF.Sigmoid), [gab], [nbeta])
                for r in range(2):
                    ps = C.nextps()
                    S.op("pe", lambda e, ps=ps, r=r: e.matmul(ps[:, 0:NT * 4], lhsT=tri[:, r, :], rhs=la[:, :, r * 4:(r + 1) * 4], start=True, stop=True),
                         [tri, la], [ps])
                    S.op("dve", lambda e, ps=ps, r=r: e.tensor_copy(out=gg[:, :, r * 4:(r + 1) * 4], in_=ps[:, 0:NT * 4].rearrange("p (n c) -> p n c", c=4)),
                         [ps], [gg])
                ps = C.nextps()
                S.op("pe", lambda e, ps=ps: e.matmul(ps[:, 0:NT * 8], lhsT=onesf[:], rhs=la[:].rearrange("p n c -> p (n c)"), start=True, stop=True),
                     [onesf, la], [ps])
                S.op("dve", lambda e, ps=ps: e.tensor_copy(out=gt[:].rearrange("p n c -> p (n c)"), in_=ps[:, 0:NT * 8]), [ps], [gt])
                S.op("act", lambda e: e.activation(out=eg[:], in_=gg[:], func=AF.Exp), [gg], [eg])
                S.op("act", lambda e: e.activation(out=cd[:], in_=gt[:], func=AF.Exp), [gt], [cd])
                S.op("dve", lambda e: e.tensor_tensor(out=ekt[:], in0=gt[:], in1=gg[:], op=ALU.subtract), [gt, gg], [ekt])
                S.op("act", lambda e: e.activation(out=ekt[:], in_=ekt[:], func=AF.Exp), [ekt], [ekt])
                S.op("dve", lambda e: e.tensor_tensor(out=beg[:], in0=nbeta[:], in1=eg[:], op=ALU.mult), [nbeta, eg], [beg])
                S.op("dve", lambda e: e.tensor_scalar_mul(out=nbeta[:], in0=nbeta[:], scalar1=-1.0), [nbeta], [nbeta])
                qT = [C.sb(f"gqT{i}", [64, 4, 128], BF16, es=pes) for i in range(2)]
                kT = [C.sb(f"gkT{i}", [64, 4, 128], BF16, es=pes) for i in range(2)]
                kv = [C.sb(f"gkv{i}", [128, 512], BF16, es=pes) for i in range(2)]
                Rm = C.sb("Rm", [128, 4, 128], es=pes)
                Ds = C.sb("Ds", [128, 4, 128], es=pes)
                DT = C.sb("DT", [128, 4, 128], es=pes)
                X = [C.sb(f"X{i}", [128, 4, 2, 128], es=pes) for i in range(2)]
                P = C.sb("P", [128, 4, 128], es=pes)
                ident_r = ident_f
                Pb = C.sb("Pb", [128, 4, 128], BF16, es=pes)
                qkT = C.sb("qkT", [128, 4, 128], BF16, es=pes)
                kg = C.sb("kg", [128, 256], BF16, es=pes)
                vb = C.sb("vb", [128, 256], BF16, es=pes)
                ktl = C.sb("ktl", [128, 256], BF16, es=pes)
                wT = C.sb("wT", [64, 4, 128], BF16, es=pes)
                ub = C.sb("ub", [128, 256], es=pes)
                u = C.sb("u", [128, 256], BF16, es=pes)
                Sf = C.sb("Sf", [64, 256], es=pes)
                Sb16 = C.sb("Sb16", [64, 256], BF16, es=pes)
                Oacc = C.sb("Oacc", [128, NT, 256], es=pes)
                ocr = C.sb("ocr", [128, 256], es=pes)
                gsq = C.sb("gosq", [128, 256], es=pes)
                gss = C.sb("goss", [128, 4], es=pes)
                gsg = C.sb("gosg", [128, 256], es=pes)
                ggate = [C.sb(f"ggate{i}", [128, 256], BF16, es=pes) for i in range(2)]
                gy = [C.sb(f"gy{i}", [128, 256], BF16, es=pes) for i in range(2)]
                seq = {0: [NLT, NLT + 1] + list(range(NLT)), 1: [NLT + 1, NLT] + list(range(NLT - 1, -1, -1))}
                step = 0
                for r in range(2 if GD >= 3 else 0):
                    S.op("pool", lambda e: e.memset(Sf[:], 0.0), [], [Sf])
                    S.op("pool", lambda e: e.memset(Sb16[:], 0.0), [], [Sb16])
                    for c in seq[r]:
                        i = step % 2
                        step += 1
                        rc = slice(r * 4, r * 4 + 4)
                        S.dma("sp", lambda e, i=i, c=c: e.dma_start(
                            out=qT[i][:], in_=GFM.t[0:2, :, c * 128:(c + 1) * 128].rearrange("g (h d) t -> d (g h) t", h=2)), [], [qT[i]])
                        S.dma("sp", lambda e, i=i, c=c: e.dma_start(
                            out=kT[i][:], in_=GFM.t[2:4, :, c * 128:(c + 1) * 128].rearrange("g (h d) t -> d (g h) t", h=2)), [], [kT[i]])
                        S.dma("act", lambda e, i=i, c=c: e.dma_start(out=kv[i][:], in_=GTM.t[c * 128:(c + 1) * 128, :]), [], [kv[i]])
                        pKK = C.nextps()
                        pQK = C.nextps()
                        for h in range(4):
                            S.op("pe", lambda e, pKK=pKK, h=h, i=i: e.matmul(pKK[:, h * 128:(h + 1) * 128], lhsT=kT[i][:, h, :], rhs=kT[i][:, h, :],
                                                                              start=True, stop=True), [kT[i]], [pKK])
                        for h in range(4):
                            S.op("pe", lambda e, pQK=pQK, h=h, i=i: e.matmul(pQK[:, h * 128:(h + 1) * 128], lhsT=kT[i][:, h, :], rhs=qT[i][:, h, :],
                                                                              start=True, stop=True), [kT[i], qT[i]], [pQK])
                        for h in range(4):
                            S.op("dve", lambda e, h=h, c=c, r=r: e.tensor_scalar(
                                out=Rm[:, h, :], in0=tri[:, r, :], scalar1=la[:, c, r * 4 + h:r * 4 + h + 1], scalar2=None, op0=ALU.mult),
                                [tri, la], [Rm])
                        pD = C.nextps()
                        pDT = C.nextps()
                        for (pp, mk) in ((pD, mD), (pDT, mDT)):
                            S.op("pe", lambda e, pp=pp, mk=mk, r=r: e.matmul(pp[:, :], lhsT=ident_f[:], rhs=mk[:, r, :, :].rearrange("p h t -> p (h t)"),
                                                                             start=True, stop=False), [ident_f, mk], [pp])
                            for h in range(4):
                                S.op("pe", lambda e, pp=pp, h=h: e.matmul(pp[:, h * 128:(h + 1) * 128], lhsT=onesf[:], rhs=Rm[:, h, :],
                                                                          start=False, stop=False), [onesf, Rm], [pp])
                                S.op("pe", lambda e, pp=pp, h=h: e.matmul(pp[:, h * 128:(h + 1) * 128], lhsT=Rm[:, h, :], rhs=nonesf[:],
                                                                          start=False, stop=(h == 3)), [nonesf, Rm], [pp])
                        S.op("act", lambda e, pD=pD: e.activation(out=Ds[:].rearrange("p h t -> p (h t)"), in_=pD[:, :], func=AF.Exp, scale=-1.0),
                             [pD], [Ds])
                        S.op("act", lambda e, pDT=pDT: e.activation(out=DT[:].rearrange("p h t -> p (h t)"), in_=pDT[:, :], func=AF.Exp), [pDT], [DT])
                        if GD < 4:
                            continue
                        Xc = X[0]
                        S.op("dve", lambda e, pKK=pKK, Xc=Xc: e.tensor_tensor(out=Xc[:, :, 0, :], in0=pKK[:, :].rearrange("p (h t) -> p h t", h=4), in1=Ds[:], op=ALU.mult),
                             [pKK, Ds], [Xc])
                        S.op("dve", lambda e, Xc=Xc, c=c, rc=rc: e.tensor_tensor(
                            out=Xc[:, :, 0, :], in0=Xc[:, :, 0, :], in1=nbeta[:, c, rc].rearrange("p (h o) -> p h o", o=1).broadcast_to([128, 4, 128]), op=ALU.mult),
                            [Xc, nbeta], [Xc])
                        S.op("dve", lambda e, pQK=pQK: e.tensor_tensor(out=qkT[:], in0=pQK[:, :].rearrange("p (h t) -> p h t", h=4), in1=DT[:], op=ALU.mult),
                             [pQK, DT], [qkT])
                        pZ = C.nextps()
                        pZr = pZ.t[:, :]
                        for h in range(4):
                            S.op("pe", lambda e, pZr=pZr, h=h, Xc=Xc: e.transpose(pZr[:, h * 128:(h + 1) * 128], Xc[:, h, 0, :], ident_r[:]), [Xc, ident_r], [pZ])
                        S.op("act", lambda e, pZ=pZ, Xc=Xc: e.copy(out=Xc[:, :, 1, :], in_=pZ[:, :].rearrange("p (h t) -> p h t", h=4)), [pZ], [Xc])
                        S.op("dve", lambda e, Xc=Xc: e.tensor_tensor(out=P[:], in0=Xc[:, :, 1, :], in1=ident_f[:].rearrange("p (o t) -> p o t", o=1).broadcast_to([128, 4, 128]), op=ALU.add),
                             [Xc, ident_f], [P])
                        for lev in range(6):
                            Xo = X[lev % 2]
                            Xn = X[(lev + 1) % 2]
                            last = lev == 5
                            pxa = C.nextps()
                            pxb = C.nextps()
                            for h in range(4):
                                pp = pxa if h < 2 else pxb
                                o = (h % 2) * 256
                                S.op("pe", lambda e, pp=pp, o=o, h=h, Xo=Xo: e.matmul(pp[:, o:o + 128], lhsT=Xo[:, h, 1, :], rhs=Xo[:, h, 0, :], start=True, stop=True),
                                     [Xo], [pp])
                                if not last:
                                    S.op("pe", lambda e, pp=pp, o=o, h=h, Xo=Xo: e.matmul(pp[:, o + 128:o + 256], lhsT=Xo[:, h, 0, :], rhs=Xo[:, h, 1, :], start=True, stop=True),
                                         [Xo], [pp])
                            S.op("act", lambda e, pxa=pxa, Xn=Xn: e.copy(out=Xn[:, 0:2, :, :].rearrange("p h x t -> p (h x t)"), in_=pxa[:, :]), [pxa], [Xn])
                            S.op("dve", lambda e, pxb=pxb, Xn=Xn: e.tensor_copy(out=Xn[:, 2:4, :, :].rearrange("p h x t -> p (h x t)"), in_=pxb[:, :]), [pxb], [Xn])
                            pP = C.nextps()
                            for h in range(4):
                                S.op("pe", lambda e, pP=pP, h=h, Xn=Xn: e.matmul(pP[:, h * 128:(h + 1) * 128], lhsT=Xn[:, h, 0, :], rhs=P[:, h, :], start=True, stop=True),
                                     [Xn, P], [pP])
                            S.op("dve", lambda e, pP=pP: e.tensor_tensor(out=P[:].rearrange("p h t -> p (h t)"), in0=P[:].rearrange("p h t -> p (h t)"), in1=pP[:, :], op=ALU.add),
                                 [pP, P], [P])
                        if GD < 5:
                            continue
                        S.op("act", lambda e: e.copy(out=Pb[:], in_=P[:]), [P], [Pb])
                        S.op("dve", lambda e, i=i, c=c, rc=rc: e.tensor_tensor(
                            out=kg[:].rearrange("p (h d) -> p h d", h=4), in0=kv[i][:, 0:256].rearrange("p (h d) -> p h d", h=4),
                            in1=beg[:, c, rc].rearrange("p (h o) -> p h o", o=1).broadcast_to([128, 4, 64]), op=ALU.mult), [kv[i], beg], [kg])
                        S.op("dve", lambda e, i=i, c=c, rc=rc: e.tensor_tensor(
                            out=vb[:].rearrange("p (h d) -> p h d", h=4), in0=kv[i][:, 256:512].rearrange("p (h d) -> p h d", h=4),
                            in1=nbeta[:, c, rc].rearrange("p (h o) -> p h o", o=1).broadcast_to([128, 4, 64]), op=ALU.mult), [kv[i], nbeta], [vb])
                        S.op("dve", lambda e, i=i, c=c, rc=rc: e.tensor_tensor(
                            out=ktl[:].rearrange("p (h d) -> p h d", h=4), in0=kv[i][:, 0:256].rearrange("p (h d) -> p h d", h=4),
                            in1=ekt[:, c, rc].rearrange("p (h o) -> p h o", o=1).broadcast_to([128, 4, 64]), op=ALU.mult), [kv[i], ekt], [ktl])
                        pw = C.nextps()
                        pu = C.nextps()
                        for h in range(4):
                            S.op("pe", lambda e, pw=pw, h=h: e.matmul(pw[0:64, h * 128:(h + 1) * 128], lhsT=kg[:, h * 64:(h + 1) * 64], rhs=Pb[:, h, :], start=True, stop=True),
                                 [kg, Pb], [pw])
                            S.op("pe", lambda e, pu=pu, h=h: e.matmul(pu[:, h * 64:(h + 1) * 64], lhsT=Pb[:, h, :], rhs=vb[:, h * 64:(h + 1) * 64], start=True, stop=True),
                                 [vb, Pb], [pu])
                        S.op("dve", lambda e, pw=pw: e.tensor_copy(out=wT[:].rearrange("p h t -> p (h t)"), in_=pw[0:64, :]), [pw], [wT])
                        S.op("dve", lambda e, pu=pu: e.tensor_scalar_mul(out=ub[:], in0=pu[:, 0:256], scalar1=-1.0), [pu], [ub])
                        pws = C.nextps()
                        for h in range(4):
                            S.op("pe", lambda e, pws=pws, h=h: e.matmul(pws[:, h * 64:(h + 1) * 64], lhsT=wT[:, h, :], rhs=Sb16[:, h * 64:(h + 1) * 64], start=True, stop=True),
                                 [wT, Sb16], [pws])
                        S.op("dve", lambda e, pws=pws: e.tensor_tensor(out=u[:], in0=ub[:], in1=pws[:, 0:256], op=ALU.subtract), [ub, pws], [u])
                        pcr = C.nextps()
                        pin = C.nextps()
                        for h in range(4):
                            S.op("pe", lambda e, pcr=pcr, h=h, i=i: e.matmul(pcr[:, h * 64:(h + 1) * 64], lhsT=qT[i][:, h, :], rhs=Sb16[:, h * 64:(h + 1) * 64], start=True, stop=True),
                                 [qT[i], Sb16], [pcr])
                            S.op("pe", lambda e, pin=pin, h=h: e.matmul(pin[:, h * 64:(h + 1) * 64], lhsT=qkT[:, h, :], rhs=u[:, h * 64:(h + 1) * 64], start=True, stop=True),
                                 [qkT, u], [pin])
                        pS = C.nextps()
                        for h in range(4):
                            S.op("pe", lambda e, pS=pS, h=h: e.matmul(pS[0:64, h * 64:(h + 1) * 64], lhsT=ktl[:, h * 64:(h + 1) * 64], rhs=u[:, h * 64:(h + 1) * 64], start=True, stop=True),
                                 [ktl, u], [pS])
                        S.op("dve", lambda e, pcr=pcr, c=c, rc=rc: e.tensor_tensor(
                            out=ocr[:].rearrange("p (h d) -> p h d", h=4), in0=pcr[:, 0:256].rearrange("p (h d) -> p h d", h=4),
                            in1=eg[:, c, rc].rearrange("p (h o) -> p h o", o=1).broadcast_to([128, 4, 64]), op=ALU.mult), [pcr, eg], [ocr])
                        if r == 0:
                            S.op("dve", lambda e, pin=pin, c=c: e.tensor_tensor(out=Oacc[:, c, :], in0=ocr[:], in1=pin[:, 0:256], op=ALU.add), [ocr, pin], [Oacc.sub(c)])
                        else:
                            S.op("dve", lambda e, pin=pin: e.tensor_tensor(out=ocr[:], in0=ocr[:], in1=pin[:, 0:256], op=ALU.add), [ocr, pin], [ocr])
                            S.op("dve", lambda e, c=c: e.tensor_tensor(out=ocr[:], in0=ocr[:], in1=Oacc[:, c, :], op=ALU.add), [ocr, Oacc.sub(c)], [ocr])
                        S.op("dve", lambda e, c=c, rc=rc: e.tensor_tensor(
                            out=Sf[:].rearrange("p (h d) -> p h d", h=4), in0=Sf[:].rearrange("p (h d) -> p h d", h=4),
                            in1=cd[0:64, c, rc].rearrange("p (h o) -> p h o", o=1).broadcast_to([64, 4, 64]), op=ALU.mult), [Sf, cd], [Sf])
                        S.op("dve", lambda e, pS=pS: e.tensor_tensor(out=Sf[:], in0=Sf[:], in1=pS[0:64, 0:256], op=ALU.add), [Sf, pS], [Sf])
                        S.op("act", lambda e: e.copy(out=Sb16[:], in_=Sf[:]), [Sf], [Sb16])
                        if r == 1 and (full_ctx or c < NLT):
                            gt_ = ggate[i]
                            S.dma("act", lambda e, gt_=gt_, c=c: e.dma_start(out=gt_[:], in_=TMC.t[c * 128:(c + 1) * 128, :]), [], [gt_])
                            S.op("act", lambda e: e.activation(out=gsq[:], in_=ocr[:], func=AF.Square), [ocr], [gsq])
                            S.op("dve", lambda e: e.tensor_reduce(out=gss[:], in_=gsq[:].rearrange("p (h d) -> p h d", h=4), axis=AX.X, op=ALU.add), [gsq], [gss])
                            S.op("dve", lambda e: e.tensor_scalar(out=gss[:], in0=gss[:], scalar1=1.0 / 64, scalar2=1e-6, op0=ALU.mult, op1=ALU.add), [gss], [gss])
                            S.op("act", lambda e: e.activation(out=gss[:], in_=gss[:], func=AF.Sqrt), [gss], [gss])
                            S.op("dve", lambda e: e.reciprocal(out=gss[:], in_=gss[:]), [gss], [gss])
                            S.op("act", lambda e, gt_=gt_: e.activation(out=gsg[:], in_=gt_[:], func=AF.Silu), [gt_], [gsg])
                            S.op("dve", lambda e: e.tensor_tensor(
                                out=ocr[:].rearrange("p (h d) -> p h d", h=4), in0=ocr[:].rearrange("p (h d) -> p h d", h=4),
                                in1=gss[:].rearrange("p (h o) -> p h o", o=1).broadcast_to([128, 4, 64]), op=ALU.mult), [ocr, gss], [ocr])
                            S.op("dve", lambda e: e.tensor_tensor(
                                out=ocr[:].rearrange("p (h d) -> p h d", h=4), in0=ocr[:].rearrange("p (h d) -> p h d", h=4),
                                in1=gnb[:].rearrange("p (o d) -> p o d", o=1).broadcast_to([128, 4, 64]), op=ALU.mult), [ocr, gnb], [ocr])
                            yy = gy[i]
                            S.op("dve", lambda e, yy=yy: e.tensor_tensor(out=yy[:], in0=ocr[:], in1=gsg[:], op=ALU.mult), [ocr, gsg], [yy])
                            S.dma("sp", lambda e, yy=yy, c=c: e.dma_start(out=Y.t[c * 128:(c + 1) * 128, 512:768], in_=yy[:]), [yy], [Y.sub(c, 2)])
                S.barrier()
                S.emit()

        def layer_norm_tile(z, sqt, st, lnG, lnB, gi, outt, eng2="dve"):
            S.op("dve", lambda e: e.tensor_reduce(out=st[:, 0:1], in_=z[:], axis=AX.X, op=ALU.add), [z], [st])
            S.op("dve", lambda e: e.tensor_tensor(out=sqt[:], in0=z[:], in1=z[:], op=ALU.mult), [z], [sqt])
            S.op("dve", lambda e: e.tensor_reduce(out=st[:, 1:2], in_=sqt[:], axis=AX.X, op=ALU.add), [sqt], [st])
            S.op("dve", lambda e: e.tensor_scalar_mul(out=st[:, 0:2], in0=st[:, 0:2], scalar1=1.0 / D), [st], [st])
            S.op("dve", lambda e: e.tensor_tensor(out=st[:, 2:3], in0=st[:, 0:1], in1=st[:, 0:1], op=ALU.mult), [st], [st])
            S.op("dve", lambda e: e.tensor_tensor(out=st[:, 2:3], in0=st[:, 1:2], in1=st[:, 2:3], op=ALU.subtract), [st], [st])
            S.op("dve", lambda e: e.tensor_scalar_add(out=st[:, 2:3], in0=st[:, 2:3], scalar1=LN_EPS), [st], [st])
            S.op("act", lambda e: e.activation(out=st[:, 2:3], in_=st[:, 2:3], func=AF.Sqrt), [st], [st])
            S.op("dve", lambda e: e.reciprocal(out=st[:, 2:3], in_=st[:, 2:3]), [st], [st])
            S.op("dve", lambda e: e.tensor_scalar(out=sqt[:], in0=z[:], scalar1=st[:, 0:1], scalar2=st[:, 2:3], op0=ALU.subtract, op1=ALU.mult),
                 [z, st], [sqt])
            S.op("pool", lambda e: e.tensor_tensor(out=sqt[:], in0=sqt[:], in1=lnG[:, gi, :], op=ALU.mult), [sqt, lnG], [sqt])
            S.op("pool", lambda e: e.tensor_tensor(out=outt[:], in0=sqt[:], in1=lnB[:, gi, :], op=ALU.add), [sqt, lnB], [outt])

        def phase_E(l, full_ctx):
            src = h0 if l == 0 else H.t
            with ExitStack() as pes:
                wo = C.sb("wo", [128, 8, D], BF16, es=pes)
                lnG = C.sb("lnG", [128, 2, D], es=pes)
                lnB = C.sb("lnB", [128, 2, D], es=pes)
                rw = C.sb("rw", [128, 8, 128], es=pes)
                LG = C.sb("LG", [16, T], es=pes)
                yts = [C.sb(f"yt{i}", [128, D], BF16, es=pes) for i in range(2)]
                hts = [C.sb(f"eht{i}", [128, D], es=pes) for i in range(2)]
                YT = C.sb("YT", [128, 8, 128], BF16, es=pes)
                z = C.sb("z", [128, D], es=pes)
                sqt = C.sb("sqt", [128, D], es=pes)
                st = C.sb("st", [128, 4], es=pes)
                hn = [C.sb(f"hn{i}", [128, D], es=pes) for i in range(2)]
                u2b = [C.sb(f"u2b{i}", [128, D], BF16, es=pes) for i in range(2)]
                u2t = C.sb("u2t", [128, D], es=pes)
                u2T = C.sb("u2T", [128, 8, 128], es=pes)
                for k in range(8):
                    S.dma("pool", lambda e, k=k: e.dma_start(out=wo[:, k, :], in_=w_out[l, k * 128:(k + 1) * 128, :]), [], [wo])
                S.dma("sp", lambda e: e.dma_start(out=lnG[:].rearrange("p g d -> p (g d)"), in_=ln_g[l:l + 1, :].broadcast_to([128, 2 * D])), [], [lnG])
                S.dma("sp", lambda e: e.dma_start(out=lnB[:].rearrange("p g d -> p (g d)"), in_=ln_b[l:l + 1, :].broadcast_to([128, 2 * D])), [], [lnB])
                S.dma("sp", lambda e: e.dma_start(out=rw[:], in_=router_wp[l, :, :].rearrange("(k p) e -> p k e", p=128)), [], [rw])
                tiles = list(range(NT)) if full_ctx else list(range(NLT))
                for ti, t in enumerate(tiles):
                    lc = 0 if t < NLT else 1
                    yt = yts[ti % 2]
                    ht = hts[ti % 2]
                    S.dma("sp", lambda e, yt=yt, t=t: e.dma_start(out=yt[:], in_=Y.t[t * 128:(t + 1) * 128, :]), [], [yt])
                    S.dma("act", lambda e, ht=ht, t=t: e.dma_start(out=ht[:], in_=src[t * 128:(t + 1) * 128, :]), ["H/%d" % t], [ht])
                    ps = C.nextps()
                    pv = ps.t[:, :].bitcast(BF16)
                    for k in range(8):
                        S.op("pe", lambda e, pv=pv, yt=yt, k=k: e.transpose(pv[:, k * 128:(k + 1) * 128], yt[:, k * 128:(k + 1) * 128], ident_b[:]),
                             [yt, ident_b], [ps])
                    S.op("act", lambda e, pv=pv: e.copy(out=YT[:].rearrange("p k t -> p (k t)"), in_=pv[:, :]), [ps], [YT])
                    for half in range(2):
                        pm = C.nextps()
                        for k in range(8):
                            S.op("pe", lambda e, pm=pm, k=k, half=half: e.matmul(pm[:, :], lhsT=YT[:, k, :], rhs=wo[:, k, half * 512:(half + 1) * 512],
                                                                                 start=(k == 0), stop=(k == 7)), [YT, wo], [pm])
                        S.op("dve", lambda e, pm=pm, half=half, lc=lc: e.tensor_tensor(
                            out=z[:, half * 512:(half + 1) * 512], in0=pm[:, :], in1=gateB[:, 0, lc, half * 512:(half + 1) * 512], op=ALU.mult),
                            [pm, gateB], [z])
                    S.op("dve", lambda e, ht=ht: e.scalar_tensor_tensor(out=z[:], in0=ht[:], scalar=ALPHA, in1=z[:], op0=ALU.mult, op1=ALU.add),
                         [ht, z], [z])
                    h_ = hn[ti % 2]
                    layer_norm_tile(z, sqt, st, lnG, lnB, 0, h_)
                    S.dma("sp", lambda e, h_=h_, t=t: e.dma_start(out=H.t[t * 128:(t + 1) * 128, :], in_=h_[:]), [h_], ["H/%d" % t])
                    ub_ = u2b[ti % 2]
                    S.op("pool", lambda e, h_=h_, lc=lc: e.tensor_tensor(out=u2t[:], in0=h_[:], in1=gateB[:, 3, lc, :], op=ALU.mult), [h_, gateB], [u2t])
                    S.op("pool", lambda e, ub_=ub_, lc=lc: e.tensor_tensor(out=ub_[:], in0=u2t[:], in1=gateB[:, 2, lc, :], op=ALU.add), [u2t, gateB], [ub_])
                    S.dma("act", lambda e, ub_=ub_, t=t: e.dma_start(out=U2.t[t * 128:(t + 1) * 128, :], in_=ub_[:]), [ub_], [U2.sub(t)])
                    for half in range(2):
                        pt = C.nextps()
                        for j in range(4):
                            k = half * 4 + j
                            S.op("pe", lambda e, pt=pt, h_=h_, j=j, k=k: e.transpose(pt[:, j * 128:(j + 1) * 128], h_[:, k * 128:(k + 1) * 128], ident_f[:]),
                                 [h_, ident_f], [pt])
                        for j in range(4):
                            k = half * 4 + j
                            S.op("act", lambda e, pt=pt, j=j, k=k, lc=lc: e.activation(
                                out=u2T[:, k, :], in_=pt[:, j * 128:(j + 1) * 128], func=AF.Identity,
                                bias=modT[:, k, 2, lc:lc + 1], scale=modT[:, k, 3, lc:lc + 1]), [pt, modT], [u2T])
                    pl_ = C.nextps()
                    for k in range(8):
                        S.op("pe", lambda e, pl_=pl_, k=k: e.matmul(pl_[0:16, 0:128], lhsT=rw[:, k, 0:16], rhs=u2T[:, k, :], start=(k == 0), stop=(k == 7)),
                             [rw, u2T], [pl_])
                    S.op("dve", lambda e, pl_=pl_, t=t: e.tensor_copy(out=LG[:, t * 128:(t + 1) * 128], in_=pl_[0:16, 0:128]), [pl_], [LG])
                S.dma("sp", lambda e: e.dma_start(out=LGD.t[:, :], in_=LG[:]), [LG], [LGD])
                S.barrier()
                S.emit()

        def moe_segs(full_ctx):
            segs = [(0, NLT, NLAT // 8, 0, 0)]
            if full_ctx:
                segs.append((NLT, 2, 32, NLAT // 8, 512))
            return segs

        def phase_F(l, full_ctx):
            segs = moe_segs(full_ctx)
            NTOT = sum(sg[2] for sg in segs)
            with ExitStack() as pes:
                LG = C.sb("fLG", [16, T], es=pes)
                aff = C.sb("aff", [16, T], es=pes)
                work = C.sb("work", [16, NLAT], es=pes)
                m8 = C.sb("m8", [16, 8], es=pes)
                key = C.sb("key", [16, T], es=pes)
                ones16 = C.sb("ones16", [16, 16], es=pes)
                onesr = C.sb("onesr", [16, NLAT], es=pes)
                id16p = C.sb("id16p", [16, 128], es=pes)
                keyTok = C.sb("keyTok", [128, NT, 16], es=pes)
                iot = C.sb("iot", [128, 512], es=pes)
                S.dma("sp", lambda e: e.dma_start(out=LG[:], in_=LGD.t[:, :]), [], [LG])
                S.dma("sp", lambda e: e.dma_start(out=iot[:], in_=iota512[:, :]), [], [iot])
                S.op("pool", lambda e: e.memset(ones16[:], 1.0), [], [ones16])
                S.op("pool", lambda e: e.memset(onesr[:], 1.0), [], [onesr])
                S.op("pool", lambda e: e.memset(id16p[:], 0.0), [], [id16p])
                S.op("dve", lambda e: e.tensor_copy(out=id16p[:, 0:16], in_=ident_f[0:16, 0:16]), [id16p, ident_f], [id16p])
                S.op("act", lambda e: e.activation(out=aff[:], in_=LG[:], func=AF.Exp), [LG], [aff])
                for c0 in range(0, T, 512):
                    cw = min(512, T - c0)
                    ps = C.nextps()
                    S.op("pe", lambda e, ps=ps, c0=c0, cw=cw: e.matmul(ps[0:16, 0:cw], lhsT=ones16[:], rhs=aff[:, c0:c0 + cw], start=True, stop=True),
                         [ones16, aff], [ps])
                    S.op("dve", lambda e, ps=ps, c0=c0, cw=cw: e.reciprocal(out=LG[:, c0:c0 + cw], in_=ps[0:16, 0:cw]), [ps], [LG])
                S.op("dve", lambda e: e.tensor_tensor(out=aff[:], in0=aff[:], in1=LG[:], op=ALU.mult), [aff, LG], [aff])
                for (t0, ntl, cap, c0, r0) in segs:
                    n_ = ntl * 128
                    a0 = t0 * 128
                    S.op("dve", lambda e, a0=a0, n_=n_: e.tensor_copy(out=work[:, 0:n_], in_=aff[:, a0:a0 + n_]), [aff], [work])
                    for rnd in range(cap // 8):
                        S.op("dve", lambda e, n_=n_: e.max(out=m8[:], in_=work[:, 0:n_]), [work], [m8])
                        S.op("dve", lambda e, n_=n_: e.match_replace(out=work[:, 0:n_], in_to_replace=m8[:], in_values=work[:, 0:n_], imm_value=-1.0),
                             [work, m8], [work])
                    S.op("dve", lambda e, n_=n_: e.tensor_single_scalar(out=work[:, 0:n_], in_=work[:, 0:n_], scalar=0.0, op=ALU.is_lt), [work], [work])
                    S.op("dve", lambda e, a0=a0, n_=n_: e.tensor_tensor_scan(out=key[:, a0:a0 + n_], data0=onesr[:, 0:n_], data1=work[:, 0:n_], initial=0.0,
                                                                            op0=ALU.mult, op1=ALU.add), [work, onesr], [key])
                    S.op("dve", lambda e, a0=a0, n_=n_: e.tensor_tensor(out=key[:, a0:a0 + n_], in0=key[:, a0:a0 + n_], in1=work[:, 0:n_], op=ALU.mult), [key, work], [key])
                    S.op("dve", lambda e, a0=a0, n_=n_: e.tensor_scalar_add(out=key[:, a0:a0 + n_], in0=key[:, a0:a0 + n_], scalar1=-1.0), [key], [key])
                S.dma("sp", lambda e: e.dma_start(out=KEYD.t[:, :], in_=key[:]), [key], [KEYD])
                S.dma("sp", lambda e: e.dma_start(out=AFFD.t[:, :], in_=aff[:]), [aff], [AFFD])
                tl_all = [t for (t0, ntl, cap, c0, r0) in segs for t in range(t0, t0 + ntl)]
                for q0 in range(0, len(tl_all), 4):
                    grp = tl_all[q0:q0 + 4]
                    ps = C.nextps()
                    for qi, t in enumerate(grp):
                        S.op("pe", lambda e, ps=ps, qi=qi, t=t: e.matmul(ps[:, qi * 128:(qi + 1) * 128], lhsT=key[:, t * 128:(t + 1) * 128], rhs=id16p[:],
                                                                          start=True, stop=True), [key, id16p], [ps])
                    for qi, t in enumerate(grp):
                        S.op("dve", lambda e, ps=ps, qi=qi, t=t: e.tensor_copy(out=keyTok[:, t, :], in_=ps[:, qi * 128:qi * 128 + 16]), [ps], [keyTok])
                S.dma("sp", lambda e: e.dma_start(out=KTD.t[:, :], in_=keyTok[:].rearrange("p n c -> p (n c)")), [keyTok], [KTD])
                S.barrier()
                S.emit()
            with ExitStack() as pes:
                keyTok = C.sb("keyTok2", [128, NT, 16], es=pes)
                iot = C.sb("iot2", [128, 512], es=pes)
                Se = C.sb("Se", [128, NT, 512], BF16, es=pes)
                xinT = C.sb("xinT", [128, 8, NTOT], BF16, es=pes)
                hT = C.sb("hT", [128, 16, NTOT], BF16, es=pes)
                yacc = C.sb("yacc", [128, 5, D], es=pes)
                yw = C.sb("yw", [128, 5, D], BF16, es=pes)
                u2s = [C.sb(f"u2s{i}", [128, D], BF16, es=pes) for i in range(2)]
                wg = [C.sb(f"wg{i}", [128, 8, 512], BF16, es=pes) for i in range(2)]
                wu = [C.sb(f"wu{i}", [128, 8, 512], BF16, es=pes) for i in range(2)]
                wd = [C.sb(f"wd{i}", [128, 4, D], BF16, es=pes) for i in range(2)]
                sg_ = [C.sb(f"sgt{i}", [128, 512], es=pes) for i in range(2)]
                S.dma("sp", lambda e: e.dma_start(out=keyTok[:].rearrange("p n c -> p (n c)"), in_=KTD.t[:, :]), [], [keyTok])
                S.dma("sp", lambda e: e.dma_start(out=iot[:], in_=iota512[:, :]), [], [iot])
                C.rot = [4, 5, 6, 7]
                wcnt = 0
                ucnt = 0
                for ex in range(16):
                    for (t0, ntl, cap, c0, r0) in segs:
                        for t in range(t0, t0 + ntl):
                            S.op("dve", lambda e, t=t, cap=cap, ex=ex: e.tensor_scalar(
                                out=Se[:, t, 0:cap], in0=iot[:, 0:cap], scalar1=keyTok[:, t, ex:ex + 1], scalar2=None, op0=ALU.is_equal),
                                [iot, keyTok], [Se.sub(t)])
                        for kh in range(2):
                            for ti, t in enumerate(range(t0, t0 + ntl)):
                                ut = u2s[ucnt % 2]
                                ucnt += 1
                                S.dma("sp", lambda e, ut=ut, t=t: e.dma_start(out=ut[:], in_=U2.t[t * 128:(t + 1) * 128, :]), [], [ut])
                                for kk in range(4):
                                    k = kh * 4 + kk
                                    S.op("pe", lambda e, ut=ut, t=t, k=k, kk=kk, cap=cap, ti=ti, ntl=ntl: e.matmul(
                                        C.ps[kk][:, 0:cap], lhsT=ut[:, k * 128:(k + 1) * 128], rhs=Se[:, t, 0:cap],
                                        start=(ti == 0), stop=(ti == ntl - 1)), [ut, Se.sub(t)], [C.ps[kk]])
                            for kk in range(4):
                                k = kh * 4 + kk
                                S.op("act", lambda e, k=k, kk=kk, c0=c0, cap=cap: e.copy(out=xinT[:, k, c0:c0 + cap], in_=C.ps[kk][:, 0:cap]),
                                     [C.ps[kk]], [xinT])
                    jts = []
                    for (t0, ntl, cap, c0, r0) in segs:
                        for j0 in range(0, cap, 128):
                            jts.append((r0 + j0, min(128, cap - j0), c0 + j0))
                    for fg in range(4):
                        g_, u_, d_ = wg[wcnt % 2], wu[wcnt % 2], wd[wcnt % 2]
                        wcnt += 1
                        for k in range(8):
                            S.dma("pool", lambda e, g_=g_, ex=ex, fg=fg, k=k: e.dma_start(
                                out=g_[:, k, :], in_=w_gate[l, ex, k * 128:(k + 1) * 128, fg * 512:(fg + 1) * 512]), [], [g_])
                            S.dma("pool", lambda e, u_=u_, ex=ex, fg=fg, k=k: e.dma_start(
                                out=u_[:, k, :], in_=w_up[l, ex, k * 128:(k + 1) * 128, fg * 512:(fg + 1) * 512]), [], [u_])
                        for cc in range(4):
                            S.dma("pool", lambda e, d_=d_, ex=ex, fg=fg, cc=cc: e.dma_start(
                                out=d_[:, cc, :], in_=w_down[l, ex, fg * 512 + cc * 128:fg * 512 + (cc + 1) * 128, :]), [], [d_])
                        for fc in range(4):
                            f = fg * 4 + fc
                            for (t0, ntl, cap, c0, r0) in segs:
                                pg = C.nextps()
                                pu_ = C.nextps()
                                for k in range(8):
                                    S.op("pe", lambda e, pg=pg, g_=g_, k=k, fc=fc, c0=c0, cap=cap: e.matmul(
                                        pg[:, 0:cap], lhsT=g_[:, k, fc * 128:(fc + 1) * 128], rhs=xinT[:, k, c0:c0 + cap], start=(k == 0), stop=(k == 7)),
                                        [g_, xinT], [pg])
                                for k in range(8):
                                    S.op("pe", lambda e, pu_=pu_, u_=u_, k=k, fc=fc, c0=c0, cap=cap: e.matmul(
                                        pu_[:, 0:cap], lhsT=u_[:, k, fc * 128:(fc + 1) * 128], rhs=xinT[:, k, c0:c0 + cap], start=(k == 0), stop=(k == 7)),
                                        [u_, xinT], [pu_])
                                sgt = sg_[f % 2]
                                S.op("act", lambda e, pg=pg, sgt=sgt, cap=cap: e.activation(out=sgt[:, 0:cap], in_=pg[:, 0:cap], func=AF.Silu), [pg], [sgt])
                                S.op("dve", lambda e, pu_=pu_, sgt=sgt, f=f, c0=c0, cap=cap: e.tensor_tensor(
                                    out=hT[:, f, c0:c0 + cap], in0=pu_[:, 0:cap], in1=sgt[:, 0:cap], op=ALU.mult), [pu_, sgt], [hT])
                        for ji, (ro, rows, co) in enumerate(jts):
                            for half in range(2):
                                py = C.nextps()
                                for fc in range(4):
                                    S.op("pe", lambda e, py=py, d_=d_, fc=fc, fg=fg, co=co, rows=rows, half=half: e.matmul(
                                        py[0:rows, :], lhsT=hT[:, fg * 4 + fc, co:co + rows], rhs=d_[:, fc, half * 512:(half + 1) * 512],
                                        start=(fc == 0), stop=(fc == 3)), [hT, d_], [py])
                                if fg == 0:
                                    S.op("dve", lambda e, py=py, ji=ji, rows=rows, half=half: e.tensor_copy(
                                        out=yacc[0:rows, ji, half * 512:(half + 1) * 512], in_=py[0:rows, :]), [py], [yacc])
                                elif fg < 3:
                                    S.op("dve", lambda e, py=py, ji=ji, rows=rows, half=half: e.tensor_tensor(
                                        out=yacc[0:rows, ji, half * 512:(half + 1) * 512], in0=yacc[0:rows, ji, half * 512:(half + 1) * 512],
                                        in1=py[0:rows, :], op=ALU.add), [py, yacc], [yacc])
                                else:
                                    S.op("dve", lambda e, py=py, ji=ji, rows=rows, half=half: e.tensor_tensor(
                                        out=yw[0:rows, ji, half * 512:(half + 1) * 512], in0=yacc[0:rows, ji, half * 512:(half + 1) * 512],
                                        in1=py[0:rows, :], op=ALU.add), [py, yacc], [yw])
                    for ji, (ro, rows, co) in enumerate(jts):
                        S.dma("act", lambda e, ji=ji, ro=ro, rows=rows, ex=ex: e.dma_start(out=YE.t[ex, ro:ro + rows, :], in_=yw[0:rows, ji, :]),
                              [yw], [YE.sub(ex, ji)])
                C.rot = None
                S.barrier()
                S.emit()

        def phase_G(l, full_ctx, last):
            segs = moe_segs(full_ctx)
            with ExitStack() as pes:
                key = C.sb("gkey", [16, T], es=pes)
                aff = C.sb("gaff", [16, T], es=pes)
                selE = C.sb("selE", [16, 16, 128], es=pes)
                jcol = C.sb("jcol", [128, 4], es=pes)
                lnG = C.sb("glnG", [128, 2, D], es=pes)
                lnB = C.sb("glnB", [128, 2, D], es=pes)
                acc = C.sb("acc", [128, 8, D], es=pes)
                yes = [C.sb(f"ye{i}", [128, 4, D], BF16, es=pes) for i in range(2)]
                Ws = [C.sb(f"W{i}", [128, 4, 512], BF16, es=pes) for i in range(2)]
                affB = [C.sb(f"affB{i}", [128, 512], es=pes) for i in range(2)]
                hts = [C.sb(f"ght{i}", [128, D], es=pes) for i in range(2)]
                z = C.sb("gz", [128, D], es=pes)
                sqt = C.sb("gsqt", [128, D], es=pes)
                st = C.sb("gst", [128, 4], es=pes)
                hn = [C.sb(f"ghn{i}", [128, D], es=pes) for i in range(2)]
                S.dma("sp", lambda e: e.dma_start(out=key[:], in_=KEYD.t[:, :]), [], [key])
                S.dma("sp", lambda e: e.dma_start(out=aff[:], in_=AFFD.t[:, :]), [], [aff])
                S.dma("sp", lambda e: e.dma_start(out=selE[:], in_=selE_in[:, :, :]), [], [selE])
                S.dma("sp", lambda e: e.dma_start(out=jcol[:], in_=jcol_in[:, :]), [], [jcol])
                S.dma("sp", lambda e: e.dma_start(out=lnG[:].rearrange("p g d -> p (g d)"), in_=ln_g[l:l + 1, :].broadcast_to([128, 2 * D])), [], [lnG])
                S.dma("sp", lambda e: e.dma_start(out=lnB[:].rearrange("p g d -> p (g d)"), in_=ln_b[l:l + 1, :].broadcast_to([128, 2 * D])), [], [lnB])
                cnt = 0
                for (t0, ntl, cap, c0, r0) in segs:
                    lc = 0 if t0 < NLT else 1
                    njt = (cap + 127) // 128
                    for gq in range(t0, t0 + ntl, 8):
                        gtiles = list(range(gq, min(gq + 8, t0 + ntl)))
                        for ex in range(16):
                            ye = yes[ex % 2]
                            for jt in range(njt):
                                rows = min(128, cap - jt * 128)
                                S.dma("sp", lambda e, ye=ye, jt=jt, rows=rows, ex=ex, r0=r0: e.dma_start(
                                    out=ye[0:rows, jt, :], in_=YE.t[ex, r0 + jt * 128:r0 + jt * 128 + rows, :]), [ye], [ye])
                            for q0 in range(0, len(gtiles), 4):
                                ch = gtiles[q0:q0 + 4]
                                a0 = ch[0] * 128
                                cw = len(ch) * 128
                                pk = C.nextps()
                                pa = C.nextps()
                                S.op("pe", lambda e, pk=pk, ex=ex, a0=a0, cw=cw: e.matmul(pk[:, 0:cw], lhsT=selE[:, ex, :], rhs=key[:, a0:a0 + cw], start=True, stop=True),
                                     [selE, key], [pk])
                                S.op("pe", lambda e, pa=pa, ex=ex, a0=a0, cw=cw: e.matmul(pa[:, 0:cw], lhsT=selE[:, ex, :], rhs=aff[:, a0:a0 + cw], start=True, stop=True),
                                     [selE, aff], [pa])
                                ab = affB[cnt % 2]
                                W = Ws[cnt % 2]
                                cnt += 1
                                S.op("act", lambda e, pa=pa, ab=ab, cw=cw: e.copy(out=ab[:, 0:cw], in_=pa[:, 0:cw]), [pa], [ab])
                                for jt in range(njt):
                                    S.op("dve", lambda e, pk=pk, ab=ab, W=W, jt=jt, cw=cw: e.scalar_tensor_tensor(
                                        out=W[:, jt, 0:cw], in0=pk[:, 0:cw], scalar=jcol[:, jt:jt + 1], in1=ab[:, 0:cw], op0=ALU.is_equal, op1=ALU.mult),
                                        [pk, ab, jcol], [W])
                                for qi, t in enumerate(ch):
                                    ti = t - gq
                                    for half in range(2):
                                        py = C.nextps()
                                        for jt in range(njt):
                                            rows = min(128, cap - jt * 128)
                                            S.op("pe", lambda e, py=py, W=W, ye=ye, jt=jt, rows=rows, qi=qi, half=half, njt=njt: e.matmul(
                                                py[:, :], lhsT=W[0:rows, jt, qi * 128:(qi + 1) * 128], rhs=ye[0:rows, jt, half * 512:(half + 1) * 512],
                                                start=(jt == 0), stop=(jt == njt - 1)), [W, ye], [py])
                                        if ex == 0:
                                            S.op("dve", lambda e, py=py, ti=ti, half=half: e.tensor_copy(out=acc[:, ti, half * 512:(half + 1) * 512], in_=py[:, :]),
                                                 [py], [acc.sub(ti)])
                                        else:
                                            S.op("dve", lambda e, py=py, ti=ti, half=half: e.tensor_tensor(
                                                out=acc[:, ti, half * 512:(half + 1) * 512], in0=acc[:, ti, half * 512:(half + 1) * 512], in1=py[:, :], op=ALU.add),
                                                [py, acc.sub(ti)], [acc.sub(ti)])
                        for t in gtiles:
                            ti = t - gq
                            ht = hts[t % 2]
                            S.dma("act", lambda e, ht=ht, t=t: e.dma_start(out=ht[:], in_=H.t[t * 128:(t + 1) * 128, :]), ["H/%d" % t], [ht])
                            S.op("pool", lambda e, ti=ti, lc=lc: e.tensor_tensor(out=z[:], in0=acc[:, ti, :], in1=gateB[:, 1, lc, :], op=ALU.mult),
                                 [acc.sub(ti), gateB], [z])
                            S.op("dve", lambda e, ht=ht: e.scalar_tensor_tensor(out=z[:], in0=ht[:], scalar=ALPHA, in1=z[:], op0=ALU.mult, op1=ALU.add),
                                 [ht, z], [z])
                            h_ = hn[t % 2]
                            layer_norm_tile(z, sqt, st, lnG, lnB, 1, h_)
                            if last:
                                S.dma("sp", lambda e, h_=h_, t=t: e.dma_start(out=out[t * 128:(t + 1) * 128, :], in_=h_[:]), [h_], ["out/%d" % t])
                            else:
                                S.dma("sp", lambda e, h_=h_, t=t: e.dma_start(out=H.t[t * 128:(t + 1) * 128, :], in_=h_[:]), [h_], ["H/%d" % t])
                S.barrier()
                S.emit()

        for l in range(NL):
            full_ctx = l < DEPTH - 1
            if "A" in phases.split(","):
                phase_A(l)
            if "BC" in phases.split(","):
                phase_BC(l)
            if "ret" in phases.split(","):
                phase_ret(l, full_ctx)
            if "diff" in phases.split(","):
                phase_diff(l, full_ctx)
            if "swa" in phases.split(","):
                phase_swa(l, full_ctx)
            if "gdn" in phases.split(","):
                phase_gdn(l, full_ctx)
            if "E" in phases.split(","):
                phase_E(l, full_ctx)
            if "F" in phases.split(","):
                phase_F(l, full_ctx)
            if "G" in phases.split(","):
                phase_G(l, full_ctx, last=(l == NL - 1 and "keepH" not in dbg))
    dbg_t["ninst"] = S.ninst
    return nc, dbg_t


def prep_shared(inputs, NLT, NL):
    sh = {}
    ada_b = np.asarray(inputs["ada_b"], np.float32)[:NL]
    sh["ada_w"] = np.ascontiguousarray(np.asarray(inputs["ada_w"], np.float32)[:NL])
    sh["ada_b"] = np.ascontiguousarray(ada_b)
    sh["ada_bT"] = np.ascontiguousarray(ada_b.reshape(NL, 48, 128).transpose(0, 2, 1))
    w_in = np.asarray(inputs["w_in"], np.float32)
    sh["w_in"] = np.stack([build_w_in_ext(w_in[l]) for l in range(NL)])
    sh["w_out"] = np.ascontiguousarray(np.asarray(inputs["w_out"], np.float32)[:NL])
    sh["rope"] = rope_tables(NLT)
    sh["ret_decay"] = np.ascontiguousarray(np.asarray(inputs["ret_decay"], np.float32)[:NL].reshape(NL, 8))
    sh["ln_g"] = np.ascontiguousarray(np.asarray(inputs["ln_g"], np.float32)[:NL].reshape(NL, 2 * D))
    sh["ln_b"] = np.ascontiguousarray(np.asarray(inputs["ln_b"], np.float32)[:NL].reshape(NL, 2 * D))
    sh["diff_lambda"] = np.ascontiguousarray(np.asarray(inputs["diff_lambda"], np.float32)[:NL].reshape(NL, 128))
    sh["diff_norm"] = np.ascontiguousarray(np.asarray(inputs["diff_norm"], np.float32)[:NL])
    sh["swa_sink"] = np.ascontiguousarray(np.asarray(inputs["swa_sink"], np.float32)[:NL])
    rw = np.asarray(inputs["router_w"], np.float32)[:NL]
    rwp = np.zeros((NL, D, 128), np.float32)
    rwp[:, :, :16] = rw
    sh["router_wp"] = rwp
    sh["w_gate"] = np.ascontiguousarray(np.asarray(inputs["w_gate"], np.float32)[:NL])
    sh["w_up"] = np.ascontiguousarray(np.asarray(inputs["w_up"], np.float32)[:NL])
    sh["w_down"] = np.ascontiguousarray(np.asarray(inputs["w_down"], np.float32)[:NL])
    sh["iota512"] = np.ascontiguousarray(np.broadcast_to(np.arange(512, dtype=np.float32)[None, :], (128, 512)))
    sh["jcol_in"] = np.ascontiguousarray(np.arange(128, dtype=np.float32)[:, None] + 128.0 * np.arange(4, dtype=np.float32)[None, :])
    se = np.zeros((16, 16, 128), np.float32)
    for e_ in range(16):
        se[e_, e_, :] = 1.0
    sh["selE_in"] = se
    gc = np.asarray(inputs["gdn_conv"], np.float32)[:NL]
    sh["gdn_convT"] = np.ascontiguousarray(gc.reshape(NL, 3, 6, 128).transpose(0, 3, 2, 1).reshape(NL, 128, 18))
    sh["gdn_a_log"] = np.ascontiguousarray(np.asarray(inputs["gdn_a_log"], np.float32)[:NL].reshape(NL, 8))
    sh["gdn_dt_bias"] = np.ascontiguousarray(np.asarray(inputs["gdn_dt_bias"], np.float32)[:NL].reshape(NL, 8))
    sh["gdn_norm"] = np.ascontiguousarray(np.asarray(inputs["gdn_norm"], np.float32)[:NL])
    i = np.arange(128, dtype=np.float32)
    sh["cmask"] = np.ascontiguousarray(i[None, :] - i[:, None])
    return sh


def prep_core(inputs, b, NLT):
    m = {}
    x = np.asarray(inputs["x"], np.float32)[b, :NLT * 128]
    ctx = np.asarray(inputs["ctx"], np.float32)[b]
    m["h0"] = np.ascontiguousarray(np.concatenate([x, ctx], 0))
    c = np.asarray(inputs["c"], np.float32)[b]
    cx = np.asarray(inputs["c_ctx"], np.float32)
    m["cc"] = np.ascontiguousarray(np.stack([c, cx], -1).reshape(8, 128, 2).transpose(1, 0, 2))
    return m


def kernel(**inputs):
    NLT, NL = 32, 4
    nc, _ = build(NLT, NL)
    sh = prep_shared(inputs, NLT, NL)
    in_maps = []
    for b in range(8):
        m = dict(sh)
        m.update(prep_core(inputs, b, NLT))
        in_maps.append(m)
    res = run_bass_kernel_spmd(nc, in_maps, core_ids=list(range(8)))
    return np.stack([np.asarray(r["out"], np.float32) for r in res.results], 0)
```

```python
import math
import numpy as np
import ml_dtypes
from contextlib import ExitStack, nullcontext
import concourse.bass as bass
import concourse.mybir as mybir
from concourse.bass_utils import run_bass_kernel_spmd

F32 = mybir.dt.float32
BF16 = mybir.dt.bfloat16
F32R = mybir.dt.float32r
AF = mybir.ActivationFunctionType
ALU = mybir.AluOpType
AX = mybir.AxisListType

ENGS = ["pe", "dve", "act", "pool", "sp"]
DMA_RING = 6
SEM_EPOCH = 20000

D = 1024
DEPTH = 4
ALPHA = (2 * DEPTH) ** 0.25
LN_EPS = 1e-5
IN_W = 3344
NEG = -30000.0


class Sched:
    def __init__(self, nc, es, same_engine_sync=True):
        self.nc = nc
        self.es = es
        self.q = {e: [] for e in ENGS}
        self.cnt = {e: 0 for e in ENGS}
        self.epoch = {e: 0 for e in ENGS}
        self.sem = {e: es.enter_context(nc.semaphore(f"s_{e}_0")) for e in ENGS}
        self.dq = ["sp", "act", "pool"]
        self.dsem = {e: [es.enter_context(nc.semaphore(f"d_{e}_{i}")) for i in range(DMA_RING)]
                     for e in self.dq}
        self.dcnt = {e: 0 for e in self.dq}
        self.lastw = {}
        self.readers = {}
        self.seen = {}
        self.same = same_engine_sync
        self.ninst = 0

    def _key(self, k):
        return k if isinstance(k, str) else k.k

    def _need(self, eng, tok, waits):
        if tok is None:
            return
        sem, val, prod = tok
        if prod == eng and (eng == "pe" or not self.same):
            return
        kk = (eng, id(sem))
        if self.seen.get(kk, 0) >= val:
            return
        self.seen[kk] = val
        waits.append((sem, val))

    def _deps(self, eng, reads, writes):
        waits = []
        for k in reads:
            k = self._key(k)
            self._need(eng, self.lastw.get(k), waits)
            if k.startswith("ps"):
                for t in self.readers.get(k, ()):
                    if t[2] != eng:
                        self._need(eng, t, waits)
        for k in writes:
            k = self._key(k)
            self._need(eng, self.lastw.get(k), waits)
            for t in self.readers.get(k, ()):
                self._need(eng, t, waits)
        return waits

    def _commit(self, tok, reads, writes):
        for k in reads:
            self.readers.setdefault(self._key(k), []).append(tok)
        for k in writes:
            k = self._key(k)
            self.lastw[k] = tok
            self.readers[k] = []

    def op(self, eng, fn, reads=(), writes=()):
        waits = self._deps(eng, reads, writes)
        if self.cnt[eng] >= SEM_EPOCH:
            self.epoch[eng] += 1
            self.sem[eng] = self.es.enter_context(self.nc.semaphore(f"s_{eng}_{self.epoch[eng]}"))
            self.cnt[eng] = 0
        self.cnt[eng] += 1
        tok = (self.sem[eng], self.cnt[eng], eng)
        self.q[eng].append((waits, fn, self.sem[eng], 1))
        self._commit(tok, reads, writes)
        self.ninst += 1

    def dma(self, qn, fn, reads=(), writes=()):
        waits = self._deps(qn, reads, writes)
        i = self.dcnt[qn]
        self.dcnt[qn] += 1
        slot, rnd = i % DMA_RING, i // DMA_RING
        sem = self.dsem[qn][slot]
        if rnd > 0:
            self._need(qn, (sem, 16 * rnd, None), waits)
        tok = (sem, 16 * (rnd + 1), None)
        self.q[qn].append((waits, fn, sem, 16))
        self._commit(tok, reads, writes)
        self.ninst += 1

    def barrier(self):
        toks = []
        for e in ENGS:
            if self.cnt[e] > 0:
                toks.append((self.sem[e], self.cnt[e], e))
        for qn in self.dq:
            n = self.dcnt[qn]
            for slot in range(DMA_RING):
                if n > slot:
                    rounds = (n - 1 - slot) // DMA_RING + 1
                    toks.append((self.dsem[qn][slot], 16 * rounds, None))
        for e in ENGS:
            waits = []
            for t in toks:
                if t[2] == e:
                    continue
                self._need(e, t, waits)
            self.q[e].append((waits, None, None, 0))
        self.lastw = {}
        self.readers = {}

    def emit(self):
        nc = self.nc
        q = self.q

        def run(e, name):
            for waits, fn, sem, inc in q[name]:
                for (s, v) in waits:
                    e.wait_ge(s, v)
                if fn is not None:
                    fn(e).then_inc(sem, inc)

        with nc.Block() as block:
            @block.tensor
            def _(e):
                run(e, "pe")

            @block.vector
            def _(e):
                run(e, "dve")

            @block.scalar
            def _(e):
                run(e, "act")

            @block.gpsimd
            def _(e):
                run(e, "pool")

            @block.sync
            def _(e):
                run(e, "sp")
        self.q = {e: [] for e in ENGS}


class Buf:
    def __init__(self, t, k):
        self.t = t
        self.k = k

    def sub(self, *idx):
        return self.k + "/" + "/".join(str(i) for i in idx)

    def __getitem__(self, key):
        return self.t[key]


class Ctx:
    def __init__(self, nc, es, same_engine_sync=True):
        self.nc = nc
        self.es = es
        self.S = Sched(nc, es, same_engine_sync)
        self.ps = [self.psum(f"ps{i}") for i in range(8)]
        self.psi = 0
        self.uid = 0

    def sb(self, name, shape, dtype=F32, es=None):
        self.uid += 1
        nm = f"{name}_{self.uid}"
        t = (es or self.es).enter_context(self.nc.sbuf_tensor(nm, list(shape), dtype))
        return Buf(t, nm)

    def psum(self, name, shape=(128, 512), dtype=F32):
        t = self.es.enter_context(self.nc.psum_tensor(name, list(shape), dtype))
        return Buf(t, name)

    def dram(self, name, shape, dtype=F32, kind="Internal"):
        t = self.nc.dram_tensor(name, list(shape), dtype, kind=kind)
        return Buf(t, name)

    def nextps(self):
        r = getattr(self, "rot", None) or list(range(8))
        p = self.ps[r[self.psi % len(r)]]
        self.psi += 1
        return p


def fm_groups():
    def swp(cols, blk):
        cols = np.asarray(cols)
        half = blk // 2
        c = cols.reshape(-1, blk)
        return np.concatenate([c[:, half:], c[:, :half]], 1).reshape(-1)
    g = []
    def add_rot(name, start, width, blk, kind):
        for i in range(width // 128):
            cols = np.arange(start + i * 128, start + (i + 1) * 128)
            g.append((f"{name}{i}", cols, kind, swp(cols, blk)))
    add_rot("ret_q", 0, 256, 64, 0)
    add_rot("ret_k", 256, 256, 64, 0)
    add_rot("diff_q", 1024, 256, 32, 1)
    add_rot("diff_k", 1280, 256, 32, 1)
    for i in range(6):
        g.append((f"gdn{i}", np.arange(1792 + i * 128, 1792 + (i + 1) * 128), None, None))
    add_rot("swa_q", 2832, 256, 64, 2)
    add_rot("swa_k", 3088, 128, 64, 2)
    return g


TM_GROUPS = [("ret_vg", 512, 512), ("diff_v", 1536, 256), ("gdn_gab", 2560, 272), ("swa_v", 3216, 128)]


def build_w_in_ext(w_in_l):
    cols = []
    for name, c, kind, sw in fm_groups():
        cols.append(c)
        if kind is not None:
            cols.append(sw)
    for name, st, w in TM_GROUPS:
        cols.append(np.arange(st, st + w))
    cols = np.concatenate(cols)
    return np.ascontiguousarray(w_in_l[:, cols])


def rope_tables(NLT):
    n = NLT * 128
    T = n + 256
    tabs = np.zeros((3, 2, 128, T), np.float32)
    tabs[:, 0] = 1.0
    f32 = np.float32
    theta = (1.0 / (f32(10000.0) ** np.linspace(0.0, 1.0, 32, dtype=np.float32))).astype(np.float32)
    ang_ret = (np.arange(n, dtype=np.float32)[:, None] * theta).astype(np.float32)
    def axial(rot_dim):
        rows = n // 64
        row = np.repeat(np.arange(rows, dtype=np.float32), 64)
        col = np.tile(np.arange(64, dtype=np.float32), rows)
        nf = rot_dim // 4
        inv = (f32(10000.0) ** (-np.arange(nf, dtype=np.float32) / f32(nf))).astype(np.float32)
        return np.concatenate([row[:, None] * inv, col[:, None] * inv], -1).astype(np.float32)
    ang_diff = axial(32)
    ang_swa = axial(64)
    for kind, (ang, blk) in enumerate([(ang_ret, 64), (ang_diff, 32), (ang_swa, 64)]):
        half = blk // 2
        for f in range(128):
            d = f % blk
            j = d % half
            c = np.cos(ang[:, j]).astype(np.float32)
            s = np.sin(ang[:, j]).astype(np.float32)
            tabs[kind, 0, f, :n] = c
            tabs[kind, 1, f, :n] = -s if d < half else s
    return tabs


def build(NLT=32, NL=4, dbg=(), phases="A,BC,ret,diff,swa,gdn,E,F,G"):
    nc = bass.Bass("TRN2", target_bir_lowering=False)
    NT = NLT + 2
    T = NT * 128
    NLAT = NLT * 128
    FMG = fm_groups()
    NFM = len(FMG)
    NWE = sum(128 * (2 if g[2] is not None else 1) for g in FMG) + sum(w for _, _, w in TM_GROUPS)

    def din(name, shape, dt=F32):
        return nc.dram_tensor(name, list(shape), dt, kind="ExternalInput")

    h0 = din("h0", [T, D])
    cc = din("cc", [128, 8, 2])
    ada_w = din("ada_w", [NL, D, 6 * D])
    ada_b = din("ada_b", [NL, 6 * D])
    ada_bT = din("ada_bT", [NL, 128, 48])
    w_in = din("w_in", [NL, D, NWE])
    w_out = din("w_out", [NL, D, D])
    rope = din("rope", [3, 2, 128, T])
    ret_decay = din("ret_decay", [NL, 8])
    ln_g = din("ln_g", [NL, 2 * D])
    ln_b = din("ln_b", [NL, 2 * D])
    cmask = din("cmask", [128, 128])
    diff_lambda = din("diff_lambda", [NL, 128])
    diff_norm = din("diff_norm", [NL, 64])
    swa_sink = din("swa_sink", [NL, 4])
    router_wp = din("router_wp", [NL, D, 128])
    w_gate = din("w_gate", [NL, 16, D, 2 * D])
    w_up = din("w_up", [NL, 16, D, 2 * D])
    w_down = din("w_down", [NL, 16, 2 * D, D])
    iota512 = din("iota512", [128, 512])
    jcol_in = din("jcol_in", [128, 4])
    selE_in = din("selE_in", [16, 16, 128])
    gdn_convT = din("gdn_convT", [NL, 128, 18])
    gdn_a_log = din("gdn_a_log", [NL, 8])
    gdn_dt_bias = din("gdn_dt_bias", [NL, 8])
    gdn_norm = din("gdn_norm", [NL, 64])
    out = nc.dram_tensor("out", [NLAT, D], F32, kind="ExternalOutput")

    dbg_t = {}

    with ExitStack() as es:
        C = Ctx(nc, es)
        S = C.S
        H = C.dram("H", [T, D], kind=("ExternalOutput" if "H" in dbg else "Internal"))
        FMO = C.dram("FMO", [NFM, 128, T], BF16)
        TMA = C.dram("TMA", [T, 512], BF16)
        TMB = C.dram("TMB", [T, 256], BF16)
        TMC = C.dram("TMC", [T, 256], BF16)
        GAB = C.dram("GAB", [128, NT * 16], F32)
        TMD = C.dram("TMD", [T, 128], BF16)
        GFM = C.dram("GFM", [4, 128, T], BF16)
        GTM = C.dram("GTM", [T, 512], BF16)
        U2 = C.dram("U2", [T, D], BF16)
        LGD = C.dram("LGD", [16, T], F32)
        YE = C.dram("YE", [16, 640, D], BF16)
        KTD = C.dram("KTD", [128, NT * 16], F32)
        KEYD = C.dram("KEYD", [16, T], F32)
        AFFD = C.dram("AFFD", [16, T], F32)
        Y = C.dram("Y", [T, D], BF16, kind=("ExternalOutput" if "Y" in dbg else "Internal"))
        if "FMO" in dbg:
            dbg_t["FMO"] = FMO
        fm_index = {g[0]: i for i, g in enumerate(FMG)}

        ident_f = C.sb("ident_f", [128, 128])
        ident_b = C.sb("ident_b", [128, 128], BF16)
        condT = C.sb("condT", [128, 8, 2])
        condB = [C.sb(f"condB{i}", [128, 8, 128]) for i in range(2)]
        modT = C.sb("modT", [128, 8, 4, 2])
        gateB = C.sb("gateB", [128, 4, 2, D])
        relpos = C.sb("relpos", [128, 128])

        S.op("pool", lambda e: e.memset(ident_f[:], 0.0), [], [ident_f])
        S.op("pool", lambda e: e.affine_select(out=ident_f[:], in_=ident_f[:], pattern=[[-1, 128]],
                                               compare_op=ALU.not_equal, fill=1.0, base=0, channel_multiplier=1),
             [ident_f], [ident_f])
        S.op("dve", lambda e: e.tensor_copy(out=ident_b[:], in_=ident_f[:]), [ident_f], [ident_b])
        S.dma("sp", lambda e: e.dma_start(out=condT[:], in_=cc[:, :, :]), [], [condT])
        S.dma("sp", lambda e: e.dma_start(out=relpos[:], in_=cmask[:, :]), [], [relpos])
        S.op("act", lambda e: e.activation(out=condT[:], in_=condT[:], func=AF.Silu), [condT], [condT])
        for lc in range(2):
            for k in range(8):
                S.op("dve", lambda e, lc=lc, k=k: e.tensor_copy(
                    out=condB[lc][:, k, :], in_=condT[:, k, lc:lc + 1].broadcast_to([128, 128])),
                    [condT], [condB[lc]])
        S.barrier()
        S.emit()

        def phase_A(l):
            with ExitStack() as pes:
                wa = [C.sb(f"wa{i}", [128, 8, 512], es=pes) for i in range(2)]
                bbc = [C.sb(f"bbc{i}", [128, 512], es=pes) for i in range(2)]
                bT = C.sb("bT", [128, 48], es=pes)
                S.dma("sp", lambda e: e.dma_start(out=bT[:], in_=ada_bT[l, :, :]), [], [bT])
                import os
                ADBG = int(os.environ.get("ADBG", 9))
                for g in range(12):
                    m, half = g // 2, g % 2
                    w = wa[g % 2]
                    S.dma("sp", lambda e, w=w, g=g: e.dma_start(
                        out=w[:], in_=ada_w[l, :, g * 512:(g + 1) * 512].rearrange("(k p) n -> p k n", p=128)),
                        [], [w])
                    if m in (2, 5, 3, 4):
                        mi = {2: 0, 5: 1, 3: 2, 4: 3}[m]
                        bb = bbc[g % 2]
                        S.dma("act", lambda e, bb=bb, g=g: e.dma_start(
                            out=bb[:], in_=ada_b[l:l + 1, g * 512:(g + 1) * 512].broadcast_to([128, 512])), [], [bb])
                        if m == 4:
                            S.op("dve", lambda e, bb=bb: e.tensor_scalar_add(out=bb[:], in0=bb[:], scalar1=1.0), [bb], [bb])
                        for lc in range(2):
                            ps = C.nextps()
                            for k in range(8):
                                S.op("pe", lambda e, ps=ps, w=w, lc=lc, k=k: e.matmul(
                                    ps[:, :], lhsT=condB[lc][:, k, :], rhs=w[:, k, :], start=(k == 0), stop=(k == 7)),
                                    [condB[lc], w], [ps])
                            S.op("dve", lambda e, ps=ps, bb=bb, mi=mi, lc=lc, half=half: e.tensor_tensor(
                                out=gateB[:, mi, lc, half * 512:(half + 1) * 512], in0=ps[:, :], in1=bb[:], op=ALU.add),
                                [ps, bb], [gateB])
                    if m in (0, 1, 3, 4):
                        mi = {0: 0, 1: 1, 3: 2, 4: 3}[m]
                        for j in range(4):
                            kk = half * 4 + j
                            ps = C.nextps()
                            for lc in range(2):
                                for k in range(8):
                                    S.op("pe", lambda e, ps=ps, w=w, j=j, k=k, lc=lc: e.matmul(
                                        ps[:, lc * 128:(lc + 1) * 128], lhsT=w[:, k, j * 128:(j + 1) * 128],
                                        rhs=condB[lc][:, k, :], start=(k == 0), stop=(k == 7)), [condB[lc], w], [ps])
                            addone = 1.0 if m in (1, 4) else 0.0
                            for lc in range(2):
                                S.op("dve", lambda e, ps=ps, kk=kk, mi=mi, m=m, addone=addone, lc=lc: e.tensor_scalar(
                                    out=modT[:, kk, mi, lc:lc + 1], in0=ps[:, lc * 128:lc * 128 + 1],
                                    scalar1=bT[:, m * 8 + kk:m * 8 + kk + 1],
                                    scalar2=addone, op0=ALU.add, op1=ALU.add), [ps, bT], [modT])
                S.barrier()
                S.emit()

        def phase_BC(l):
            src = h0 if l == 0 else H.t
            with ExitStack() as pes:
                uT = C.sb("uT", [128, 8, T], BF16, es=pes)
                wsb = C.sb("wsb", [128, 8, NWE], BF16, es=pes)
                bes = ExitStack()
                hts = [C.sb(f"ht{i}", [128, D], es=bes) for i in range(2)]
                for k in range(8):
                    S.dma("pool", lambda e, k=k: e.dma_start(out=wsb[:, k, :], in_=w_in[l, k * 128:(k + 1) * 128, :]),
                          [], [wsb.sub(k)])
                wkeys = [wsb.sub(k) for k in range(8)]
                for t in range(NT):
                    lc = 0 if t < NLT else 1
                    ht = hts[t % 2]
                    S.dma("sp", lambda e, ht=ht, t=t: e.dma_start(out=ht[:], in_=src[t * 128:(t + 1) * 128, :]),
                          ["H/%d" % t], [ht])
                    for half in range(2):
                        ps = C.nextps()
                        for j in range(4):
                            k = half * 4 + j
                            S.op("pe", lambda e, ps=ps, ht=ht, j=j, k=k: e.transpose(
                                ps[:, j * 128:(j + 1) * 128], ht[:, k * 128:(k + 1) * 128], ident_f[:]),
                                [ht, ident_f], [ps])
                        for j in range(4):
                            k = half * 4 + j
                            S.op("act", lambda e, ps=ps, j=j, k=k, t=t, lc=lc: e.activation(
                                out=uT[:, k, t * 128:(t + 1) * 128], in_=ps[:, j * 128:(j + 1) * 128], func=AF.Identity,
                                bias=modT[:, k, 0, lc:lc + 1], scale=modT[:, k, 1, lc:lc + 1]),
                                [ps, modT], [uT.sub(t)])
                S.barrier()
                S.emit()
                bes.close()
                import os
                BDBG = int(os.environ.get("BDBG", 9))
                tabs1 = [C.sb(f"tab_{kd}", [128, 2, 512], es=pes) for kd in range(3)]
                tabs = [tabs1, tabs1]
                evs = [C.sb(f"ev{i}", [128, 512], BF16, es=pes) for i in range(3)]
                tmp = [C.sb(f"tmp{i}", [128, 512], es=pes) for i in range(2)]
                tmo = [C.sb(f"tmo{i}", [128, 512], BF16, es=pes) for i in range(2)]
                gabAll = C.sb("gabAll", [128, NT * 16], es=pes)
                offs = {}
                o = 0
                for name, c, kind, sw in FMG:
                    offs[name] = o
                    o += 128 * (2 if kind is not None else 1)
                for name, st, w in TM_GROUPS:
                    offs[name] = o
                    o += w
                nblk = (T + 511) // 512
                evc = 0
                for b in range(nblk if BDBG >= 2 else 0):
                    t0 = b * 512
                    tw = min(512, T - t0)
                    tiles = list(range(t0 // 128, (t0 + tw) // 128))
                    ukeys = [uT.sub(t) for t in tiles]
                    tb = tabs[b % 2]
                    for kd in range(3):
                        S.dma("act", lambda e, tb=tb, kd=kd, t0=t0, tw=tw: e.dma_start(
                            out=tb[kd][:, :, 0:tw], in_=rope[kd, :, :, t0:t0 + tw].rearrange("c p t -> p c t")),
                            [], [tb[kd]])
                    for gi, (name, c, kind, sw) in enumerate(FMG):
                        co = offs[name]
                        psa = C.nextps()
                        for k in range(8):
                            S.op("pe", lambda e, psa=psa, k=k, co=co, t0=t0, tw=tw: e.matmul(
                                psa[:, 0:tw], lhsT=wsb[:, k, co:co + 128], rhs=uT[:, k, t0:t0 + tw],
                                start=(k == 0), stop=(k == 7)), wkeys + ukeys, [psa])
                        ev = evs[evc % 3]
                        evc += 1
                        if kind is None:
                            S.op("act", lambda e, ev=ev, psa=psa, tw=tw: e.copy(out=ev[:, 0:tw], in_=psa[:, 0:tw]),
                                 [psa], [ev])
                        else:
                            psb = C.nextps()
                            for k in range(8):
                                S.op("pe", lambda e, psb=psb, k=k, co=co, t0=t0, tw=tw: e.matmul(
                                    psb[:, 0:tw], lhsT=wsb[:, k, co + 128:co + 256], rhs=uT[:, k, t0:t0 + tw],
                                    start=(k == 0), stop=(k == 7)), wkeys + ukeys, [psb])
                            ta, tbb = tmp
                            S.op("dve", lambda e, ta=ta, psa=psa, tb=tb, kind=kind, tw=tw: e.tensor_tensor(
                                out=ta[:, 0:tw], in0=psa[:, 0:tw], in1=tb[kind][:, 0, 0:tw], op=ALU.mult),
                                [psa, tb[kind]], [ta])
                            S.op("dve", lambda e, tbb=tbb, psb=psb, tb=tb, kind=kind, tw=tw: e.tensor_tensor(
                                out=tbb[:, 0:tw], in0=psb[:, 0:tw], in1=tb[kind][:, 1, 0:tw], op=ALU.mult),
                                [psb, tb[kind]], [tbb])
                            S.op("pool", lambda e, ev=ev, ta=ta, tbb=tbb, tw=tw: e.tensor_tensor(
                                out=ev[:, 0:tw], in0=ta[:, 0:tw], in1=tbb[:, 0:tw], op=ALU.add), [ta, tbb], [ev])
                        S.dma("sp", lambda e, ev=ev, gi=gi, t0=t0, tw=tw: e.dma_start(
                            out=FMO.t[gi, :, t0:t0 + tw], in_=ev[:, 0:tw]), [ev], ["FMO/%d/%d" % (gi, b)])
                    for t in (tiles if BDBG >= 3 else []):
                        for (name, st, w), dst in zip(TM_GROUPS, [TMA, TMB, TMC, TMD]):
                            co = offs[name]
                            ps = C.nextps()
                            for k in range(8):
                                S.op("pe", lambda e, ps=ps, k=k, co=co, w=w, t=t: e.matmul(
                                    ps[:, 0:w], lhsT=uT[:, k, t * 128:(t + 1) * 128], rhs=wsb[:, k, co:co + w],
                                    start=(k == 0), stop=(k == 7)), wkeys + [uT.sub(t)], [ps])
                            ob = tmo[evc % 2]
                            evc += 1
                            wd = min(w, 256) if name == "gdn_gab" else w
                            S.op("act", lambda e, ob=ob, ps=ps, wd=wd: e.copy(out=ob[:, 0:wd], in_=ps[:, 0:wd]),
                                 [ps], [ob])
                            S.dma("sp", lambda e, ob=ob, dst=dst, t=t, wd=wd: e.dma_start(
                                out=dst.t[t * 128:(t + 1) * 128, 0:wd], in_=ob[:, 0:wd]), [ob],
                                [dst.sub(t)])
                            if name == "gdn_gab" and BDBG >= 4:
                                S.op("dve", lambda e, ps=ps, t=t: e.tensor_copy(out=gabAll[:, t * 16:(t + 1) * 16], in_=ps[:, 256:272]),
                                     [ps], [gabAll])
                if BDBG >= 4:
                    S.dma("sp", lambda e: e.dma_start(out=GAB.t[:, :], in_=gabAll[:]), [gabAll], [GAB])
                S.barrier()
                S.emit()

        def phase_ret(l, full_ctx, ges):
            gq = fm_index["ret_q0"]
            gk = fm_index["ret_k0"]
            seq_f = [NLT, NLT + 1] + list(range(NLT))
            with nullcontext(ges) as pes:
                rd = C.sb("rd", [128, 8], es=pes)
                lg = C.sb("lg", [128, 8], es=pes)
                MT = C.sb("MT", [128, 4, 128], es=pes)
                mtmp = C.sb("mtmp", [128, 128], es=pes)
                mtmp2 = C.sb("mtmp2", [128, 128], es=pes)
                pcol = C.sb("pcol", [128, 4], es=pes)
                kdec = C.sb("kdec", [128, 2, 256], es=pes)
                qdec = C.sb("qdec", [128, 2, 256], es=pes)
                cdec = C.sb("cdec", [128, 2, 256], es=pes)
                dcol = C.sb("dcol", [128, 8], es=pes)
                S.dma("sp", lambda e: e.dma_start(out=rd[:], in_=ret_decay[l:l + 1, :].broadcast_to([128, 8])), [], [rd])
                S.op("act", lambda e: e.activation(out=lg[:], in_=rd[:], func=AF.Exp, scale=-1.0), [rd], [lg])
                S.op("act", lambda e: e.activation(out=lg[:], in_=lg[:], func=AF.Ln, bias=1.0), [lg], [lg])
                S.op("dve", lambda e: e.tensor_scalar_mul(out=lg[:], in0=lg[:], scalar1=-1.0), [lg], [lg])
                S.op("dve", lambda e: e.tensor_scalar(out=pcol[:, 3:4], in0=relpos[:, 0:1], scalar1=-1.0, scalar2=None,
                                                      op0=ALU.mult), [relpos], [pcol])
                S.op("dve", lambda e: e.tensor_scalar_add(out=pcol[:, 0:1], in0=pcol[:, 3:4], scalar1=1.0), [pcol], [pcol])
                S.op("dve", lambda e: e.tensor_scalar(out=pcol[:, 1:2], in0=pcol[:, 3:4], scalar1=-1.0, scalar2=128.0,
                                                      op0=ALU.mult, op1=ALU.add), [pcol], [pcol])
                S.op("dve", lambda e: e.tensor_scalar(out=pcol[:, 2:3], in0=pcol[:, 3:4], scalar1=-1.0, scalar2=127.0,
                                                      op0=ALU.mult, op1=ALU.add), [pcol], [pcol])
                for dr in range(2):
                    for h in range(4):
                        c = dr * 4 + h
                        if dr == 0:
                            S.op("dve", lambda e: e.tensor_scalar_max(out=mtmp[:], in0=relpos[:], scalar1=0.0), [relpos], [mtmp])
                        else:
                            S.op("dve", lambda e: e.tensor_scalar(out=mtmp[:], in0=relpos[:], scalar1=-1.0, scalar2=0.0,
                                                                  op0=ALU.mult, op1=ALU.max), [relpos], [mtmp])
                        S.op("act", lambda e, c=c: e.activation(out=mtmp[:], in_=mtmp[:], func=AF.Exp, scale=lg[:, c:c + 1]),
                             [mtmp, lg], [mtmp])
                        if dr == 0:
                            S.op("dve", lambda e: e.tensor_scalar(out=mtmp2[:], in0=relpos[:], scalar1=0.0, scalar2=0.125,
                                                                  op0=ALU.is_ge, op1=ALU.mult), [relpos], [mtmp2])
                            S.op("dve", lambda e, h=h: e.tensor_tensor(out=MT[:, h, :], in0=mtmp[:], in1=mtmp2[:], op=ALU.mult),
                                 [mtmp, mtmp2], [MT])
                        else:
                            S.op("dve", lambda e: e.tensor_scalar(out=mtmp2[:], in0=relpos[:], scalar1=0.0, scalar2=0.125,
                                                                  op0=ALU.is_le, op1=ALU.mult), [relpos], [mtmp2])
                            S.op("dve", lambda e: e.tensor_tensor(out=mtmp[:], in0=mtmp[:], in1=mtmp2[:], op=ALU.mult),
                                 [mtmp, mtmp2], [mtmp])
                            S.op("dve", lambda e, h=h: e.tensor_tensor(out=MT[:, h, :], in0=MT[:, h, :], in1=mtmp[:], op=ALU.add),
                                 [mtmp, MT], [MT])
                        S.op("act", lambda e, c=c, dr=dr: e.activation(out=dcol[:, 0:1], in_=pcol[:, (0 if dr == 0 else 1):(1 if dr == 0 else 2)],
                                                                        func=AF.Exp, scale=lg[:, c:c + 1]), [pcol, lg], [dcol])
                        S.op("dve", lambda e, dr=dr, h=h: e.tensor_scalar_mul(
                            out=qdec[:, dr, h * 64:(h + 1) * 64], in0=dcol[:, 0:1].broadcast_to([128, 64]), scalar1=0.125),
                            [dcol], [qdec])
                        S.op("act", lambda e, c=c, dr=dr: e.activation(out=dcol[:, 1:2], in_=pcol[:, (2 if dr == 0 else 3):(3 if dr == 0 else 4)],
                                                                        func=AF.Exp, scale=lg[:, c:c + 1]), [pcol, lg], [dcol])
                        S.op("dve", lambda e, dr=dr, h=h: e.tensor_copy(
                            out=kdec[:, dr, h * 64:(h + 1) * 64], in_=dcol[:, 1:2].broadcast_to([128, 64])), [dcol], [kdec])
                        S.op("act", lambda e, c=c: e.activation(out=dcol[:, 2:3], in_=lg[:, c:c + 1], func=AF.Exp, scale=128.0),
                             [lg], [dcol])
                        S.op("dve", lambda e, dr=dr, h=h: e.tensor_copy(
                            out=cdec[:, dr, h * 64:(h + 1) * 64], in_=dcol[:, 2:3].broadcast_to([128, 64])), [dcol], [cdec])
                import os
                RDBG = int(os.environ.get("RDBG", 9))
                Sprev = C.sb("Sprev", [64, 2, NT, 256], BF16, es=pes)
                dSb = C.sb("dSb", [64, NT, 256], es=pes)
                Srun = C.sb("Srun", [64, 2, 256], es=pes)
                kts = [C.sb(f"kT{i}", [64, 4, 128], BF16, es=pes) for i in range(2)]
                qts = [C.sb(f"qT{i}", [64, 4, 128], BF16, es=pes) for i in range(2)]
                vgs = [C.sb(f"vg{i}", [128, 512], BF16, es=pes) for i in range(2)]
                ktok = [C.sb(f"ktok{i}", [128, 256], BF16, es=pes) for i in range(2)]
                kd = [C.sb(f"kd{i}", [128, 2, 256], BF16, es=pes) for i in range(2)]
                S.op("pool", lambda e: e.memset(Srun[:], 0.0), [], [Srun])

                def load_k(c, i):
                    S.dma("sp", lambda e: e.dma_start(
                        out=kts[i][:], in_=FMO.t[gk:gk + 2, :, c * 128:(c + 1) * 128].rearrange("g (h d) t -> d (g h) t", h=2)),
                        [f"FMO/{gk}/{c // 4}", f"FMO/{gk + 1}/{c // 4}"], [kts[i]])

                def make_ktok(c, i):
                    ps = C.nextps()
                    pv = ps.t[:, 0:128].bitcast(BF16)
                    for h in range(4):
                        S.op("pe", lambda e, pv=pv, h=h: e.transpose(pv[:, h * 64:(h + 1) * 64], kts[i][:, h, :], ident_b[0:64, 0:64]),
                             [kts[i], ident_b], [ps])
                    S.op("act", lambda e, pv=pv: e.copy(out=ktok[i][:], in_=pv[:, :]), [ps], [ktok[i]])

                R2 = int(os.environ.get("R2", 9))
                for n_, c in enumerate(seq_f if RDBG >= 2 else []):
                    yield None
                    C.rot = None
                    i = n_ % 2
                    load_k(c, i)
                    S.dma("act", lambda e, c=c, i=i: e.dma_start(out=vgs[i][:], in_=TMA.t[c * 128:(c + 1) * 128, :]),
                          [TMA.sub(c)], [vgs[i]])
                    if R2 < 2:
                        continue
                    make_ktok(c, i)
                    if R2 < 3:
                        continue
                    S.op("dve", lambda e, i=i: e.tensor_tensor(
                        out=kd[i][:], in0=ktok[i][:].rearrange("p (o c) -> p o c", o=1).broadcast_to([128, 2, 256]),
                        in1=kdec[:], op=ALU.mult), [ktok[i], kdec], [kd[i]])
                    if R2 < 4:
                        continue
                    ps = C.nextps()
                    for dr in range(2):
                        for h in range(4):
                            S.op("pe", lambda e, ps=ps, dr=dr, h=h, i=i: e.matmul(
                                ps[0:64, dr * 256 + h * 64:dr * 256 + (h + 1) * 64], lhsT=kd[i][:, dr, h * 64:(h + 1) * 64],
                                rhs=vgs[i][:, h * 64:(h + 1) * 64], start=True, stop=True), [kd[i], vgs[i]], [ps])
                    if R2 < 5:
                        continue
                    S.op("act", lambda e, c=c: e.copy(out=Sprev[:, 0, c, :], in_=Srun[:, 0, :]), [Srun], [Sprev.sub(0, c)])
                    if R2 < 6:
                        continue
                    S.op("dve", lambda e: e.tensor_tensor(out=Srun[:, 0, :], in0=Srun[:, 0, :], in1=cdec[0:64, 0, :], op=ALU.mult),
                         [Srun, cdec], [Srun])
                    if R2 < 7:
                        continue
                    S.op("dve", lambda e, ps=ps: e.tensor_tensor(out=Srun[:, 0, :], in0=Srun[:, 0, :], in1=ps[0:64, 0:256], op=ALU.add),
                         [Srun, ps], [Srun])
                    if R2 < 8:
                        continue
                    S.op("dve", lambda e, ps=ps, c=c: e.tensor_copy(out=dSb[:, c, :], in_=ps[0:64, 256:512]), [ps], [dSb.sub(c)])
                seq_b = [NLT + 1, NLT] + list(range(NLT - 1, -1, -1))
                for c in (seq_b if RDBG >= 3 else []):
                    S.op("act", lambda e, c=c: e.copy(out=Sprev[:, 1, c, :], in_=Srun[:, 1, :]), [Srun], [Sprev.sub(1, c)])
                    S.op("dve", lambda e: e.tensor_tensor(out=Srun[:, 1, :], in0=Srun[:, 1, :], in1=cdec[0:64, 1, :], op=ALU.mult),
                         [Srun, cdec], [Srun])
                    S.op("dve", lambda e, c=c: e.tensor_tensor(out=Srun[:, 1, :], in0=Srun[:, 1, :], in1=dSb[:, c, :], op=ALU.add),
                         [Srun, dSb.sub(c)], [Srun])
                AMs = [C.sb(f"AM{i}", [128, 4, 128], BF16, es=pes) for i in range(2)]
                osum = C.sb("osum", [128, 256], es=pes)
                t1 = C.sb("t1", [128, 256], es=pes)
                sq = C.sb("sq", [128, 256], es=pes)
                ss = C.sb("ss", [128, 4], es=pes)
                sg = C.sb("sg", [128, 256], es=pes)
                ys = [C.sb(f"y{i}", [128, 256], BF16, es=pes) for i in range(2)]
                chunks = list(range(NT)) if full_ctx else list(range(NLT))
                if RDBG < 4:
                    chunks = []
                for n_, c in enumerate(chunks):
                    yield None
                    C.rot = None
                    i = n_ % 2
                    load_k(c, i)
                    S.dma("sp", lambda e, c=c, i=i: e.dma_start(
                        out=qts[i][:], in_=FMO.t[gq:gq + 2, :, c * 128:(c + 1) * 128].rearrange("g (h d) t -> d (g h) t", h=2)),
                        [f"FMO/{gq}/{c // 4}", f"FMO/{gq + 1}/{c // 4}"], [qts[i]])
                    S.dma("act", lambda e, c=c, i=i: e.dma_start(out=vgs[i][:], in_=TMA.t[c * 128:(c + 1) * 128, :]),
                          [TMA.sub(c)], [vgs[i]])
                    psA = C.nextps()
                    for h in range(4):
                        S.op("pe", lambda e, psA=psA, h=h, i=i: e.matmul(
                            psA[:, h * 128:(h + 1) * 128], lhsT=kts[i][:, h, :], rhs=qts[i][:, h, :], start=True, stop=True),
                            [kts[i], qts[i]], [psA])
                    S.op("dve", lambda e, psA=psA, i=i: e.tensor_tensor(
                        out=AMs[i][:].rearrange("p h t -> p (h t)"), in0=psA[:, :], in1=MT[:].rearrange("p h t -> p (h t)"),
                        op=ALU.mult), [psA, MT], [AMs[i]])
                    psO = C.nextps()
                    psX = C.nextps()
                    for h in range(4):
                        S.op("pe", lambda e, psO=psO, h=h, i=i: e.matmul(
                            psO[:, h * 64:(h + 1) * 64], lhsT=AMs[i][:, h, :], rhs=vgs[i][:, h * 64:(h + 1) * 64],
                            start=True, stop=True), [AMs[i], vgs[i]], [psO])
                        S.op("pe", lambda e, psO=psO, h=h, i=i, c=c: e.matmul(
                            psO[:, 256 + h * 64:256 + (h + 1) * 64], lhsT=qts[i][:, h, :], rhs=Sprev[:, 0, c, h * 64:(h + 1) * 64],
                            start=True, stop=True), [qts[i], Sprev.sub(0, c)], [psO])
                        S.op("pe", lambda e, psX=psX, h=h, i=i, c=c: e.matmul(
                            psX[:, h * 64:(h + 1) * 64], lhsT=qts[i][:, h, :], rhs=Sprev[:, 1, c, h * 64:(h + 1) * 64],
                            start=True, stop=True), [qts[i], Sprev.sub(1, c)], [psX])
                    S.op("dve", lambda e, psO=psO: e.tensor_tensor(out=t1[:], in0=psO[:, 256:512], in1=qdec[:, 0, :], op=ALU.mult),
                         [psO, qdec], [t1])
                    S.op("dve", lambda e, psO=psO: e.tensor_tensor(out=osum[:], in0=psO[:, 0:256], in1=t1[:], op=ALU.add),
                         [psO, t1], [osum])
                    S.op("dve", lambda e, psX=psX: e.tensor_tensor(out=t1[:], in0=psX[:, 0:256], in1=qdec[:, 1, :], op=ALU.mult),
                         [psX, qdec], [t1])
                    S.op("dve", lambda e: e.tensor_tensor(out=osum[:], in0=osum[:], in1=t1[:], op=ALU.add), [osum, t1], [osum])
                    S.op("act", lambda e: e.activation(out=sq[:], in_=osum[:], func=AF.Square), [osum], [sq])
                    S.op("dve", lambda e: e.tensor_reduce(out=ss[:], in_=sq[:].rearrange("p (h d) -> p h d", h=4), axis=AX.X, op=ALU.add),
                         [sq], [ss])
                    S.op("dve", lambda e: e.tensor_scalar(out=ss[:], in0=ss[:], scalar1=1.0 / 64, scalar2=1e-6, op0=ALU.mult, op1=ALU.add),
                         [ss], [ss])
                    S.op("act", lambda e: e.activation(out=ss[:], in_=ss[:], func=AF.Sqrt), [ss], [ss])
                    S.op("dve", lambda e: e.reciprocal(out=ss[:], in_=ss[:]), [ss], [ss])
                    S.op("act", lambda e, i=i: e.activation(out=sg[:], in_=vgs[i][:, 256:512], func=AF.Silu), [vgs[i]], [sg])
                    S.op("dve", lambda e: e.tensor_tensor(
                        out=osum[:].rearrange("p (h d) -> p h d", h=4), in0=osum[:].rearrange("p (h d) -> p h d", h=4),
                        in1=ss[:].rearrange("p (h o) -> p h o", o=1).broadcast_to([128, 4, 64]), op=ALU.mult), [osum, ss], [osum])
                    S.op("dve", lambda e, i=i: e.tensor_tensor(out=ys[i][:], in0=osum[:], in1=sg[:], op=ALU.mult), [osum, sg], [ys[i]])
                    S.dma("sp", lambda e, i=i, c=c: e.dma_start(out=Y.t[c * 128:(c + 1) * 128, 0:256], in_=ys[i][:]),
                          [ys[i]], [Y.sub(c, 0)])
                yield "END"

        def phase_diff(l, full_ctx):
            gq = fm_index["diff_q0"]
            gk = fm_index["diff_k0"]
            lam_init = 0.8 - 0.6 * math.exp(-0.3 * l)
            scale = 32 ** -0.5
            with ExitStack() as pes:
                C.rot = [0, 1, 2]
                kT8 = C.sb("kT8", [32, 8, T], BF16, es=pes)
                V1 = C.sb("V1", [128, NT, 4, 65], BF16, es=pes)
                qT8s = [C.sb(f"qT8_{i}", [32, 8, 512], BF16, es=pes) for i in range(2)]
                Es = [C.sb(f"E{i}", [128, 512], BF16, es=pes) for i in range(3)]
                dl = C.sb("dl", [128, 128], es=pes)
                dl2 = C.sb("dl2", [128, 2], es=pes)
                nlam = C.sb("nlam", [128, 1], es=pes)
                gn = C.sb("gn", [128, 64], es=pes)
                od = C.sb("od", [128, 4, 256], es=pes)
                oT = [C.sb(f"oT{i}", [65, 512], es=pes) for i in range(2)]
                rr = C.sb("rr", [128, 4], es=pes)
                a_ = C.sb("a_", [128, 64], es=pes)
                sq = C.sb("dsq", [128, 4, 256], es=pes)
                ss = C.sb("dss", [128, 16], es=pes)
                yo = [C.sb(f"dy{i}", [128, 4, 256], BF16, es=pes) for i in range(2)]
                S.dma("sp", lambda e: e.dma_start(out=dl[:], in_=diff_lambda[l:l + 1, :].broadcast_to([128, 128])), [], [dl])
                S.dma("sp", lambda e: e.dma_start(out=gn[:], in_=diff_norm[l:l + 1, :].broadcast_to([128, 64])), [], [gn])
                dl4 = dl[:].rearrange("p (a b d) -> p a b d", a=2, b=2)
                S.op("dve", lambda e: e.tensor_tensor(out=dl4[:, :, 0, :], in0=dl4[:, :, 0, :], in1=dl4[:, :, 1, :], op=ALU.mult), [dl], [dl])
                S.op("dve", lambda e: e.tensor_reduce(out=dl2[:], in_=dl4[:, :, 0, :], axis=AX.X, op=ALU.add), [dl], [dl2])
                S.op("act", lambda e: e.activation(out=dl2[:], in_=dl2[:], func=AF.Exp), [dl2], [dl2])
                S.op("dve", lambda e: e.tensor_tensor(out=nlam[:], in0=dl2[:, 1:2], in1=dl2[:, 0:1], op=ALU.subtract), [dl2], [nlam])
                S.op("dve", lambda e: e.tensor_scalar_add(out=nlam[:], in0=nlam[:], scalar1=-lam_init), [nlam], [nlam])
                S.op("dve", lambda e: e.tensor_scalar_mul(out=gn[:], in0=gn[:], scalar1=(1.0 - lam_init)), [gn], [gn])
                S.dma("sp", lambda e: e.dma_start(out=kT8[:], in_=FMO.t[gk:gk + 2, :, :].rearrange("g (x d) t -> d (g x) t", d=32)),
                      [], [kT8])
                S.op("pool", lambda e: e.memset(V1[:], 1.0), [], [V1])
                for h in range(4):
                    for n0 in range(0, NT, 8):
                        n1 = min(NT, n0 + 8)
                        S.dma("act", lambda e, h=h, n0=n0, n1=n1: e.dma_start(
                            out=V1[:, n0:n1, h, 0:64],
                            in_=TMB.t[n0 * 128:n1 * 128, h * 64:(h + 1) * 64].rearrange("(n p) e -> p n e", p=128)), [V1], [V1])
                blocks = [(b * 512, 512, list(range(NT))) for b in range(NLAT // 512)]
                if full_ctx:
                    blocks.append((NLAT, 256, [NLT, NLT + 1]))
                ec = 0
                for bi, (q0, qw, ktiles) in enumerate(blocks):
                    qT8 = qT8s[bi % 2]
                    nqs = qw // 128
                    S.dma("sp", lambda e, qT8=qT8, q0=q0, qw=qw: e.dma_start(
                        out=qT8[:, :, 0:qw], in_=FMO.t[gq:gq + 2, :, q0:q0 + qw].rearrange("g (x d) t -> d (g x) t", d=32)),
                        [], [qT8])
                    LA = 2
                    steps = [(h, ki, kt, c) for h in range(4) for ki, kt in enumerate(ktiles) for c in range(2)]
                    pend = {}

                    def emit_S(si, qT8=qT8, qw=qw):
                        h, ki, kt, c = steps[si]
                        hc = h * 2 + c
                        pss = C.nextps()
                        S.op("pe", lambda e, pss=pss, hc=hc, kt=kt, qT8=qT8, qw=qw: e.matmul(
                            pss[:, 0:qw], lhsT=kT8[:, hc, kt * 128:(kt + 1) * 128], rhs=qT8[:, hc, 0:qw],
                            start=True, stop=True), [kT8, qT8], [pss])
                        pend[si] = pss

                    def emit_rest(si, qw=qw, nqs=nqs, nk=len(ktiles)):
                        nonlocal ec
                        h, ki, kt, c = steps[si]
                        acc = [C.ps[4 + (h % 2) * 2], C.ps[5 + (h % 2) * 2]]
                        pss = pend.pop(si)
                        E = Es[ec % 3]
                        ec += 1
                        S.op("act", lambda e, E=E, pss=pss, qw=qw: e.activation(
                            out=E[:, 0:qw], in_=pss[:, 0:qw], func=AF.Exp, scale=scale), [pss], [E])
                        S.op("pe", lambda e, E=E, c=c, kt=kt, h=h, ki=ki, qw=qw, nk=nk, ac=acc[c]: e.matmul(
                            ac[0:65, 0:qw], lhsT=V1[:, kt, h, :], rhs=E[:, 0:qw],
                            start=(ki == 0), stop=(ki == nk - 1)), [E, V1], [acc[c]])
                        if not (ki == nk - 1 and c == 1):
                            return
                        pt = C.ps[3]
                        for c_ in range(2):
                            S.op("dve", lambda e, c_=c_, qw=qw, ac=acc[c_]: e.tensor_copy(out=oT[c_][:, 0:qw], in_=ac[0:65, 0:qw]), [acc[c_]], [oT[c_]])
                            for qs in range(nqs):
                                S.op("pe", lambda e, pt=pt, c_=c_, qs=qs: e.transpose(
                                    pt[:, qs * 65:(qs + 1) * 65], oT[c_][:, qs * 128:(qs + 1) * 128], ident_f[0:65, 0:65]),
                                    [oT[c_], ident_f], [pt])
                            S.op("dve", lambda e, pt=pt, nqs=nqs: e.reciprocal(out=rr[:, 0:nqs], in_=pt[:, 0:nqs * 65].rearrange("p (q e) -> p q e", e=65)[:, :, 64]),
                                 [pt, od], [rr])
                            if c_ == 0:
                                for qs in range(nqs):
                                    S.op("dve", lambda e, qs=qs, h=h, pt=pt: e.tensor_scalar(
                                        out=od[:, qs, h * 64:(h + 1) * 64], in0=pt[:, qs * 65:qs * 65 + 64], scalar1=rr[:, qs:qs + 1],
                                        scalar2=None, op0=ALU.mult), [pt, rr], [od])
                            else:
                                S.op("dve", lambda e, nqs=nqs: e.tensor_scalar(out=rr[:, 0:nqs], in0=rr[:, 0:nqs], scalar1=nlam[:, 0:1], scalar2=None,
                                                                      op0=ALU.mult), [rr, nlam], [rr])
                                for qs in range(nqs):
                                    S.op("dve", lambda e, qs=qs, h=h, pt=pt: e.scalar_tensor_tensor(
                                        out=od[:, qs, h * 64:(h + 1) * 64], in0=pt[:, qs * 65:qs * 65 + 64], scalar=rr[:, qs:qs + 1],
                                        in1=od[:, qs, h * 64:(h + 1) * 64], op0=ALU.mult, op1=ALU.add), [pt, rr, od], [od])

                    for si in range(len(steps) + LA):
                        if si < len(steps):
                            emit_S(si)
                        if si >= LA:
                            emit_rest(si - LA)
                    y = yo[bi % 2]
                    S.op("act", lambda e, nqs=nqs: e.activation(out=sq[:, 0:nqs, :], in_=od[:, 0:nqs, :], func=AF.Square), [od], [sq])
                    S.op("dve", lambda e, nqs=nqs: e.tensor_reduce(out=ss[:, 0:nqs * 4], in_=sq[:, 0:nqs, :].rearrange("p q (h d) -> p (q h) d", h=4),
                                                          axis=AX.X, op=ALU.add), [sq], [ss])
                    S.op("dve", lambda e, nqs=nqs: e.tensor_scalar(out=ss[:], in0=ss[:], scalar1=1.0 / 64, scalar2=1e-6, op0=ALU.mult, op1=ALU.add),
                         [ss], [ss])
                    S.op("act", lambda e, nqs=nqs: e.activation(out=ss[:], in_=ss[:], func=AF.Sqrt), [ss], [ss])
                    S.op("dve", lambda e, nqs=nqs: e.reciprocal(out=ss[:], in_=ss[:]), [ss], [ss])
                    S.op("dve", lambda e, nqs=nqs: e.tensor_tensor(
                        out=od[:, 0:nqs, :].rearrange("p q (h d) -> p (q h) d", h=4), in0=od[:, 0:nqs, :].rearrange("p q (h d) -> p (q h) d", h=4),
                        in1=ss[:, 0:nqs * 4].rearrange("p (x o) -> p x o", o=1).broadcast_to([128, nqs * 4, 64]), op=ALU.mult), [od, ss], [od])
                    S.op("dve", lambda e, y=y, nqs=nqs: e.tensor_tensor(
                        out=y[:, 0:nqs, :].rearrange("p q (h d) -> p (q h) d", h=4), in0=od[:, 0:nqs, :].rearrange("p q (h d) -> p (q h) d", h=4),
                        in1=gn[:].rearrange("p (o d) -> p o d", o=1).broadcast_to([128, nqs * 4, 64]), op=ALU.mult), [od, gn], [y])
                    S.dma("sp", lambda e, y=y, q0=q0, qw=qw, nqs=nqs: e.dma_start(
                        out=Y.t[q0:q0 + qw, 256:512].rearrange("(q p) c -> p q c", p=128), in_=y[:, 0:nqs, :]), [y], [Y.sub(bi, 1)])
                C.rot = None
                S.barrier()
                S.emit()

        def phase_swa(l, full_ctx, ges):
            gq = fm_index["swa_q0"]
            gk = fm_index["swa_k0"]
            scale = 0.125
            with nullcontext(ges) as pes:
                C.rot = [0, 1, 2, 3]
                skT = C.sb("skT", [64, 2, T], BF16, es=pes)
                sqT = C.sb("sqT", [64, 4, T], BF16, es=pes)
                V1 = C.sb("sV1", [128, NT, 2, 65], BF16, es=pes)
                nm = C.sb("nm", [128, 2, 2, 128], BF16, es=pes)
                Es = [C.sb(f"sE{i}", [128, 256], BF16, es=pes) for i in range(3)]
                oT = C.sb("soT", [65, 512], es=pes)
                es_ = C.sb("esink", [128, 4], es=pes)
                den = C.sb("den", [128, 4], es=pes)
                ys = [C.sb(f"sy{i}", [128, 256], BF16, es=pes) for i in range(2)]
                S.dma("sp", lambda e: e.dma_start(out=es_[:], in_=swa_sink[l:l + 1, :].broadcast_to([128, 4])), [], [es_])
                S.op("act", lambda e: e.activation(out=es_[:], in_=es_[:], func=AF.Exp), [es_], [es_])
                for r in range(2):
                    S.op("dve", lambda e, r=r: e.tensor_scalar(out=nm[:, 0, r, :], in0=relpos[:], scalar1=0.0, scalar2=NEG,
                                                               op0=ALU.is_gt, op1=ALU.mult), [relpos], [nm])
                    S.op("dve", lambda e, r=r: e.tensor_scalar(out=nm[:, 1, r, :], in0=relpos[:], scalar1=0.0, scalar2=NEG,
                                                               op0=ALU.is_lt, op1=ALU.mult), [relpos], [nm])
                S.dma("sp", lambda e: e.dma_start(out=skT[:], in_=FMO.t[gk, :, :].rearrange("(h d) t -> d h t", h=2)), [], [skT])
                S.dma("sp", lambda e: e.dma_start(out=sqT[:], in_=FMO.t[gq:gq + 2, :, :].rearrange("g (h d) t -> d (g h) t", h=2)), [], [sqT])
                S.op("pool", lambda e: e.memset(V1[:], 1.0), [], [V1])
                for g in range(2):
                    for n0 in range(0, NT, 8):
                        n1 = min(NT, n0 + 8)
                        S.dma("act", lambda e, g=g, n0=n0, n1=n1: e.dma_start(
                            out=V1[:, n0:n1, g, 0:64],
                            in_=TMD.t[n0 * 128:n1 * 128, g * 64:(g + 1) * 64].rearrange("(n p) e -> p n e", p=128)), [V1], [V1])
                tiles = list(range(NT)) if full_ctx else list(range(NLT))
                ec = 0
                for ti, t in enumerate(tiles):
                    yield None
                    C.rot = [0, 1, 2, 3]
                    if t < NLT:
                        keys = []
                        if t > 0:
                            keys.append((t - 1, 0))
                        keys.append((t, None))
                        if t < NLT - 1:
                            keys.append((t + 1, 1))
                        keys += [(NLT, None), (NLT + 1, None)]
                    else:
                        keys = [(NLT, None), (NLT + 1, None)]
                    accs = [C.ps[4 + (ti % 2) * 2], C.ps[5 + (ti % 2) * 2]]
                    for g in range(2):
                        for ki, (kt, mk) in enumerate(keys):
                            pss = C.nextps()
                            S.op("pe", lambda e, pss=pss, g=g, kt=kt, t=t, mk=mk: e.matmul(
                                pss[:, 0:256], lhsT=skT[:, g, kt * 128:(kt + 1) * 128], rhs=sqT[:, 2 * g:2 * g + 2, t * 128:(t + 1) * 128],
                                start=True, stop=(mk is None)), [skT, sqT], [pss])
                            if mk is not None:
                                S.op("pe", lambda e, pss=pss, mk=mk: e.matmul(
                                    pss[:, 0:256], lhsT=ident_b[:], rhs=nm[:, mk, :, :], start=False, stop=True), [ident_b, nm], [pss])
                            E = Es[ec % 3]
                            ec += 1
                            S.op("act", lambda e, E=E, pss=pss: e.activation(out=E[:], in_=pss[:, 0:256], func=AF.Exp, scale=scale),
                                 [pss], [E])
                            S.op("pe", lambda e, E=E, g=g, kt=kt, ki=ki, nk=len(keys), ac=accs[g]: e.matmul(
                                ac[0:65, 0:256], lhsT=V1[:, kt, g, :], rhs=E[:], start=(ki == 0), stop=(ki == nk - 1)),
                                [E, V1], [accs[g]])
                        S.op("dve", lambda e, g=g, ac=accs[g]: e.tensor_copy(out=oT[:, g * 256:(g + 1) * 256], in_=ac[0:65, 0:256]),
                             [accs[g]], [oT])
                    pt = C.nextps()
                    for h in range(4):
                        S.op("pe", lambda e, pt=pt, h=h: e.transpose(pt[:, h * 65:(h + 1) * 65], oT[:, h * 128:(h + 1) * 128],
                                                                     ident_f[0:65, 0:65]), [oT, ident_f], [pt])
                    S.op("dve", lambda e, pt=pt: e.tensor_tensor(
                        out=den[:], in0=pt[:, 0:260].rearrange("p (h e) -> p h e", e=65)[:, :, 64], in1=es_[:], op=ALU.add),
                        [pt, es_], [den])
                    S.op("dve", lambda e: e.reciprocal(out=den[:], in_=den[:]), [den], [den])
                    y = ys[ti % 2]
                    S.op("dve", lambda e, pt=pt, y=y: e.tensor_tensor(
                        out=y[:].rearrange("p (h d) -> p h d", h=4), in0=pt[:, 0:260].rearrange("p (h e) -> p h e", e=65)[:, :, 0:64],
                        in1=den[:].rearrange("p (h o) -> p h o", o=1).broadcast_to([128, 4, 64]), op=ALU.mult), [pt, den], [y])
                    S.dma("sp", lambda e, y=y, t=t: e.dma_start(out=Y.t[t * 128:(t + 1) * 128, 768:1024], in_=y[:]), [y], [Y.sub(t, 3)])
                C.rot = None
                yield "END"

        def phase_gdn1(l, full_ctx, ges):
            g0 = fm_index["gdn0"]
            BIG = 30000.0
            with nullcontext(ges) as pes:
                cw = C.sb("cw", [128, 18], es=pes)
                bones = C.sb("bones", [128, 128], es=pes)
                xb = [C.sb(f"xb{i}", [128, 514], BF16, es=pes) for i in range(2)]
                yc = C.sb("yc", [128, 512], es=pes)
                ysl = C.sb("ysl", [128, 512], es=pes)
                sq = C.sb("gsq", [128, 512], es=pes)
                rs = C.sb("grs", [128, 512], es=pes)
                ynb1 = C.sb("ynb0", [128, 512], BF16, es=pes)
                ytm1 = C.sb("ytm0", [128, 4, 128], BF16, es=pes)
                ynb = [ynb1, ynb1]
                ytm = [ytm1, ytm1]
                S.dma("sp", lambda e: e.dma_start(out=cw[:], in_=gdn_convT[l, :, :]), [], [cw])
                S.op("pool", lambda e: e.memset(bones[:], 0.0), [], [bones])
                S.op("pool", lambda e: e.memset(bones[0:64, 0:64], 1.0), [bones], [bones])
                S.op("pool", lambda e: e.memset(bones[64:128, 64:128], 1.0), [bones], [bones])
                seqs = [(0, NLAT)] + [(NLAT, T)]
                bi = 0
                for gi in range(6):
                    for (s0, s1) in seqs:
                        for t0 in range(s0, s1, 512):
                            yield None
                            C.rot = None
                            tw = min(512, s1 - t0)
                            x = xb[bi % 2]
                            lo = max(t0 - 1, s0)
                            hi = min(t0 + tw + 1, s1)
                            if lo == t0 or hi == t0 + tw:
                                S.op("pool", lambda e, x=x: e.memset(x[:], 0.0), [], [x])
                            S.dma("sp", lambda e, x=x, gi=gi, lo=lo, hi=hi, t0=t0: e.dma_start(
                                out=x[:, lo - (t0 - 1):hi - (t0 - 1)], in_=FMO.t[g0 + gi, :, lo:hi]), [x], [x])
                            S.op("dve", lambda e, x=x, gi=gi, tw=tw: e.tensor_scalar(
                                out=yc[:, 0:tw], in0=x[:, 1:1 + tw], scalar1=cw[:, gi * 3 + 1:gi * 3 + 2], scalar2=None, op0=ALU.mult),
                                [x, cw], [yc])
                            S.op("dve", lambda e, x=x, gi=gi, tw=tw: e.scalar_tensor_tensor(
                                out=yc[:, 0:tw], in0=x[:, 0:tw], scalar=cw[:, gi * 3:gi * 3 + 1], in1=yc[:, 0:tw],
                                op0=ALU.mult, op1=ALU.add), [x, cw, yc], [yc])
                            S.op("dve", lambda e, x=x, gi=gi, tw=tw: e.scalar_tensor_tensor(
                                out=yc[:, 0:tw], in0=x[:, 2:2 + tw], scalar=cw[:, gi * 3 + 2:gi * 3 + 3], in1=yc[:, 0:tw],
                                op0=ALU.mult, op1=ALU.add), [x, cw, yc], [yc])
                            S.op("act", lambda e, tw=tw: e.activation(out=ysl[:, 0:tw], in_=yc[:, 0:tw], func=AF.Silu), [yc], [ysl])
                            yn = ynb[bi % 2]
                            if gi < 4:
                                S.op("act", lambda e, tw=tw: e.activation(out=sq[:, 0:tw], in_=ysl[:, 0:tw], func=AF.Square), [ysl], [sq])
                                ps = C.nextps()
                                S.op("pe", lambda e, ps=ps, tw=tw: e.matmul(ps[:, 0:tw], lhsT=bones[:], rhs=sq[:, 0:tw], start=True, stop=True),
                                     [bones, sq], [ps])
                                S.op("dve", lambda e, ps=ps, tw=tw: e.tensor_scalar_add(out=rs[:, 0:tw], in0=ps[:, 0:tw], scalar1=1e-6), [ps], [rs])
                                S.op("act", lambda e, tw=tw: e.activation(out=rs[:, 0:tw], in_=rs[:, 0:tw], func=AF.Sqrt), [rs], [rs])
                                S.op("dve", lambda e, tw=tw: e.reciprocal(out=rs[:, 0:tw], in_=rs[:, 0:tw]), [rs], [rs])
                                if gi < 2:
                                    S.op("dve", lambda e, tw=tw, yn=yn: e.scalar_tensor_tensor(
                                        out=yn[:, 0:tw], in0=ysl[:, 0:tw], scalar=0.125, in1=rs[:, 0:tw], op0=ALU.mult, op1=ALU.mult),
                                        [ysl, rs], [yn])
                                else:
                                    S.op("dve", lambda e, tw=tw, yn=yn: e.tensor_tensor(out=yn[:, 0:tw], in0=ysl[:, 0:tw], in1=rs[:, 0:tw], op=ALU.mult),
                                         [ysl, rs], [yn])
                                S.dma("act", lambda e, yn=yn, gi=gi, t0=t0, tw=tw: e.dma_start(out=GFM.t[gi, :, t0:t0 + tw], in_=yn[:, 0:tw]),
                                      [yn], [GFM.sub(gi, t0)])
                            else:
                                S.op("dve", lambda e, tw=tw, yn=yn: e.tensor_copy(out=yn[:, 0:tw], in_=ysl[:, 0:tw]), [ysl], [yn])
                            if gi >= 2:
                                ps = C.nextps()
                                pv = ps.t[:, 0:256].bitcast(BF16)
                                nq = tw // 128
                                for q in range(nq):
                                    S.op("pe", lambda e, pv=pv, q=q, yn=yn: e.transpose(pv[:, q * 128:(q + 1) * 128], yn[:, q * 128:(q + 1) * 128], ident_b[:]),
                                         [yn, ident_b], [ps])
                                yt = ytm[bi % 2]
                                S.op("act", lambda e, pv=pv, yt=yt, nq=nq: e.copy(out=yt[:, 0:nq, :].rearrange("p q c -> p (q c)"), in_=pv[:, 0:nq * 128]),
                                     [ps], [yt])
                                S.dma("act", lambda e, yt=yt, gi=gi, t0=t0, tw=tw, nq=nq: e.dma_start(
                                    out=GTM.t[t0:t0 + tw, (gi - 2) * 128:(gi - 1) * 128].rearrange("(q p) c -> p q c", p=128), in_=yt[:, 0:nq, :]),
                                    [yt], [GTM.sub(gi, t0)])
                            bi += 1
                yield "END"

        def phase_gdn23(l, full_ctx, ges):
            g0 = fm_index["gdn0"]
            BIG = 30000.0
            import os
            GD = int(os.environ.get("GD", 9))
            if GD < 2:
                yield "END"
                return
            with nullcontext(ges) as pes:
                gab = C.sb("gab", [128, NT, 16], es=pes)
                par = C.sb("gpar", [128, 16], es=pes)
                la = C.sb("la", [128, NT, 8], es=pes)
                nbeta = C.sb("nbeta", [128, NT, 8], es=pes)
                gg = C.sb("gg", [128, NT, 8], es=pes)
                gt = C.sb("gt", [128, NT, 8], es=pes)
                eg = C.sb("eg", [128, NT, 8], es=pes)
                ekt = C.sb("ekt", [128, NT, 8], es=pes)
                cd = C.sb("cd", [128, NT, 8], es=pes)
                beg = C.sb("beg", [128, NT, 8], es=pes)
                tri = C.sb("tri", [128, 2, 128], es=pes)
                onesf = C.sb("onesf", [128, 128], es=pes)
                nonesf = C.sb("nonesf", [128, 128], es=pes)
                mD = C.sb("mD", [128, 2, 4, 128], es=pes)
                mDT = C.sb("mDT", [128, 2, 4, 128], es=pes)
                gnb = C.sb("gnb", [128, 64], es=pes)
                S.dma("sp", lambda e: e.dma_start(out=gab[:].rearrange("p n c -> p (n c)"), in_=GAB.t[:, :]), [], [gab])
                S.dma("sp", lambda e: e.dma_start(out=par[:, 0:8], in_=gdn_a_log[l:l + 1, :].broadcast_to([128, 8])), [], [par])
                S.dma("sp", lambda e: e.dma_start(out=par[:, 8:16], in_=gdn_dt_bias[l:l + 1, :].broadcast_to([128, 8])), [par], [par])
                S.dma("sp", lambda e: e.dma_start(out=gnb[:], in_=gdn_norm[l:l + 1, :].broadcast_to([128, 64])), [], [gnb])
                S.op("pool", lambda e: e.memset(onesf[:], 1.0), [], [onesf])
                S.op("pool", lambda e: e.memset(nonesf[:], -1.0), [], [nonesf])
                S.op("dve", lambda e: e.tensor_single_scalar(out=tri[:, 0, :], in_=relpos[:], scalar=0.0, op=ALU.is_ge), [relpos], [tri])
                S.op("dve", lambda e: e.tensor_single_scalar(out=tri[:, 1, :], in_=relpos[:], scalar=0.0, op=ALU.is_le), [relpos], [tri])
                for h in range(4):
                    S.op("dve", lambda e, h=h: e.tensor_scalar(out=mD[:, 0, h, :], in0=relpos[:], scalar1=0.0, scalar2=BIG, op0=ALU.is_ge, op1=ALU.mult), [relpos], [mD])
                    S.op("dve", lambda e, h=h: e.tensor_scalar(out=mD[:, 1, h, :], in0=relpos[:], scalar1=0.0, scalar2=BIG, op0=ALU.is_le, op1=ALU.mult), [relpos], [mD])
                    S.op("dve", lambda e, h=h: e.tensor_scalar(out=mDT[:, 0, h, :], in0=relpos[:], scalar1=0.0, scalar2=-BIG, op0=ALU.is_lt, op1=ALU.mult), [relpos], [mDT])
                    S.op("dve", lambda e, h=h: e.tensor_scalar(out=mDT[:, 1, h, :], in0=relpos[:], scalar1=0.0, scalar2=-BIG, op0=ALU.is_gt, op1=ALU.mult), [relpos], [mDT])
                S.op("act", lambda e: e.activation(out=par[:, 0:8], in_=par[:, 0:8], func=AF.Exp), [par], [par])
                S.op("dve", lambda e: e.tensor_tensor(out=la[:], in0=gab[:, :, 0:8],
                                                      in1=par[:, 8:16].rearrange("p (o c) -> p o c", o=1).broadcast_to([128, NT, 8]), op=ALU.add),
                     [gab, par], [la])
                S.op("act", lambda e: e.activation(out=la[:], in_=la[:], func=AF.Exp), [la], [la])
                S.op("act", lambda e: e.activation(out=la[:], in_=la[:], func=AF.Ln, bias=1.0), [la], [la])
                S.op("dve", lambda e: e.scalar_tensor_tensor(
                    out=la[:], in0=la[:], scalar=-1.0, in1=par[:, 0:8].rearrange("p (o c) -> p o c", o=1).broadcast_to([128, NT, 8]),
                    op0=ALU.mult, op1=ALU.mult), [la, par], [la])
                S.op("act", lambda e: e.activation(out=nbeta[:], in_=gab[:, :, 8:16], func=AF.Sigmoid), [gab], [nbeta])
                for r in range(2):
                    ps = C.nextps()
                    S.op("pe", lambda e, ps=ps, r=r: e.matmul(ps[:, 0:NT * 4], lhsT=tri[:, r, :], rhs=la[:, :, r * 4:(r + 1) * 4], start=True, stop=True),
                         [tri, la], [ps])
                    S.op("dve", lambda e, ps=ps, r=r: e.tensor_copy(out=gg[:, :, r * 4:(r + 1) * 4], in_=ps[:, 0:NT * 4].rearrange("p (n c) -> p n c", c=4)),
                         [ps], [gg])
                ps = C.nextps()
                S.op("pe", lambda e, ps=ps: e.matmul(ps[:, 0:NT * 8], lhsT=onesf[:], rhs=la[:].rearrange("p n c -> p (n c)"), start=True, stop=True),
                     [onesf, la], [ps])
                S.op("dve", lambda e, ps=ps: e.tensor_copy(out=gt[:].rearrange("p n c -> p (n c)"), in_=ps[:, 0:NT * 8]), [ps], [gt])
                S.op("act", lambda e: e.activation(out=eg[:], in_=gg[:], func=AF.Exp), [gg], [eg])
                S.op("act", lambda e: e.activation(out=cd[:], in_=gt[:], func=AF.Exp), [gt], [cd])
                S.op("dve", lambda e: e.tensor_tensor(out=ekt[:], in0=gt[:], in1=gg[:], op=ALU.subtract), [gt, gg], [ekt])
                S.op("act", lambda e: e.activation(out=ekt[:], in_=ekt[:], func=AF.Exp), [ekt], [ekt])
                S.op("dve", lambda e: e.tensor_tensor(out=beg[:], in0=nbeta[:], in1=eg[:], op=ALU.mult), [nbeta, eg], [beg])
                S.op("dve", lambda e: e.tensor_scalar_mul(out=nbeta[:], in0=nbeta[:], scalar1=-1.0), [nbeta], [nbeta])
                qT = [C.sb(f"gqT{i}", [64, 4, 128], BF16, es=pes) for i in range(2)]
                kT = [C.sb(f"gkT{i}", [64, 4, 128], BF16, es=pes) for i in range(2)]
                kv = [C.sb(f"gkv{i}", [128, 512], BF16, es=pes) for i in range(2)]
                Rm = C.sb("Rm", [128, 4, 128], es=pes)
                Ds = C.sb("Ds", [128, 4, 128], es=pes)
                DT = C.sb("DT", [128, 4, 128], es=pes)
                X = [C.sb(f"X{i}", [128, 4, 2, 128], es=pes) for i in range(2)]
                P = C.sb("P", [128, 4, 128], es=pes)
                ident_r = ident_f
                Pb = C.sb("Pb", [128, 4, 128], BF16, es=pes)
                qkT = C.sb("qkT", [128, 4, 128], BF16, es=pes)
                kg = C.sb("kg", [128, 256], BF16, es=pes)
                vb = C.sb("vb", [128, 256], BF16, es=pes)
                ktl = C.sb("ktl", [128, 256], BF16, es=pes)
                wT = C.sb("wT", [64, 4, 128], BF16, es=pes)
                ub = C.sb("ub", [128, 256], es=pes)
                u = C.sb("u", [128, 256], BF16, es=pes)
                Sf = C.sb("Sf", [64, 256], es=pes)
                Sb16 = C.sb("Sb16", [64, 256], BF16, es=pes)
                Oacc = C.sb("Oacc", [128, NT, 256], es=pes)
                ocr = C.sb("ocr", [128, 256], es=pes)
                gsq = C.sb("gosq", [128, 256], es=pes)
                gss = C.sb("goss", [128, 4], es=pes)
                gsg = C.sb("gosg", [128, 256], es=pes)
                ggate = [C.sb(f"ggate{i}", [128, 256], BF16, es=pes) for i in range(2)]
                gy = [C.sb(f"gy{i}", [128, 256], BF16, es=pes) for i in range(2)]
                seq = {0: [NLT, NLT + 1] + list(range(NLT)), 1: [NLT + 1, NLT] + list(range(NLT - 1, -1, -1))}
                step = 0
                for r in range(2 if GD >= 3 else 0):
                    S.op("pool", lambda e: e.memset(Sf[:], 0.0), [], [Sf])
                    S.op("pool", lambda e: e.memset(Sb16[:], 0.0), [], [Sb16])
                    for c in seq[r]:
                        yield None
                        C.rot = None
                        i = step % 2
                        step += 1
                        rc = slice(r * 4, r * 4 + 4)
                        S.dma("sp", lambda e, i=i, c=c: e.dma_start(
                            out=qT[i][:], in_=GFM.t[0:2, :, c * 128:(c + 1) * 128].rearrange("g (h d) t -> d (g h) t", h=2)), [], [qT[i]])
                        S.dma("sp", lambda e, i=i, c=c: e.dma_start(
                            out=kT[i][:], in_=GFM.t[2:4, :, c * 128:(c + 1) * 128].rearrange("g (h d) t -> d (g h) t", h=2)), [], [kT[i]])
                        S.dma("act", lambda e, i=i, c=c: e.dma_start(out=kv[i][:], in_=GTM.t[c * 128:(c + 1) * 128, :]), [], [kv[i]])
                        pKK = C.nextps()
                        pQK = C.nextps()
                        for h in range(4):
                            S.op("pe", lambda e, pKK=pKK, h=h, i=i: e.matmul(pKK[:, h * 128:(h + 1) * 128], lhsT=kT[i][:, h, :], rhs=kT[i][:, h, :],
                                                                              start=True, stop=True), [kT[i]], [pKK])
                        for h in range(4):
                            S.op("pe", lambda e, pQK=pQK, h=h, i=i: e.matmul(pQK[:, h * 128:(h + 1) * 128], lhsT=kT[i][:, h, :], rhs=qT[i][:, h, :],
                                                                              start=True, stop=True), [kT[i], qT[i]], [pQK])
                        for h in range(4):
                            S.op("dve", lambda e, h=h, c=c, r=r: e.tensor_scalar(
                                out=Rm[:, h, :], in0=tri[:, r, :], scalar1=la[:, c, r * 4 + h:r * 4 + h + 1], scalar2=None, op0=ALU.mult),
                                [tri, la], [Rm])
                        pD = C.nextps()
                        pDT = C.nextps()
                        for (pp, mk) in ((pD, mD), (pDT, mDT)):
                            S.op("pe", lambda e, pp=pp, mk=mk, r=r: e.matmul(pp[:, :], lhsT=ident_f[:], rhs=mk[:, r, :, :].rearrange("p h t -> p (h t)"),
                                                                             start=True, stop=False), [ident_f, mk], [pp])
                            for h in range(4):
                                S.op("pe", lambda e, pp=pp, h=h: e.matmul(pp[:, h * 128:(h + 1) * 128], lhsT=onesf[:], rhs=Rm[:, h, :],
                                                                          start=False, stop=False), [onesf, Rm], [pp])
                                S.op("pe", lambda e, pp=pp, h=h: e.matmul(pp[:, h * 128:(h + 1) * 128], lhsT=Rm[:, h, :], rhs=nonesf[:],
                                                                          start=False, stop=(h == 3)), [nonesf, Rm], [pp])
                        S.op("act", lambda e, pD=pD: e.activation(out=Ds[:].rearrange("p h t -> p (h t)"), in_=pD[:, :], func=AF.Exp, scale=-1.0),
                             [pD], [Ds])
                        S.op("act", lambda e, pDT=pDT: e.activation(out=DT[:].rearrange("p h t -> p (h t)"), in_=pDT[:, :], func=AF.Exp), [pDT], [DT])
                        if GD < 4:
                            continue
                        Xc = X[0]
                        S.op("dve", lambda e, pKK=pKK, Xc=Xc: e.tensor_tensor(out=Xc[:, :, 0, :], in0=pKK[:, :].rearrange("p (h t) -> p h t", h=4), in1=Ds[:], op=ALU.mult),
                             [pKK, Ds], [Xc])
                        S.op("dve", lambda e, Xc=Xc, c=c, rc=rc: e.tensor_tensor(
                            out=Xc[:, :, 0, :], in0=Xc[:, :, 0, :], in1=nbeta[:, c, rc].rearrange("p (h o) -> p h o", o=1).broadcast_to([128, 4, 128]), op=ALU.mult),
                            [Xc, nbeta], [Xc])
                        S.op("dve", lambda e, pQK=pQK: e.tensor_tensor(out=qkT[:], in0=pQK[:, :].rearrange("p (h t) -> p h t", h=4), in1=DT[:], op=ALU.mult),
                             [pQK, DT], [qkT])
                        pZ = C.nextps()
                        pZr = pZ.t[:, :]
                        for h in range(4):
                            S.op("pe", lambda e, pZr=pZr, h=h, Xc=Xc: e.transpose(pZr[:, h * 128:(h + 1) * 128], Xc[:, h, 0, :], ident_r[:]), [Xc, ident_r], [pZ])
                        S.op("act", lambda e, pZ=pZ, Xc=Xc: e.copy(out=Xc[:, :, 1, :], in_=pZ[:, :].rearrange("p (h t) -> p h t", h=4)), [pZ], [Xc])
                        S.op("dve", lambda e, Xc=Xc: e.tensor_tensor(out=P[:], in0=Xc[:, :, 1, :], in1=ident_f[:].rearrange("p (o t) -> p o t", o=1).broadcast_to([128, 4, 128]), op=ALU.add),
                             [Xc, ident_f], [P])
                        for lev in range(6):
                            Xo = X[lev % 2]
                            Xn = X[(lev + 1) % 2]
                            last = lev == 5
                            pxa = C.nextps()
                            pxb = C.nextps()
                            for h in range(4):
                                pp = pxa if h < 2 else pxb
                                o = (h % 2) * 256
                                S.op("pe", lambda e, pp=pp, o=o, h=h, Xo=Xo: e.matmul(pp[:, o:o + 128], lhsT=Xo[:, h, 1, :], rhs=Xo[:, h, 0, :], start=True, stop=True),
                                     [Xo], [pp])
                                if not last:
                                    S.op("pe", lambda e, pp=pp, o=o, h=h, Xo=Xo: e.matmul(pp[:, o + 128:o + 256], lhsT=Xo[:, h, 0, :], rhs=Xo[:, h, 1, :], start=True, stop=True),
                                         [Xo], [pp])
                            S.op("act", lambda e, pxa=pxa, Xn=Xn: e.copy(out=Xn[:, 0:2, :, :].rearrange("p h x t -> p (h x t)"), in_=pxa[:, :]), [pxa], [Xn])
                            S.op("dve", lambda e, pxb=pxb, Xn=Xn: e.tensor_copy(out=Xn[:, 2:4, :, :].rearrange("p h x t -> p (h x t)"), in_=pxb[:, :]), [pxb], [Xn])
                            pP = C.nextps()
                            for h in range(4):
                                S.op("pe", lambda e, pP=pP, h=h, Xn=Xn: e.matmul(pP[:, h * 128:(h + 1) * 128], lhsT=Xn[:, h, 0, :], rhs=P[:, h, :], start=True, stop=True),
                                     [Xn, P], [pP])
                            S.op("dve", lambda e, pP=pP: e.tensor_tensor(out=P[:].rearrange("p h t -> p (h t)"), in0=P[:].rearrange("p h t -> p (h t)"), in1=pP[:, :], op=ALU.add),
                                 [pP, P], [P])
                        if GD < 5:
                            continue
                        S.op("act", lambda e: e.copy(out=Pb[:], in_=P[:]), [P], [Pb])
                        S.op("dve", lambda e, i=i, c=c, rc=rc: e.tensor_tensor(
                            out=kg[:].rearrange("p (h d) -> p h d", h=4), in0=kv[i][:, 0:256].rearrange("p (h d) -> p h d", h=4),
                            in1=beg[:, c, rc].rearrange("p (h o) -> p h o", o=1).broadcast_to([128, 4, 64]), op=ALU.mult), [kv[i], beg], [kg])
                        S.op("dve", lambda e, i=i, c=c, rc=rc: e.tensor_tensor(
                            out=vb[:].rearrange("p (h d) -> p h d", h=4), in0=kv[i][:, 256:512].rearrange("p (h d) -> p h d", h=4),
                            in1=nbeta[:, c, rc].rearrange("p (h o) -> p h o", o=1).broadcast_to([128, 4, 64]), op=ALU.mult), [kv[i], nbeta], [vb])
                        S.op("dve", lambda e, i=i, c=c, rc=rc: e.tensor_tensor(
                            out=ktl[:].rearrange("p (h d) -> p h d", h=4), in0=kv[i][:, 0:256].rearrange("p (h d) -> p h d", h=4),
                            in1=ekt[:, c, rc].rearrange("p (h o) -> p h o", o=1).broadcast_to([128, 4, 64]), op=ALU.mult), [kv[i], ekt], [ktl])
                        pw = C.nextps()
                        pu = C.nextps()
                        for h in range(4):
                            S.op("pe", lambda e, pw=pw, h=h: e.matmul(pw[0:64, h * 128:(h + 1) * 128], lhsT=kg[:, h * 64:(h + 1) * 64], rhs=Pb[:, h, :], start=True, stop=True),
                                 [kg, Pb], [pw])
                            S.op("pe", lambda e, pu=pu, h=h: e.matmul(pu[:, h * 64:(h + 1) * 64], lhsT=Pb[:, h, :], rhs=vb[:, h * 64:(h + 1) * 64], start=True, stop=True),
                                 [vb, Pb], [pu])
                        S.op("dve", lambda e, pw=pw: e.tensor_copy(out=wT[:].rearrange("p h t -> p (h t)"), in_=pw[0:64, :]), [pw], [wT])
                        S.op("dve", lambda e, pu=pu: e.tensor_scalar_mul(out=ub[:], in0=pu[:, 0:256], scalar1=-1.0), [pu], [ub])
                        pws = C.nextps()
                        for h in range(4):
                            S.op("pe", lambda e, pws=pws, h=h: e.matmul(pws[:, h * 64:(h + 1) * 64], lhsT=wT[:, h, :], rhs=Sb16[:, h * 64:(h + 1) * 64], start=True, stop=True),
                                 [wT, Sb16], [pws])
                        S.op("dve", lambda e, pws=pws: e.tensor_tensor(out=u[:], in0=ub[:], in1=pws[:, 0:256], op=ALU.subtract), [ub, pws], [u])
                        pcr = C.nextps()
                        pin = C.nextps()
                        for h in range(4):
                            S.op("pe", lambda e, pcr=pcr, h=h, i=i: e.matmul(pcr[:, h * 64:(h + 1) * 64], lhsT=qT[i][:, h, :], rhs=Sb16[:, h * 64:(h + 1) * 64], start=True, stop=True),
                                 [qT[i], Sb16], [pcr])
                            S.op("pe", lambda e, pin=pin, h=h: e.matmul(pin[:, h * 64:(h + 1) * 64], lhsT=qkT[:, h, :], rhs=u[:, h * 64:(h + 1) * 64], start=True, stop=True),
                                 [qkT, u], [pin])
                        pS = C.nextps()
                        for h in range(4):
                            S.op("pe", lambda e, pS=pS, h=h: e.matmul(pS[0:64, h * 64:(h + 1) * 64], lhsT=ktl[:, h * 64:(h + 1) * 64], rhs=u[:, h * 64:(h + 1) * 64], start=True, stop=True),
                                 [ktl, u], [pS])
                        S.op("dve", lambda e, pcr=pcr, c=c, rc=rc: e.tensor_tensor(
                            out=ocr[:].rearrange("p (h d) -> p h d", h=4), in0=pcr[:, 0:256].rearrange("p (h d) -> p h d", h=4),
                            in1=eg[:, c, rc].rearrange("p (h o) -> p h o", o=1).broadcast_to([128, 4, 64]), op=ALU.mult), [pcr, eg], [ocr])
                        if r == 0:
                            S.op("dve", lambda e, pin=pin, c=c: e.tensor_tensor(out=Oacc[:, c, :], in0=ocr[:], in1=pin[:, 0:256], op=ALU.add), [ocr, pin], [Oacc.sub(c)])
                        else:
                            S.op("dve", lambda e, pin=pin: e.tensor_tensor(out=ocr[:], in0=ocr[:], in1=pin[:, 0:256], op=ALU.add), [ocr, pin], [ocr])
                            S.op("dve", lambda e, c=c: e.tensor_tensor(out=ocr[:], in0=ocr[:], in1=Oacc[:, c, :], op=ALU.add), [ocr, Oacc.sub(c)], [ocr])
                        S.op("dve", lambda e, c=c, rc=rc: e.tensor_tensor(
                            out=Sf[:].rearrange("p (h d) -> p h d", h=4), in0=Sf[:].rearrange("p (h d) -> p h d", h=4),
                            in1=cd[0:64, c, rc].rearrange("p (h o) -> p h o", o=1).broadcast_to([64, 4, 64]), op=ALU.mult), [Sf, cd], [Sf])
                        S.op("dve", lambda e, pS=pS: e.tensor_tensor(out=Sf[:], in0=Sf[:], in1=pS[0:64, 0:256], op=ALU.add), [Sf, pS], [Sf])
                        S.op("act", lambda e: e.copy(out=Sb16[:], in_=Sf[:]), [Sf], [Sb16])
                        if r == 1 and (full_ctx or c < NLT):
                            gt_ = ggate[i]
                            S.dma("act", lambda e, gt_=gt_, c=c: e.dma_start(out=gt_[:], in_=TMC.t[c * 128:(c + 1) * 128, :]), [], [gt_])
                            S.op("act", lambda e: e.activation(out=gsq[:], in_=ocr[:], func=AF.Square), [ocr], [gsq])
                            S.op("dve", lambda e: e.tensor_reduce(out=gss[:], in_=gsq[:].rearrange("p (h d) -> p h d", h=4), axis=AX.X, op=ALU.add), [gsq], [gss])
                            S.op("dve", lambda e: e.tensor_scalar(out=gss[:], in0=gss[:], scalar1=1.0 / 64, scalar2=1e-6, op0=ALU.mult, op1=ALU.add), [gss], [gss])
                            S.op("act", lambda e: e.activation(out=gss[:], in_=gss[:], func=AF.Sqrt), [gss], [gss])
                            S.op("dve", lambda e: e.reciprocal(out=gss[:], in_=gss[:]), [gss], [gss])
                            S.op("act", lambda e, gt_=gt_: e.activation(out=gsg[:], in_=gt_[:], func=AF.Silu), [gt_], [gsg])
                            S.op("dve", lambda e: e.tensor_tensor(
                                out=ocr[:].rearrange("p (h d) -> p h d", h=4), in0=ocr[:].rearrange("p (h d) -> p h d", h=4),
                                in1=gss[:].rearrange("p (h o) -> p h o", o=1).broadcast_to([128, 4, 64]), op=ALU.mult), [ocr, gss], [ocr])
                            S.op("dve", lambda e: e.tensor_tensor(
                                out=ocr[:].rearrange("p (h d) -> p h d", h=4), in0=ocr[:].rearrange("p (h d) -> p h d", h=4),
                                in1=gnb[:].rearrange("p (o d) -> p o d", o=1).broadcast_to([128, 4, 64]), op=ALU.mult), [ocr, gnb], [ocr])
                            yy = gy[i]
                            S.op("dve", lambda e, yy=yy: e.tensor_tensor(out=yy[:], in0=ocr[:], in1=gsg[:], op=ALU.mult), [ocr, gsg], [yy])
                            S.dma("sp", lambda e, yy=yy, c=c: e.dma_start(out=Y.t[c * 128:(c + 1) * 128, 512:768], in_=yy[:]), [yy], [Y.sub(c, 2)])
                yield "END"

        def layer_norm_tile(z, sqt, st, lnG, lnB, gi, outt, eng2="dve"):
            S.op("dve", lambda e: e.tensor_reduce(out=st[:, 0:1], in_=z[:], axis=AX.X, op=ALU.add), [z], [st])
            S.op("dve", lambda e: e.tensor_tensor(out=sqt[:], in0=z[:], in1=z[:], op=ALU.mult), [z], [sqt])
            S.op("dve", lambda e: e.tensor_reduce(out=st[:, 1:2], in_=sqt[:], axis=AX.X, op=ALU.add), [sqt], [st])
            S.op("dve", lambda e: e.tensor_scalar_mul(out=st[:, 0:2], in0=st[:, 0:2], scalar1=1.0 / D), [st], [st])
            S.op("dve", lambda e: e.tensor_tensor(out=st[:, 2:3], in0=st[:, 0:1], in1=st[:, 0:1], op=ALU.mult), [st], [st])
            S.op("dve", lambda e: e.tensor_tensor(out=st[:, 2:3], in0=st[:, 1:2], in1=st[:, 2:3], op=ALU.subtract), [st], [st])
            S.op("dve", lambda e: e.tensor_scalar_add(out=st[:, 2:3], in0=st[:, 2:3], scalar1=LN_EPS), [st], [st])
            S.op("act", lambda e: e.activation(out=st[:, 2:3], in_=st[:, 2:3], func=AF.Sqrt), [st], [st])
            S.op("dve", lambda e: e.reciprocal(out=st[:, 2:3], in_=st[:, 2:3]), [st], [st])
            S.op("dve", lambda e: e.tensor_scalar(out=sqt[:], in0=z[:], scalar1=st[:, 0:1], scalar2=st[:, 2:3], op0=ALU.subtract, op1=ALU.mult),
                 [z, st], [sqt])
            S.op("pool", lambda e: e.tensor_tensor(out=sqt[:], in0=sqt[:], in1=lnG[:, gi, :], op=ALU.mult), [sqt, lnG], [sqt])
            S.op("pool", lambda e: e.tensor_tensor(out=outt[:], in0=sqt[:], in1=lnB[:, gi, :], op=ALU.add), [sqt, lnB], [outt])

        def phase_E(l, full_ctx):
            src = h0 if l == 0 else H.t
            with ExitStack() as pes:
                wo = C.sb("wo", [128, 8, D], BF16, es=pes)
                lnG = C.sb("lnG", [128, 2, D], es=pes)
                lnB = C.sb("lnB", [128, 2, D], es=pes)
                rw = C.sb("rw", [128, 8, 128], es=pes)
                LG = C.sb("LG", [16, T], es=pes)
                yts = [C.sb(f"yt{i}", [128, D], BF16, es=pes) for i in range(2)]
                hts = [C.sb(f"eht{i}", [128, D], es=pes) for i in range(2)]
                YT = C.sb("YT", [128, 8, 128], BF16, es=pes)
                z = C.sb("z", [128, D], es=pes)
                sqt = C.sb("sqt", [128, D], es=pes)
                st = C.sb("st", [128, 4], es=pes)
                hn = [C.sb(f"hn{i}", [128, D], es=pes) for i in range(2)]
                u2b = [C.sb(f"u2b{i}", [128, D], BF16, es=pes) for i in range(2)]
                u2t = C.sb("u2t", [128, D], es=pes)
                u2T = C.sb("u2T", [128, 8, 128], es=pes)
                for k in range(8):
                    S.dma("pool", lambda e, k=k: e.dma_start(out=wo[:, k, :], in_=w_out[l, k * 128:(k + 1) * 128, :]), [], [wo])
                S.dma("sp", lambda e: e.dma_start(out=lnG[:].rearrange("p g d -> p (g d)"), in_=ln_g[l:l + 1, :].broadcast_to([128, 2 * D])), [], [lnG])
                S.dma("sp", lambda e: e.dma_start(out=lnB[:].rearrange("p g d -> p (g d)"), in_=ln_b[l:l + 1, :].broadcast_to([128, 2 * D])), [], [lnB])
                S.dma("sp", lambda e: e.dma_start(out=rw[:], in_=router_wp[l, :, :].rearrange("(k p) e -> p k e", p=128)), [], [rw])
                tiles = list(range(NT)) if full_ctx else list(range(NLT))
                for ti, t in enumerate(tiles):
                    lc = 0 if t < NLT else 1
                    yt = yts[ti % 2]
                    ht = hts[ti % 2]
                    S.dma("sp", lambda e, yt=yt, t=t: e.dma_start(out=yt[:], in_=Y.t[t * 128:(t + 1) * 128, :]), [], [yt])
                    S.dma("act", lambda e, ht=ht, t=t: e.dma_start(out=ht[:], in_=src[t * 128:(t + 1) * 128, :]), ["H/%d" % t], [ht])
                    ps = C.nextps()
                    pv = ps.t[:, :].bitcast(BF16)
                    for k in range(8):
                        S.op("pe", lambda e, pv=pv, yt=yt, k=k: e.transpose(pv[:, k * 128:(k + 1) * 128], yt[:, k * 128:(k + 1) * 128], ident_b[:]),
                             [yt, ident_b], [ps])
                    S.op("act", lambda e, pv=pv: e.copy(out=YT[:].rearrange("p k t -> p (k t)"), in_=pv[:, :]), [ps], [YT])
                    for half in range(2):
                        pm = C.nextps()
                        for k in range(8):
                            S.op("pe", lambda e, pm=pm, k=k, half=half: e.matmul(pm[:, :], lhsT=YT[:, k, :], rhs=wo[:, k, half * 512:(half + 1) * 512],
                                                                                 start=(k == 0), stop=(k == 7)), [YT, wo], [pm])
                        S.op("dve", lambda e, pm=pm, half=half, lc=lc: e.tensor_tensor(
                            out=z[:, half * 512:(half + 1) * 512], in0=pm[:, :], in1=gateB[:, 0, lc, half * 512:(half + 1) * 512], op=ALU.mult),
                            [pm, gateB], [z])
                    S.op("dve", lambda e, ht=ht: e.scalar_tensor_tensor(out=z[:], in0=ht[:], scalar=ALPHA, in1=z[:], op0=ALU.mult, op1=ALU.add),
                         [ht, z], [z])
                    h_ = hn[ti % 2]
                    layer_norm_tile(z, sqt, st, lnG, lnB, 0, h_)
                    S.dma("sp", lambda e, h_=h_, t=t: e.dma_start(out=H.t[t * 128:(t + 1) * 128, :], in_=h_[:]), [h_], ["H/%d" % t])
                    ub_ = u2b[ti % 2]
                    S.op("pool", lambda e, h_=h_, lc=lc: e.tensor_tensor(out=u2t[:], in0=h_[:], in1=gateB[:, 3, lc, :], op=ALU.mult), [h_, gateB], [u2t])
                    S.op("pool", lambda e, ub_=ub_, lc=lc: e.tensor_tensor(out=ub_[:], in0=u2t[:], in1=gateB[:, 2, lc, :], op=ALU.add), [u2t, gateB], [ub_])
                    S.dma("act", lambda e, ub_=ub_, t=t: e.dma_start(out=U2.t[t * 128:(t + 1) * 128, :], in_=ub_[:]), [ub_], [U2.sub(t)])
                    for half in range(2):
                        pt = C.nextps()
                        for j in range(4):
                            k = half * 4 + j
                            S.op("pe", lambda e, pt=pt, h_=h_, j=j, k=k: e.transpose(pt[:, j * 128:(j + 1) * 128], h_[:, k * 128:(k + 1) * 128], ident_f[:]),
                                 [h_, ident_f], [pt])
                        for j in range(4):
                            k = half * 4 + j
                            S.op("act", lambda e, pt=pt, j=j, k=k, lc=lc: e.activation(
                                out=u2T[:, k, :], in_=pt[:, j * 128:(j + 1) * 128], func=AF.Identity,
                                bias=modT[:, k, 2, lc:lc + 1], scale=modT[:, k, 3, lc:lc + 1]), [pt, modT], [u2T])
                    pl_ = C.nextps()
                    for k in range(8):
                        S.op("pe", lambda e, pl_=pl_, k=k: e.matmul(pl_[0:16, 0:128], lhsT=rw[:, k, 0:16], rhs=u2T[:, k, :], start=(k == 0), stop=(k == 7)),
                             [rw, u2T], [pl_])
                    S.op("dve", lambda e, pl_=pl_, t=t: e.tensor_copy(out=LG[:, t * 128:(t + 1) * 128], in_=pl_[0:16, 0:128]), [pl_], [LG])
                S.dma("sp", lambda e: e.dma_start(out=LGD.t[:, :], in_=LG[:]), [LG], [LGD])
                S.barrier()
                S.emit()

        def moe_segs(full_ctx):
            segs = [(0, NLT, NLAT // 8, 0, 0)]
            if full_ctx:
                segs.append((NLT, 2, 32, NLAT // 8, 512))
            return segs

        def phase_F(l, full_ctx):
            segs = moe_segs(full_ctx)
            NTOT = sum(sg[2] for sg in segs)
            with ExitStack() as pes:
                LG = C.sb("fLG", [16, T], es=pes)
                aff = C.sb("aff", [16, T], es=pes)
                work = C.sb("work", [16, NLAT], es=pes)
                m8 = C.sb("m8", [16, 8], es=pes)
                key = C.sb("key", [16, T], es=pes)
                ones16 = C.sb("ones16", [16, 16], es=pes)
                onesr = C.sb("onesr", [16, NLAT], es=pes)
                id16p = C.sb("id16p", [16, 128], es=pes)
                keyTok = C.sb("keyTok", [128, NT, 16], es=pes)
                iot = C.sb("iot", [128, 512], es=pes)
                S.dma("sp", lambda e: e.dma_start(out=LG[:], in_=LGD.t[:, :]), [], [LG])
                S.dma("sp", lambda e: e.dma_start(out=iot[:], in_=iota512[:, :]), [], [iot])
                S.op("pool", lambda e: e.memset(ones16[:], 1.0), [], [ones16])
                S.op("pool", lambda e: e.memset(onesr[:], 1.0), [], [onesr])
                S.op("pool", lambda e: e.memset(id16p[:], 0.0), [], [id16p])
                S.op("dve", lambda e: e.tensor_copy(out=id16p[:, 0:16], in_=ident_f[0:16, 0:16]), [id16p, ident_f], [id16p])
                S.op("act", lambda e: e.activation(out=aff[:], in_=LG[:], func=AF.Exp), [LG], [aff])
                for c0 in range(0, T, 512):
                    cw = min(512, T - c0)
                    ps = C.nextps()
                    S.op("pe", lambda e, ps=ps, c0=c0, cw=cw: e.matmul(ps[0:16, 0:cw], lhsT=ones16[:], rhs=aff[:, c0:c0 + cw], start=True, stop=True),
                         [ones16, aff], [ps])
                    S.op("dve", lambda e, ps=ps, c0=c0, cw=cw: e.reciprocal(out=LG[:, c0:c0 + cw], in_=ps[0:16, 0:cw]), [ps], [LG])
                S.op("dve", lambda e: e.tensor_tensor(out=aff[:], in0=aff[:], in1=LG[:], op=ALU.mult), [aff, LG], [aff])
                for (t0, ntl, cap, c0, r0) in segs:
                    n_ = ntl * 128
                    a0 = t0 * 128
                    S.op("dve", lambda e, a0=a0, n_=n_: e.tensor_copy(out=work[:, 0:n_], in_=aff[:, a0:a0 + n_]), [aff], [work])
                    for rnd in range(cap // 8):
                        S.op("dve", lambda e, n_=n_: e.max(out=m8[:], in_=work[:, 0:n_]), [work], [m8])
                        S.op("dve", lambda e, n_=n_: e.match_replace(out=work[:, 0:n_], in_to_replace=m8[:], in_values=work[:, 0:n_], imm_value=-1.0),
                             [work, m8], [work])
                    S.op("dve", lambda e, n_=n_: e.tensor_single_scalar(out=work[:, 0:n_], in_=work[:, 0:n_], scalar=0.0, op=ALU.is_lt), [work], [work])
                    S.op("dve", lambda e, a0=a0, n_=n_: e.tensor_tensor_scan(out=key[:, a0:a0 + n_], data0=onesr[:, 0:n_], data1=work[:, 0:n_], initial=0.0,
                                                                            op0=ALU.mult, op1=ALU.add), [work, onesr], [key])
                    S.op("dve", lambda e, a0=a0, n_=n_: e.tensor_tensor(out=key[:, a0:a0 + n_], in0=key[:, a0:a0 + n_], in1=work[:, 0:n_], op=ALU.mult), [key, work], [key])
                    S.op("dve", lambda e, a0=a0, n_=n_: e.tensor_scalar_add(out=key[:, a0:a0 + n_], in0=key[:, a0:a0 + n_], scalar1=-1.0), [key], [key])
                S.dma("sp", lambda e: e.dma_start(out=KEYD.t[:, :], in_=key[:]), [key], [KEYD])
                S.dma("sp", lambda e: e.dma_start(out=AFFD.t[:, :], in_=aff[:]), [aff], [AFFD])
                tl_all = [t for (t0, ntl, cap, c0, r0) in segs for t in range(t0, t0 + ntl)]
                for q0 in range(0, len(tl_all), 4):
                    grp = tl_all[q0:q0 + 4]
                    ps = C.nextps()
                    for qi, t in enumerate(grp):
                        S.op("pe", lambda e, ps=ps, qi=qi, t=t: e.matmul(ps[:, qi * 128:(qi + 1) * 128], lhsT=key[:, t * 128:(t + 1) * 128], rhs=id16p[:],
                                                                          start=True, stop=True), [key, id16p], [ps])
                    for qi, t in enumerate(grp):
                        S.op("dve", lambda e, ps=ps, qi=qi, t=t: e.tensor_copy(out=keyTok[:, t, :], in_=ps[:, qi * 128:qi * 128 + 16]), [ps], [keyTok])
                S.dma("sp", lambda e: e.dma_start(out=KTD.t[:, :], in_=keyTok[:].rearrange("p n c -> p (n c)")), [keyTok], [KTD])
                S.barrier()
                S.emit()
            with ExitStack() as pes:
                keyTok = C.sb("keyTok2", [128, NT, 16], es=pes)
                iot = C.sb("iot2", [128, 512], es=pes)
                Se = C.sb("Se", [128, NT, 512], BF16, es=pes)
                xinT = C.sb("xinT", [128, 8, NTOT], BF16, es=pes)
                hT = C.sb("hT", [128, 16, NTOT], BF16, es=pes)
                yacc = C.sb("yacc", [128, 5, D], es=pes)
                yw = C.sb("yw", [128, 5, D], BF16, es=pes)
                u2s = [C.sb(f"u2s{i}", [128, D], BF16, es=pes) for i in range(2)]
                wg = [C.sb(f"wg{i}", [128, 8, 512], BF16, es=pes) for i in range(2)]
                wu = [C.sb(f"wu{i}", [128, 8, 512], BF16, es=pes) for i in range(2)]
                wd = [C.sb(f"wd{i}", [128, 4, D], BF16, es=pes) for i in range(2)]
                sg_ = [C.sb(f"sgt{i}", [128, 512], es=pes) for i in range(2)]
                S.dma("sp", lambda e: e.dma_start(out=keyTok[:].rearrange("p n c -> p (n c)"), in_=KTD.t[:, :]), [], [keyTok])
                S.dma("sp", lambda e: e.dma_start(out=iot[:], in_=iota512[:, :]), [], [iot])
                C.rot = [4, 5, 6, 7]
                wcnt = 0
                ucnt = 0
                for ex in range(16):
                    for (t0, ntl, cap, c0, r0) in segs:
                        for t in range(t0, t0 + ntl):
                            S.op("dve", lambda e, t=t, cap=cap, ex=ex: e.tensor_scalar(
                                out=Se[:, t, 0:cap], in0=iot[:, 0:cap], scalar1=keyTok[:, t, ex:ex + 1], scalar2=None, op0=ALU.is_equal),
                                [iot, keyTok], [Se.sub(t)])
                        for kh in range(2):
                            for ti, t in enumerate(range(t0, t0 + ntl)):
                                ut = u2s[ucnt % 2]
                                ucnt += 1
                                S.dma("sp", lambda e, ut=ut, t=t: e.dma_start(out=ut[:], in_=U2.t[t * 128:(t + 1) * 128, :]), [], [ut])
                                for kk in range(4):
                                    k = kh * 4 + kk
                                    S.op("pe", lambda e, ut=ut, t=t, k=k, kk=kk, cap=cap, ti=ti, ntl=ntl: e.matmul(
                                        C.ps[kk][:, 0:cap], lhsT=ut[:, k * 128:(k + 1) * 128], rhs=Se[:, t, 0:cap],
                                        start=(ti == 0), stop=(ti == ntl - 1)), [ut, Se.sub(t)], [C.ps[kk]])
                            for kk in range(4):
                                k = kh * 4 + kk
                                S.op("act", lambda e, k=k, kk=kk, c0=c0, cap=cap: e.copy(out=xinT[:, k, c0:c0 + cap], in_=C.ps[kk][:, 0:cap]),
                                     [C.ps[kk]], [xinT])
                    jts = []
                    for (t0, ntl, cap, c0, r0) in segs:
                        for j0 in range(0, cap, 128):
                            jts.append((r0 + j0, min(128, cap - j0), c0 + j0))
                    for fg in range(4):
                        g_, u_, d_ = wg[wcnt % 2], wu[wcnt % 2], wd[wcnt % 2]
                        wcnt += 1
                        for k in range(8):
                            S.dma("pool", lambda e, g_=g_, ex=ex, fg=fg, k=k: e.dma_start(
                                out=g_[:, k, :], in_=w_gate[l, ex, k * 128:(k + 1) * 128, fg * 512:(fg + 1) * 512]), [], [g_])
                            S.dma("pool", lambda e, u_=u_, ex=ex, fg=fg, k=k: e.dma_start(
                                out=u_[:, k, :], in_=w_up[l, ex, k * 128:(k + 1) * 128, fg * 512:(fg + 1) * 512]), [], [u_])
                        for cc in range(4):
                            S.dma("pool", lambda e, d_=d_, ex=ex, fg=fg, cc=cc: e.dma_start(
                                out=d_[:, cc, :], in_=w_down[l, ex, fg * 512 + cc * 128:fg * 512 + (cc + 1) * 128, :]), [], [d_])
                        for fc in range(4):
                            f = fg * 4 + fc
                            for (t0, ntl, cap, c0, r0) in segs:
                                pg = C.nextps()
                                pu_ = C.nextps()
                                for k in range(8):
                                    S.op("pe", lambda e, pg=pg, g_=g_, k=k, fc=fc, c0=c0, cap=cap: e.matmul(
                                        pg[:, 0:cap], lhsT=g_[:, k, fc * 128:(fc + 1) * 128], rhs=xinT[:, k, c0:c0 + cap], start=(k == 0), stop=(k == 7)),
                                        [g_, xinT], [pg])
                                for k in range(8):
                                    S.op("pe", lambda e, pu_=pu_, u_=u_, k=k, fc=fc, c0=c0, cap=cap: e.matmul(
                                        pu_[:, 0:cap], lhsT=u_[:, k, fc * 128:(fc + 1) * 128], rhs=xinT[:, k, c0:c0 + cap], start=(k == 0), stop=(k == 7)),
                                        [u_, xinT], [pu_])
                                sgt = sg_[f % 2]
                                S.op("act", lambda e, pg=pg, sgt=sgt, cap=cap: e.activation(out=sgt[:, 0:cap], in_=pg[:, 0:cap], func=AF.Silu), [pg], [sgt])
                                S.op("dve", lambda e, pu_=pu_, sgt=sgt, f=f, c0=c0, cap=cap: e.tensor_tensor(
                                    out=hT[:, f, c0:c0 + cap], in0=pu_[:, 0:cap], in1=sgt[:, 0:cap], op=ALU.mult), [pu_, sgt], [hT])
                        for ji, (ro, rows, co) in enumerate(jts):
                            for half in range(2):
                                py = C.nextps()
                                for fc in range(4):
                                    S.op("pe", lambda e, py=py, d_=d_, fc=fc, fg=fg, co=co, rows=rows, half=half: e.matmul(
                                        py[0:rows, :], lhsT=hT[:, fg * 4 + fc, co:co + rows], rhs=d_[:, fc, half * 512:(half + 1) * 512],
                                        start=(fc == 0), stop=(fc == 3)), [hT, d_], [py])
                                if fg == 0:
                                    S.op("dve", lambda e, py=py, ji=ji, rows=rows, half=half: e.tensor_copy(
                                        out=yacc[0:rows, ji, half * 512:(half + 1) * 512], in_=py[0:rows, :]), [py], [yacc])
                                elif fg < 3:
                                    S.op("dve", lambda e, py=py, ji=ji, rows=rows, half=half: e.tensor_tensor(
                                        out=yacc[0:rows, ji, half * 512:(half + 1) * 512], in0=yacc[0:rows, ji, half * 512:(half + 1) * 512],
                                        in1=py[0:rows, :], op=ALU.add), [py, yacc], [yacc])
                                else:
                                    S.op("dve", lambda e, py=py, ji=ji, rows=rows, half=half: e.tensor_tensor(
                                        out=yw[0:rows, ji, half * 512:(half + 1) * 512], in0=yacc[0:rows, ji, half * 512:(half + 1) * 512],
                                        in1=py[0:rows, :], op=ALU.add), [py, yacc], [yw])
                    for ji, (ro, rows, co) in enumerate(jts):
                        S.dma("act", lambda e, ji=ji, ro=ro, rows=rows, ex=ex: e.dma_start(out=YE.t[ex, ro:ro + rows, :], in_=yw[0:rows, ji, :]),
                              [yw], [YE.sub(ex, ji)])
                C.rot = None
                S.barrier()
                S.emit()

        def phase_G(l, full_ctx, last):
            segs = moe_segs(full_ctx)
            with ExitStack() as pes:
                key = C.sb("gkey", [16, T], es=pes)
                aff = C.sb("gaff", [16, T], es=pes)
                selE = C.sb("selE", [16, 16, 128], es=pes)
                jcol = C.sb("jcol", [128, 4], es=pes)
                lnG = C.sb("glnG", [128, 2, D], es=pes)
                lnB = C.sb("glnB", [128, 2, D], es=pes)
                acc = C.sb("acc", [128, 8, D], es=pes)
                yes = [C.sb(f"ye{i}", [128, 4, D], BF16, es=pes) for i in range(2)]
                Ws = [C.sb(f"W{i}", [128, 4, 512], BF16, es=pes) for i in range(2)]
                affB = [C.sb(f"affB{i}", [128, 512], es=pes) for i in range(2)]
                hts = [C.sb(f"ght{i}", [128, D], es=pes) for i in range(2)]
                z = C.sb("gz", [128, D], es=pes)
                sqt = C.sb("gsqt", [128, D], es=pes)
                st = C.sb("gst", [128, 4], es=pes)
                hn = [C.sb(f"ghn{i}", [128, D], es=pes) for i in range(2)]
                S.dma("sp", lambda e: e.dma_start(out=key[:], in_=KEYD.t[:, :]), [], [key])
                S.dma("sp", lambda e: e.dma_start(out=aff[:], in_=AFFD.t[:, :]), [], [aff])
                S.dma("sp", lambda e: e.dma_start(out=selE[:], in_=selE_in[:, :, :]), [], [selE])
                S.dma("sp", lambda e: e.dma_start(out=jcol[:], in_=jcol_in[:, :]), [], [jcol])
                S.dma("sp", lambda e: e.dma_start(out=lnG[:].rearrange("p g d -> p (g d)"), in_=ln_g[l:l + 1, :].broadcast_to([128, 2 * D])), [], [lnG])
                S.dma("sp", lambda e: e.dma_start(out=lnB[:].rearrange("p g d -> p (g d)"), in_=ln_b[l:l + 1, :].broadcast_to([128, 2 * D])), [], [lnB])
                cnt = 0
                for (t0, ntl, cap, c0, r0) in segs:
                    lc = 0 if t0 < NLT else 1
                    njt = (cap + 127) // 128
                    for gq in range(t0, t0 + ntl, 8):
                        gtiles = list(range(gq, min(gq + 8, t0 + ntl)))
                        for ex in range(16):
                            ye = yes[ex % 2]
                            for jt in range(njt):
                                rows = min(128, cap - jt * 128)
                                S.dma("sp", lambda e, ye=ye, jt=jt, rows=rows, ex=ex, r0=r0: e.dma_start(
                                    out=ye[0:rows, jt, :], in_=YE.t[ex, r0 + jt * 128:r0 + jt * 128 + rows, :]), [ye], [ye])
                            for q0 in range(0, len(gtiles), 4):
                                ch = gtiles[q0:q0 + 4]
                                a0 = ch[0] * 128
                                cw = len(ch) * 128
                                pk = C.nextps()
                                pa = C.nextps()
                                S.op("pe", lambda e, pk=pk, ex=ex, a0=a0, cw=cw: e.matmul(pk[:, 0:cw], lhsT=selE[:, ex, :], rhs=key[:, a0:a0 + cw], start=True, stop=True),
                                     [selE, key], [pk])
                                S.op("pe", lambda e, pa=pa, ex=ex, a0=a0, cw=cw: e.matmul(pa[:, 0:cw], lhsT=selE[:, ex, :], rhs=aff[:, a0:a0 + cw], start=True, stop=True),
                                     [selE, aff], [pa])
                                ab = affB[cnt % 2]
                                W = Ws[cnt % 2]
                                cnt += 1
                                S.op("act", lambda e, pa=pa, ab=ab, cw=cw: e.copy(out=ab[:, 0:cw], in_=pa[:, 0:cw]), [pa], [ab])
                                for jt in range(njt):
                                    S.op("dve", lambda e, pk=pk, ab=ab, W=W, jt=jt, cw=cw: e.scalar_tensor_tensor(
                                        out=W[:, jt, 0:cw], in0=pk[:, 0:cw], scalar=jcol[:, jt:jt + 1], in1=ab[:, 0:cw], op0=ALU.is_equal, op1=ALU.mult),
                                        [pk, ab, jcol], [W])
                                for qi, t in enumerate(ch):
                                    ti = t - gq
                                    for half in range(2):
                                        py = C.nextps()
                                        for jt in range(njt):
                                            rows = min(128, cap - jt * 128)
                                            S.op("pe", lambda e, py=py, W=W, ye=ye, jt=jt, rows=rows, qi=qi, half=half, njt=njt: e.matmul(
                                                py[:, :], lhsT=W[0:rows, jt, qi * 128:(qi + 1) * 128], rhs=ye[0:rows, jt, half * 512:(half + 1) * 512],
                                                start=(jt == 0), stop=(jt == njt - 1)), [W, ye], [py])
                                        if ex == 0:
                                            S.op("dve", lambda e, py=py, ti=ti, half=half: e.tensor_copy(out=acc[:, ti, half * 512:(half + 1) * 512], in_=py[:, :]),
                                                 [py], [acc.sub(ti)])
                                        else:
                                            S.op("dve", lambda e, py=py, ti=ti, half=half: e.tensor_tensor(
                                                out=acc[:, ti, half * 512:(half + 1) * 512], in0=acc[:, ti, half * 512:(half + 1) * 512], in1=py[:, :], op=ALU.add),
                                                [py, acc.sub(ti)], [acc.sub(ti)])
                        for t in gtiles:
                            ti = t - gq
                            ht = hts[t % 2]
                            S.dma("act", lambda e, ht=ht, t=t: e.dma_start(out=ht[:], in_=H.t[t * 128:(t + 1) * 128, :]), ["H/%d" % t], [ht])
                            S.op("pool", lambda e, ti=ti, lc=lc: e.tensor_tensor(out=z[:], in0=acc[:, ti, :], in1=gateB[:, 1, lc, :], op=ALU.mult),
                                 [acc.sub(ti), gateB], [z])
                            S.op("dve", lambda e, ht=ht: e.scalar_tensor_tensor(out=z[:], in0=ht[:], scalar=ALPHA, in1=z[:], op0=ALU.mult, op1=ALU.add),
                                 [ht, z], [z])
                            h_ = hn[t % 2]
                            layer_norm_tile(z, sqt, st, lnG, lnB, 1, h_)
                            if last:
                                S.dma("sp", lambda e, h_=h_, t=t: e.dma_start(out=out[t * 128:(t + 1) * 128, :], in_=h_[:]), [h_], ["out/%d" % t])
                            else:
                                S.dma("sp", lambda e, h_=h_, t=t: e.dma_start(out=H.t[t * 128:(t + 1) * 128, :], in_=h_[:]), [h_], ["H/%d" % t])
                S.barrier()
                S.emit()

        def run_group(gens):
            active = list(gens)
            done = []
            while active:
                for g in list(active):
                    r = next(g)
                    if r == "END":
                        active.remove(g)
                        done.append(g)
            C.rot = None
            if done:
                S.barrier()
                S.emit()
            for g in done:
                try:
                    next(g)
                except StopIteration:
                    pass

        for l in range(NL):
            full_ctx = l < DEPTH - 1
            if "A" in phases.split(","):
                phase_A(l)
            if "BC" in phases.split(","):
                phase_BC(l)
            ph = phases.split(",")
            ges = ExitStack()
            gens = []
            if "ret" in ph:
                gens.append(phase_ret(l, full_ctx, ges))
            if "gdn" in ph:
                gens.append(phase_gdn1(l, full_ctx, ges))
            run_group(gens)
            ges.close()
            ges = ExitStack()
            gens = []
            if "swa" in ph:
                gens.append(phase_swa(l, full_ctx, ges))
            if "gdn" in ph:
                gens.append(phase_gdn23(l, full_ctx, ges))
            run_group(gens)
            ges.close()
            if "diff" in phases.split(","):
                phase_diff(l, full_ctx)

            if "E" in phases.split(","):
                phase_E(l, full_ctx)
            if "F" in phases.split(","):
                phase_F(l, full_ctx)
            if "G" in phases.split(","):
                phase_G(l, full_ctx, last=(l == NL - 1 and "keepH" not in dbg))
    dbg_t["ninst"] = S.ninst
    return nc, dbg_t


def prep_shared(inputs, NLT, NL):
    sh = {}
    ada_b = np.asarray(inputs["ada_b"], np.float32)[:NL]
    sh["ada_w"] = np.ascontiguousarray(np.asarray(inputs["ada_w"], np.float32)[:NL])
    sh["ada_b"] = np.ascontiguousarray(ada_b)
    sh["ada_bT"] = np.ascontiguousarray(ada_b.reshape(NL, 48, 128).transpose(0, 2, 1))
    w_in = np.asarray(inputs["w_in"], np.float32)
    sh["w_in"] = np.stack([build_w_in_ext(w_in[l]) for l in range(NL)])
    sh["w_out"] = np.ascontiguousarray(np.asarray(inputs["w_out"], np.float32)[:NL])
    sh["rope"] = rope_tables(NLT)
    sh["ret_decay"] = np.ascontiguousarray(np.asarray(inputs["ret_decay"], np.float32)[:NL].reshape(NL, 8))
    sh["ln_g"] = np.ascontiguousarray(np.asarray(inputs["ln_g"], np.float32)[:NL].reshape(NL, 2 * D))
    sh["ln_b"] = np.ascontiguousarray(np.asarray(inputs["ln_b"], np.float32)[:NL].reshape(NL, 2 * D))
    sh["diff_lambda"] = np.ascontiguousarray(np.asarray(inputs["diff_lambda"], np.float32)[:NL].reshape(NL, 128))
    sh["diff_norm"] = np.ascontiguousarray(np.asarray(inputs["diff_norm"], np.float32)[:NL])
    sh["swa_sink"] = np.ascontiguousarray(np.asarray(inputs["swa_sink"], np.float32)[:NL])
    rw = np.asarray(inputs["router_w"], np.float32)[:NL]
    rwp = np.zeros((NL, D, 128), np.float32)
    rwp[:, :, :16] = rw
    sh["router_wp"] = rwp
    sh["w_gate"] = np.ascontiguousarray(np.asarray(inputs["w_gate"], np.float32)[:NL])
    sh["w_up"] = np.ascontiguousarray(np.asarray(inputs["w_up"], np.float32)[:NL])
    sh["w_down"] = np.ascontiguousarray(np.asarray(inputs["w_down"], np.float32)[:NL])
    sh["iota512"] = np.ascontiguousarray(np.broadcast_to(np.arange(512, dtype=np.float32)[None, :], (128, 512)))
    sh["jcol_in"] = np.ascontiguousarray(np.arange(128, dtype=np.float32)[:, None] + 128.0 * np.arange(4, dtype=np.float32)[None, :])
    se = np.zeros((16, 16, 128), np.float32)
    for e_ in range(16):
        se[e_, e_, :] = 1.0
    sh["selE_in"] = se
    gc = np.asarray(inputs["gdn_conv"], np.float32)[:NL]
    sh["gdn_convT"] = np.ascontiguousarray(gc.reshape(NL, 3, 6, 128).transpose(0, 3, 2, 1).reshape(NL, 128, 18))
    sh["gdn_a_log"] = np.ascontiguousarray(np.asarray(inputs["gdn_a_log"], np.float32)[:NL].reshape(NL, 8))
    sh["gdn_dt_bias"] = np.ascontiguousarray(np.asarray(inputs["gdn_dt_bias"], np.float32)[:NL].reshape(NL, 8))
    sh["gdn_norm"] = np.ascontiguousarray(np.asarray(inputs["gdn_norm"], np.float32)[:NL])
    i = np.arange(128, dtype=np.float32)
    sh["cmask"] = np.ascontiguousarray(i[None, :] - i[:, None])
    return sh


def prep_core(inputs, b, NLT):
    m = {}
    x = np.asarray(inputs["x"], np.float32)[b, :NLT * 128]
    ctx = np.asarray(inputs["ctx"], np.float32)[b]
    m["h0"] = np.ascontiguousarray(np.concatenate([x, ctx], 0))
    c = np.asarray(inputs["c"], np.float32)[b]
    cx = np.asarray(inputs["c_ctx"], np.float32)
    m["cc"] = np.ascontiguousarray(np.stack([c, cx], -1).reshape(8, 128, 2).transpose(1, 0, 2))
    return m


def kernel(**inputs):
    NLT, NL = 32, 4
    nc, _ = build(NLT, NL)
    sh = prep_shared(inputs, NLT, NL)
    in_maps = []
    for b in range(8):
        m = dict(sh)
        m.update(prep_core(inputs, b, NLT))
        in_maps.append(m)
    res = run_bass_kernel_spmd(nc, in_maps, core_ids=list(range(8)))
    return np.stack([np.asarray(r["out"], np.float32) for r in res.results], 0)
```

```python
import math
import numpy as np
import ml_dtypes
from contextlib import ExitStack
import concourse.bass as bass
import concourse.mybir as mybir
from concourse.bass_utils import run_bass_kernel_spmd

F32 = mybir.dt.float32
BF16 = mybir.dt.bfloat16
F32R = mybir.dt.float32r
AF = mybir.ActivationFunctionType
ALU = mybir.AluOpType
AX = mybir.AxisListType

ENGS = ["pe", "dve", "act", "pool", "sp"]
DMA_RING = 6
SEM_EPOCH = 20000

D = 1024
DEPTH = 4
ALPHA = (2 * DEPTH) ** 0.25
LN_EPS = 1e-5
IN_W = 3344
NEG = -30000.0


class Sched:
    def __init__(self, nc, es, same_engine_sync=True):
        self.nc = nc
        self.es = es
        self.q = {e: [] for e in ENGS}
        self.cnt = {e: 0 for e in ENGS}
        self.epoch = {e: 0 for e in ENGS}
        self.sem = {e: es.enter_context(nc.semaphore(f"s_{e}_0")) for e in ENGS}
        self.dq = ["sp", "act", "pool"]
        self.dsem = {e: [es.enter_context(nc.semaphore(f"d_{e}_{i}")) for i in range(DMA_RING)]
                     for e in self.dq}
        self.dcnt = {e: 0 for e in self.dq}
        self.lastw = {}
        self.readers = {}
        self.seen = {}
        self.same = same_engine_sync
        self.ninst = 0

    def _key(self, k):
        return k if isinstance(k, str) else k.k

    def _need(self, eng, tok, waits):
        if tok is None:
            return
        sem, val, prod = tok
        if prod == eng and (eng == "pe" or not self.same):
            return
        kk = (eng, id(sem))
        if self.seen.get(kk, 0) >= val:
            return
        self.seen[kk] = val
        waits.append((sem, val))

    def _deps(self, eng, reads, writes):
        waits = []
        for k in reads:
            k = self._key(k)
            self._need(eng, self.lastw.get(k), waits)
            if k.startswith("ps"):
                for t in self.readers.get(k, ()):
                    if t[2] != eng:
                        self._need(eng, t, waits)
        for k in writes:
            k = self._key(k)
            self._need(eng, self.lastw.get(k), waits)
            for t in self.readers.get(k, ()):
                self._need(eng, t, waits)
        return waits

    def _commit(self, tok, reads, writes):
        for k in reads:
            self.readers.setdefault(self._key(k), []).append(tok)
        for k in writes:
            k = self._key(k)
            self.lastw[k] = tok
            self.readers[k] = []

    def op(self, eng, fn, reads=(), writes=()):
        waits = self._deps(eng, reads, writes)
        if self.cnt[eng] >= SEM_EPOCH:
            self.epoch[eng] += 1
            self.sem[eng] = self.es.enter_context(self.nc.semaphore(f"s_{eng}_{self.epoch[eng]}"))
            self.cnt[eng] = 0
        self.cnt[eng] += 1
        tok = (self.sem[eng], self.cnt[eng], eng)
        self.q[eng].append((waits, fn, self.sem[eng], 1))
        self._commit(tok, reads, writes)
        self.ninst += 1

    def dma(self, qn, fn, reads=(), writes=()):
        waits = self._deps(qn, reads, writes)
        i = self.dcnt[qn]
        self.dcnt[qn] += 1
        slot, rnd = i % DMA_RING, i // DMA_RING
        sem = self.dsem[qn][slot]
        if rnd > 0:
            self._need(qn, (sem, 16 * rnd, None), waits)
        tok = (sem, 16 * (rnd + 1), None)
        self.q[qn].append((waits, fn, sem, 16))
        self._commit(tok, reads, writes)
        self.ninst += 1

    def barrier(self):
        toks = []
        for e in ENGS:
            if self.cnt[e] > 0:
                toks.append((self.sem[e], self.cnt[e], e))
        for qn in self.dq:
            n = self.dcnt[qn]
            for slot in range(DMA_RING):
                if n > slot:
                    rounds = (n - 1 - slot) // DMA_RING + 1
                    toks.append((self.dsem[qn][slot], 16 * rounds, None))
        for e in ENGS:
            waits = []
            for t in toks:
                if t[2] == e:
                    continue
                self._need(e, t, waits)
            self.q[e].append((waits, None, None, 0))
        self.lastw = {}
        self.readers = {}

    def emit(self):
        nc = self.nc
        q = self.q

        def run(e, name):
            for waits, fn, sem, inc in q[name]:
                for (s, v) in waits:
                    e.wait_ge(s, v)
                if fn is not None:
                    fn(e).then_inc(sem, inc)

        with nc.Block() as block:
            @block.tensor
            def _(e):
                run(e, "pe")

            @block.vector
            def _(e):
                run(e, "dve")

            @block.scalar
            def _(e):
                run(e, "act")

            @block.gpsimd
            def _(e):
                run(e, "pool")

            @block.sync
            def _(e):
                run(e, "sp")
        self.q = {e: [] for e in ENGS}


class Buf:
    def __init__(self, t, k):
        self.t = t
        self.k = k

    def sub(self, *idx):
        return self.k + "/" + "/".join(str(i) for i in idx)

    def __getitem__(self, key):
        return self.t[key]


class Ctx:
    def __init__(self, nc, es, same_engine_sync=True):
        self.nc = nc
        self.es = es
        self.S = Sched(nc, es, same_engine_sync)
        self.ps = [self.psum(f"ps{i}") for i in range(8)]
        self.psi = 0
        self.uid = 0

    def sb(self, name, shape, dtype=F32, es=None):
        self.uid += 1
        nm = f"{name}_{self.uid}"
        t = (es or self.es).enter_context(self.nc.sbuf_tensor(nm, list(shape), dtype))
        return Buf(t, nm)

    def psum(self, name, shape=(128, 512), dtype=F32):
        t = self.es.enter_context(self.nc.psum_tensor(name, list(shape), dtype))
        return Buf(t, name)

    def dram(self, name, shape, dtype=F32, kind="Internal"):
        t = self.nc.dram_tensor(name, list(shape), dtype, kind=kind)
        return Buf(t, name)

    def nextps(self):
        r = getattr(self, "rot", None) or list(range(8))
        p = self.ps[r[self.psi % len(r)]]
        self.psi += 1
        return p


def fm_groups():
    def swp(cols, blk):
        cols = np.asarray(cols)
        half = blk // 2
        c = cols.reshape(-1, blk)
        return np.concatenate([c[:, half:], c[:, :half]], 1).reshape(-1)
    g = []
    def add_rot(name, start, width, blk, kind):
        for i in range(width // 128):
            cols = np.arange(start + i * 128, start + (i + 1) * 128)
            g.append((f"{name}{i}", cols, kind, swp(cols, blk)))
    add_rot("ret_q", 0, 256, 64, 0)
    add_rot("ret_k", 256, 256, 64, 0)
    add_rot("diff_q", 1024, 256, 32, 1)
    add_rot("diff_k", 1280, 256, 32, 1)
    for i in range(6):
        g.append((f"gdn{i}", np.arange(1792 + i * 128, 1792 + (i + 1) * 128), None, None))
    add_rot("swa_q", 2832, 256, 64, 2)
    add_rot("swa_k", 3088, 128, 64, 2)
    return g


TM_GROUPS = [("ret_vg", 512, 512), ("diff_v", 1536, 256), ("gdn_gab", 2560, 272), ("swa_v", 3216, 128)]


def build_w_in_ext(w_in_l):
    cols = []
    for name, c, kind, sw in fm_groups():
        cols.append(c)
        if kind is not None:
            cols.append(sw)
    for name, st, w in TM_GROUPS:
        cols.append(np.arange(st, st + w))
    cols = np.concatenate(cols)
    return np.ascontiguousarray(w_in_l[:, cols])


def rope_tables(NLT):
    n = NLT * 128
    T = n + 256
    tabs = np.zeros((3, 2, 128, T), np.float32)
    tabs[:, 0] = 1.0
    f32 = np.float32
    theta = (1.0 / (f32(10000.0) ** np.linspace(0.0, 1.0, 32, dtype=np.float32))).astype(np.float32)
    ang_ret = (np.arange(n, dtype=np.float32)[:, None] * theta).astype(np.float32)
    def axial(rot_dim):
        rows = n // 64
        row = np.repeat(np.arange(rows, dtype=np.float32), 64)
        col = np.tile(np.arange(64, dtype=np.float32), rows)
        nf = rot_dim // 4
        inv = (f32(10000.0) ** (-np.arange(nf, dtype=np.float32) / f32(nf))).astype(np.float32)
        return np.concatenate([row[:, None] * inv, col[:, None] * inv], -1).astype(np.float32)
    ang_diff = axial(32)
    ang_swa = axial(64)
    for kind, (ang, blk) in enumerate([(ang_ret, 64), (ang_diff, 32), (ang_swa, 64)]):
        half = blk // 2
        for f in range(128):
            d = f % blk
            j = d % half
            c = np.cos(ang[:, j]).astype(np.float32)
            s = np.sin(ang[:, j]).astype(np.float32)
            tabs[kind, 0, f, :n] = c
            tabs[kind, 1, f, :n] = -s if d < half else s
    return tabs


def build(NLT=32, NL=4, dbg=(), phases="A,BC,ret,diff,swa,gdn,E,F,G"):
    nc = bass.Bass("TRN2", target_bir_lowering=False)
    NT = NLT + 2
    T = NT * 128
    NLAT = NLT * 128
    FMG = fm_groups()
    NFM = len(FMG)
    NWE = sum(128 * (2 if g[2] is not None else 1) for g in FMG) + sum(w for _, _, w in TM_GROUPS)

    def din(name, shape, dt=F32):
        return nc.dram_tensor(name, list(shape), dt, kind="ExternalInput")

    h0 = din("h0", [T, D])
    cc = din("cc", [128, 8, 2])
    ada_w = din("ada_w", [NL, D, 6 * D])
    ada_b = din("ada_b", [NL, 6 * D])
    ada_bT = din("ada_bT", [NL, 128, 48])
    w_in = din("w_in", [NL, D, NWE])
    w_out = din("w_out", [NL, D, D])
    rope = din("rope", [3, 2, 128, T])
    ret_decay = din("ret_decay", [NL, 8])
    ln_g = din("ln_g", [NL, 2 * D])
    ln_b = din("ln_b", [NL, 2 * D])
    cmask = din("cmask", [128, 128])
    diff_lambda = din("diff_lambda", [NL, 128])
    diff_norm = din("diff_norm", [NL, 64])
    swa_sink = din("swa_sink", [NL, 4])
    router_wp = din("router_wp", [NL, D, 128])
    w_gate = din("w_gate", [NL, 16, D, 2 * D])
    w_up = din("w_up", [NL, 16, D, 2 * D])
    w_down = din("w_down", [NL, 16, 2 * D, D])
    iota512 = din("iota512", [128, 512])
    jcol_in = din("jcol_in", [128, 4])
    selE_in = din("selE_in", [16, 16, 128])
    gdn_convT = din("gdn_convT", [NL, 128, 18])
    gdn_a_log = din("gdn_a_log", [NL, 8])
    gdn_dt_bias = din("gdn_dt_bias", [NL, 8])
    gdn_norm = din("gdn_norm", [NL, 64])
    out = nc.dram_tensor("out", [NLAT, D], F32, kind="ExternalOutput")

    dbg_t = {}

    with ExitStack() as es:
        C = Ctx(nc, es)
        S = C.S
        H = C.dram("H", [T, D], kind=("ExternalOutput" if "H" in dbg else "Internal"))
        FMO = C.dram("FMO", [NFM, 128, T], BF16)
        TMA = C.dram("TMA", [T, 512], BF16)
        TMB = C.dram("TMB", [T, 256], BF16)
        TMC = C.dram("TMC", [T, 256], BF16)
        GAB = C.dram("GAB", [128, NT * 16], F32)
        TMD = C.dram("TMD", [T, 128], BF16)
        GFM = C.dram("GFM", [4, 128, T], BF16)
        GTM = C.dram("GTM", [T, 512], BF16)
        U2 = C.dram("U2", [T, D], BF16)
        LGD = C.dram("LGD", [16, T], F32)
        YE = C.dram("YE", [16, 640, D], BF16)
        KTD = C.dram("KTD", [128, NT * 16], F32)
        KEYD = C.dram("KEYD", [16, T], F32)
        AFFD = C.dram("AFFD", [16, T], F32)
        Y = C.dram("Y", [T, D], BF16, kind=("ExternalOutput" if "Y" in dbg else "Internal"))
        if "FMO" in dbg:
            dbg_t["FMO"] = FMO
        fm_index = {g[0]: i for i, g in enumerate(FMG)}

        ident_f = C.sb("ident_f", [128, 128])
        ident_b = C.sb("ident_b", [128, 128], BF16)
        condT = C.sb("condT", [128, 8, 2])
        condB = [C.sb(f"condB{i}", [128, 8, 128]) for i in range(2)]
        modT = C.sb("modT", [128, 8, 4, 2])
        gateB = C.sb("gateB", [128, 4, 2, D])
        relpos = C.sb("relpos", [128, 128])

        S.op("pool", lambda e: e.memset(ident_f[:], 0.0), [], [ident_f])
        S.op("pool", lambda e: e.affine_select(out=ident_f[:], in_=ident_f[:], pattern=[[-1, 128]],
                                               compare_op=ALU.not_equal, fill=1.0, base=0, channel_multiplier=1),
             [ident_f], [ident_f])
        S.op("dve", lambda e: e.tensor_copy(out=ident_b[:], in_=ident_f[:]), [ident_f], [ident_b])
        S.dma("sp", lambda e: e.dma_start(out=condT[:], in_=cc[:, :, :]), [], [condT])
        S.dma("sp", lambda e: e.dma_start(out=relpos[:], in_=cmask[:, :]), [], [relpos])
        S.op("act", lambda e: e.activation(out=condT[:], in_=condT[:], func=AF.Silu), [condT], [condT])
        for lc in range(2):
            for k in range(8):
                S.op("dve", lambda e, lc=lc, k=k: e.tensor_copy(
                    out=condB[lc][:, k, :], in_=condT[:, k, lc:lc + 1].broadcast_to([128, 128])),
                    [condT], [condB[lc]])
        S.barrier()
        S.emit()

        def phase_A(l):
            with ExitStack() as pes:
                wa = [C.sb(f"wa{i}", [128, 8, 512], es=pes) for i in range(2)]
                bbc = [C.sb(f"bbc{i}", [128, 512], es=pes) for i in range(2)]
                bT = C.sb("bT", [128, 48], es=pes)
                S.dma("sp", lambda e: e.dma_start(out=bT[:], in_=ada_bT[l, :, :]), [], [bT])
                import os
                ADBG = int(os.environ.get("ADBG", 9))
                for g in range(12):
                    m, half = g // 2, g % 2
                    w = wa[g % 2]
                    S.dma("sp", lambda e, w=w, g=g: e.dma_start(
                        out=w[:], in_=ada_w[l, :, g * 512:(g + 1) * 512].rearrange("(k p) n -> p k n", p=128)),
                        [], [w])
                    if m in (2, 5, 3, 4):
                        mi = {2: 0, 5: 1, 3: 2, 4: 3}[m]
                        bb = bbc[g % 2]
                        S.dma("act", lambda e, bb=bb, g=g: e.dma_start(
                            out=bb[:], in_=ada_b[l:l + 1, g * 512:(g + 1) * 512].broadcast_to([128, 512])), [], [bb])
                        if m == 4:
                            S.op("dve", lambda e, bb=bb: e.tensor_scalar_add(out=bb[:], in0=bb[:], scalar1=1.0), [bb], [bb])
                        for lc in range(2):
                            ps = C.nextps()
                            for k in range(8):
                                S.op("pe", lambda e, ps=ps, w=w, lc=lc, k=k: e.matmul(
                                    ps[:, :], lhsT=condB[lc][:, k, :], rhs=w[:, k, :], start=(k == 0), stop=(k == 7)),
                                    [condB[lc], w], [ps])
                            S.op("dve", lambda e, ps=ps, bb=bb, mi=mi, lc=lc, half=half: e.tensor_tensor(
                                out=gateB[:, mi, lc, half * 512:(half + 1) * 512], in0=ps[:, :], in1=bb[:], op=ALU.add),
                                [ps, bb], [gateB])
                    if m in (0, 1, 3, 4):
                        mi = {0: 0, 1: 1, 3: 2, 4: 3}[m]
                        for j in range(4):
                            kk = half * 4 + j
                            ps = C.nextps()
                            for lc in range(2):
                                for k in range(8):
                                    S.op("pe", lambda e, ps=ps, w=w, j=j, k=k, lc=lc: e.matmul(
                                        ps[:, lc * 128:(lc + 1) * 128], lhsT=w[:, k, j * 128:(j + 1) * 128],
                                        rhs=condB[lc][:, k, :], start=(k == 0), stop=(k == 7)), [condB[lc], w], [ps])
                            addone = 1.0 if m in (1, 4) else 0.0
                            for lc in range(2):
                                S.op("dve", lambda e, ps=ps, kk=kk, mi=mi, m=m, addone=addone, lc=lc: e.tensor_scalar(
                                    out=modT[:, kk, mi, lc:lc + 1], in0=ps[:, lc * 128:lc * 128 + 1],
                                    scalar1=bT[:, m * 8 + kk:m * 8 + kk + 1],
                                    scalar2=addone, op0=ALU.add, op1=ALU.add), [ps, bT], [modT])
                S.barrier()
                S.emit()

        def phase_BC(l):
            src = h0 if l == 0 else H.t
            with ExitStack() as pes:
                uT = C.sb("uT", [128, 8, T], BF16, es=pes)
                wsb = C.sb("wsb", [128, 8, NWE], BF16, es=pes)
                bes = ExitStack()
                hts = [C.sb(f"ht{i}", [128, D], es=bes) for i in range(2)]
                for k in range(8):
                    S.dma("pool", lambda e, k=k: e.dma_start(out=wsb[:, k, :], in_=w_in[l, k * 128:(k + 1) * 128, :]),
                          [], [wsb.sub(k)])
                wkeys = [wsb.sub(k) for k in range(8)]
                for t in range(NT):
                    lc = 0 if t < NLT else 1
                    ht = hts[t % 2]
                    S.dma("sp", lambda e, ht=ht, t=t: e.dma_start(out=ht[:], in_=src[t * 128:(t + 1) * 128, :]),
                          ["H/%d" % t], [ht])
                    for half in range(2):
                        ps = C.nextps()
                        for j in range(4):
                            k = half * 4 + j
                            S.op("pe", lambda e, ps=ps, ht=ht, j=j, k=k: e.transpose(
                                ps[:, j * 128:(j + 1) * 128], ht[:, k * 128:(k + 1) * 128], ident_f[:]),
                                [ht, ident_f], [ps])
                        for j in range(4):
                            k = half * 4 + j
                            S.op("act", lambda e, ps=ps, j=j, k=k, t=t, lc=lc: e.activation(
                                out=uT[:, k, t * 128:(t + 1) * 128], in_=ps[:, j * 128:(j + 1) * 128], func=AF.Identity,
                                bias=modT[:, k, 0, lc:lc + 1], scale=modT[:, k, 1, lc:lc + 1]),
                                [ps, modT], [uT.sub(t)])
                S.barrier()
                S.emit()
                bes.close()
                import os
                BDBG = int(os.environ.get("BDBG", 9))
                tabs1 = [C.sb(f"tab_{kd}", [128, 2, 512], es=pes) for kd in range(3)]
                tabs = [tabs1, tabs1]
                evs = [C.sb(f"ev{i}", [128, 512], BF16, es=pes) for i in range(3)]
                tmp = [C.sb(f"tmp{i}", [128, 512], es=pes) for i in range(2)]
                tmo = [C.sb(f"tmo{i}", [128, 512], BF16, es=pes) for i in range(2)]
                gabAll = C.sb("gabAll", [128, NT * 16], es=pes)
                offs = {}
                o = 0
                for name, c, kind, sw in FMG:
                    offs[name] = o
                    o += 128 * (2 if kind is not None else 1)
                for name, st, w in TM_GROUPS:
                    offs[name] = o
                    o += w
                nblk = (T + 511) // 512
                evc = 0
                for b in range(nblk if BDBG >= 2 else 0):
                    t0 = b * 512
                    tw = min(512, T - t0)
                    tiles = list(range(t0 // 128, (t0 + tw) // 128))
                    ukeys = [uT.sub(t) for t in tiles]
                    tb = tabs[b % 2]
                    for kd in range(3):
                        S.dma("act", lambda e, tb=tb, kd=kd, t0=t0, tw=tw: e.dma_start(
                            out=tb[kd][:, :, 0:tw], in_=rope[kd, :, :, t0:t0 + tw].rearrange("c p t -> p c t")),
                            [], [tb[kd]])
                    for gi, (name, c, kind, sw) in enumerate(FMG):
                        co = offs[name]
                        psa = C.nextps()
                        for k in range(8):
                            S.op("pe", lambda e, psa=psa, k=k, co=co, t0=t0, tw=tw: e.matmul(
                                psa[:, 0:tw], lhsT=wsb[:, k, co:co + 128], rhs=uT[:, k, t0:t0 + tw],
                                start=(k == 0), stop=(k == 7)), wkeys + ukeys, [psa])
                        ev = evs[evc % 3]
                        evc += 1
                        if kind is None:
                            S.op("act", lambda e, ev=ev, psa=psa, tw=tw: e.copy(out=ev[:, 0:tw], in_=psa[:, 0:tw]),
                                 [psa], [ev])
                        else:
                            psb = C.nextps()
                            for k in range(8):
                                S.op("pe", lambda e, psb=psb, k=k, co=co, t0=t0, tw=tw: e.matmul(
                                    psb[:, 0:tw], lhsT=wsb[:, k, co + 128:co + 256], rhs=uT[:, k, t0:t0 + tw],
                                    start=(k == 0), stop=(k == 7)), wkeys + ukeys, [psb])
                            ta, tbb = tmp
                            S.op("dve", lambda e, ta=ta, psa=psa, tb=tb, kind=kind, tw=tw: e.tensor_tensor(
                                out=ta[:, 0:tw], in0=psa[:, 0:tw], in1=tb[kind][:, 0, 0:tw], op=ALU.mult),
                                [psa, tb[kind]], [ta])
                            S.op("dve", lambda e, tbb=tbb, psb=psb, tb=tb, kind=kind, tw=tw: e.tensor_tensor(
                                out=tbb[:, 0:tw], in0=psb[:, 0:tw], in1=tb[kind][:, 1, 0:tw], op=ALU.mult),
                                [psb, tb[kind]], [tbb])
                            S.op("pool", lambda e, ev=ev, ta=ta, tbb=tbb, tw=tw: e.tensor_tensor(
                                out=ev[:, 0:tw], in0=ta[:, 0:tw], in1=tbb[:, 0:tw], op=ALU.add), [ta, tbb], [ev])
                        S.dma("sp", lambda e, ev=ev, gi=gi, t0=t0, tw=tw: e.dma_start(
                            out=FMO.t[gi, :, t0:t0 + tw], in_=ev[:, 0:tw]), [ev], ["FMO/%d/%d" % (gi, b)])
                    for t in (tiles if BDBG >= 3 else []):
                        for (name, st, w), dst in zip(TM_GROUPS, [TMA, TMB, TMC, TMD]):
                            co = offs[name]
                            ps = C.nextps()
                            for k in range(8):
                                S.op("pe", lambda e, ps=ps, k=k, co=co, w=w, t=t: e.matmul(
                                    ps[:, 0:w], lhsT=uT[:, k, t * 128:(t + 1) * 128], rhs=wsb[:, k, co:co + w],
                                    start=(k == 0), stop=(k == 7)), wkeys + [uT.sub(t)], [ps])
                            ob = tmo[evc % 2]
                            evc += 1
                            wd = min(w, 256) if name == "gdn_gab" else w
                            S.op("act", lambda e, ob=ob, ps=ps, wd=wd: e.copy(out=ob[:, 0:wd], in_=ps[:, 0:wd]),
                                 [ps], [ob])
                            S.dma("sp", lambda e, ob=ob, dst=dst, t=t, wd=wd: e.dma_start(
                                out=dst.t[t * 128:(t + 1) * 128, 0:wd], in_=ob[:, 0:wd]), [ob],
                                [dst.sub(t)])
                            if name == "gdn_gab" and BDBG >= 4:
                                S.op("dve", lambda e, ps=ps, t=t: e.tensor_copy(out=gabAll[:, t * 16:(t + 1) * 16], in_=ps[:, 256:272]),
                                     [ps], [gabAll])
                if BDBG >= 4:
                    S.dma("sp", lambda e: e.dma_start(out=GAB.t[:, :], in_=gabAll[:]), [gabAll], [GAB])
                S.barrier()
                S.emit()

        def phase_ret(l, full_ctx):
            gq = fm_index["ret_q0"]
            gk = fm_index["ret_k0"]
            seq_f = [NLT, NLT + 1] + list(range(NLT))
            with ExitStack() as pes:
                rd = C.sb("rd", [128, 8], es=pes)
                lg = C.sb("lg", [128, 8], es=pes)
                MT = C.sb("MT", [128, 4, 128], es=pes)
                mtmp = C.sb("mtmp", [128, 128], es=pes)
                mtmp2 = C.sb("mtmp2", [128, 128], es=pes)
                pcol = C.sb("pcol", [128, 4], es=pes)
                kdec = C.sb("kdec", [128, 2, 256], es=pes)
                qdec = C.sb("qdec", [128, 2, 256], es=pes)
                cdec = C.sb("cdec", [128, 2, 256], es=pes)
                dcol = C.sb("dcol", [128, 8], es=pes)
                S.dma("sp", lambda e: e.dma_start(out=rd[:], in_=ret_decay[l:l + 1, :].broadcast_to([128, 8])), [], [rd])
                S.op("act", lambda e: e.activation(out=lg[:], in_=rd[:], func=AF.Exp, scale=-1.0), [rd], [lg])
                S.op("act", lambda e: e.activation(out=lg[:], in_=lg[:], func=AF.Ln, bias=1.0), [lg], [lg])
                S.op("dve", lambda e: e.tensor_scalar_mul(out=lg[:], in0=lg[:], scalar1=-1.0), [lg], [lg])
                S.op("dve", lambda e: e.tensor_scalar(out=pcol[:, 3:4], in0=relpos[:, 0:1], scalar1=-1.0, scalar2=None,
                                                      op0=ALU.mult), [relpos], [pcol])
                S.op("dve", lambda e: e.tensor_scalar_add(out=pcol[:, 0:1], in0=pcol[:, 3:4], scalar1=1.0), [pcol], [pcol])
                S.op("dve", lambda e: e.tensor_scalar(out=pcol[:, 1:2], in0=pcol[:, 3:4], scalar1=-1.0, scalar2=128.0,
                                                      op0=ALU.mult, op1=ALU.add), [pcol], [pcol])
                S.op("dve", lambda e: e.tensor_scalar(out=pcol[:, 2:3], in0=pcol[:, 3:4], scalar1=-1.0, scalar2=127.0,
                                                      op0=ALU.mult, op1=ALU.add), [pcol], [pcol])
                for dr in range(2):
                    for h in range(4):
                        c = dr * 4 + h
                        if dr == 0:
                            S.op("dve", lambda e: e.tensor_scalar_max(out=mtmp[:], in0=relpos[:], scalar1=0.0), [relpos], [mtmp])
                        else:
                            S.op("dve", lambda e: e.tensor_scalar(out=mtmp[:], in0=relpos[:], scalar1=-1.0, scalar2=0.0,
                                                                  op0=ALU.mult, op1=ALU.max), [relpos], [mtmp])
                        S.op("act", lambda e, c=c: e.activation(out=mtmp[:], in_=mtmp[:], func=AF.Exp, scale=lg[:, c:c + 1]),
                             [mtmp, lg], [mtmp])
                        if dr == 0:
                            S.op("dve", lambda e: e.tensor_scalar(out=mtmp2[:], in0=relpos[:], scalar1=0.0, scalar2=0.125,
                                                                  op0=ALU.is_ge, op1=ALU.mult), [relpos], [mtmp2])
                            S.op("dve", lambda e, h=h: e.tensor_tensor(out=MT[:, h, :], in0=mtmp[:], in1=mtmp2[:], op=ALU.mult),
                                 [mtmp, mtmp2], [MT])
                        else:
                            S.op("dve", lambda e: e.tensor_scalar(out=mtmp2[:], in0=relpos[:], scalar1=0.0, scalar2=0.125,
                                                                  op0=ALU.is_le, op1=ALU.mult), [relpos], [mtmp2])
                            S.op("dve", lambda e: e.tensor_tensor(out=mtmp[:], in0=mtmp[:], in1=mtmp2[:], op=ALU.mult),
                                 [mtmp, mtmp2], [mtmp])
                            S.op("dve", lambda e, h=h: e.tensor_tensor(out=MT[:, h, :], in0=MT[:, h, :], in1=mtmp[:], op=ALU.add),
                                 [mtmp, MT], [MT])
                        S.op("act", lambda e, c=c, dr=dr: e.activation(out=dcol[:, 0:1], in_=pcol[:, (0 if dr == 0 else 1):(1 if dr == 0 else 2)],
                                                                        func=AF.Exp, scale=lg[:, c:c + 1]), [pcol, lg], [dcol])
                        S.op("dve", lambda e, dr=dr, h=h: e.tensor_scalar_mul(
                            out=qdec[:, dr, h * 64:(h + 1) * 64], in0=dcol[:, 0:1].broadcast_to([128, 64]), scalar1=0.125),
                            [dcol], [qdec])
                        S.op("act", lambda e, c=c, dr=dr: e.activation(out=dcol[:, 1:2], in_=pcol[:, (2 if dr == 0 else 3):(3 if dr == 0 else 4)],
                                                                        func=AF.Exp, scale=lg[:, c:c + 1]), [pcol, lg], [dcol])
                        S.op("dve", lambda e, dr=dr, h=h: e.tensor_copy(
                            out=kdec[:, dr, h * 64:(h + 1) * 64], in_=dcol[:, 1:2].broadcast_to([128, 64])), [dcol], [kdec])
                        S.op("act", lambda e, c=c: e.activation(out=dcol[:, 2:3], in_=lg[:, c:c + 1], func=AF.Exp, scale=128.0),
                             [lg], [dcol])
                        S.op("dve", lambda e, dr=dr, h=h: e.tensor_copy(
                            out=cdec[:, dr, h * 64:(h + 1) * 64], in_=dcol[:, 2:3].broadcast_to([128, 64])), [dcol], [cdec])
                import os
                RDBG = int(os.environ.get("RDBG", 9))
                Sprev = C.sb("Sprev", [64, 2, NT, 256], BF16, es=pes)
                dSb = C.sb("dSb", [64, NT, 256], es=pes)
                Srun = C.sb("Srun", [64, 2, 256], es=pes)
                kts = [C.sb(f"kT{i}", [64, 4, 128], BF16, es=pes) for i in range(2)]
                qts = [C.sb(f"qT{i}", [64, 4, 128], BF16, es=pes) for i in range(2)]
                vgs = [C.sb(f"vg{i}", [128, 512], BF16, es=pes) for i in range(2)]
                ktok = [C.sb(f"ktok{i}", [128, 256], BF16, es=pes) for i in range(2)]
                kd = [C.sb(f"kd{i}", [128, 2, 256], BF16, es=pes) for i in range(2)]
                S.op("pool", lambda e: e.memset(Srun[:], 0.0), [], [Srun])

                def load_k(c, i):
                    S.dma("sp", lambda e: e.dma_start(
                        out=kts[i][:], in_=FMO.t[gk:gk + 2, :, c * 128:(c + 1) * 128].rearrange("g (h d) t -> d (g h) t", h=2)),
                        [f"FMO/{gk}/{c // 4}", f"FMO/{gk + 1}/{c // 4}"], [kts[i]])

                def make_ktok(c, i):
                    ps = C.nextps()
                    pv = ps.t[:, 0:128].bitcast(BF16)
                    for h in range(4):
                        S.op("pe", lambda e, pv=pv, h=h: e.transpose(pv[:, h * 64:(h + 1) * 64], kts[i][:, h, :], ident_b[0:64, 0:64]),
                             [kts[i], ident_b], [ps])
                    S.op("act", lambda e, pv=pv: e.copy(out=ktok[i][:], in_=pv[:, :]), [ps], [ktok[i]])

                R2 = int(os.environ.get("R2", 9))
                for n_, c in enumerate(seq_f if RDBG >= 2 else []):
                    i = n_ % 2
                    load_k(c, i)
                    S.dma("act", lambda e, c=c, i=i: e.dma_start(out=vgs[i][:], in_=TMA.t[c * 128:(c + 1) * 128, :]),
                          [TMA.sub(c)], [vgs[i]])
                    if R2 < 2:
                        continue
                    make_ktok(c, i)
                    if R2 < 3:
                        continue
                    S.op("dve", lambda e, i=i: e.tensor_tensor(
                        out=kd[i][:], in0=ktok[i][:].rearrange("p (o c) -> p o c", o=1).broadcast_to([128, 2, 256]),
                        in1=kdec[:], op=ALU.mult), [ktok[i], kdec], [kd[i]])
                    if R2 < 4:
                        continue
                    ps = C.nextps()
                    for dr in range(2):
                        for h in range(4):
                            S.op("pe", lambda e, ps=ps, dr=dr, h=h, i=i: e.matmul(
                                ps[0:64, dr * 256 + h * 64:dr * 256 + (h + 1) * 64], lhsT=kd[i][:, dr, h * 64:(h + 1) * 64],
                                rhs=vgs[i][:, h * 64:(h + 1) * 64], start=True, stop=True), [kd[i], vgs[i]], [ps])
                    if R2 < 5:
                        continue
                    S.op("act", lambda e, c=c: e.copy(out=Sprev[:, 0, c, :], in_=Srun[:, 0, :]), [Srun], [Sprev.sub(0, c)])
                    if R2 < 6:
                        continue
                    S.op("dve", lambda e: e.tensor_tensor(out=Srun[:, 0, :], in0=Srun[:, 0, :], in1=cdec[0:64, 0, :], op=ALU.mult),
                         [Srun, cdec], [Srun])
                    if R2 < 7:
                        continue
                    S.op("dve", lambda e, ps=ps: e.tensor_tensor(out=Srun[:, 0, :], in0=Srun[:, 0, :], in1=ps[0:64, 0:256], op=ALU.add),
                         [Srun, ps], [Srun])
                    if R2 < 8:
                        continue
                    S.op("dve", lambda e, ps=ps, c=c: e.tensor_copy(out=dSb[:, c, :], in_=ps[0:64, 256:512]), [ps], [dSb.sub(c)])
                seq_b = [NLT + 1, NLT] + list(range(NLT - 1, -1, -1))
                for c in (seq_b if RDBG >= 3 else []):
                    S.op("act", lambda e, c=c: e.copy(out=Sprev[:, 1, c, :], in_=Srun[:, 1, :]), [Srun], [Sprev.sub(1, c)])
                    S.op("dve", lambda e: e.tensor_tensor(out=Srun[:, 1, :], in0=Srun[:, 1, :], in1=cdec[0:64, 1, :], op=ALU.mult),
                         [Srun, cdec], [Srun])
                    S.op("dve", lambda e, c=c: e.tensor_tensor(out=Srun[:, 1, :], in0=Srun[:, 1, :], in1=dSb[:, c, :], op=ALU.add),
                         [Srun, dSb.sub(c)], [Srun])
                AMs = [C.sb(f"AM{i}", [128, 4, 128], BF16, es=pes) for i in range(2)]
                osum = C.sb("osum", [128, 256], es=pes)
                t1 = C.sb("t1", [128, 256], es=pes)
                sq = C.sb("sq", [128, 256], es=pes)
                ss = C.sb("ss", [128, 4], es=pes)
                sg = C.sb("sg", [128, 256], es=pes)
                ys = [C.sb(f"y{i}", [128, 256], BF16, es=pes) for i in range(2)]
                chunks = list(range(NT)) if full_ctx else list(range(NLT))
                if RDBG < 4:
                    chunks = []
                for n_, c in enumerate(chunks):
                    i = n_ % 2
                    load_k(c, i)
                    S.dma("sp", lambda e, c=c, i=i: e.dma_start(
                        out=qts[i][:], in_=FMO.t[gq:gq + 2, :, c * 128:(c + 1) * 128].rearrange("g (h d) t -> d (g h) t", h=2)),
                        [f"FMO/{gq}/{c // 4}", f"FMO/{gq + 1}/{c // 4}"], [qts[i]])
                    S.dma("act", lambda e, c=c, i=i: e.dma_start(out=vgs[i][:], in_=TMA.t[c * 128:(c + 1) * 128, :]),
                          [TMA.sub(c)], [vgs[i]])
                    psA = C.nextps()
                    for h in range(4):
                        S.op("pe", lambda e, psA=psA, h=h, i=i: e.matmul(
                            psA[:, h * 128:(h + 1) * 128], lhsT=kts[i][:, h, :], rhs=qts[i][:, h, :], start=True, stop=True),
                            [kts[i], qts[i]], [psA])
                    S.op("dve", lambda e, psA=psA, i=i: e.tensor_tensor(
                        out=AMs[i][:].rearrange("p h t -> p (h t)"), in0=psA[:, :], in1=MT[:].rearrange("p h t -> p (h t)"),
                        op=ALU.mult), [psA, MT], [AMs[i]])
                    psO = C.nextps()
                    psX = C.nextps()
                    for h in range(4):
                        S.op("pe", lambda e, psO=psO, h=h, i=i: e.matmul(
                            psO[:, h * 64:(h + 1) * 64], lhsT=AMs[i][:, h, :], rhs=vgs[i][:, h * 64:(h + 1) * 64],
                            start=True, stop=True), [AMs[i], vgs[i]], [psO])
                        S.op("pe", lambda e, psO=psO, h=h, i=i, c=c: e.matmul(
                            psO[:, 256 + h * 64:256 + (h + 1) * 64], lhsT=qts[i][:, h, :], rhs=Sprev[:, 0, c, h * 64:(h + 1) * 64],
                            start=True, stop=True), [qts[i], Sprev.sub(0, c)], [psO])
                        S.op("pe", lambda e, psX=psX, h=h, i=i, c=c: e.matmul(
                            psX[:, h * 64:(h + 1) * 64], lhsT=qts[i][:, h, :], rhs=Sprev[:, 1, c, h * 64:(h + 1) * 64],
                            start=True, stop=True), [qts[i], Sprev.sub(1, c)], [psX])
                    S.op("dve", lambda e, psO=psO: e.tensor_tensor(out=t1[:], in0=psO[:, 256:512], in1=qdec[:, 0, :], op=ALU.mult),
                         [psO, qdec], [t1])
                    S.op("dve", lambda e, psO=psO: e.tensor_tensor(out=osum[:], in0=psO[:, 0:256], in1=t1[:], op=ALU.add),
                         [psO, t1], [osum])
                    S.op("dve", lambda e, psX=psX: e.tensor_tensor(out=t1[:], in0=psX[:, 0:256], in1=qdec[:, 1, :], op=ALU.mult),
                         [psX, qdec], [t1])
                    S.op("dve", lambda e: e.tensor_tensor(out=osum[:], in0=osum[:], in1=t1[:], op=ALU.add), [osum, t1], [osum])
                    S.op("act", lambda e: e.activation(out=sq[:], in_=osum[:], func=AF.Square), [osum], [sq])
                    S.op("dve", lambda e: e.tensor_reduce(out=ss[:], in_=sq[:].rearrange("p (h d) -> p h d", h=4), axis=AX.X, op=ALU.add),
                         [sq], [ss])
                    S.op("dve", lambda e: e.tensor_scalar(out=ss[:], in0=ss[:], scalar1=1.0 / 64, scalar2=1e-6, op0=ALU.mult, op1=ALU.add),
                         [ss], [ss])
                    S.op("act", lambda e: e.activation(out=ss[:], in_=ss[:], func=AF.Sqrt), [ss], [ss])
                    S.op("dve", lambda e: e.reciprocal(out=ss[:], in_=ss[:]), [ss], [ss])
                    S.op("act", lambda e, i=i: e.activation(out=sg[:], in_=vgs[i][:, 256:512], func=AF.Silu), [vgs[i]], [sg])
                    S.op("dve", lambda e: e.tensor_tensor(
                        out=osum[:].rearrange("p (h d) -> p h d", h=4), in0=osum[:].rearrange("p (h d) -> p h d", h=4),
                        in1=ss[:].rearrange("p (h o) -> p h o", o=1).broadcast_to([128, 4, 64]), op=ALU.mult), [osum, ss], [osum])
                    S.op("dve", lambda e, i=i: e.tensor_tensor(out=ys[i][:], in0=osum[:], in1=sg[:], op=ALU.mult), [osum, sg], [ys[i]])
                    S.dma("sp", lambda e, i=i, c=c: e.dma_start(out=Y.t[c * 128:(c + 1) * 128, 0:256], in_=ys[i][:]),
                          [ys[i]], [Y.sub(c, 0)])
                S.barrier()
                S.emit()

        def phase_diff(l, full_ctx):
            gq = fm_index["diff_q0"]
            gk = fm_index["diff_k0"]
            lam_init = 0.8 - 0.6 * math.exp(-0.3 * l)
            scale = 32 ** -0.5
            with ExitStack() as pes:
                C.rot = [0, 1, 2]
                kT8 = C.sb("kT8", [32, 8, T], BF16, es=pes)
                V1 = C.sb("V1", [128, NT, 4, 65], BF16, es=pes)
                qT8s = [C.sb(f"qT8_{i}", [32, 8, 512], BF16, es=pes) for i in range(2)]
                Es = [C.sb(f"E{i}", [128, 512], BF16, es=pes) for i in range(3)]
                dl = C.sb("dl", [128, 128], es=pes)
                dl2 = C.sb("dl2", [128, 2], es=pes)
                nlam = C.sb("nlam", [128, 1], es=pes)
                gn = C.sb("gn", [128, 64], es=pes)
                od = C.sb("od", [128, 4, 256], es=pes)
                oT = [C.sb(f"oT{i}", [65, 512], es=pes) for i in range(2)]
                rr = C.sb("rr", [128, 4], es=pes)
                a_ = C.sb("a_", [128, 64], es=pes)
                sq = C.sb("dsq", [128, 4, 256], es=pes)
                ss = C.sb("dss", [128, 16], es=pes)
                yo = [C.sb(f"dy{i}", [128, 4, 256], BF16, es=pes) for i in range(2)]
                S.dma("sp", lambda e: e.dma_start(out=dl[:], in_=diff_lambda[l:l + 1, :].broadcast_to([128, 128])), [], [dl])
                S.dma("sp", lambda e: e.dma_start(out=gn[:], in_=diff_norm[l:l + 1, :].broadcast_to([128, 64])), [], [gn])
                dl4 = dl[:].rearrange("p (a b d) -> p a b d", a=2, b=2)
                S.op("dve", lambda e: e.tensor_tensor(out=dl4[:, :, 0, :], in0=dl4[:, :, 0, :], in1=dl4[:, :, 1, :], op=ALU.mult), [dl], [dl])
                S.op("dve", lambda e: e.tensor_reduce(out=dl2[:], in_=dl4[:, :, 0, :], axis=AX.X, op=ALU.add), [dl], [dl2])
                S.op("act", lambda e: e.activation(out=dl2[:], in_=dl2[:], func=AF.Exp), [dl2], [dl2])
                S.op("dve", lambda e: e.tensor_tensor(out=nlam[:], in0=dl2[:, 1:2], in1=dl2[:, 0:1], op=ALU.subtract), [dl2], [nlam])
                S.op("dve", lambda e: e.tensor_scalar_add(out=nlam[:], in0=nlam[:], scalar1=-lam_init), [nlam], [nlam])
                S.op("dve", lambda e: e.tensor_scalar_mul(out=gn[:], in0=gn[:], scalar1=(1.0 - lam_init)), [gn], [gn])
                S.dma("sp", lambda e: e.dma_start(out=kT8[:], in_=FMO.t[gk:gk + 2, :, :].rearrange("g (x d) t -> d (g x) t", d=32)),
                      [], [kT8])
                S.op("pool", lambda e: e.memset(V1[:], 1.0), [], [V1])
                for h in range(4):
                    for n0 in range(0, NT, 8):
                        n1 = min(NT, n0 + 8)
                        S.dma("act", lambda e, h=h, n0=n0, n1=n1: e.dma_start(
                            out=V1[:, n0:n1, h, 0:64],
                            in_=TMB.t[n0 * 128:n1 * 128, h * 64:(h + 1) * 64].rearrange("(n p) e -> p n e", p=128)), [V1], [V1])
                blocks = [(b * 512, 512, list(range(NT))) for b in range(NLAT // 512)]
                if full_ctx:
                    blocks.append((NLAT, 256, [NLT, NLT + 1]))
                ec = 0
                for bi, (q0, qw, ktiles) in enumerate(blocks):
                    qT8 = qT8s[bi % 2]
                    nqs = qw // 128
                    S.dma("sp", lambda e, qT8=qT8, q0=q0, qw=qw: e.dma_start(
                        out=qT8[:, :, 0:qw], in_=FMO.t[gq:gq + 2, :, q0:q0 + qw].rearrange("g (x d) t -> d (g x) t", d=32)),
                        [], [qT8])
                    LA = 2
                    steps = [(h, ki, kt, c) for h in range(4) for ki, kt in enumerate(ktiles) for c in range(2)]
                    pend = {}

                    def emit_S(si, qT8=qT8, qw=qw):
                        h, ki, kt, c = steps[si]
                        hc = h * 2 + c
                        pss = C.nextps()
                        S.op("pe", lambda e, pss=pss, hc=hc, kt=kt, qT8=qT8, qw=qw: e.matmul(
                            pss[:, 0:qw], lhsT=kT8[:, hc, kt * 128:(kt + 1) * 128], rhs=qT8[:, hc, 0:qw],
                            start=True, stop=True), [kT8, qT8], [pss])
                        pend[si] = pss

                    def emit_rest(si, qw=qw, nqs=nqs, nk=len(ktiles)):
                        nonlocal ec
                        h, ki, kt, c = steps[si]
                        acc = [C.ps[4 + (h % 2) * 2], C.ps[5 + (h % 2) * 2]]
                        pss = pend.pop(si)
                        E = Es[ec % 3]
                        ec += 1
                        S.op("act", lambda e, E=E, pss=pss, qw=qw: e.activation(
                            out=E[:, 0:qw], in_=pss[:, 0:qw], func=AF.Exp, scale=scale), [pss], [E])
                        S.op("pe", lambda e, E=E, c=c, kt=kt, h=h, ki=ki, qw=qw, nk=nk, ac=acc[c]: e.matmul(
                            ac[0:65, 0:qw], lhsT=V1[:, kt, h, :], rhs=E[:, 0:qw],
                            start=(ki == 0), stop=(ki == nk - 1)), [E, V1], [acc[c]])
                        if not (ki == nk - 1 and c == 1):
                            return
                        pt = C.ps[3]
                        for c_ in range(2):
                            S.op("dve", lambda e, c_=c_, qw=qw, ac=acc[c_]: e.tensor_copy(out=oT[c_][:, 0:qw], in_=ac[0:65, 0:qw]), [acc[c_]], [oT[c_]])
                            for qs in range(nqs):
                                S.op("pe", lambda e, pt=pt, c_=c_, qs=qs: e.transpose(
                                    pt[:, qs * 65:(qs + 1) * 65], oT[c_][:, qs * 128:(qs + 1) * 128], ident_f[0:65, 0:65]),
                                    [oT[c_], ident_f], [pt])
                            S.op("dve", lambda e, pt=pt, nqs=nqs: e.reciprocal(out=rr[:, 0:nqs], in_=pt[:, 0:nqs * 65].rearrange("p (q e) -> p q e", e=65)[:, :, 64]),
                                 [pt, od], [rr])
                            if c_ == 0:
                                for qs in range(nqs):
                                    S.op("dve", lambda e, qs=qs, h=h, pt=pt: e.tensor_scalar(
                                        out=od[:, qs, h * 64:(h + 1) * 64], in0=pt[:, qs * 65:qs * 65 + 64], scalar1=rr[:, qs:qs + 1],
                                        scalar2=None, op0=ALU.mult), [pt, rr], [od])
                            else:
                                S.op("dve", lambda e, nqs=nqs: e.tensor_scalar(out=rr[:, 0:nqs], in0=rr[:, 0:nqs], scalar1=nlam[:, 0:1], scalar2=None,
                                                                      op0=ALU.mult), [rr, nlam], [rr])
                                for qs in range(nqs):
                                    S.op("dve", lambda e, qs=qs, h=h, pt=pt: e.scalar_tensor_tensor(
                                        out=od[:, qs, h * 64:(h + 1) * 64], in0=pt[:, qs * 65:qs * 65 + 64], scalar=rr[:, qs:qs + 1],
                                        in1=od[:, qs, h * 64:(h + 1) * 64], op0=ALU.mult, op1=ALU.add), [pt, rr, od], [od])

                    for si in range(len(steps) + LA):
                        if si < len(steps):
                            emit_S(si)
                        if si >= LA:
                            emit_rest(si - LA)
                    y = yo[bi % 2]
                    S.op("act", lambda e, nqs=nqs: e.activation(out=sq[:, 0:nqs, :], in_=od[:, 0:nqs, :], func=AF.Square), [od], [sq])
                    S.op("dve", lambda e, nqs=nqs: e.tensor_reduce(out=ss[:, 0:nqs * 4], in_=sq[:, 0:nqs, :].rearrange("p q (h d) -> p (q h) d", h=4),
                                                          axis=AX.X, op=ALU.add), [sq], [ss])
                    S.op("dve", lambda e, nqs=nqs: e.tensor_scalar(out=ss[:], in0=ss[:], scalar1=1.0 / 64, scalar2=1e-6, op0=ALU.mult, op1=ALU.add),
                         [ss], [ss])
                    S.op("act", lambda e, nqs=nqs: e.activation(out=ss[:], in_=ss[:], func=AF.Sqrt), [ss], [ss])
                    S.op("dve", lambda e, nqs=nqs: e.reciprocal(out=ss[:], in_=ss[:]), [ss], [ss])
                    S.op("dve", lambda e, nqs=nqs: e.tensor_tensor(
                        out=od[:, 0:nqs, :].rearrange("p q (h d) -> p (q h) d", h=4), in0=od[:, 0:nqs, :].rearrange("p q (h d) -> p (q h) d", h=4),
                        in1=ss[:, 0:nqs * 4].rearrange("p (x o) -> p x o", o=1).broadcast_to([128, nqs * 4, 64]), op=ALU.mult), [od, ss], [od])
                    S.op("dve", lambda e, y=y, nqs=nqs: e.tensor_tensor(
                        out=y[:, 0:nqs, :].rearrange("p q (h d) -> p (q h) d", h=4), in0=od[:, 0:nqs, :].rearrange("p q (h d) -> p (q h) d", h=4),
                        in1=gn[:].rearrange("p (o d) -> p o d", o=1).broadcast_to([128, nqs * 4, 64]), op=ALU.mult), [od, gn], [y])
                    S.dma("sp", lambda e, y=y, q0=q0, qw=qw, nqs=nqs: e.dma_start(
                        out=Y.t[q0:q0 + qw, 256:512].rearrange("(q p) c -> p q c", p=128), in_=y[:, 0:nqs, :]), [y], [Y.sub(bi, 1)])
                C.rot = None
                S.barrier()
                S.emit()

        def phase_swa(l, full_ctx):
            gq = fm_index["swa_q0"]
            gk = fm_index["swa_k0"]
            scale = 0.125
            with ExitStack() as pes:
                C.rot = [0, 1, 2]
                skT = C.sb("skT", [64, 2, T], BF16, es=pes)
                sqT = C.sb("sqT", [64, 4, T], BF16, es=pes)
                V1 = C.sb("sV1", [128, NT, 2, 65], BF16, es=pes)
                nm = C.sb("nm", [128, 2, 2, 128], BF16, es=pes)
                Es = [C.sb(f"sE{i}", [128, 256], BF16, es=pes) for i in range(3)]
                oT = C.sb("soT", [65, 512], es=pes)
                es_ = C.sb("esink", [128, 4], es=pes)
                den = C.sb("den", [128, 4], es=pes)
                ys = [C.sb(f"sy{i}", [128, 256], BF16, es=pes) for i in range(2)]
                S.dma("sp", lambda e: e.dma_start(out=es_[:], in_=swa_sink[l:l + 1, :].broadcast_to([128, 4])), [], [es_])
                S.op("act", lambda e: e.activation(out=es_[:], in_=es_[:], func=AF.Exp), [es_], [es_])
                for r in range(2):
                    S.op("dve", lambda e, r=r: e.tensor_scalar(out=nm[:, 0, r, :], in0=relpos[:], scalar1=0.0, scalar2=NEG,
                                                               op0=ALU.is_gt, op1=ALU.mult), [relpos], [nm])
                    S.op("dve", lambda e, r=r: e.tensor_scalar(out=nm[:, 1, r, :], in0=relpos[:], scalar1=0.0, scalar2=NEG,
                                                               op0=ALU.is_lt, op1=ALU.mult), [relpos], [nm])
                S.dma("sp", lambda e: e.dma_start(out=skT[:], in_=FMO.t[gk, :, :].rearrange("(h d) t -> d h t", h=2)), [], [skT])
                S.dma("sp", lambda e: e.dma_start(out=sqT[:], in_=FMO.t[gq:gq + 2, :, :].rearrange("g (h d) t -> d (g h) t", h=2)), [], [sqT])
                S.op("pool", lambda e: e.memset(V1[:], 1.0), [], [V1])
                for g in range(2):
                    for n0 in range(0, NT, 8):
                        n1 = min(NT, n0 + 8)
                        S.dma("act", lambda e, g=g, n0=n0, n1=n1: e.dma_start(
                            out=V1[:, n0:n1, g, 0:64],
                            in_=TMD.t[n0 * 128:n1 * 128, g * 64:(g + 1) * 64].rearrange("(n p) e -> p n e", p=128)), [V1], [V1])
                tiles = list(range(NT)) if full_ctx else list(range(NLT))
                ec = 0
                for ti, t in enumerate(tiles):
                    if t < NLT:
                        keys = []
                        if t > 0:
                            keys.append((t - 1, 0))
                        keys.append((t, None))
                        if t < NLT - 1:
                            keys.append((t + 1, 1))
                        keys += [(NLT, None), (NLT + 1, None)]
                    else:
                        keys = [(NLT, None), (NLT + 1, None)]
                    accs = [C.ps[4 + (ti % 2) * 2], C.ps[5 + (ti % 2) * 2]]
                    LA = 2
                    steps = [(g, ki, kt, mk) for g in range(2) for ki, (kt, mk) in enumerate(keys)]
                    pend = {}

                    def emit_S(si, t=t):
                        g, ki, kt, mk = steps[si]
                        pss = C.nextps()
                        S.op("pe", lambda e, pss=pss, g=g, kt=kt, t=t, mk=mk: e.matmul(
                            pss[:, 0:256], lhsT=skT[:, g, kt * 128:(kt + 1) * 128], rhs=sqT[:, 2 * g:2 * g + 2, t * 128:(t + 1) * 128],
                            start=True, stop=(mk is None)), [skT, sqT], [pss])
                        if mk is not None:
                            S.op("pe", lambda e, pss=pss, mk=mk: e.matmul(
                                pss[:, 0:256], lhsT=ident_b[:], rhs=nm[:, mk, :, :], start=False, stop=True), [ident_b, nm], [pss])
                        pend[si] = pss

                    def emit_rest(si, nk=len(keys), accs=accs):
                        nonlocal ec
                        g, ki, kt, mk = steps[si]
                        pss = pend.pop(si)
                        E = Es[ec % 3]
                        ec += 1
                        S.op("act", lambda e, E=E, pss=pss: e.activation(out=E[:], in_=pss[:, 0:256], func=AF.Exp, scale=scale),
                             [pss], [E])
                        S.op("pe", lambda e, E=E, g=g, kt=kt, ki=ki, nk=nk, ac=accs[g]: e.matmul(
                            ac[0:65, 0:256], lhsT=V1[:, kt, g, :], rhs=E[:], start=(ki == 0), stop=(ki == nk - 1)),
                            [E, V1], [accs[g]])
                        if ki == nk - 1:
                            S.op("dve", lambda e, g=g, ac=accs[g]: e.tensor_copy(out=oT[:, g * 256:(g + 1) * 256], in_=ac[0:65, 0:256]),
                                 [accs[g]], [oT])

                    for si in range(len(steps) + LA):
                        if si < len(steps):
                            emit_S(si)
                        if si >= LA:
                            emit_rest(si - LA)
                    pt = C.ps[3]
                    for h in range(4):
                        S.op("pe", lambda e, pt=pt, h=h: e.transpose(pt[:, h * 65:(h + 1) * 65], oT[:, h * 128:(h + 1) * 128],
                                                                     ident_f[0:65, 0:65]), [oT, ident_f], [pt])
                    S.op("dve", lambda e, pt=pt: e.tensor_tensor(
                        out=den[:], in0=pt[:, 0:260].rearrange("p (h e) -> p h e", e=65)[:, :, 64], in1=es_[:], op=ALU.add),
                        [pt, es_], [den])
                    S.op("dve", lambda e: e.reciprocal(out=den[:], in_=den[:]), [den], [den])
                    y = ys[ti % 2]
                    S.op("dve", lambda e, pt=pt, y=y: e.tensor_tensor(
                        out=y[:].rearrange("p (h d) -> p h d", h=4), in0=pt[:, 0:260].rearrange("p (h e) -> p h e", e=65)[:, :, 0:64],
                        in1=den[:].rearrange("p (h o) -> p h o", o=1).broadcast_to([128, 4, 64]), op=ALU.mult), [pt, den], [y])
                    S.dma("sp", lambda e, y=y, t=t: e.dma_start(out=Y.t[t * 128:(t + 1) * 128, 768:1024], in_=y[:]), [y], [Y.sub(t, 3)])
                C.rot = None
                S.barrier()
                S.emit()

        def phase_gdn(l, full_ctx):
            g0 = fm_index["gdn0"]
            BIG = 30000.0
            with ExitStack() as pes:
                cw = C.sb("cw", [128, 18], es=pes)
                bones = C.sb("bones", [128, 128], es=pes)
                xb = [C.sb(f"xb{i}", [128, 514], BF16, es=pes) for i in range(2)]
                yc = C.sb("yc", [128, 512], es=pes)
                ysl = C.sb("ysl", [128, 512], es=pes)
                sq = C.sb("gsq", [128, 512], es=pes)
                rs = C.sb("grs", [128, 512], es=pes)
                ynb = [C.sb(f"ynb{i}", [128, 512], BF16, es=pes) for i in range(2)]
                ytm = [C.sb(f"ytm{i}", [128, 4, 128], BF16, es=pes) for i in range(2)]
                S.dma("sp", lambda e: e.dma_start(out=cw[:], in_=gdn_convT[l, :, :]), [], [cw])
                S.op("pool", lambda e: e.memset(bones[:], 0.0), [], [bones])
                S.op("pool", lambda e: e.memset(bones[0:64, 0:64], 1.0), [bones], [bones])
                S.op("pool", lambda e: e.memset(bones[64:128, 64:128], 1.0), [bones], [bones])
                seqs = [(0, NLAT)] + [(NLAT, T)]
                bi = 0
                for gi in range(6):
                    for (s0, s1) in seqs:
                        for t0 in range(s0, s1, 512):
                            tw = min(512, s1 - t0)
                            x = xb[bi % 2]
                            lo = max(t0 - 1, s0)
                            hi = min(t0 + tw + 1, s1)
                            if lo == t0 or hi == t0 + tw:
                                S.op("pool", lambda e, x=x: e.memset(x[:], 0.0), [], [x])
                            S.dma("sp", lambda e, x=x, gi=gi, lo=lo, hi=hi, t0=t0: e.dma_start(
                                out=x[:, lo - (t0 - 1):hi - (t0 - 1)], in_=FMO.t[g0 + gi, :, lo:hi]), [x], [x])
                            S.op("dve", lambda e, x=x, gi=gi, tw=tw: e.tensor_scalar(
                                out=yc[:, 0:tw], in0=x[:, 1:1 + tw], scalar1=cw[:, gi * 3 + 1:gi * 3 + 2], scalar2=None, op0=ALU.mult),
                                [x, cw], [yc])
                            S.op("dve", lambda e, x=x, gi=gi, tw=tw: e.scalar_tensor_tensor(
                                out=yc[:, 0:tw], in0=x[:, 0:tw], scalar=cw[:, gi * 3:gi * 3 + 1], in1=yc[:, 0:tw],
                                op0=ALU.mult, op1=ALU.add), [x, cw, yc], [yc])
                            S.op("dve", lambda e, x=x, gi=gi, tw=tw: e.scalar_tensor_tensor(
                                out=yc[:, 0:tw], in0=x[:, 2:2 + tw], scalar=cw[:, gi * 3 + 2:gi * 3 + 3], in1=yc[:, 0:tw],
                                op0=ALU.mult, op1=ALU.add), [x, cw, yc], [yc])
                            S.op("act", lambda e, tw=tw: e.activation(out=ysl[:, 0:tw], in_=yc[:, 0:tw], func=AF.Silu), [yc], [ysl])
                            yn = ynb[bi % 2]
                            if gi < 4:
                                S.op("act", lambda e, tw=tw: e.activation(out=sq[:, 0:tw], in_=ysl[:, 0:tw], func=AF.Square), [ysl], [sq])
                                ps = C.nextps()
                                S.op("pe", lambda e, ps=ps, tw=tw: e.matmul(ps[:, 0:tw], lhsT=bones[:], rhs=sq[:, 0:tw], start=True, stop=True),
                                     [bones, sq], [ps])
                                S.op("dve", lambda e, ps=ps, tw=tw: e.tensor_scalar_add(out=rs[:, 0:tw], in0=ps[:, 0:tw], scalar1=1e-6), [ps], [rs])
                                S.op("act", lambda e, tw=tw: e.activation(out=rs[:, 0:tw], in_=rs[:, 0:tw], func=AF.Sqrt), [rs], [rs])
                                S.op("dve", lambda e, tw=tw: e.reciprocal(out=rs[:, 0:tw], in_=rs[:, 0:tw]), [rs], [rs])
                                if gi < 2:
                                    S.op("dve", lambda e, tw=tw, yn=yn: e.scalar_tensor_tensor(
                                        out=yn[:, 0:tw], in0=ysl[:, 0:tw], scalar=0.125, in1=rs[:, 0:tw], op0=ALU.mult, op1=ALU.mult),
                                        [ysl, rs], [yn])
                                else:
                                    S.op("dve", lambda e, tw=tw, yn=yn: e.tensor_tensor(out=yn[:, 0:tw], in0=ysl[:, 0:tw], in1=rs[:, 0:tw], op=ALU.mult),
                                         [ysl, rs], [yn])
                                S.dma("act", lambda e, yn=yn, gi=gi, t0=t0, tw=tw: e.dma_start(out=GFM.t[gi, :, t0:t0 + tw], in_=yn[:, 0:tw]),
                                      [yn], [GFM.sub(gi, t0)])
                            else:
                                S.op("dve", lambda e, tw=tw, yn=yn: e.tensor_copy(out=yn[:, 0:tw], in_=ysl[:, 0:tw]), [ysl], [yn])
                            if gi >= 2:
                                ps = C.nextps()
                                pv = ps.t[:, 0:256].bitcast(BF16)
                                nq = tw // 128
                                for q in range(nq):
                                    S.op("pe", lambda e, pv=pv, q=q, yn=yn: e.transpose(pv[:, q * 128:(q + 1) * 128], yn[:, q * 128:(q + 1) * 128], ident_b[:]),
                                         [yn, ident_b], [ps])
                                yt = ytm[bi % 2]
                                S.op("act", lambda e, pv=pv, yt=yt, nq=nq: e.copy(out=yt[:, 0:nq, :].rearrange("p q c -> p (q c)"), in_=pv[:, 0:nq * 128]),
                                     [ps], [yt])
                                S.dma("act", lambda e, yt=yt, gi=gi, t0=t0, tw=tw, nq=nq: e.dma_start(
                                    out=GTM.t[t0:t0 + tw, (gi - 2) * 128:(gi - 1) * 128].rearrange("(q p) c -> p q c", p=128), in_=yt[:, 0:nq, :]),
                                    [yt], [GTM.sub(gi, t0)])
                            bi += 1
                S.barrier()
                S.emit()
            import os
            GD = int(os.environ.get("GD", 9))
            if GD < 2:
                return
            with ExitStack() as pes:
                gab = C.sb("gab", [128, NT, 16], es=pes)
                par = C.sb("gpar", [128, 16], es=pes)
                la = C.sb("la", [128, NT, 8], es=pes)
                nbeta = C.sb("nbeta", [128, NT, 8], es=pes)
                gg = C.sb("gg", [128, NT, 8], es=pes)
                gt = C.sb("gt", [128, NT, 8], es=pes)
                eg = C.sb("eg", [128, NT, 8], es=pes)
                ekt = C.sb("ekt", [128, NT, 8], es=pes)
                cd = C.sb("cd", [128, NT, 8], es=pes)
                beg = C.sb("beg", [128, NT, 8], es=pes)
                tri = C.sb("tri", [128, 2, 128], es=pes)
                onesf = C.sb("onesf", [128, 128], es=pes)
                nonesf = C.sb("nonesf", [128, 128], es=pes)
                mD = C.sb("mD", [128, 2, 4, 128], es=pes)
                mDT = C.sb("mDT", [128, 2, 4, 128], es=pes)
                gnb = C.sb("gnb", [128, 64], es=pes)
                S.dma("sp", lambda e: e.dma_start(out=gab[:].rearrange("p n c -> p (n c)"), in_=GAB.t[:, :]), [], [gab])
                S.dma("sp", lambda e: e.dma_start(out=par[:, 0:8], in_=gdn_a_log[l:l + 1, :].broadcast_to([128, 8])), [], [par])
                S.dma("sp", lambda e: e.dma_start(out=par[:, 8:16], in_=gdn_dt_bias[l:l + 1, :].broadcast_to([128, 8])), [par], [par])
                S.dma("sp", lambda e: e.dma_start(out=gnb[:], in_=gdn_norm[l:l + 1, :].broadcast_to([128, 64])), [], [gnb])
                S.op("pool", lambda e: e.memset(onesf[:], 1.0), [], [onesf])
                S.op("pool", lambda e: e.memset(nonesf[:], -1.0), [], [nonesf])
                S.op("dve", lambda e: e.tensor_single_scalar(out=tri[:, 0, :], in_=relpos[:], scalar=0.0, op=ALU.is_ge), [relpos], [tri])
                S.op("dve", lambda e: e.tensor_single_scalar(out=tri[:, 1, :], in_=relpos[:], scalar=0.0, op=ALU.is_le), [relpos], [tri])
                for h in range(4):
                    S.op("dve", lambda e, h=h: e.tensor_scalar(out=mD[:, 0, h, :], in0=relpos[:], scalar1=0.0, scalar2=BIG, op0=ALU.is_ge, op1=ALU.mult), [relpos], [mD])
                    S.op("dve", lambda e, h=h: e.tensor_scalar(out=mD[:, 1, h, :], in0=relpos[:], scalar1=0.0, scalar2=BIG, op0=ALU.is_le, op1=ALU.mult), [relpos], [mD])
                    S.op("dve", lambda e, h=h: e.tensor_scalar(out=mDT[:, 0, h, :], in0=relpos[:], scalar1=0.0, scalar2=-BIG, op0=ALU.is_lt, op1=ALU.mult), [relpos], [mDT])
                    S.op("dve", lambda e, h=h: e.tensor_scalar(out=mDT[:, 1, h, :], in0=relpos[:], scalar1=0.0, scalar2=-BIG, op0=ALU.is_gt, op1=ALU.mult), [relpos], [mDT])
                S.op("act", lambda e: e.activation(out=par[:, 0:8], in_=par[:, 0:8], func=AF.Exp), [par], [par])
                S.op("dve", lambda e: e.tensor_tensor(out=la[:], in0=gab[:, :, 0:8],
                                                      in1=par[:, 8:16].rearrange("p (o c) -> p o c", o=1).broadcast_to([128, NT, 8]), op=ALU.add),
                     [gab, par], [la])
                S.op("act", lambda e: e.activation(out=la[:], in_=la[:], func=AF.Exp), [la], [la])
                S.op("act", lambda e: e.activation(out=la[:], in_=la[:], func=AF.Ln, bias=1.0), [la], [la])
                S.op("dve", lambda e: e.scalar_tensor_tensor(
                    out=la[:], in0=la[:], scalar=-1.0, in1=par[:, 0:8].rearrange("p (o c) -> p o c", o=1).broadcast_to([128, NT, 8]),
                    op0=ALU.mult, op1=ALU.mult), [la, par], [la])
                S.op("act", lambda e: e.activation(out=nbeta[:], in_=gab[:, :, 8:16], func=AF.Sigmoid), [gab], [nbeta])
                for r in range(2):
                    ps = C.nextps()
                    S.op("pe", lambda e, ps=ps, r=r: e.matmul(ps[:, 0:NT * 4], lhsT=tri[:, r, :], rhs=la[:, :, r * 4:(r + 1) * 4], start=True, stop=True),
                         [tri, la], [ps])
                    S.op("dve", lambda e, ps=ps, r=r: e.tensor_copy(out=gg[:, :, r * 4:(r + 1) * 4], in_=ps[:, 0:NT * 4].rearrange("p (n c) -> p n c", c=4)),
                         [ps], [gg])
                ps = C.nextps()
                S.op("pe", lambda e, ps=ps: e.matmul(ps[:, 0:NT * 8], lhsT=onesf[:], rhs=la[:].rearrange("p n c -> p (n c)"), start=True, stop=True),
                     [onesf, la], [ps])
                S.op("dve", lambda e, ps=ps: e.tensor_copy(out=gt[:].rearrange("p n c -> p (n c)"), in_=ps[:, 0:NT * 8]), [ps], [gt])
                S.op("act", lambda e: e.activation(out=eg[:], in_=gg[:], func=AF.Exp), [gg], [eg])
                S.op("act", lambda e: e.activation(out=cd[:], in_=gt[:], func=AF.Exp), [gt], [cd])
                S.op("dve", lambda e: e.tensor_tensor(out=ekt[:], in0=gt[:], in1=gg[:], op=ALU.subtract), [gt, gg], [ekt])
                S.op("act", lambda e: e.activation(out=ekt[:], in_=ekt[:], func=AF.Exp), [ekt], [ekt])
                S.op("dve", lambda e: e.tensor_tensor(out=beg[:], in0=nbeta[:], in1=eg[:], op=ALU.mult), [nbeta, eg], [beg])
                S.op("dve", lambda e: e.tensor_scalar_mul(out=nbeta[:], in0=nbeta[:], scalar1=-1.0), [nbeta], [nbeta])
                qT = [C.sb(f"gqT{i}", [64, 4, 128], BF16, es=pes) for i in range(2)]
                kT = [C.sb(f"gkT{i}", [64, 4, 128], BF16, es=pes) for i in range(2)]
                kv = [C.sb(f"gkv{i}", [128, 512], BF16, es=pes) for i in range(2)]
                Rm = C.sb("Rm", [128, 4, 128], es=pes)
                Ds = C.sb("Ds", [128, 4, 128], es=pes)
                DT = C.sb("DT", [128, 4, 128], es=pes)
                X = [C.sb(f"X{i}", [128, 4, 2, 128], es=pes) for i in range(2)]
                P = C.sb("P", [128, 4, 128], es=pes)
                ident_r = ident_f
                Pb = C.sb("Pb", [128, 4, 128], BF16, es=pes)
                qkT = C.sb("qkT", [128, 4, 128], BF16, es=pes)
                kg = C.sb("kg", [128, 256], BF16, es=pes)
                vb = C.sb("vb", [128, 256], BF16, es=pes)
                ktl = C.sb("ktl", [128, 256], BF16, es=pes)
                wT = C.sb("wT", [64, 4, 128], BF16, es=pes)
                ub = C.sb("ub", [128, 256], es=pes)
                u = C.sb("u", [128, 256], BF16, es=pes)
                Sf = C.sb("Sf", [64, 256], es=pes)
                Sb16 = C.sb("Sb16", [64, 256], BF16, es=pes)
                Oacc = C.sb("Oacc", [128, NT, 256], es=pes)
                ocr = C.sb("ocr", [128, 256], es=pes)
                gsq = C.sb("gosq", [128, 256], es=pes)
                gss = C.sb("goss", [128, 4], es=pes)
                gsg = C.sb("gosg", [128, 256], es=pes)
                ggate = [C.sb(f"ggate{i}", [128, 256], BF16, es=pes) for i in range(2)]
                gy = [C.sb(f"gy{i}", [128, 256], BF16, es=pes) for i in range(2)]
                seq = {0: [NLT, NLT + 1] + list(range(NLT)), 1: [NLT + 1, NLT] + list(range(NLT - 1, -1, -1))}
                step = 0
                for r in range(2 if GD >= 3 else 0):
                    S.op("pool", lambda e: e.memset(Sf[:], 0.0), [], [Sf])
                    S.op("pool", lambda e: e.memset(Sb16[:], 0.0), [], [Sb16])
                    for c in seq[r]:
                        i = step % 2
                        step += 1
                        rc = slice(r * 4, r * 4 + 4)
                        S.dma("sp", lambda e, i=i, c=c: e.dma_start(
                            out=qT[i][:], in_=GFM.t[0:2, :, c * 128:(c + 1) * 128].rearrange("g (h d) t -> d (g h) t", h=2)), [], [qT[i]])
                        S.dma("sp", lambda e, i=i, c=c: e.dma_start(
                            out=kT[i][:], in_=GFM.t[2:4, :, c * 128:(c + 1) * 128].rearrange("g (h d) t -> d (g h) t", h=2)), [], [kT[i]])
                        S.dma("act", lambda e, i=i, c=c: e.dma_start(out=kv[i][:], in_=GTM.t[c * 128:(c + 1) * 128, :]), [], [kv[i]])
                        pKK = C.nextps()
                        pQK = C.nextps()
                        for h in range(4):
                            S.op("pe", lambda e, pKK=pKK, h=h, i=i: e.matmul(pKK[:, h * 128:(h + 1) * 128], lhsT=kT[i][:, h, :], rhs=kT[i][:, h, :],
                                                                              start=True, stop=True), [kT[i]], [pKK])
                        for h in range(4):
                            S.op("pe", lambda e, pQK=pQK, h=h, i=i: e.matmul(pQK[:, h * 128:(h + 1) * 128], lhsT=kT[i][:, h, :], rhs=qT[i][:, h, :],
                                                                              start=True, stop=True), [kT[i], qT[i]], [pQK])
                        for h in range(4):
                            S.op("dve", lambda e, h=h, c=c, r=r: e.tensor_scalar(
                                out=Rm[:, h, :], in0=tri[:, r, :], scalar1=la[:, c, r * 4 + h:r * 4 + h + 1], scalar2=None, op0=ALU.mult),
                                [tri, la], [Rm])
                        pD = C.nextps()
                        pDT = C.nextps()
                        for (pp, mk) in ((pD, mD), (pDT, mDT)):
                            S.op("pe", lambda e, pp=pp, mk=mk, r=r: e.matmul(pp[:, :], lhsT=ident_f[:], rhs=mk[:, r, :, :].rearrange("p h t -> p (h t)"),
                                                                             start=True, stop=False), [ident_f, mk], [pp])
                            for h in range(4):
                                S.op("pe", lambda e, pp=pp, h=h: e.matmul(pp[:, h * 128:(h + 1) * 128], lhsT=onesf[:], rhs=Rm[:, h, :],
                                                                          start=False, stop=False), [onesf, Rm], [pp])
                                S.op("pe", lambda e, pp=pp, h=h: e.matmul(pp[:, h * 128:(h + 1) * 128], lhsT=Rm[:, h, :], rhs=nonesf[:],
                                                                          start=False, stop=(h == 3)), [nonesf, Rm], [pp])
                        S.op("act", lambda e, pD=pD: e.activation(out=Ds[:].rearrange("p h t -> p (h t)"), in_=pD[:, :], func=AF.Exp, scale=-1.0),
                             [pD], [Ds])
                        S.op("act", lambda e, pDT=pDT: e.activation(out=DT[:].rearrange("p h t -> p (h t)"), in_=pDT[:, :], func=AF.Exp), [pDT], [DT])
                        if GD < 4:
                            continue
                        Xc = X[0]
                        S.op("dve", lambda e, pKK=pKK, Xc=Xc: e.tensor_tensor(out=Xc[:, :, 0, :], in0=pKK[:, :].rearrange("p (h t) -> p h t", h=4), in1=Ds[:], op=ALU.mult),
                             [pKK, Ds], [Xc])
                        S.op("dve", lambda e, Xc=Xc, c=c, rc=rc: e.tensor_tensor(
                            out=Xc[:, :, 0, :], in0=Xc[:, :, 0, :], in1=nbeta[:, c, rc].rearrange("p (h o) -> p h o", o=1).broadcast_to([128, 4, 128]), op=ALU.mult),
                            [Xc, nbeta], [Xc])
                        S.op("dve", lambda e, pQK=pQK: e.tensor_tensor(out=qkT[:], in0=pQK[:, :].rearrange("p (h t) -> p h t", h=4), in1=DT[:], op=ALU.mult),
                             [pQK, DT], [qkT])
                        pZ = C.nextps()
                        pZr = pZ.t[:, :]
                        for h in range(4):
                            S.op("pe", lambda e, pZr=pZr, h=h, Xc=Xc: e.transpose(pZr[:, h * 128:(h + 1) * 128], Xc[:, h, 0, :], ident_r[:]), [Xc, ident_r], [pZ])
                        S.op("act", lambda e, pZ=pZ, Xc=Xc: e.copy(out=Xc[:, :, 1, :], in_=pZ[:, :].rearrange("p (h t) -> p h t", h=4)), [pZ], [Xc])
                        S.op("dve", lambda e, Xc=Xc: e.tensor_tensor(out=P[:], in0=Xc[:, :, 1, :], in1=ident_f[:].rearrange("p (o t) -> p o t", o=1).broadcast_to([128, 4, 128]), op=ALU.add),
                             [Xc, ident_f], [P])
                        for lev in range(6):
                            Xo = X[lev % 2]
                            Xn = X[(lev + 1) % 2]
                            last = lev == 5
                            pxa = C.nextps()
                            pxb = C.nextps()
                            for h in range(4):
                                pp = pxa if h < 2 else pxb
                                o = (h % 2) * 256
                                S.op("pe", lambda e, pp=pp, o=o, h=h, Xo=Xo: e.matmul(pp[:, o:o + 128], lhsT=Xo[:, h, 1, :], rhs=Xo[:, h, 0, :], start=True, stop=True),
                                     [Xo], [pp])
                                if not last:
                                    S.op("pe", lambda e, pp=pp, o=o, h=h, Xo=Xo: e.matmul(pp[:, o + 128:o + 256], lhsT=Xo[:, h, 0, :], rhs=Xo[:, h, 1, :], start=True, stop=True),
                                         [Xo], [pp])
                            S.op("act", lambda e, pxa=pxa, Xn=Xn: e.copy(out=Xn[:, 0:2, :, :].rearrange("p h x t -> p (h x t)"), in_=pxa[:, :]), [pxa], [Xn])
                            S.op("dve", lambda e, pxb=pxb, Xn=Xn: e.tensor_copy(out=Xn[:, 2:4, :, :].rearrange("p h x t -> p (h x t)"), in_=pxb[:, :]), [pxb], [Xn])
                            pP = C.nextps()
                            for h in range(4):
                                S.op("pe", lambda e, pP=pP, h=h, Xn=Xn: e.matmul(pP[:, h * 128:(h + 1) * 128], lhsT=Xn[:, h, 0, :], rhs=P[:, h, :], start=True, stop=True),
                                     [Xn, P], [pP])
                            S.op("dve", lambda e, pP=pP: e.tensor_tensor(out=P[:].rearrange("p h t -> p (h t)"), in0=P[:].rearrange("p h t -> p (h t)"), in1=pP[:, :], op=ALU.add),
                                 [pP, P], [P])
                        if GD < 5:
                            continue
                        S.op("act", lambda e: e.copy(out=Pb[:], in_=P[:]), [P], [Pb])
                        S.op("dve", lambda e, i=i, c=c, rc=rc: e.tensor_tensor(
                            out=kg[:].rearrange("p (h d) -> p h d", h=4), in0=kv[i][:, 0:256].rearrange("p (h d) -> p h d", h=4),
                            in1=beg[:, c, rc].rearrange("p (h o) -> p h o", o=1).broadcast_to([128, 4, 64]), op=ALU.mult), [kv[i], beg], [kg])
                        S.op("dve", lambda e, i=i, c=c, rc=rc: e.tensor_tensor(
                            out=vb[:].rearrange("p (h d) -> p h d", h=4), in0=kv[i][:, 256:512].rearrange("p (h d) -> p h d", h=4),
                            in1=nbeta[:, c, rc].rearrange("p (h o) -> p h o", o=1).broadcast_to([128, 4, 64]), op=ALU.mult), [kv[i], nbeta], [vb])
                        S.op("dve", lambda e, i=i, c=c, rc=rc: e.tensor_tensor(
                            out=ktl[:].rearrange("p (h d) -> p h d", h=4), in0=kv[i][:, 0:256].rearrange("p (h d) -> p h d", h=4),
                            in1=ekt[:, c, rc].rearrange("p (h o) -> p h o", o=1).broadcast_to([128, 4, 64]), op=ALU.mult), [kv[i], ekt], [ktl])
                        pw = C.nextps()
                        pu = C.nextps()
                        for h in range(4):
                            S.op("pe", lambda e, pw=pw, h=h: e.matmul(pw[0:64, h * 128:(h + 1) * 128], lhsT=kg[:, h * 64:(h + 1) * 64], rhs=Pb[:, h, :], start=True, stop=True),
                                 [kg, Pb], [pw])
                            S.op("pe", lambda e, pu=pu, h=h: e.matmul(pu[:, h * 64:(h + 1) * 64], lhsT=Pb[:, h, :], rhs=vb[:, h * 64:(h + 1) * 64], start=True, stop=True),
                                 [vb, Pb], [pu])
                        S.op("dve", lambda e, pw=pw: e.tensor_copy(out=wT[:].rearrange("p h t -> p (h t)"), in_=pw[0:64, :]), [pw], [wT])
                        S.op("dve", lambda e, pu=pu: e.tensor_scalar_mul(out=ub[:], in0=pu[:, 0:256], scalar1=-1.0), [pu], [ub])
                        pws = C.nextps()
                        for h in range(4):
                            S.op("pe", lambda e, pws=pws, h=h: e.matmul(pws[:, h * 64:(h + 1) * 64], lhsT=wT[:, h, :], rhs=Sb16[:, h * 64:(h + 1) * 64], start=True, stop=True),
                                 [wT, Sb16], [pws])
                        S.op("dve", lambda e, pws=pws: e.tensor_tensor(out=u[:], in0=ub[:], in1=pws[:, 0:256], op=ALU.subtract), [ub, pws], [u])
                        pcr = C.nextps()
                        pin = C.nextps()
                        for h in range(4):
                            S.op("pe", lambda e, pcr=pcr, h=h, i=i: e.matmul(pcr[:, h * 64:(h + 1) * 64], lhsT=qT[i][:, h, :], rhs=Sb16[:, h * 64:(h + 1) * 64], start=True, stop=True),
                                 [qT[i], Sb16], [pcr])
                            S.op("pe", lambda e, pin=pin, h=h: e.matmul(pin[:, h * 64:(h + 1) * 64], lhsT=qkT[:, h, :], rhs=u[:, h * 64:(h + 1) * 64], start=True, stop=True),
                                 [qkT, u], [pin])
                        pS = C.nextps()
                        for h in range(4):
                            S.op("pe", lambda e, pS=pS, h=h: e.matmul(pS[0:64, h * 64:(h + 1) * 64], lhsT=ktl[:, h * 64:(h + 1) * 64], rhs=u[:, h * 64:(h + 1) * 64], start=True, stop=True),
                                 [ktl, u], [pS])
                        S.op("dve", lambda e, pcr=pcr, c=c, rc=rc: e.tensor_tensor(
                            out=ocr[:].rearrange("p (h d) -> p h d", h=4), in0=pcr[:, 0:256].rearrange("p (h d) -> p h d", h=4),
                            in1=eg[:, c, rc].rearrange("p (h o) -> p h o", o=1).broadcast_to([128, 4, 64]), op=ALU.mult), [pcr, eg], [ocr])
                        if r == 0:
                            S.op("dve", lambda e, pin=pin, c=c: e.tensor_tensor(out=Oacc[:, c, :], in0=ocr[:], in1=pin[:, 0:256], op=ALU.add), [ocr, pin], [Oacc.sub(c)])
                        else:
                            S.op("dve", lambda e, pin=pin: e.tensor_tensor(out=ocr[:], in0=ocr[:], in1=pin[:, 0:256], op=ALU.add), [ocr, pin], [ocr])
                            S.op("dve", lambda e, c=c: e.tensor_tensor(out=ocr[:], in0=ocr[:], in1=Oacc[:, c, :], op=ALU.add), [ocr, Oacc.sub(c)], [ocr])
                        S.op("dve", lambda e, c=c, rc=rc: e.tensor_tensor(
                            out=Sf[:].rearrange("p (h d) -> p h d", h=4), in0=Sf[:].rearrange("p (h d) -> p h d", h=4),
                            in1=cd[0:64, c, rc].rearrange("p (h o) -> p h o", o=1).broadcast_to([64, 4, 64]), op=ALU.mult), [Sf, cd], [Sf])
                        S.op("dve", lambda e, pS=pS: e.tensor_tensor(out=Sf[:], in0=Sf[:], in1=pS[0:64, 0:256], op=ALU.add), [Sf, pS], [Sf])
                        S.op("act", lambda e: e.copy(out=Sb16[:], in_=Sf[:]), [Sf], [Sb16])
                        if r == 1 and (full_ctx or c < NLT):
                            gt_ = ggate[i]
                            S.dma("act", lambda e, gt_=gt_, c=c: e.dma_start(out=gt_[:], in_=TMC.t[c * 128:(c + 1) * 128, :]), [], [gt_])
                            S.op("act", lambda e: e.activation(out=gsq[:], in_=ocr[:], func=AF.Square), [ocr], [gsq])
                            S.op("dve", lambda e: e.tensor_reduce(out=gss[:], in_=gsq[:].rearrange("p (h d) -> p h d", h=4), axis=AX.X, op=ALU.add), [gsq], [gss])
                            S.op("dve", lambda e: e.tensor_scalar(out=gss[:], in0=gss[:], scalar1=1.0 / 64, scalar2=1e-6, op0=ALU.mult, op1=ALU.add), [gss], [gss])
                            S.op("act", lambda e: e.activation(out=gss[:], in_=gss[:], func=AF.Sqrt), [gss], [gss])
                            S.op("dve", lambda e: e.reciprocal(out=gss[:], in_=gss[:]), [gss], [gss])
                            S.op("act", lambda e, gt_=gt_: e.activation(out=gsg[:], in_=gt_[:], func=AF.Silu), [gt_], [gsg])
                            S.op("dve", lambda e: e.tensor_tensor(
                                out=ocr[:].rearrange("p (h d) -> p h d", h=4), in0=ocr[:].rearrange("p (h d) -> p h d", h=4),
                                in1=gss[:].rearrange("p (h o) -> p h o", o=1).broadcast_to([128, 4, 64]), op=ALU.mult), [ocr, gss], [ocr])
                            S.op("dve", lambda e: e.tensor_tensor(
                                out=ocr[:].rearrange("p (h d) -> p h d", h=4), in0=ocr[:].rearrange("p (h d) -> p h d", h=4),
                                in1=gnb[:].rearrange("p (o d) -> p o d", o=1).broadcast_to([128, 4, 64]), op=ALU.mult), [ocr, gnb], [ocr])
                            yy = gy[i]
                            S.op("dve", lambda e, yy=yy: e.tensor_tensor(out=yy[:], in0=ocr[:], in1=gsg[:], op=ALU.mult), [ocr, gsg], [yy])
                            S.dma("sp", lambda e, yy=yy, c=c: e.dma_start(out=Y.t[c * 128:(c + 1) * 128, 512:768], in_=yy[:]), [yy], [Y.sub(c, 2)])
                S.barrier()
                S.emit()

        def layer_norm_tile(z, sqt, st, lnG, lnB, gi, outt, eng2="dve"):
            S.op("dve", lambda e: e.tensor_reduce(out=st[:, 0:1], in_=z[:], axis=AX.X, op=ALU.add), [z], [st])
            S.op("dve", lambda e: e.tensor_tensor(out=sqt[:], in0=z[:], in1=z[:], op=ALU.mult), [z], [sqt])
            S.op("dve", lambda e: e.tensor_reduce(out=st[:, 1:2], in_=sqt[:], axis=AX.X, op=ALU.add), [sqt], [st])
            S.op("dve", lambda e: e.tensor_scalar_mul(out=st[:, 0:2], in0=st[:, 0:2], scalar1=1.0 / D), [st], [st])
            S.op("dve", lambda e: e.tensor_tensor(out=st[:, 2:3], in0=st[:, 0:1], in1=st[:, 0:1], op=ALU.mult), [st], [st])
            S.op("dve", lambda e: e.tensor_tensor(out=st[:, 2:3], in0=st[:, 1:2], in1=st[:, 2:3], op=ALU.subtract), [st], [st])
            S.op("dve", lambda e: e.tensor_scalar_add(out=st[:, 2:3], in0=st[:, 2:3], scalar1=LN_EPS), [st], [st])
            S.op("act", lambda e: e.activation(out=st[:, 2:3], in_=st[:, 2:3], func=AF.Sqrt), [st], [st])
            S.op("dve", lambda e: e.reciprocal(out=st[:, 2:3], in_=st[:, 2:3]), [st], [st])
            S.op("dve", lambda e: e.tensor_scalar(out=sqt[:], in0=z[:], scalar1=st[:, 0:1], scalar2=st[:, 2:3], op0=ALU.subtract, op1=ALU.mult),
                 [z, st], [sqt])
            S.op("pool", lambda e: e.tensor_tensor(out=sqt[:], in0=sqt[:], in1=lnG[:, gi, :], op=ALU.mult), [sqt, lnG], [sqt])
            S.op("pool", lambda e: e.tensor_tensor(out=outt[:], in0=sqt[:], in1=lnB[:, gi, :], op=ALU.add), [sqt, lnB], [outt])

        def phase_E(l, full_ctx):
            src = h0 if l == 0 else H.t
            with ExitStack() as pes:
                wo = C.sb("wo", [128, 8, D], BF16, es=pes)
                lnG = C.sb("lnG", [128, 2, D], es=pes)
                lnB = C.sb("lnB", [128, 2, D], es=pes)
                rw = C.sb("rw", [128, 8, 128], es=pes)
                LG = C.sb("LG", [16, T], es=pes)
                yts = [C.sb(f"yt{i}", [128, D], BF16, es=pes) for i in range(2)]
                hts = [C.sb(f"eht{i}", [128, D], es=pes) for i in range(2)]
                YT = C.sb("YT", [128, 8, 128], BF16, es=pes)
                z = C.sb("z", [128, D], es=pes)
                sqt = C.sb("sqt", [128, D], es=pes)
                st = C.sb("st", [128, 4], es=pes)
                hn = [C.sb(f"hn{i}", [128, D], es=pes) for i in range(2)]
                u2b = [C.sb(f"u2b{i}", [128, D], BF16, es=pes) for i in range(2)]
                u2t = C.sb("u2t", [128, D], es=pes)
                u2T = C.sb("u2T", [128, 8, 128], es=pes)
                for k in range(8):
                    S.dma("pool", lambda e, k=k: e.dma_start(out=wo[:, k, :], in_=w_out[l, k * 128:(k + 1) * 128, :]), [], [wo])
                S.dma("sp", lambda e: e.dma_start(out=lnG[:].rearrange("p g d -> p (g d)"), in_=ln_g[l:l + 1, :].broadcast_to([128, 2 * D])), [], [lnG])
                S.dma("sp", lambda e: e.dma_start(out=lnB[:].rearrange("p g d -> p (g d)"), in_=ln_b[l:l + 1, :].broadcast_to([128, 2 * D])), [], [lnB])
                S.dma("sp", lambda e: e.dma_start(out=rw[:], in_=router_wp[l, :, :].rearrange("(k p) e -> p k e", p=128)), [], [rw])
                tiles = list(range(NT)) if full_ctx else list(range(NLT))
                for ti, t in enumerate(tiles):
                    lc = 0 if t < NLT else 1
                    yt = yts[ti % 2]
                    ht = hts[ti % 2]
                    S.dma("sp", lambda e, yt=yt, t=t: e.dma_start(out=yt[:], in_=Y.t[t * 128:(t + 1) * 128, :]), [], [yt])
                    S.dma("act", lambda e, ht=ht, t=t: e.dma_start(out=ht[:], in_=src[t * 128:(t + 1) * 128, :]), ["H/%d" % t], [ht])
                    ps = C.nextps()
                    pv = ps.t[:, :].bitcast(BF16)
                    for k in range(8):
                        S.op("pe", lambda e, pv=pv, yt=yt, k=k: e.transpose(pv[:, k * 128:(k + 1) * 128], yt[:, k * 128:(k + 1) * 128], ident_b[:]),
                             [yt, ident_b], [ps])
                    S.op("act", lambda e, pv=pv: e.copy(out=YT[:].rearrange("p k t -> p (k t)"), in_=pv[:, :]), [ps], [YT])
                    for half in range(2):
                        pm = C.nextps()
                        for k in range(8):
                            S.op("pe", lambda e, pm=pm, k=k, half=half: e.matmul(pm[:, :], lhsT=YT[:, k, :], rhs=wo[:, k, half * 512:(half + 1) * 512],
                                                                                 start=(k == 0), stop=(k == 7)), [YT, wo], [pm])
                        S.op("dve", lambda e, pm=pm, half=half, lc=lc: e.tensor_tensor(
                            out=z[:, half * 512:(half + 1) * 512], in0=pm[:, :], in1=gateB[:, 0, lc, half * 512:(half + 1) * 512], op=ALU.mult),
                            [pm, gateB], [z])
                    S.op("dve", lambda e, ht=ht: e.scalar_tensor_tensor(out=z[:], in0=ht[:], scalar=ALPHA, in1=z[:], op0=ALU.mult, op1=ALU.add),
                         [ht, z], [z])
                    h_ = hn[ti % 2]
                    layer_norm_tile(z, sqt, st, lnG, lnB, 0, h_)
                    S.dma("sp", lambda e, h_=h_, t=t: e.dma_start(out=H.t[t * 128:(t + 1) * 128, :], in_=h_[:]), [h_], ["H/%d" % t])
                    ub_ = u2b[ti % 2]
                    S.op("pool", lambda e, h_=h_, lc=lc: e.tensor_tensor(out=u2t[:], in0=h_[:], in1=gateB[:, 3, lc, :], op=ALU.mult), [h_, gateB], [u2t])
                    S.op("pool", lambda e, ub_=ub_, lc=lc: e.tensor_tensor(out=ub_[:], in0=u2t[:], in1=gateB[:, 2, lc, :], op=ALU.add), [u2t, gateB], [ub_])
                    S.dma("act", lambda e, ub_=ub_, t=t: e.dma_start(out=U2.t[t * 128:(t + 1) * 128, :], in_=ub_[:]), [ub_], [U2.sub(t)])
                    for half in range(2):
                        pt = C.nextps()
                        for j in range(4):
                            k = half * 4 + j
                            S.op("pe", lambda e, pt=pt, h_=h_, j=j, k=k: e.transpose(pt[:, j * 128:(j + 1) * 128], h_[:, k * 128:(k + 1) * 128], ident_f[:]),
                                 [h_, ident_f], [pt])
                        for j in range(4):
                            k = half * 4 + j
                            S.op("act", lambda e, pt=pt, j=j, k=k, lc=lc: e.activation(
                                out=u2T[:, k, :], in_=pt[:, j * 128:(j + 1) * 128], func=AF.Identity,
                                bias=modT[:, k, 2, lc:lc + 1], scale=modT[:, k, 3, lc:lc + 1]), [pt, modT], [u2T])
                    pl_ = C.nextps()
                    for k in range(8):
                        S.op("pe", lambda e, pl_=pl_, k=k: e.matmul(pl_[0:16, 0:128], lhsT=rw[:, k, 0:16], rhs=u2T[:, k, :], start=(k == 0), stop=(k == 7)),
                             [rw, u2T], [pl_])
                    S.op("dve", lambda e, pl_=pl_, t=t: e.tensor_copy(out=LG[:, t * 128:(t + 1) * 128], in_=pl_[0:16, 0:128]), [pl_], [LG])
                S.dma("sp", lambda e: e.dma_start(out=LGD.t[:, :], in_=LG[:]), [LG], [LGD])
                S.barrier()
                S.emit()

        def moe_segs(full_ctx):
            segs = [(0, NLT, NLAT // 8, 0, 0)]
            if full_ctx:
                segs.append((NLT, 2, 32, NLAT // 8, 512))
            return segs

        def phase_F(l, full_ctx):
            segs = moe_segs(full_ctx)
            NTOT = sum(sg[2] for sg in segs)
            with ExitStack() as pes:
                LG = C.sb("fLG", [16, T], es=pes)
                aff = C.sb("aff", [16, T], es=pes)
                work = C.sb("work", [16, NLAT], es=pes)
                m8 = C.sb("m8", [16, 8], es=pes)
                key = C.sb("key", [16, T], es=pes)
                ones16 = C.sb("ones16", [16, 16], es=pes)
                onesr = C.sb("onesr", [16, NLAT], es=pes)
                id16p = C.sb("id16p", [16, 128], es=pes)
                keyTok = C.sb("keyTok", [128, NT, 16], es=pes)
                iot = C.sb("iot", [128, 512], es=pes)
                S.dma("sp", lambda e: e.dma_start(out=LG[:], in_=LGD.t[:, :]), [], [LG])
                S.dma("sp", lambda e: e.dma_start(out=iot[:], in_=iota512[:, :]), [], [iot])
                S.op("pool", lambda e: e.memset(ones16[:], 1.0), [], [ones16])
                S.op("pool", lambda e: e.memset(onesr[:], 1.0), [], [onesr])
                S.op("pool", lambda e: e.memset(id16p[:], 0.0), [], [id16p])
                S.op("dve", lambda e: e.tensor_copy(out=id16p[:, 0:16], in_=ident_f[0:16, 0:16]), [id16p, ident_f], [id16p])
                S.op("act", lambda e: e.activation(out=aff[:], in_=LG[:], func=AF.Exp), [LG], [aff])
                for c0 in range(0, T, 512):
                    cw = min(512, T - c0)
                    ps = C.nextps()
                    S.op("pe", lambda e, ps=ps, c0=c0, cw=cw: e.matmul(ps[0:16, 0:cw], lhsT=ones16[:], rhs=aff[:, c0:c0 + cw], start=True, stop=True),
                         [ones16, aff], [ps])
                    S.op("dve", lambda e, ps=ps, c0=c0, cw=cw: e.reciprocal(out=LG[:, c0:c0 + cw], in_=ps[0:16, 0:cw]), [ps], [LG])
                S.op("dve", lambda e: e.tensor_tensor(out=aff[:], in0=aff[:], in1=LG[:], op=ALU.mult), [aff, LG], [aff])
                for (t0, ntl, cap, c0, r0) in segs:
                    n_ = ntl * 128
                    a0 = t0 * 128
                    S.op("dve", lambda e, a0=a0, n_=n_: e.tensor_copy(out=work[:, 0:n_], in_=aff[:, a0:a0 + n_]), [aff], [work])
                    for rnd in range(cap // 8):
                        S.op("dve", lambda e, n_=n_: e.max(out=m8[:], in_=work[:, 0:n_]), [work], [m8])
                        S.op("dve", lambda e, n_=n_: e.match_replace(out=work[:, 0:n_], in_to_replace=m8[:], in_values=work[:, 0:n_], imm_value=-1.0),
                             [work, m8], [work])
                    S.op("dve", lambda e, n_=n_: e.tensor_single_scalar(out=work[:, 0:n_], in_=work[:, 0:n_], scalar=0.0, op=ALU.is_lt), [work], [work])
                    S.op("dve", lambda e, a0=a0, n_=n_: e.tensor_tensor_scan(out=key[:, a0:a0 + n_], data0=onesr[:, 0:n_], data1=work[:, 0:n_], initial=0.0,
                                                                            op0=ALU.mult, op1=ALU.add), [work, onesr], [key])
                    S.op("dve", lambda e, a0=a0, n_=n_: e.tensor_tensor(out=key[:, a0:a0 + n_], in0=key[:, a0:a0 + n_], in1=work[:, 0:n_], op=ALU.mult), [key, work], [key])
                    S.op("dve", lambda e, a0=a0, n_=n_: e.tensor_scalar_add(out=key[:, a0:a0 + n_], in0=key[:, a0:a0 + n_], scalar1=-1.0), [key], [key])
                S.dma("sp", lambda e: e.dma_start(out=KEYD.t[:, :], in_=key[:]), [key], [KEYD])
                S.dma("sp", lambda e: e.dma_start(out=AFFD.t[:, :], in_=aff[:]), [aff], [AFFD])
                tl_all = [t for (t0, ntl, cap, c0, r0) in segs for t in range(t0, t0 + ntl)]
                for q0 in range(0, len(tl_all), 4):
                    grp = tl_all[q0:q0 + 4]
                    ps = C.nextps()
                    for qi, t in enumerate(grp):
                        S.op("pe", lambda e, ps=ps, qi=qi, t=t: e.matmul(ps[:, qi * 128:(qi + 1) * 128], lhsT=key[:, t * 128:(t + 1) * 128], rhs=id16p[:],
                                                                          start=True, stop=True), [key, id16p], [ps])
                    for qi, t in enumerate(grp):
                        S.op("dve", lambda e, ps=ps, qi=qi, t=t: e.tensor_copy(out=keyTok[:, t, :], in_=ps[:, qi * 128:qi * 128 + 16]), [ps], [keyTok])
                S.dma("sp", lambda e: e.dma_start(out=KTD.t[:, :], in_=keyTok[:].rearrange("p n c -> p (n c)")), [keyTok], [KTD])
                S.barrier()
                S.emit()
            with ExitStack() as pes:
                keyTok = C.sb("keyTok2", [128, NT, 16], es=pes)
                iot = C.sb("iot2", [128, 512], es=pes)
                Se = C.sb("Se", [128, NT, 512], BF16, es=pes)
                xinT = C.sb("xinT", [128, 8, NTOT], BF16, es=pes)
                hT = C.sb("hT", [128, 16, NTOT], BF16, es=pes)
                yacc = C.sb("yacc", [128, 5, D], es=pes)
                yw = C.sb("yw", [128, 5, D], BF16, es=pes)
                u2s = [C.sb(f"u2s{i}", [128, D], BF16, es=pes) for i in range(2)]
                wg = [C.sb(f"wg{i}", [128, 8, 512], BF16, es=pes) for i in range(2)]
                wu = [C.sb(f"wu{i}", [128, 8, 512], BF16, es=pes) for i in range(2)]
                wd = [C.sb(f"wd{i}", [128, 4, D], BF16, es=pes) for i in range(2)]
                sg_ = [C.sb(f"sgt{i}", [128, 512], es=pes) for i in range(2)]
                S.dma("sp", lambda e: e.dma_start(out=keyTok[:].rearrange("p n c -> p (n c)"), in_=KTD.t[:, :]), [], [keyTok])
                S.dma("sp", lambda e: e.dma_start(out=iot[:], in_=iota512[:, :]), [], [iot])
                C.rot = [4, 5, 6, 7]
                wcnt = 0
                ucnt = 0
                for ex in range(16):
                    for (t0, ntl, cap, c0, r0) in segs:
                        for t in range(t0, t0 + ntl):
                            S.op("dve", lambda e, t=t, cap=cap, ex=ex: e.tensor_scalar(
                                out=Se[:, t, 0:cap], in0=iot[:, 0:cap], scalar1=keyTok[:, t, ex:ex + 1], scalar2=None, op0=ALU.is_equal),
                                [iot, keyTok], [Se.sub(t)])
                        for kh in range(2):
                            for ti, t in enumerate(range(t0, t0 + ntl)):
                                ut = u2s[ucnt % 2]
                                ucnt += 1
                                S.dma("sp", lambda e, ut=ut, t=t: e.dma_start(out=ut[:], in_=U2.t[t * 128:(t + 1) * 128, :]), [], [ut])
                                for kk in range(4):
                                    k = kh * 4 + kk
                                    S.op("pe", lambda e, ut=ut, t=t, k=k, kk=kk, cap=cap, ti=ti, ntl=ntl: e.matmul(
                                        C.ps[kk][:, 0:cap], lhsT=ut[:, k * 128:(k + 1) * 128], rhs=Se[:, t, 0:cap],
                                        start=(ti == 0), stop=(ti == ntl - 1)), [ut, Se.sub(t)], [C.ps[kk]])
                            for kk in range(4):
                                k = kh * 4 + kk
                                S.op("act", lambda e, k=k, kk=kk, c0=c0, cap=cap: e.copy(out=xinT[:, k, c0:c0 + cap], in_=C.ps[kk][:, 0:cap]),
                                     [C.ps[kk]], [xinT])
                    jts = []
                    for (t0, ntl, cap, c0, r0) in segs:
                        for j0 in range(0, cap, 128):
                            jts.append((r0 + j0, min(128, cap - j0), c0 + j0))
                    for fg in range(4):
                        g_, u_, d_ = wg[wcnt % 2], wu[wcnt % 2], wd[wcnt % 2]
                        wcnt += 1
                        for k in range(8):
                            S.dma("pool", lambda e, g_=g_, ex=ex, fg=fg, k=k: e.dma_start(
                                out=g_[:, k, :], in_=w_gate[l, ex, k * 128:(k + 1) * 128, fg * 512:(fg + 1) * 512]), [], [g_])
                            S.dma("pool", lambda e, u_=u_, ex=ex, fg=fg, k=k: e.dma_start(
                                out=u_[:, k, :], in_=w_up[l, ex, k * 128:(k + 1) * 128, fg * 512:(fg + 1) * 512]), [], [u_])
                        for cc in range(4):
                            S.dma("pool", lambda e, d_=d_, ex=ex, fg=fg, cc=cc: e.dma_start(
                                out=d_[:, cc, :], in_=w_down[l, ex, fg * 512 + cc * 128:fg * 512 + (cc + 1) * 128, :]), [], [d_])
                        for fc in range(4):
                            f = fg * 4 + fc
                            for (t0, ntl, cap, c0, r0) in segs:
                                pg = C.nextps()
                                pu_ = C.nextps()
                                for k in range(8):
                                    S.op("pe", lambda e, pg=pg, g_=g_, k=k, fc=fc, c0=c0, cap=cap: e.matmul(
                                        pg[:, 0:cap], lhsT=g_[:, k, fc * 128:(fc + 1) * 128], rhs=xinT[:, k, c0:c0 + cap], start=(k == 0), stop=(k == 7)),
                                        [g_, xinT], [pg])
                                for k in range(8):
                                    S.op("pe", lambda e, pu_=pu_, u_=u_, k=k, fc=fc, c0=c0, cap=cap: e.matmul(
                                        pu_[:, 0:cap], lhsT=u_[:, k, fc * 128:(fc + 1) * 128], rhs=xinT[:, k, c0:c0 + cap], start=(k == 0), stop=(k == 7)),
                                        [u_, xinT], [pu_])
                                sgt = sg_[f % 2]
                                S.op("act", lambda e, pg=pg, sgt=sgt, cap=cap: e.activation(out=sgt[:, 0:cap], in_=pg[:, 0:cap], func=AF.Silu), [pg], [sgt])
                                S.op("dve", lambda e, pu_=pu_, sgt=sgt, f=f, c0=c0, cap=cap: e.tensor_tensor(
                                    out=hT[:, f, c0:c0 + cap], in0=pu_[:, 0:cap], in1=sgt[:, 0:cap], op=ALU.mult), [pu_, sgt], [hT])
                        for ji, (ro, rows, co) in enumerate(jts):
                            for half in range(2):
                                py = C.nextps()
                                for fc in range(4):
                                    S.op("pe", lambda e, py=py, d_=d_, fc=fc, fg=fg, co=co, rows=rows, half=half: e.matmul(
                                        py[0:rows, :], lhsT=hT[:, fg * 4 + fc, co:co + rows], rhs=d_[:, fc, half * 512:(half + 1) * 512],
                                        start=(fc == 0), stop=(fc == 3)), [hT, d_], [py])
                                if fg == 0:
                                    S.op("dve", lambda e, py=py, ji=ji, rows=rows, half=half: e.tensor_copy(
                                        out=yacc[0:rows, ji, half * 512:(half + 1) * 512], in_=py[0:rows, :]), [py], [yacc])
                                elif fg < 3:
                                    S.op("dve", lambda e, py=py, ji=ji, rows=rows, half=half: e.tensor_tensor(
                                        out=yacc[0:rows, ji, half * 512:(half + 1) * 512], in0=yacc[0:rows, ji, half * 512:(half + 1) * 512],
                                        in1=py[0:rows, :], op=ALU.add), [py, yacc], [yacc])
                                else:
                                    S.op("dve", lambda e, py=py, ji=ji, rows=rows, half=half: e.tensor_tensor(
                                        out=yw[0:rows, ji, half * 512:(half + 1) * 512], in0=yacc[0:rows, ji, half * 512:(half + 1) * 512],
                                        in1=py[0:rows, :], op=ALU.add), [py, yacc], [yw])
                    for ji, (ro, rows, co) in enumerate(jts):
                        S.dma("act", lambda e, ji=ji, ro=ro, rows=rows, ex=ex: e.dma_start(out=YE.t[ex, ro:ro + rows, :], in_=yw[0:rows, ji, :]),
                              [yw], [YE.sub(ex, ji)])
                C.rot = None
                S.barrier()
                S.emit()

        def phase_G(l, full_ctx, last):
            segs = moe_segs(full_ctx)
            with ExitStack() as pes:
                key = C.sb("gkey", [16, T], es=pes)
                aff = C.sb("gaff", [16, T], es=pes)
                selE = C.sb("selE", [16, 16, 128], es=pes)
                jcol = C.sb("jcol", [128, 4], es=pes)
                lnG = C.sb("glnG", [128, 2, D], es=pes)
                lnB = C.sb("glnB", [128, 2, D], es=pes)
                acc = C.sb("acc", [128, 8, D], es=pes)
                yes = [C.sb(f"ye{i}", [128, 4, D], BF16, es=pes) for i in range(2)]
                Ws = [C.sb(f"W{i}", [128, 4, 512], BF16, es=pes) for i in range(2)]
                affB = [C.sb(f"affB{i}", [128, 512], es=pes) for i in range(2)]
                hts = [C.sb(f"ght{i}", [128, D], es=pes) for i in range(2)]
                z = C.sb("gz", [128, D], es=pes)
                sqt = C.sb("gsqt", [128, D], es=pes)
                st = C.sb("gst", [128, 4], es=pes)
                hn = [C.sb(f"ghn{i}", [128, D], es=pes) for i in range(2)]
                S.dma("sp", lambda e: e.dma_start(out=key[:], in_=KEYD.t[:, :]), [], [key])
                S.dma("sp", lambda e: e.dma_start(out=aff[:], in_=AFFD.t[:, :]), [], [aff])
                S.dma("sp", lambda e: e.dma_start(out=selE[:], in_=selE_in[:, :, :]), [], [selE])
                S.dma("sp", lambda e: e.dma_start(out=jcol[:], in_=jcol_in[:, :]), [], [jcol])
                S.dma("sp", lambda e: e.dma_start(out=lnG[:].rearrange("p g d -> p (g d)"), in_=ln_g[l:l + 1, :].broadcast_to([128, 2 * D])), [], [lnG])
                S.dma("sp", lambda e: e.dma_start(out=lnB[:].rearrange("p g d -> p (g d)"), in_=ln_b[l:l + 1, :].broadcast_to([128, 2 * D])), [], [lnB])
                cnt = 0
                for (t0, ntl, cap, c0, r0) in segs:
                    lc = 0 if t0 < NLT else 1
                    njt = (cap + 127) // 128
                    for gq in range(t0, t0 + ntl, 8):
                        gtiles = list(range(gq, min(gq + 8, t0 + ntl)))
                        items = [(ex, q0) for ex in range(16) for q0 in range(0, len(gtiles), 4)]
                        fr = {}

                        def front(ii, gtiles=gtiles, njt=njt, cap=cap, r0=r0):
                            ex, q0 = items[ii]
                            ye = yes[ex % 2]
                            if q0 == 0:
                                for jt in range(njt):
                                    rows = min(128, cap - jt * 128)
                                    S.dma("sp", lambda e, ye=ye, jt=jt, rows=rows, ex=ex, r0=r0: e.dma_start(
                                        out=ye[0:rows, jt, :], in_=YE.t[ex, r0 + jt * 128:r0 + jt * 128 + rows, :]), [ye], [ye])
                            ch = gtiles[q0:q0 + 4]
                            a0 = ch[0] * 128
                            cw = len(ch) * 128
                            pk = C.ps[(ii % 2) * 2]
                            pa = C.ps[(ii % 2) * 2 + 1]
                            S.op("pe", lambda e, pk=pk, ex=ex, a0=a0, cw=cw: e.matmul(pk[:, 0:cw], lhsT=selE[:, ex, :], rhs=key[:, a0:a0 + cw], start=True, stop=True),
                                 [selE, key], [pk])
                            S.op("pe", lambda e, pa=pa, ex=ex, a0=a0, cw=cw: e.matmul(pa[:, 0:cw], lhsT=selE[:, ex, :], rhs=aff[:, a0:a0 + cw], start=True, stop=True),
                                 [selE, aff], [pa])
                            ab = affB[ii % 2]
                            W = Ws[ii % 2]
                            S.op("act", lambda e, pa=pa, ab=ab, cw=cw: e.copy(out=ab[:, 0:cw], in_=pa[:, 0:cw]), [pa], [ab])
                            for jt in range(njt):
                                S.op("dve", lambda e, pk=pk, ab=ab, W=W, jt=jt, cw=cw: e.scalar_tensor_tensor(
                                    out=W[:, jt, 0:cw], in0=pk[:, 0:cw], scalar=jcol[:, jt:jt + 1], in1=ab[:, 0:cw], op0=ALU.is_equal, op1=ALU.mult),
                                    [pk, ab, jcol], [W])
                            fr[ii] = (ch, W, ye)

                        def back(ii, gq=gq, njt=njt, cap=cap):
                            ex, q0 = items[ii]
                            ch, W, ye = fr.pop(ii)
                            for qi, t in enumerate(ch):
                                ti = t - gq
                                for half in range(2):
                                    py = C.nextps()
                                    for jt in range(njt):
                                        rows = min(128, cap - jt * 128)
                                        S.op("pe", lambda e, py=py, W=W, ye=ye, jt=jt, rows=rows, qi=qi, half=half, njt=njt: e.matmul(
                                            py[:, :], lhsT=W[0:rows, jt, qi * 128:(qi + 1) * 128], rhs=ye[0:rows, jt, half * 512:(half + 1) * 512],
                                            start=(jt == 0), stop=(jt == njt - 1)), [W, ye], [py])
                                    if ex == 0:
                                        S.op("dve", lambda e, py=py, ti=ti, half=half: e.tensor_copy(out=acc[:, ti, half * 512:(half + 1) * 512], in_=py[:, :]),
                                             [py], [acc.sub(ti)])
                                    else:
                                        S.op("dve", lambda e, py=py, ti=ti, half=half: e.tensor_tensor(
                                            out=acc[:, ti, half * 512:(half + 1) * 512], in0=acc[:, ti, half * 512:(half + 1) * 512], in1=py[:, :], op=ALU.add),
                                            [py, acc.sub(ti)], [acc.sub(ti)])

                        C.rot = [4, 5, 6, 7]
                        for ii in range(len(items) + 1):
                            if ii < len(items):
                                front(ii)
                            if ii >= 1:
                                back(ii - 1)
                        C.rot = None
                        for t in gtiles:
                            ti = t - gq
                            ht = hts[t % 2]
                            S.dma("act", lambda e, ht=ht, t=t: e.dma_start(out=ht[:], in_=H.t[t * 128:(t + 1) * 128, :]), ["H/%d" % t], [ht])
                            S.op("pool", lambda e, ti=ti, lc=lc: e.tensor_tensor(out=z[:], in0=acc[:, ti, :], in1=gateB[:, 1, lc, :], op=ALU.mult),
                                 [acc.sub(ti), gateB], [z])
                            S.op("dve", lambda e, ht=ht: e.scalar_tensor_tensor(out=z[:], in0=ht[:], scalar=ALPHA, in1=z[:], op0=ALU.mult, op1=ALU.add),
                                 [ht, z], [z])
                            h_ = hn[t % 2]
                            layer_norm_tile(z, sqt, st, lnG, lnB, 1, h_)
                            if last:
                                S.dma("sp", lambda e, h_=h_, t=t: e.dma_start(out=out[t * 128:(t + 1) * 128, :], in_=h_[:]), [h_], ["out/%d" % t])
                            else:
                                S.dma("sp", lambda e, h_=h_, t=t: e.dma_start(out=H.t[t * 128:(t + 1) * 128, :], in_=h_[:]), [h_], ["H/%d" % t])
                S.barrier()
                S.emit()

        for l in range(NL):
            full_ctx = l < DEPTH - 1
            if "A" in phases.split(","):
                phase_A(l)
            if "BC" in phases.split(","):
                phase_BC(l)
            if "ret" in phases.split(","):
                phase_ret(l, full_ctx)
            if "diff" in phases.split(","):
                phase_diff(l, full_ctx)
            if "swa" in phases.split(","):
                phase_swa(l, full_ctx)
            if "gdn" in phases.split(","):
                phase_gdn(l, full_ctx)
            if "E" in phases.split(","):
                phase_E(l, full_ctx)
            if "F" in phases.split(","):
                phase_F(l, full_ctx)
            if "G" in phases.split(","):
                phase_G(l, full_ctx, last=(l == NL - 1 and "keepH" not in dbg))
    dbg_t["ninst"] = S.ninst
    return nc, dbg_t


def prep_shared(inputs, NLT, NL):
    sh = {}
    ada_b = np.asarray(inputs["ada_b"], np.float32)[:NL]
    sh["ada_w"] = np.ascontiguousarray(np.asarray(inputs["ada_w"], np.float32)[:NL])
    sh["ada_b"] = np.ascontiguousarray(ada_b)
    sh["ada_bT"] = np.ascontiguousarray(ada_b.reshape(NL, 48, 128).transpose(0, 2, 1))
    w_in = np.asarray(inputs["w_in"], np.float32)
    sh["w_in"] = np.stack([build_w_in_ext(w_in[l]) for l in range(NL)])
    sh["w_out"] = np.ascontiguousarray(np.asarray(inputs["w_out"], np.float32)[:NL])
    sh["rope"] = rope_tables(NLT)
    sh["ret_decay"] = np.ascontiguousarray(np.asarray(inputs["ret_decay"], np.float32)[:NL].reshape(NL, 8))
    sh["ln_g"] = np.ascontiguousarray(np.asarray(inputs["ln_g"], np.float32)[:NL].reshape(NL, 2 * D))
    sh["ln_b"] = np.ascontiguousarray(np.asarray(inputs["ln_b"], np.float32)[:NL].reshape(NL, 2 * D))
    sh["diff_lambda"] = np.ascontiguousarray(np.asarray(inputs["diff_lambda"], np.float32)[:NL].reshape(NL, 128))
    sh["diff_norm"] = np.ascontiguousarray(np.asarray(inputs["diff_norm"], np.float32)[:NL])
    sh["swa_sink"] = np.ascontiguousarray(np.asarray(inputs["swa_sink"], np.float32)[:NL])
    rw = np.asarray(inputs["router_w"], np.float32)[:NL]
    rwp = np.zeros((NL, D, 128), np.float32)
    rwp[:, :, :16] = rw
    sh["router_wp"] = rwp
    sh["w_gate"] = np.ascontiguousarray(np.asarray(inputs["w_gate"], np.float32)[:NL])
    sh["w_up"] = np.ascontiguousarray(np.asarray(inputs["w_up"], np.float32)[:NL])
    sh["w_down"] = np.ascontiguousarray(np.asarray(inputs["w_down"], np.float32)[:NL])
    sh["iota512"] = np.ascontiguousarray(np.broadcast_to(np.arange(512, dtype=np.float32)[None, :], (128, 512)))
    sh["jcol_in"] = np.ascontiguousarray(np.arange(128, dtype=np.float32)[:, None] + 128.0 * np.arange(4, dtype=np.float32)[None, :])
    se = np.zeros((16, 16, 128), np.float32)
    for e_ in range(16):
        se[e_, e_, :] = 1.0
    sh["selE_in"] = se
    gc = np.asarray(inputs["gdn_conv"], np.float32)[:NL]
    sh["gdn_convT"] = np.ascontiguousarray(gc.reshape(NL, 3, 6, 128).transpose(0, 3, 2, 1).reshape(NL, 128, 18))
    sh["gdn_a_log"] = np.ascontiguousarray(np.asarray(inputs["gdn_a_log"], np.float32)[:NL].reshape(NL, 8))
    sh["gdn_dt_bias"] = np.ascontiguousarray(np.asarray(inputs["gdn_dt_bias"], np.float32)[:NL].reshape(NL, 8))
    sh["gdn_norm"] = np.ascontiguousarray(np.asarray(inputs["gdn_norm"], np.float32)[:NL])
    i = np.arange(128, dtype=np.float32)
    sh["cmask"] = np.ascontiguousarray(i[None, :] - i[:, None])
    return sh


def prep_core(inputs, b, NLT):
    m = {}
    x = np.asarray(inputs["x"], np.float32)[b, :NLT * 128]
    ctx = np.asarray(inputs["ctx"], np.float32)[b]
    m["h0"] = np.ascontiguousarray(np.concatenate([x, ctx], 0))
    c = np.asarray(inputs["c"], np.float32)[b]
    cx = np.asarray(inputs["c_ctx"], np.float32)
    m["cc"] = np.ascontiguousarray(np.stack([c, cx], -1).reshape(8, 128, 2).transpose(1, 0, 2))
    return m


def kernel(**inputs):
    NLT, NL = 32, 4
    nc, _ = build(NLT, NL)
    sh = prep_shared(inputs, NLT, NL)
    in_maps = []
    for b in range(8):
        m = dict(sh)
        m.update(prep_core(inputs, b, NLT))
        in_maps.append(m)
    res = run_bass_kernel_spmd(nc, in_maps, core_ids=list(range(8)))
    return np.stack([np.asarray(r["out"], np.float32) for r in res.results], 0)
```

```python
import math
import numpy as np
import ml_dtypes
from contextlib import ExitStack
import concourse.bass as bass
import concourse.mybir as mybir
from concourse.bass_utils import run_bass_kernel_spmd

F32 = mybir.dt.float32
BF16 = mybir.dt.bfloat16
F32R = mybir.dt.float32r
AF = mybir.ActivationFunctionType
ALU = mybir.AluOpType
AX = mybir.AxisListType

ENGS = ["pe", "dve", "act", "pool", "sp"]
DMA_RING = 6
SEM_EPOCH = 20000

D = 1024
DEPTH = 4
ALPHA = (2 * DEPTH) ** 0.25
LN_EPS = 1e-5
IN_W = 3344
NEG = -30000.0


class Sched:
    def __init__(self, nc, es, same_engine_sync=True):
        self.nc = nc
        self.es = es
        self.q = {e: [] for e in ENGS}
        self.cnt = {e: 0 for e in ENGS}
        self.epoch = {e: 0 for e in ENGS}
        self.sem = {e: es.enter_context(nc.semaphore(f"s_{e}_0")) for e in ENGS}
        self.dq = ["sp", "act", "pool"]
        self.dsem = {e: [es.enter_context(nc.semaphore(f"d_{e}_{i}")) for i in range(DMA_RING)]
                     for e in self.dq}
        self.dcnt = {e: 0 for e in self.dq}
        self.lastw = {}
        self.readers = {}
        self.seen = {}
        self.same = same_engine_sync
        self.ninst = 0

    def _key(self, k):
        return k if isinstance(k, str) else k.k

    def _need(self, eng, tok, waits):
        if tok is None:
            return
        sem, val, prod = tok
        if prod == eng and (eng == "pe" or not self.same):
            return
        kk = (eng, id(sem))
        if self.seen.get(kk, 0) >= val:
            return
        self.seen[kk] = val
        waits.append((sem, val))

    def _deps(self, eng, reads, writes):
        waits = []
        for k in reads:
            k = self._key(k)
            self._need(eng, self.lastw.get(k), waits)
            if k.startswith("ps"):
                for t in self.readers.get(k, ()):
                    if t[2] != eng:
                        self._need(eng, t, waits)
        for k in writes:
            k = self._key(k)
            self._need(eng, self.lastw.get(k), waits)
            for t in self.readers.get(k, ()):
                self._need(eng, t, waits)
        return waits

    def _commit(self, tok, reads, writes):
        for k in reads:
            self.readers.setdefault(self._key(k), []).append(tok)
        for k in writes:
            k = self._key(k)
            self.lastw[k] = tok
            self.readers[k] = []

    def op(self, eng, fn, reads=(), writes=()):
        waits = self._deps(eng, reads, writes)
        if self.cnt[eng] >= SEM_EPOCH:
            self.epoch[eng] += 1
            self.sem[eng] = self.es.enter_context(self.nc.semaphore(f"s_{eng}_{self.epoch[eng]}"))
            self.cnt[eng] = 0
        self.cnt[eng] += 1
        tok = (self.sem[eng], self.cnt[eng], eng)
        self.q[eng].append((waits, fn, self.sem[eng], 1))
        self._commit(tok, reads, writes)
        self.ninst += 1

    def dma(self, qn, fn, reads=(), writes=()):
        waits = self._deps(qn, reads, writes)
        i = self.dcnt[qn]
        self.dcnt[qn] += 1
        slot, rnd = i % DMA_RING, i // DMA_RING
        sem = self.dsem[qn][slot]
        if rnd > 0:
            self._need(qn, (sem, 16 * rnd, None), waits)
        tok = (sem, 16 * (rnd + 1), None)
        self.q[qn].append((waits, fn, sem, 16))
        self._commit(tok, reads, writes)
        self.ninst += 1

    def barrier(self):
        toks = []
        for e in ENGS:
            if self.cnt[e] > 0:
                toks.append((self.sem[e], self.cnt[e], e))
        for qn in self.dq:
            n = self.dcnt[qn]
            for slot in range(DMA_RING):
                if n > slot:
                    rounds = (n - 1 - slot) // DMA_RING + 1
                    toks.append((self.dsem[qn][slot], 16 * rounds, None))
        for e in ENGS:
            waits = []
            for t in toks:
                if t[2] == e:
                    continue
                self._need(e, t, waits)
            self.q[e].append((waits, None, None, 0))
        self.lastw = {}
        self.readers = {}

    def emit(self):
        nc = self.nc
        q = self.q

        def run(e, name):
            for waits, fn, sem, inc in q[name]:
                for (s, v) in waits:
                    e.wait_ge(s, v)
                if fn is not None:
                    fn(e).then_inc(sem, inc)

        with nc.Block() as block:
            @block.tensor
            def _(e):
                run(e, "pe")

            @block.vector
            def _(e):
                run(e, "dve")

            @block.scalar
            def _(e):
                run(e, "act")

            @block.gpsimd
            def _(e):
                run(e, "pool")

            @block.sync
            def _(e):
                run(e, "sp")
        self.q = {e: [] for e in ENGS}


class Buf:
    def __init__(self, t, k):
        self.t = t
        self.k = k

    def sub(self, *idx):
        return self.k + "/" + "/".join(str(i) for i in idx)

    def __getitem__(self, key):
        return self.t[key]


class Ctx:
    def __init__(self, nc, es, same_engine_sync=True):
        self.nc = nc
        self.es = es
        self.S = Sched(nc, es, same_engine_sync)
        self.ps = [self.psum(f"ps{i}") for i in range(8)]
        self.psi = 0
        self.uid = 0

    def sb(self, name, shape, dtype=F32, es=None):
        self.uid += 1
        nm = f"{name}_{self.uid}"
        t = (es or self.es).enter_context(self.nc.sbuf_tensor(nm, list(shape), dtype))
        return Buf(t, nm)

    def psum(self, name, shape=(128, 512), dtype=F32):
        t = self.es.enter_context(self.nc.psum_tensor(name, list(shape), dtype))
        return Buf(t, name)

    def dram(self, name, shape, dtype=F32, kind="Internal"):
        t = self.nc.dram_tensor(name, list(shape), dtype, kind=kind)
        return Buf(t, name)

    def nextps(self):
        r = getattr(self, "rot", None) or list(range(8))
        p = self.ps[r[self.psi % len(r)]]
        self.psi += 1
        return p


def fm_groups():
    def swp(cols, blk):
        cols = np.asarray(cols)
        half = blk // 2
        c = cols.reshape(-1, blk)
        return np.concatenate([c[:, half:], c[:, :half]], 1).reshape(-1)
    g = []
    def add_rot(name, start, width, blk, kind):
        for i in range(width // 128):
            cols = np.arange(start + i * 128, start + (i + 1) * 128)
            g.append((f"{name}{i}", cols, kind, swp(cols, blk)))
    add_rot("ret_q", 0, 256, 64, 0)
    add_rot("ret_k", 256, 256, 64, 0)
    add_rot("diff_q", 1024, 256, 32, 1)
    add_rot("diff_k", 1280, 256, 32, 1)
    for i in range(6):
        g.append((f"gdn{i}", np.arange(1792 + i * 128, 1792 + (i + 1) * 128), None, None))
    add_rot("swa_q", 2832, 256, 64, 2)
    add_rot("swa_k", 3088, 128, 64, 2)
    return g


TM_GROUPS = [("ret_vg", 512, 512), ("diff_v", 1536, 256), ("gdn_gab", 2560, 272), ("swa_v", 3216, 128)]


def build_w_in_ext(w_in_l):
    cols = []
    for name, c, kind, sw in fm_groups():
        cols.append(c)
        if kind is not None:
            cols.append(sw)
    for name, st, w in TM_GROUPS:
        cols.append(np.arange(st, st + w))
    cols = np.concatenate(cols)
    return np.ascontiguousarray(w_in_l[:, cols])


def rope_tables(NLT):
    n = NLT * 128
    T = n + 256
    tabs = np.zeros((3, 2, 128, T), np.float32)
    tabs[:, 0] = 1.0
    f32 = np.float32
    theta = (1.0 / (f32(10000.0) ** np.linspace(0.0, 1.0, 32, dtype=np.float32))).astype(np.float32)
    ang_ret = (np.arange(n, dtype=np.float32)[:, None] * theta).astype(np.float32)
    def axial(rot_dim):
        rows = n // 64
        row = np.repeat(np.arange(rows, dtype=np.float32), 64)
        col = np.tile(np.arange(64, dtype=np.float32), rows)
        nf = rot_dim // 4
        inv = (f32(10000.0) ** (-np.arange(nf, dtype=np.float32) / f32(nf))).astype(np.float32)
        return np.concatenate([row[:, None] * inv, col[:, None] * inv], -1).astype(np.float32)
    ang_diff = axial(32)
    ang_swa = axial(64)
    for kind, (ang, blk) in enumerate([(ang_ret, 64), (ang_diff, 32), (ang_swa, 64)]):
        half = blk // 2
        for f in range(128):
            d = f % blk
            j = d % half
            c = np.cos(ang[:, j]).astype(np.float32)
            s = np.sin(ang[:, j]).astype(np.float32)
            tabs[kind, 0, f, :n] = c
            tabs[kind, 1, f, :n] = -s if d < half else s
    return tabs


def build(NLT=32, NL=4, dbg=(), phases="A,BC,ret,diff,swa,gdn,E,F,G"):
    nc = bass.Bass("TRN2", target_bir_lowering=False)
    NT = NLT + 2
    T = NT * 128
    NLAT = NLT * 128
    FMG = fm_groups()
    NFM = len(FMG)
    NWE = sum(128 * (2 if g[2] is not None else 1) for g in FMG) + sum(w for _, _, w in TM_GROUPS)

    def din(name, shape, dt=F32):
        return nc.dram_tensor(name, list(shape), dt, kind="ExternalInput")

    h0 = din("h0", [T, D])
    cc = din("cc", [128, 8, 2])
    ada_w = din("ada_w", [NL, D, 6 * D])
    ada_b = din("ada_b", [NL, 6 * D])
    ada_bT = din("ada_bT", [NL, 128, 48])
    w_in = din("w_in", [NL, D, NWE])
    w_out = din("w_out", [NL, D, D])
    rope = din("rope", [3, 2, 128, T])
    ret_decay = din("ret_decay", [NL, 8])
    ln_g = din("ln_g", [NL, 2 * D])
    ln_b = din("ln_b", [NL, 2 * D])
    cmask = din("cmask", [128, 128])
    diff_lambda = din("diff_lambda", [NL, 128])
    diff_norm = din("diff_norm", [NL, 64])
    swa_sink = din("swa_sink", [NL, 4])
    router_wp = din("router_wp", [NL, D, 128])
    w_gate = din("w_gate", [NL, 16, D, 2 * D])
    w_up = din("w_up", [NL, 16, D, 2 * D])
    w_down = din("w_down", [NL, 16, 2 * D, D])
    iota512 = din("iota512", [128, 512])
    jcol_in = din("jcol_in", [128, 4])
    selE_in = din("selE_in", [16, 16, 128])
    gdn_convT = din("gdn_convT", [NL, 128, 18])
    gdn_a_log = din("gdn_a_log", [NL, 8])
    gdn_dt_bias = din("gdn_dt_bias", [NL, 8])
    gdn_norm = din("gdn_norm", [NL, 64])
    out = nc.dram_tensor("out", [NLAT, D], F32, kind="ExternalOutput")

    dbg_t = {}

    with ExitStack() as es:
        C = Ctx(nc, es)
        S = C.S
        H = C.dram("H", [T, D], kind=("ExternalOutput" if "H" in dbg else "Internal"))
        FMO = C.dram("FMO", [NFM, 128, T], BF16)
        TMA = C.dram("TMA", [T, 512], BF16)
        TMB = C.dram("TMB", [T, 256], BF16)
        TMC = C.dram("TMC", [T, 256], BF16)
        GAB = C.dram("GAB", [128, NT * 16], F32)
        TMD = C.dram("TMD", [T, 128], BF16)
        GFM = C.dram("GFM", [4, 128, T], BF16)
        GTM = C.dram("GTM", [T, 512], BF16)
        U2 = C.dram("U2", [T, D], BF16)
        LGD = C.dram("LGD", [16, T], F32)
        YE = C.dram("YE", [16, 640, D], BF16)
        KTD = C.dram("KTD", [128, NT * 16], F32)
        KEYD = C.dram("KEYD", [16, T], F32)
        AFFD = C.dram("AFFD", [16, T], F32)
        Y = C.dram("Y", [T, D], BF16, kind=("ExternalOutput" if "Y" in dbg else "Internal"))
        if "FMO" in dbg:
            dbg_t["FMO"] = FMO
        fm_index = {g[0]: i for i, g in enumerate(FMG)}

        ident_f = C.sb("ident_f", [128, 128])
        ident_b = C.sb("ident_b", [128, 128], BF16)
        condT = C.sb("condT", [128, 8, 2])
        condB = [C.sb(f"condB{i}", [128, 8, 128]) for i in range(2)]
        modT = C.sb("modT", [128, 8, 4, 2])
        gateB = C.sb("gateB", [128, 4, 2, D])
        relpos = C.sb("relpos", [128, 128])

        S.op("pool", lambda e: e.memset(ident_f[:], 0.0), [], [ident_f])
        S.op("pool", lambda e: e.affine_select(out=ident_f[:], in_=ident_f[:], pattern=[[-1, 128]],
                                               compare_op=ALU.not_equal, fill=1.0, base=0, channel_multiplier=1),
             [ident_f], [ident_f])
        S.op("dve", lambda e: e.tensor_copy(out=ident_b[:], in_=ident_f[:]), [ident_f], [ident_b])
        S.dma("sp", lambda e: e.dma_start(out=condT[:], in_=cc[:, :, :]), [], [condT])
        S.dma("sp", lambda e: e.dma_start(out=relpos[:], in_=cmask[:, :]), [], [relpos])
        S.op("act", lambda e: e.activation(out=condT[:], in_=condT[:], func=AF.Silu), [condT], [condT])
        for lc in range(2):
            for k in range(8):
                S.op("dve", lambda e, lc=lc, k=k: e.tensor_copy(
                    out=condB[lc][:, k, :], in_=condT[:, k, lc:lc + 1].broadcast_to([128, 128])),
                    [condT], [condB[lc]])
        S.barrier()
        S.emit()

        def phase_A(l):
            with ExitStack() as pes:
                wa = [C.sb(f"wa{i}", [128, 8, 512], es=pes) for i in range(2)]
                bbc = [C.sb(f"bbc{i}", [128, 512], es=pes) for i in range(2)]
                bT = C.sb("bT", [128, 48], es=pes)
                S.dma("sp", lambda e: e.dma_start(out=bT[:], in_=ada_bT[l, :, :]), [], [bT])
                import os
                ADBG = int(os.environ.get("ADBG", 9))
                for g in range(12):
                    m, half = g // 2, g % 2
                    w = wa[g % 2]
                    S.dma("sp", lambda e, w=w, g=g: e.dma_start(
                        out=w[:], in_=ada_w[l, :, g * 512:(g + 1) * 512].rearrange("(k p) n -> p k n", p=128)),
                        [], [w])
                    if m in (2, 5, 3, 4):
                        mi = {2: 0, 5: 1, 3: 2, 4: 3}[m]
                        bb = bbc[g % 2]
                        S.dma("act", lambda e, bb=bb, g=g: e.dma_start(
                            out=bb[:], in_=ada_b[l:l + 1, g * 512:(g + 1) * 512].broadcast_to([128, 512])), [], [bb])
                        if m == 4:
                            S.op("dve", lambda e, bb=bb: e.tensor_scalar_add(out=bb[:], in0=bb[:], scalar1=1.0), [bb], [bb])
                        for lc in range(2):
                            ps = C.nextps()
                            for k in range(8):
                                S.op("pe", lambda e, ps=ps, w=w, lc=lc, k=k: e.matmul(
                                    ps[:, :], lhsT=condB[lc][:, k, :], rhs=w[:, k, :], start=(k == 0), stop=(k == 7)),
                                    [condB[lc], w], [ps])
                            S.op("dve", lambda e, ps=ps, bb=bb, mi=mi, lc=lc, half=half: e.tensor_tensor(
                                out=gateB[:, mi, lc, half * 512:(half + 1) * 512], in0=ps[:, :], in1=bb[:], op=ALU.add),
                                [ps, bb], [gateB])
                    if m in (0, 1, 3, 4):
                        mi = {0: 0, 1: 1, 3: 2, 4: 3}[m]
                        for j in range(4):
                            kk = half * 4 + j
                            ps = C.nextps()
                            for lc in range(2):
                                for k in range(8):
                                    S.op("pe", lambda e, ps=ps, w=w, j=j, k=k, lc=lc: e.matmul(
                                        ps[:, lc * 128:(lc + 1) * 128], lhsT=w[:, k, j * 128:(j + 1) * 128],
                                        rhs=condB[lc][:, k, :], start=(k == 0), stop=(k == 7)), [condB[lc], w], [ps])
                            addone = 1.0 if m in (1, 4) else 0.0
                            for lc in range(2):
                                S.op("dve", lambda e, ps=ps, kk=kk, mi=mi, m=m, addone=addone, lc=lc: e.tensor_scalar(
                                    out=modT[:, kk, mi, lc:lc + 1], in0=ps[:, lc * 128:lc * 128 + 1],
                                    scalar1=bT[:, m * 8 + kk:m * 8 + kk + 1],
                                    scalar2=addone, op0=ALU.add, op1=ALU.add), [ps, bT], [modT])
                S.barrier()
                S.emit()

        def phase_BC(l):
            src = h0 if l == 0 else H.t
            with ExitStack() as pes:
                uT = C.sb("uT", [128, 8, T], BF16, es=pes)
                wsb = C.sb("wsb", [128, 8, NWE], BF16, es=pes)
                bes = ExitStack()
                hts = [C.sb(f"ht{i}", [128, D], es=bes) for i in range(2)]
                for k in range(8):
                    S.dma("pool", lambda e, k=k: e.dma_start(out=wsb[:, k, :], in_=w_in[l, k * 128:(k + 1) * 128, :]),
                          [], [wsb.sub(k)])
                wkeys = [wsb.sub(k) for k in range(8)]
                for t in range(NT):
                    lc = 0 if t < NLT else 1
                    ht = hts[t % 2]
                    S.dma("sp", lambda e, ht=ht, t=t: e.dma_start(out=ht[:], in_=src[t * 128:(t + 1) * 128, :]),
                          ["H/%d" % t], [ht])
                    for half in range(2):
                        ps = C.nextps()
                        for j in range(4):
                            k = half * 4 + j
                            S.op("pe", lambda e, ps=ps, ht=ht, j=j, k=k: e.transpose(
                                ps[:, j * 128:(j + 1) * 128], ht[:, k * 128:(k + 1) * 128], ident_f[:]),
                                [ht, ident_f], [ps])
                        for j in range(4):
                            k = half * 4 + j
                            S.op("act", lambda e, ps=ps, j=j, k=k, t=t, lc=lc: e.activation(
                                out=uT[:, k, t * 128:(t + 1) * 128], in_=ps[:, j * 128:(j + 1) * 128], func=AF.Identity,
                                bias=modT[:, k, 0, lc:lc + 1], scale=modT[:, k, 1, lc:lc + 1]),
                                [ps, modT], [uT.sub(t)])
                S.barrier()
                S.emit()
                bes.close()
                import os
                BDBG = int(os.environ.get("BDBG", 9))
                tabs1 = [C.sb(f"tab_{kd}", [128, 2, 512], es=pes) for kd in range(3)]
                tabs = [tabs1, tabs1]
                evs = [C.sb(f"ev{i}", [128, 512], BF16, es=pes) for i in range(3)]
                tmp = [C.sb(f"tmp{i}", [128, 512], es=pes) for i in range(2)]
                tmo = [C.sb(f"tmo{i}", [128, 512], BF16, es=pes) for i in range(2)]
                gabAll = C.sb("gabAll", [128, NT * 16], es=pes)
                offs = {}
                o = 0
                for name, c, kind, sw in FMG:
                    offs[name] = o
                    o += 128 * (2 if kind is not None else 1)
                for name, st, w in TM_GROUPS:
                    offs[name] = o
                    o += w
                nblk = (T + 511) // 512
                evc = 0
                for b in range(nblk if BDBG >= 2 else 0):
                    t0 = b * 512
                    tw = min(512, T - t0)
                    tiles = list(range(t0 // 128, (t0 + tw) // 128))
                    ukeys = [uT.sub(t) for t in tiles]
                    tb = tabs[b % 2]
                    for kd in range(3):
                        S.dma("act", lambda e, tb=tb, kd=kd, t0=t0, tw=tw: e.dma_start(
                            out=tb[kd][:, :, 0:tw], in_=rope[kd, :, :, t0:t0 + tw].rearrange("c p t -> p c t")),
                            [], [tb[kd]])
                    for gi, (name, c, kind, sw) in enumerate(FMG):
                        co = offs[name]
                        psa = C.nextps()
                        for k in range(8):
                            S.op("pe", lambda e, psa=psa, k=k, co=co, t0=t0, tw=tw: e.matmul(
                                psa[:, 0:tw], lhsT=wsb[:, k, co:co + 128], rhs=uT[:, k, t0:t0 + tw],
                                start=(k == 0), stop=(k == 7)), wkeys + ukeys, [psa])
                        ev = evs[evc % 3]
                        evc += 1
                        if kind is None:
                            S.op("act", lambda e, ev=ev, psa=psa, tw=tw: e.copy(out=ev[:, 0:tw], in_=psa[:, 0:tw]),
                                 [psa], [ev])
                        else:
                            psb = C.nextps()
                            for k in range(8):
                                S.op("pe", lambda e, psb=psb, k=k, co=co, t0=t0, tw=tw: e.matmul(
                                    psb[:, 0:tw], lhsT=wsb[:, k, co + 128:co + 256], rhs=uT[:, k, t0:t0 + tw],
                                    start=(k == 0), stop=(k == 7)), wkeys + ukeys, [psb])
                            ta, tbb = tmp
                            S.op("dve", lambda e, ta=ta, psa=psa, tb=tb, kind=kind, tw=tw: e.tensor_tensor(
                                out=ta[:, 0:tw], in0=psa[:, 0:tw], in1=tb[kind][:, 0, 0:tw], op=ALU.mult),
                                [psa, tb[kind]], [ta])
                            S.op("dve", lambda e, tbb=tbb, psb=psb, tb=tb, kind=kind, tw=tw: e.tensor_tensor(
                                out=tbb[:, 0:tw], in0=psb[:, 0:tw], in1=tb[kind][:, 1, 0:tw], op=ALU.mult),
                                [psb, tb[kind]], [tbb])
                            S.op("pool", lambda e, ev=ev, ta=ta, tbb=tbb, tw=tw: e.tensor_tensor(
                                out=ev[:, 0:tw], in0=ta[:, 0:tw], in1=tbb[:, 0:tw], op=ALU.add), [ta, tbb], [ev])
                        S.dma("sp", lambda e, ev=ev, gi=gi, t0=t0, tw=tw: e.dma_start(
                            out=FMO.t[gi, :, t0:t0 + tw], in_=ev[:, 0:tw]), [ev], ["FMO/%d/%d" % (gi, b)])
                    for t in (tiles if BDBG >= 3 else []):
                        for (name, st, w), dst in zip(TM_GROUPS, [TMA, TMB, TMC, TMD]):
                            co = offs[name]
                            ps = C.nextps()
                            for k in range(8):
                                S.op("pe", lambda e, ps=ps, k=k, co=co, w=w, t=t: e.matmul(
                                    ps[:, 0:w], lhsT=uT[:, k, t * 128:(t + 1) * 128], rhs=wsb[:, k, co:co + w],
                                    start=(k == 0), stop=(k == 7)), wkeys + [uT.sub(t)], [ps])
                            ob = tmo[evc % 2]
                            evc += 1
                            wd = min(w, 256) if name == "gdn_gab" else w
                            S.op("act", lambda e, ob=ob, ps=ps, wd=wd: e.copy(out=ob[:, 0:wd], in_=ps[:, 0:wd]),
                                 [ps], [ob])
                            S.dma("sp", lambda e, ob=ob, dst=dst, t=t, wd=wd: e.dma_start(
                                out=dst.t[t * 128:(t + 1) * 128, 0:wd], in_=ob[:, 0:wd]), [ob],
                                [dst.sub(t)])
                            if name == "gdn_gab" and BDBG >= 4:
                                S.op("dve", lambda e, ps=ps, t=t: e.tensor_copy(out=gabAll[:, t * 16:(t + 1) * 16], in_=ps[:, 256:272]),
                                     [ps], [gabAll])
                if BDBG >= 4:
                    S.dma("sp", lambda e: e.dma_start(out=GAB.t[:, :], in_=gabAll[:]), [gabAll], [GAB])
                S.barrier()
                S.emit()

        def phase_ret(l, full_ctx):
            gq = fm_index["ret_q0"]
            gk = fm_index["ret_k0"]
            seq_f = [NLT, NLT + 1] + list(range(NLT))
            with ExitStack() as pes:
                rd = C.sb("rd", [128, 8], es=pes)
                lg = C.sb("lg", [128, 8], es=pes)
                MT = C.sb("MT", [128, 4, 128], es=pes)
                mtmp = C.sb("mtmp", [128, 128], es=pes)
                mtmp2 = C.sb("mtmp2", [128, 128], es=pes)
                pcol = C.sb("pcol", [128, 4], es=pes)
                kdec = C.sb("kdec", [128, 2, 256], es=pes)
                qdec = C.sb("qdec", [128, 2, 256], es=pes)
                cdec = C.sb("cdec", [128, 2, 256], es=pes)
                dcol = C.sb("dcol", [128, 8], es=pes)
                S.dma("sp", lambda e: e.dma_start(out=rd[:], in_=ret_decay[l:l + 1, :].broadcast_to([128, 8])), [], [rd])
                S.op("act", lambda e: e.activation(out=lg[:], in_=rd[:], func=AF.Exp, scale=-1.0), [rd], [lg])
                S.op("act", lambda e: e.activation(out=lg[:], in_=lg[:], func=AF.Ln, bias=1.0), [lg], [lg])
                S.op("dve", lambda e: e.tensor_scalar_mul(out=lg[:], in0=lg[:], scalar1=-1.0), [lg], [lg])
                S.op("dve", lambda e: e.tensor_scalar(out=pcol[:, 3:4], in0=relpos[:, 0:1], scalar1=-1.0, scalar2=None,
                                                      op0=ALU.mult), [relpos], [pcol])
                S.op("dve", lambda e: e.tensor_scalar_add(out=pcol[:, 0:1], in0=pcol[:, 3:4], scalar1=1.0), [pcol], [pcol])
                S.op("dve", lambda e: e.tensor_scalar(out=pcol[:, 1:2], in0=pcol[:, 3:4], scalar1=-1.0, scalar2=128.0,
                                                      op0=ALU.mult, op1=ALU.add), [pcol], [pcol])
                S.op("dve", lambda e: e.tensor_scalar(out=pcol[:, 2:3], in0=pcol[:, 3:4], scalar1=-1.0, scalar2=127.0,
                                                      op0=ALU.mult, op1=ALU.add), [pcol], [pcol])
                for dr in range(2):
                    for h in range(4):
                        c = dr * 4 + h
                        if dr == 0:
                            S.op("dve", lambda e: e.tensor_scalar_max(out=mtmp[:], in0=relpos[:], scalar1=0.0), [relpos], [mtmp])
                        else:
                            S.op("dve", lambda e: e.tensor_scalar(out=mtmp[:], in0=relpos[:], scalar1=-1.0, scalar2=0.0,
                                                                  op0=ALU.mult, op1=ALU.max), [relpos], [mtmp])
                        S.op("act", lambda e, c=c: e.activation(out=mtmp[:], in_=mtmp[:], func=AF.Exp, scale=lg[:, c:c + 1]),
                             [mtmp, lg], [mtmp])
                        if dr == 0:
                            S.op("dve", lambda e: e.tensor_scalar(out=mtmp2[:], in0=relpos[:], scalar1=0.0, scalar2=0.125,
                                                                  op0=ALU.is_ge, op1=ALU.mult), [relpos], [mtmp2])
                            S.op("dve", lambda e, h=h: e.tensor_tensor(out=MT[:, h, :], in0=mtmp[:], in1=mtmp2[:], op=ALU.mult),
                                 [mtmp, mtmp2], [MT])
                        else:
                            S.op("dve", lambda e: e.tensor_scalar(out=mtmp2[:], in0=relpos[:], scalar1=0.0, scalar2=0.125,
                                                                  op0=ALU.is_le, op1=ALU.mult), [relpos], [mtmp2])
                            S.op("dve", lambda e: e.tensor_tensor(out=mtmp[:], in0=mtmp[:], in1=mtmp2[:], op=ALU.mult),
                                 [mtmp, mtmp2], [mtmp])
                            S.op("dve", lambda e, h=h: e.tensor_tensor(out=MT[:, h, :], in0=MT[:, h, :], in1=mtmp[:], op=ALU.add),
                                 [mtmp, MT], [MT])
                        S.op("act", lambda e, c=c, dr=dr: e.activation(out=dcol[:, 0:1], in_=pcol[:, (0 if dr == 0 else 1):(1 if dr == 0 else 2)],
                                                                        func=AF.Exp, scale=lg[:, c:c + 1]), [pcol, lg], [dcol])
                        S.op("dve", lambda e, dr=dr, h=h: e.tensor_scalar_mul(
                            out=qdec[:, dr, h * 64:(h + 1) * 64], in0=dcol[:, 0:1].broadcast_to([128, 64]), scalar1=0.125),
                            [dcol], [qdec])
                        S.op("act", lambda e, c=c, dr=dr: e.activation(out=dcol[:, 1:2], in_=pcol[:, (2 if dr == 0 else 3):(3 if dr == 0 else 4)],
                                                                        func=AF.Exp, scale=lg[:, c:c + 1]), [pcol, lg], [dcol])
                        S.op("dve", lambda e, dr=dr, h=h: e.tensor_copy(
                            out=kdec[:, dr, h * 64:(h + 1) * 64], in_=dcol[:, 1:2].broadcast_to([128, 64])), [dcol], [kdec])
                        S.op("act", lambda e, c=c: e.activation(out=dcol[:, 2:3], in_=lg[:, c:c + 1], func=AF.Exp, scale=128.0),
                             [lg], [dcol])
                        S.op("dve", lambda e, dr=dr, h=h: e.tensor_copy(
                            out=cdec[:, dr, h * 64:(h + 1) * 64], in_=dcol[:, 2:3].broadcast_to([128, 64])), [dcol], [cdec])
                import os
                RDBG = int(os.environ.get("RDBG", 9))
                Sprev = C.sb("Sprev", [64, 2, NT, 256], BF16, es=pes)
                dSb = C.sb("dSb", [64, NT, 256], es=pes)
                Srun = C.sb("Srun", [64, 2, 256], es=pes)
                kts = [C.sb(f"kT{i}", [64, 4, 128], BF16, es=pes) for i in range(2)]
                qts = [C.sb(f"qT{i}", [64, 4, 128], BF16, es=pes) for i in range(2)]
                vgs = [C.sb(f"vg{i}", [128, 512], BF16, es=pes) for i in range(2)]
                ktok = [C.sb(f"ktok{i}", [128, 256], BF16, es=pes) for i in range(2)]
                kd = [C.sb(f"kd{i}", [128, 2, 256], BF16, es=pes) for i in range(2)]
                S.op("pool", lambda e: e.memset(Srun[:], 0.0), [], [Srun])

                def load_k(c, i):
                    S.dma("sp", lambda e: e.dma_start(
                        out=kts[i][:], in_=FMO.t[gk:gk + 2, :, c * 128:(c + 1) * 128].rearrange("g (h d) t -> d (g h) t", h=2)),
                        [f"FMO/{gk}/{c // 4}", f"FMO/{gk + 1}/{c // 4}"], [kts[i]])

                def make_ktok(c, i):
                    ps = C.nextps()
                    pv = ps.t[:, 0:128].bitcast(BF16)
                    for h in range(4):
                        S.op("pe", lambda e, pv=pv, h=h: e.transpose(pv[:, h * 64:(h + 1) * 64], kts[i][:, h, :], ident_b[0:64, 0:64]),
                             [kts[i], ident_b], [ps])
                    S.op("act", lambda e, pv=pv: e.copy(out=ktok[i][:], in_=pv[:, :]), [ps], [ktok[i]])

                R2 = int(os.environ.get("R2", 9))
                for n_, c in enumerate(seq_f if RDBG >= 2 else []):
                    i = n_ % 2
                    load_k(c, i)
                    S.dma("act", lambda e, c=c, i=i: e.dma_start(out=vgs[i][:], in_=TMA.t[c * 128:(c + 1) * 128, :]),
                          [TMA.sub(c)], [vgs[i]])
                    if R2 < 2:
                        continue
                    make_ktok(c, i)
                    if R2 < 3:
                        continue
                    S.op("dve", lambda e, i=i: e.tensor_tensor(
                        out=kd[i][:], in0=ktok[i][:].rearrange("p (o c) -> p o c", o=1).broadcast_to([128, 2, 256]),
                        in1=kdec[:], op=ALU.mult), [ktok[i], kdec], [kd[i]])
                    if R2 < 4:
                        continue
                    ps = C.nextps()
                    for dr in range(2):
                        for h in range(4):
                            S.op("pe", lambda e, ps=ps, dr=dr, h=h, i=i: e.matmul(
                                ps[0:64, dr * 256 + h * 64:dr * 256 + (h + 1) * 64], lhsT=kd[i][:, dr, h * 64:(h + 1) * 64],
                                rhs=vgs[i][:, h * 64:(h + 1) * 64], start=True, stop=True), [kd[i], vgs[i]], [ps])
                    if R2 < 5:
                        continue
                    S.op("act", lambda e, c=c: e.copy(out=Sprev[:, 0, c, :], in_=Srun[:, 0, :]), [Srun], [Sprev.sub(0, c)])
                    if R2 < 6:
                        continue
                    S.op("dve", lambda e: e.tensor_tensor(out=Srun[:, 0, :], in0=Srun[:, 0, :], in1=cdec[0:64, 0, :], op=ALU.mult),
                         [Srun, cdec], [Srun])
                    if R2 < 7:
                        continue
                    S.op("dve", lambda e, ps=ps: e.tensor_tensor(out=Srun[:, 0, :], in0=Srun[:, 0, :], in1=ps[0:64, 0:256], op=ALU.add),
                         [Srun, ps], [Srun])
                    if R2 < 8:
                        continue
                    S.op("dve", lambda e, ps=ps, c=c: e.tensor_copy(out=dSb[:, c, :], in_=ps[0:64, 256:512]), [ps], [dSb.sub(c)])
                seq_b = [NLT + 1, NLT] + list(range(NLT - 1, -1, -1))
                for c in (seq_b if RDBG >= 3 else []):
                    S.op("act", lambda e, c=c: e.copy(out=Sprev[:, 1, c, :], in_=Srun[:, 1, :]), [Srun], [Sprev.sub(1, c)])
                    S.op("dve", lambda e: e.tensor_tensor(out=Srun[:, 1, :], in0=Srun[:, 1, :], in1=cdec[0:64, 1, :], op=ALU.mult),
                         [Srun, cdec], [Srun])
                    S.op("dve", lambda e, c=c: e.tensor_tensor(out=Srun[:, 1, :], in0=Srun[:, 1, :], in1=dSb[:, c, :], op=ALU.add),
                         [Srun, dSb.sub(c)], [Srun])
                AMs = [C.sb(f"AM{i}", [128, 4, 128], BF16, es=pes) for i in range(2)]
                osum = C.sb("osum", [128, 256], es=pes)
                t1 = C.sb("t1", [128, 256], es=pes)
                sq = C.sb("sq", [128, 256], es=pes)
                ss = C.sb("ss", [128, 4], es=pes)
                sg = C.sb("sg", [128, 256], es=pes)
                ys = [C.sb(f"y{i}", [128, 256], BF16, es=pes) for i in range(2)]
                chunks = list(range(NT)) if full_ctx else list(range(NLT))
                if RDBG < 4:
                    chunks = []
                for n_, c in enumerate(chunks):
                    i = n_ % 2
                    load_k(c, i)
                    S.dma("sp", lambda e, c=c, i=i: e.dma_start(
                        out=qts[i][:], in_=FMO.t[gq:gq + 2, :, c * 128:(c + 1) * 128].rearrange("g (h d) t -> d (g h) t", h=2)),
                        [f"FMO/{gq}/{c // 4}", f"FMO/{gq + 1}/{c // 4}"], [qts[i]])
                    S.dma("act", lambda e, c=c, i=i: e.dma_start(out=vgs[i][:], in_=TMA.t[c * 128:(c + 1) * 128, :]),
                          [TMA.sub(c)], [vgs[i]])
                    psA = C.nextps()
                    for h in range(4):
                        S.op("pe", lambda e, psA=psA, h=h, i=i: e.matmul(
                            psA[:, h * 128:(h + 1) * 128], lhsT=kts[i][:, h, :], rhs=qts[i][:, h, :], start=True, stop=True),
                            [kts[i], qts[i]], [psA])
                    S.op("dve", lambda e, psA=psA, i=i: e.tensor_tensor(
                        out=AMs[i][:].rearrange("p h t -> p (h t)"), in0=psA[:, :], in1=MT[:].rearrange("p h t -> p (h t)"),
                        op=ALU.mult), [psA, MT], [AMs[i]])
                    psO = C.nextps()
                    psX = C.nextps()
                    for h in range(4):
                        S.op("pe", lambda e, psO=psO, h=h, i=i: e.matmul(
                            psO[:, h * 64:(h + 1) * 64], lhsT=AMs[i][:, h, :], rhs=vgs[i][:, h * 64:(h + 1) * 64],
                            start=True, stop=True), [AMs[i], vgs[i]], [psO])
                        S.op("pe", lambda e, psO=psO, h=h, i=i, c=c: e.matmul(
                            psO[:, 256 + h * 64:256 + (h + 1) * 64], lhsT=qts[i][:, h, :], rhs=Sprev[:, 0, c, h * 64:(h + 1) * 64],
                            start=True, stop=True), [qts[i], Sprev.sub(0, c)], [psO])
                        S.op("pe", lambda e, psX=psX, h=h, i=i, c=c: e.matmul(
                            psX[:, h * 64:(h + 1) * 64], lhsT=qts[i][:, h, :], rhs=Sprev[:, 1, c, h * 64:(h + 1) * 64],
                            start=True, stop=True), [qts[i], Sprev.sub(1, c)], [psX])
                    S.op("dve", lambda e, psO=psO: e.tensor_tensor(out=t1[:], in0=psO[:, 256:512], in1=qdec[:, 0, :], op=ALU.mult),
                         [psO, qdec], [t1])
                    S.op("dve", lambda e, psO=psO: e.tensor_tensor(out=osum[:], in0=psO[:, 0:256], in1=t1[:], op=ALU.add),
                         [psO, t1], [osum])
                    S.op("dve", lambda e, psX=psX: e.tensor_tensor(out=t1[:], in0=psX[:, 0:256], in1=qdec[:, 1, :], op=ALU.mult),
                         [psX, qdec], [t1])
                    S.op("dve", lambda e: e.tensor_tensor(out=osum[:], in0=osum[:], in1=t1[:], op=ALU.add), [osum, t1], [osum])
                    S.op("act", lambda e: e.activation(out=sq[:], in_=osum[:], func=AF.Square), [osum], [sq])
                    S.op("dve", lambda e: e.tensor_reduce(out=ss[:], in_=sq[:].rearrange("p (h d) -> p h d", h=4), axis=AX.X, op=ALU.add),
                         [sq], [ss])
                    S.op("dve", lambda e: e.tensor_scalar(out=ss[:], in0=ss[:], scalar1=1.0 / 64, scalar2=1e-6, op0=ALU.mult, op1=ALU.add),
                         [ss], [ss])
                    S.op("act", lambda e: e.activation(out=ss[:], in_=ss[:], func=AF.Sqrt), [ss], [ss])
                    S.op("dve", lambda e: e.reciprocal(out=ss[:], in_=ss[:]), [ss], [ss])
                    S.op("act", lambda e, i=i: e.activation(out=sg[:], in_=vgs[i][:, 256:512], func=AF.Silu), [vgs[i]], [sg])
                    S.op("dve", lambda e: e.tensor_tensor(
                        out=osum[:].rearrange("p (h d) -> p h d", h=4), in0=osum[:].rearrange("p (h d) -> p h d", h=4),
                        in1=ss[:].rearrange("p (h o) -> p h o", o=1).broadcast_to([128, 4, 64]), op=ALU.mult), [osum, ss], [osum])
                    S.op("dve", lambda e, i=i: e.tensor_tensor(out=ys[i][:], in0=osum[:], in1=sg[:], op=ALU.mult), [osum, sg], [ys[i]])
                    S.dma("sp", lambda e, i=i, c=c: e.dma_start(out=Y.t[c * 128:(c + 1) * 128, 0:256], in_=ys[i][:]),
                          [ys[i]], [Y.sub(c, 0)])
                S.barrier()
                S.emit()

        def phase_diff(l, full_ctx):
            gq = fm_index["diff_q0"]
            gk = fm_index["diff_k0"]
            lam_init = 0.8 - 0.6 * math.exp(-0.3 * l)
            scale = 32 ** -0.5
            with ExitStack() as pes:
                C.rot = [0, 1, 2]
                kT8 = C.sb("kT8", [32, 8, T], BF16, es=pes)
                V1 = C.sb("V1", [128, NT, 4, 65], BF16, es=pes)
                qT8s = [C.sb(f"qT8_{i}", [32, 8, 512], BF16, es=pes) for i in range(2)]
                Es = [C.sb(f"E{i}", [128, 512], BF16, es=pes) for i in range(3)]
                dl = C.sb("dl", [128, 128], es=pes)
                dl2 = C.sb("dl2", [128, 2], es=pes)
                nlam = C.sb("nlam", [128, 1], es=pes)
                gn = C.sb("gn", [128, 64], es=pes)
                od = C.sb("od", [128, 4, 256], es=pes)
                oT = [C.sb(f"oT{i}", [65, 512], es=pes) for i in range(2)]
                rr = C.sb("rr", [128, 4], es=pes)
                a_ = C.sb("a_", [128, 64], es=pes)
                sq = C.sb("dsq", [128, 4, 256], es=pes)
                ss = C.sb("dss", [128, 16], es=pes)
                yo = [C.sb(f"dy{i}", [128, 4, 256], BF16, es=pes) for i in range(2)]
                S.dma("sp", lambda e: e.dma_start(out=dl[:], in_=diff_lambda[l:l + 1, :].broadcast_to([128, 128])), [], [dl])
                S.dma("sp", lambda e: e.dma_start(out=gn[:], in_=diff_norm[l:l + 1, :].broadcast_to([128, 64])), [], [gn])
                dl4 = dl[:].rearrange("p (a b d) -> p a b d", a=2, b=2)
                S.op("dve", lambda e: e.tensor_tensor(out=dl4[:, :, 0, :], in0=dl4[:, :, 0, :], in1=dl4[:, :, 1, :], op=ALU.mult), [dl], [dl])
                S.op("dve", lambda e: e.tensor_reduce(out=dl2[:], in_=dl4[:, :, 0, :], axis=AX.X, op=ALU.add), [dl], [dl2])
                S.op("act", lambda e: e.activation(out=dl2[:], in_=dl2[:], func=AF.Exp), [dl2], [dl2])
                S.op("dve", lambda e: e.tensor_tensor(out=nlam[:], in0=dl2[:, 1:2], in1=dl2[:, 0:1], op=ALU.subtract), [dl2], [nlam])
                S.op("dve", lambda e: e.tensor_scalar_add(out=nlam[:], in0=nlam[:], scalar1=-lam_init), [nlam], [nlam])
                S.op("dve", lambda e: e.tensor_scalar_mul(out=gn[:], in0=gn[:], scalar1=(1.0 - lam_init)), [gn], [gn])
                S.dma("sp", lambda e: e.dma_start(out=kT8[:], in_=FMO.t[gk:gk + 2, :, :].rearrange("g (x d) t -> d (g x) t", d=32)),
                      [], [kT8])
                S.op("pool", lambda e: e.memset(V1[:], 1.0), [], [V1])
                for h in range(4):
                    for n0 in range(0, NT, 8):
                        n1 = min(NT, n0 + 8)
                        S.dma("act", lambda e, h=h, n0=n0, n1=n1: e.dma_start(
                            out=V1[:, n0:n1, h, 0:64],
                            in_=TMB.t[n0 * 128:n1 * 128, h * 64:(h + 1) * 64].rearrange("(n p) e -> p n e", p=128)), [V1], [V1])
                blocks = [(b * 512, 512, list(range(NT))) for b in range(NLAT // 512)]
                if full_ctx:
                    blocks.append((NLAT, 256, [NLT, NLT + 1]))
                ec = 0
                for bi, (q0, qw, ktiles) in enumerate(blocks):
                    qT8 = qT8s[bi % 2]
                    nqs = qw // 128
                    S.dma("sp", lambda e, qT8=qT8, q0=q0, qw=qw: e.dma_start(
                        out=qT8[:, :, 0:qw], in_=FMO.t[gq:gq + 2, :, q0:q0 + qw].rearrange("g (x d) t -> d (g x) t", d=32)),
                        [], [qT8])
                    LA = 2
                    steps = [(h, ki, kt, c) for h in range(4) for ki, kt in enumerate(ktiles) for c in range(2)]
                    pend = {}

                    def emit_S(si, qT8=qT8, qw=qw):
                        h, ki, kt, c = steps[si]
                        hc = h * 2 + c
                        pss = C.nextps()
                        S.op("pe", lambda e, pss=pss, hc=hc, kt=kt, qT8=qT8, qw=qw: e.matmul(
                            pss[:, 0:qw], lhsT=kT8[:, hc, kt * 128:(kt + 1) * 128], rhs=qT8[:, hc, 0:qw],
                            start=True, stop=True), [kT8, qT8], [pss])
                        pend[si] = pss

                    def emit_rest(si, qw=qw, nqs=nqs, nk=len(ktiles)):
                        nonlocal ec
                        h, ki, kt, c = steps[si]
                        acc = [C.ps[4 + (h % 2) * 2], C.ps[5 + (h % 2) * 2]]
                        pss = pend.pop(si)
                        E = Es[ec % 3]
                        ec += 1
                        S.op("act", lambda e, E=E, pss=pss, qw=qw: e.activation(
                            out=E[:, 0:qw], in_=pss[:, 0:qw], func=AF.Exp, scale=scale), [pss], [E])
                        S.op("pe", lambda e, E=E, c=c, kt=kt, h=h, ki=ki, qw=qw, nk=nk, ac=acc[c]: e.matmul(
                            ac[0:65, 0:qw], lhsT=V1[:, kt, h, :], rhs=E[:, 0:qw],
                            start=(ki == 0), stop=(ki == nk - 1)), [E, V1], [acc[c]])
                        if not (ki == nk - 1 and c == 1):
                            return
                        pt = C.ps[3]
                        for c_ in range(2):
                            S.op("dve", lambda e, c_=c_, qw=qw, ac=acc[c_]: e.tensor_copy(out=oT[c_][:, 0:qw], in_=ac[0:65, 0:qw]), [acc[c_]], [oT[c_]])
                            for qs in range(nqs):
                                S.op("pe", lambda e, pt=pt, c_=c_, qs=qs: e.transpose(
                                    pt[:, qs * 65:(qs + 1) * 65], oT[c_][:, qs * 128:(qs + 1) * 128], ident_f[0:65, 0:65]),
                                    [oT[c_], ident_f], [pt])
                            S.op("dve", lambda e, pt=pt, nqs=nqs: e.reciprocal(out=rr[:, 0:nqs], in_=pt[:, 0:nqs * 65].rearrange("p (q e) -> p q e", e=65)[:, :, 64]),
                                 [pt, od], [rr])
                            if c_ == 0:
                                for qs in range(nqs):
                                    S.op("dve", lambda e, qs=qs, h=h, pt=pt: e.tensor_scalar(
                                        out=od[:, qs, h * 64:(h + 1) * 64], in0=pt[:, qs * 65:qs * 65 + 64], scalar1=rr[:, qs:qs + 1],
                                        scalar2=None, op0=ALU.mult), [pt, rr], [od])
                            else:
                                S.op("dve", lambda e, nqs=nqs: e.tensor_scalar(out=rr[:, 0:nqs], in0=rr[:, 0:nqs], scalar1=nlam[:, 0:1], scalar2=None,
                                                                      op0=ALU.mult), [rr, nlam], [rr])
                                for qs in range(nqs):
                                    S.op("dve", lambda e, qs=qs, h=h, pt=pt: e.scalar_tensor_tensor(
                                        out=od[:, qs, h * 64:(h + 1) * 64], in0=pt[:, qs * 65:qs * 65 + 64], scalar=rr[:, qs:qs + 1],
                                        in1=od[:, qs, h * 64:(h + 1) * 64], op0=ALU.mult, op1=ALU.add), [pt, rr, od], [od])

                    for si in range(len(steps) + LA):
                        if si < len(steps):
                            emit_S(si)
                        if si >= LA:
                            emit_rest(si - LA)
                    y = yo[bi % 2]
                    S.op("act", lambda e, nqs=nqs: e.activation(out=sq[:, 0:nqs, :], in_=od[:, 0:nqs, :], func=AF.Square), [od], [sq])
                    S.op("dve", lambda e, nqs=nqs: e.tensor_reduce(out=ss[:, 0:nqs * 4], in_=sq[:, 0:nqs, :].rearrange("p q (h d) -> p (q h) d", h=4),
                                                          axis=AX.X, op=ALU.add), [sq], [ss])
                    S.op("dve", lambda e, nqs=nqs: e.tensor_scalar(out=ss[:], in0=ss[:], scalar1=1.0 / 64, scalar2=1e-6, op0=ALU.mult, op1=ALU.add),
                         [ss], [ss])
                    S.op("act", lambda e, nqs=nqs: e.activation(out=ss[:], in_=ss[:], func=AF.Sqrt), [ss], [ss])
                    S.op("dve", lambda e, nqs=nqs: e.reciprocal(out=ss[:], in_=ss[:]), [ss], [ss])
                    S.op("dve", lambda e, nqs=nqs: e.tensor_tensor(
                        out=od[:, 0:nqs, :].rearrange("p q (h d) -> p (q h) d", h=4), in0=od[:, 0:nqs, :].rearrange("p q (h d) -> p (q h) d", h=4),
                        in1=ss[:, 0:nqs * 4].rearrange("p (x o) -> p x o", o=1).broadcast_to([128, nqs * 4, 64]), op=ALU.mult), [od, ss], [od])
                    S.op("dve", lambda e, y=y, nqs=nqs: e.tensor_tensor(
                        out=y[:, 0:nqs, :].rearrange("p q (h d) -> p (q h) d", h=4), in0=od[:, 0:nqs, :].rearrange("p q (h d) -> p (q h) d", h=4),
                        in1=gn[:].rearrange("p (o d) -> p o d", o=1).broadcast_to([128, nqs * 4, 64]), op=ALU.mult), [od, gn], [y])
                    S.dma("sp", lambda e, y=y, q0=q0, qw=qw, nqs=nqs: e.dma_start(
                        out=Y.t[q0:q0 + qw, 256:512].rearrange("(q p) c -> p q c", p=128), in_=y[:, 0:nqs, :]), [y], [Y.sub(bi, 1)])
                C.rot = None
                S.barrier()
                S.emit()

        def phase_swa(l, full_ctx):
            gq = fm_index["swa_q0"]
            gk = fm_index["swa_k0"]
            scale = 0.125
            with ExitStack() as pes:
                C.rot = [0, 1, 2]
                skT = C.sb("skT", [64, 2, T], BF16, es=pes)
                sqT = C.sb("sqT", [64, 4, T], BF16, es=pes)
                V1 = C.sb("sV1", [128, NT, 2, 65], BF16, es=pes)
                nm = C.sb("nm", [128, 2, 2, 128], BF16, es=pes)
                Es = [C.sb(f"sE{i}", [128, 256], BF16, es=pes) for i in range(3)]
                oT = C.sb("soT", [65, 512], es=pes)
                es_ = C.sb("esink", [128, 4], es=pes)
                den = C.sb("den", [128, 4], es=pes)
                ys = [C.sb(f"sy{i}", [128, 256], BF16, es=pes) for i in range(2)]
                S.dma("sp", lambda e: e.dma_start(out=es_[:], in_=swa_sink[l:l + 1, :].broadcast_to([128, 4])), [], [es_])
                S.op("act", lambda e: e.activation(out=es_[:], in_=es_[:], func=AF.Exp), [es_], [es_])
                for r in range(2):
                    S.op("dve", lambda e, r=r: e.tensor_scalar(out=nm[:, 0, r, :], in0=relpos[:], scalar1=0.0, scalar2=NEG,
                                                               op0=ALU.is_gt, op1=ALU.mult), [relpos], [nm])
                    S.op("dve", lambda e, r=r: e.tensor_scalar(out=nm[:, 1, r, :], in0=relpos[:], scalar1=0.0, scalar2=NEG,
                                                               op0=ALU.is_lt, op1=ALU.mult), [relpos], [nm])
                S.dma("sp", lambda e: e.dma_start(out=skT[:], in_=FMO.t[gk, :, :].rearrange("(h d) t -> d h t", h=2)), [], [skT])
                S.dma("sp", lambda e: e.dma_start(out=sqT[:], in_=FMO.t[gq:gq + 2, :, :].rearrange("g (h d) t -> d (g h) t", h=2)), [], [sqT])
                S.op("pool", lambda e: e.memset(V1[:], 1.0), [], [V1])
                for g in range(2):
                    for n0 in range(0, NT, 8):
                        n1 = min(NT, n0 + 8)
                        S.dma("act", lambda e, g=g, n0=n0, n1=n1: e.dma_start(
                            out=V1[:, n0:n1, g, 0:64],
                            in_=TMD.t[n0 * 128:n1 * 128, g * 64:(g + 1) * 64].rearrange("(n p) e -> p n e", p=128)), [V1], [V1])
                tiles = list(range(NT)) if full_ctx else list(range(NLT))
                ec = 0
                for ti, t in enumerate(tiles):
                    if t < NLT:
                        keys = []
                        if t > 0:
                            keys.append((t - 1, 0))
                        keys.append((t, None))
                        if t < NLT - 1:
                            keys.append((t + 1, 1))
                        keys += [(NLT, None), (NLT + 1, None)]
                    else:
                        keys = [(NLT, None), (NLT + 1, None)]
                    accs = [C.ps[4 + (ti % 2) * 2], C.ps[5 + (ti % 2) * 2]]
                    LA = 2
                    steps = [(g, ki, kt, mk) for g in range(2) for ki, (kt, mk) in enumerate(keys)]
                    pend = {}

                    def emit_S(si, t=t):
                        g, ki, kt, mk = steps[si]
                        pss = C.nextps()
                        S.op("pe", lambda e, pss=pss, g=g, kt=kt, t=t, mk=mk: e.matmul(
                            pss[:, 0:256], lhsT=skT[:, g, kt * 128:(kt + 1) * 128], rhs=sqT[:, 2 * g:2 * g + 2, t * 128:(t + 1) * 128],
                            start=True, stop=(mk is None)), [skT, sqT], [pss])
                        if mk is not None:
                            S.op("pe", lambda e, pss=pss, mk=mk: e.matmul(
                                pss[:, 0:256], lhsT=ident_b[:], rhs=nm[:, mk, :, :], start=False, stop=True), [ident_b, nm], [pss])
                        pend[si] = pss

                    def emit_rest(si, nk=len(keys), accs=accs):
                        nonlocal ec
                        g, ki, kt, mk = steps[si]
                        pss = pend.pop(si)
                        E = Es[ec % 3]
                        ec += 1
                        S.op("act", lambda e, E=E, pss=pss: e.activation(out=E[:], in_=pss[:, 0:256], func=AF.Exp, scale=scale),
                             [pss], [E])
                        S.op("pe", lambda e, E=E, g=g, kt=kt, ki=ki, nk=nk, ac=accs[g]: e.matmul(
                            ac[0:65, 0:256], lhsT=V1[:, kt, g, :], rhs=E[:], start=(ki == 0), stop=(ki == nk - 1)),
                            [E, V1], [accs[g]])
                        if ki == nk - 1:
                            S.op("dve", lambda e, g=g, ac=accs[g]: e.tensor_copy(out=oT[:, g * 256:(g + 1) * 256], in_=ac[0:65, 0:256]),
                                 [accs[g]], [oT])

                    for si in range(len(steps) + LA):
                        if si < len(steps):
                            emit_S(si)
                        if si >= LA:
                            emit_rest(si - LA)
                    pt = C.ps[3]
                    for h in range(4):
                        S.op("pe", lambda e, pt=pt, h=h: e.transpose(pt[:, h * 65:(h + 1) * 65], oT[:, h * 128:(h + 1) * 128],
                                                                     ident_f[0:65, 0:65]), [oT, ident_f], [pt])
                    S.op("dve", lambda e, pt=pt: e.tensor_tensor(
                        out=den[:], in0=pt[:, 0:260].rearrange("p (h e) -> p h e", e=65)[:, :, 64], in1=es_[:], op=ALU.add),
                        [pt, es_], [den])
                    S.op("dve", lambda e: e.reciprocal(out=den[:], in_=den[:]), [den], [den])
                    y = ys[ti % 2]
                    S.op("dve", lambda e, pt=pt, y=y: e.tensor_tensor(
                        out=y[:].rearrange("p (h d) -> p h d", h=4), in0=pt[:, 0:260].rearrange("p (h e) -> p h e", e=65)[:, :, 0:64],
                        in1=den[:].rearrange("p (h o) -> p h o", o=1).broadcast_to([128, 4, 64]), op=ALU.mult), [pt, den], [y])
                    S.dma("sp", lambda e, y=y, t=t: e.dma_start(out=Y.t[t * 128:(t + 1) * 128, 768:1024], in_=y[:]), [y], [Y.sub(t, 3)])
                C.rot = None
                S.barrier()
                S.emit()

        def phase_gdn(l, full_ctx):
            g0 = fm_index["gdn0"]
            BIG = 30000.0
            with ExitStack() as pes:
                cw = C.sb("cw", [128, 18], es=pes)
                bones = C.sb("bones", [128, 128], es=pes)
                xb = [C.sb(f"xb{i}", [128, 514], BF16, es=pes) for i in range(2)]
                yc = C.sb("yc", [128, 512], es=pes)
                ysl = C.sb("ysl", [128, 512], es=pes)
                sq = C.sb("gsq", [128, 512], es=pes)
                rs = C.sb("grs", [128, 512], es=pes)
                ynb = [C.sb(f"ynb{i}", [128, 512], BF16, es=pes) for i in range(2)]
                ytm = [C.sb(f"ytm{i}", [128, 4, 128], BF16, es=pes) for i in range(2)]
                S.dma("sp", lambda e: e.dma_start(out=cw[:], in_=gdn_convT[l, :, :]), [], [cw])
                S.op("pool", lambda e: e.memset(bones[:], 0.0), [], [bones])
                S.op("pool", lambda e: e.memset(bones[0:64, 0:64], 1.0), [bones], [bones])
                S.op("pool", lambda e: e.memset(bones[64:128, 64:128], 1.0), [bones], [bones])
                seqs = [(0, NLAT)] + [(NLAT, T)]
                bi = 0
                for gi in range(6):
                    for (s0, s1) in seqs:
                        for t0 in range(s0, s1, 512):
                            tw = min(512, s1 - t0)
                            x = xb[bi % 2]
                            lo = max(t0 - 1, s0)
                            hi = min(t0 + tw + 1, s1)
                            if lo == t0 or hi == t0 + tw:
                                S.op("pool", lambda e, x=x: e.memset(x[:], 0.0), [], [x])
                            S.dma("sp", lambda e, x=x, gi=gi, lo=lo, hi=hi, t0=t0: e.dma_start(
                                out=x[:, lo - (t0 - 1):hi - (t0 - 1)], in_=FMO.t[g0 + gi, :, lo:hi]), [x], [x])
                            S.op("dve", lambda e, x=x, gi=gi, tw=tw: e.tensor_scalar(
                                out=yc[:, 0:tw], in0=x[:, 1:1 + tw], scalar1=cw[:, gi * 3 + 1:gi * 3 + 2], scalar2=None, op0=ALU.mult),
                                [x, cw], [yc])
                            S.op("dve", lambda e, x=x, gi=gi, tw=tw: e.scalar_tensor_tensor(
                                out=yc[:, 0:tw], in0=x[:, 0:tw], scalar=cw[:, gi * 3:gi * 3 + 1], in1=yc[:, 0:tw],
                                op0=ALU.mult, op1=ALU.add), [x, cw, yc], [yc])
                            S.op("dve", lambda e, x=x, gi=gi, tw=tw: e.scalar_tensor_tensor(
                                out=yc[:, 0:tw], in0=x[:, 2:2 + tw], scalar=cw[:, gi * 3 + 2:gi * 3 + 3], in1=yc[:, 0:tw],
                                op0=ALU.mult, op1=ALU.add), [x, cw, yc], [yc])
                            S.op("act", lambda e, tw=tw: e.activation(out=ysl[:, 0:tw], in_=yc[:, 0:tw], func=AF.Silu), [yc], [ysl])
                            yn = ynb[bi % 2]
                            if gi < 4:
                                S.op("act", lambda e, tw=tw: e.activation(out=sq[:, 0:tw], in_=ysl[:, 0:tw], func=AF.Square), [ysl], [sq])
                                ps = C.nextps()
                                S.op("pe", lambda e, ps=ps, tw=tw: e.matmul(ps[:, 0:tw], lhsT=bones[:], rhs=sq[:, 0:tw], start=True, stop=True),
                                     [bones, sq], [ps])
                                S.op("dve", lambda e, ps=ps, tw=tw: e.tensor_scalar_add(out=rs[:, 0:tw], in0=ps[:, 0:tw], scalar1=1e-6), [ps], [rs])
                                S.op("act", lambda e, tw=tw: e.activation(out=rs[:, 0:tw], in_=rs[:, 0:tw], func=AF.Sqrt), [rs], [rs])
                                S.op("dve", lambda e, tw=tw: e.reciprocal(out=rs[:, 0:tw], in_=rs[:, 0:tw]), [rs], [rs])
                                if gi < 2:
                                    S.op("dve", lambda e, tw=tw, yn=yn: e.scalar_tensor_tensor(
                                        out=yn[:, 0:tw], in0=ysl[:, 0:tw], scalar=0.125, in1=rs[:, 0:tw], op0=ALU.mult, op1=ALU.mult),
                                        [ysl, rs], [yn])
                                else:
                                    S.op("dve", lambda e, tw=tw, yn=yn: e.tensor_tensor(out=yn[:, 0:tw], in0=ysl[:, 0:tw], in1=rs[:, 0:tw], op=ALU.mult),
                                         [ysl, rs], [yn])
                                S.dma("act", lambda e, yn=yn, gi=gi, t0=t0, tw=tw: e.dma_start(out=GFM.t[gi, :, t0:t0 + tw], in_=yn[:, 0:tw]),
                                      [yn], [GFM.sub(gi, t0)])
                            else:
                                S.op("dve", lambda e, tw=tw, yn=yn: e.tensor_copy(out=yn[:, 0:tw], in_=ysl[:, 0:tw]), [ysl], [yn])
                            if gi >= 2:
                                ps = C.nextps()
                                pv = ps.t[:, 0:256].bitcast(BF16)
                                nq = tw // 128
                                for q in range(nq):
                                    S.op("pe", lambda e, pv=pv, q=q, yn=yn: e.transpose(pv[:, q * 128:(q + 1) * 128], yn[:, q * 128:(q + 1) * 128], ident_b[:]),
                                         [yn, ident_b], [ps])
                                yt = ytm[bi % 2]
                                S.op("act", lambda e, pv=pv, yt=yt, nq=nq: e.copy(out=yt[:, 0:nq, :].rearrange("p q c -> p (q c)"), in_=pv[:, 0:nq * 128]),
                                     [ps], [yt])
                                S.dma("act", lambda e, yt=yt, gi=gi, t0=t0, tw=tw, nq=nq: e.dma_start(
                                    out=GTM.t[t0:t0 + tw, (gi - 2) * 128:(gi - 1) * 128].rearrange("(q p) c -> p q c", p=128), in_=yt[:, 0:nq, :]),
                                    [yt], [GTM.sub(gi, t0)])
                            bi += 1
                S.barrier()
                S.emit()
            import os
            GD = int(os.environ.get("GD", 9))
            if GD < 2:
                return
            with ExitStack() as pes:
                gab = C.sb("gab", [128, NT, 16], es=pes)
                par = C.sb("gpar", [128, 16], es=pes)
                la = C.sb("la", [128, NT, 8], es=pes)
                nbeta = C.sb("nbeta", [128, NT, 8], es=pes)
                gg = C.sb("gg", [128, NT, 8], es=pes)
                gt = C.sb("gt", [128, NT, 8], es=pes)
                eg = C.sb("eg", [128, NT, 8], es=pes)
                ekt = C.sb("ekt", [128, NT, 8], es=pes)
                cd = C.sb("cd", [128, NT, 8], es=pes)
                beg = C.sb("beg", [128, NT, 8], es=pes)
                tri = C.sb("tri", [128, 2, 128], es=pes)
                onesf = C.sb("onesf", [128, 128], es=pes)
                nonesf = C.sb("nonesf", [128, 128], es=pes)
                mD = C.sb("mD", [128, 2, 4, 128], es=pes)
                mDT = C.sb("mDT", [128, 2, 4, 128], es=pes)
                gnb = C.sb("gnb", [128, 64], es=pes)
                S.dma("sp", lambda e: e.dma_start(out=gab[:].rearrange("p n c -> p (n c)"), in_=GAB.t[:, :]), [], [gab])
                S.dma("sp", lambda e: e.dma_start(out=par[:, 0:8], in_=gdn_a_log[l:l + 1, :].broadcast_to([128, 8])), [], [par])
                S.dma("sp", lambda e: e.dma_start(out=par[:, 8:16], in_=gdn_dt_bias[l:l + 1, :].broadcast_to([128, 8])), [par], [par])
                S.dma("sp", lambda e: e.dma_start(out=gnb[:], in_=gdn_norm[l:l + 1, :].broadcast_to([128, 64])), [], [gnb])
                S.op("pool", lambda e: e.memset(onesf[:], 1.0), [], [onesf])
                S.op("pool", lambda e: e.memset(nonesf[:], -1.0), [], [nonesf])
                S.op("dve", lambda e: e.tensor_single_scalar(out=tri[:, 0, :], in_=relpos[:], scalar=0.0, op=ALU.is_ge), [relpos], [tri])
                S.op("dve", lambda e: e.tensor_single_scalar(out=tri[:, 1, :], in_=relpos[:], scalar=0.0, op=ALU.is_le), [relpos], [tri])
                for h in range(4):
                    S.op("dve", lambda e, h=h: e.tensor_scalar(out=mD[:, 0, h, :], in0=relpos[:], scalar1=0.0, scalar2=BIG, op0=ALU.is_ge, op1=ALU.mult), [relpos], [mD])
                    S.op("dve", lambda e, h=h: e.tensor_scalar(out=mD[:, 1, h, :], in0=relpos[:], scalar1=0.0, scalar2=BIG, op0=ALU.is_le, op1=ALU.mult), [relpos], [mD])
                    S.op("dve", lambda e, h=h: e.tensor_scalar(out=mDT[:, 0, h, :], in0=relpos[:], scalar1=0.0, scalar2=-BIG, op0=ALU.is_lt, op1=ALU.mult), [relpos], [mDT])
                    S.op("dve", lambda e, h=h: e.tensor_scalar(out=mDT[:, 1, h, :], in0=relpos[:], scalar1=0.0, scalar2=-BIG, op0=ALU.is_gt, op1=ALU.mult), [relpos], [mDT])
                S.op("act", lambda e: e.activation(out=par[:, 0:8], in_=par[:, 0:8], func=AF.Exp), [par], [par])
                S.op("dve", lambda e: e.tensor_tensor(out=la[:], in0=gab[:, :, 0:8],
                                                      in1=par[:, 8:16].rearrange("p (o c) -> p o c", o=1).broadcast_to([128, NT, 8]), op=ALU.add),
                     [gab, par], [la])
                S.op("act", lambda e: e.activation(out=la[:], in_=la[:], func=AF.Exp), [la], [la])
                S.op("act", lambda e: e.activation(out=la[:], in_=la[:], func=AF.Ln, bias=1.0), [la], [la])
                S.op("dve", lambda e: e.scalar_tensor_tensor(
                    out=la[:], in0=la[:], scalar=-1.0, in1=par[:, 0:8].rearrange("p (o c) -> p o c", o=1).broadcast_to([128, NT, 8]),
                    op0=ALU.mult, op1=ALU.mult), [la, par], [la])
                S.op("act", lambda e: e.activation(out=nbeta[:], in_=gab[:, :, 8:16], func=AF.Sigmoid), [gab], [nbeta])
                for r in range(2):
                    ps = C.nextps()
                    S.op("pe", lambda e, ps=ps, r=r: e.matmul(ps[:, 0:NT * 4], lhsT=tri[:, r, :], rhs=la[:, :, r * 4:(r + 1) * 4], start=True, stop=True),
                         [tri, la], [ps])
                    S.op("dve", lambda e, ps=ps, r=r: e.tensor_copy(out=gg[:, :, r * 4:(r + 1) * 4], in_=ps[:, 0:NT * 4].rearrange("p (n c) -> p n c", c=4)),
                         [ps], [gg])
                ps = C.nextps()
                S.op("pe", lambda e, ps=ps: e.matmul(ps[:, 0:NT * 8], lhsT=onesf[:], rhs=la[:].rearrange("p n c -> p (n c)"), start=True, stop=True),
                     [onesf, la], [ps])
                S.op("dve", lambda e, ps=ps: e.tensor_copy(out=gt[:].rearrange("p n c -> p (n c)"), in_=ps[:, 0:NT * 8]), [ps], [gt])
                S.op("act", lambda e: e.activation(out=eg[:], in_=gg[:], func=AF.Exp), [gg], [eg])
                S.op("act", lambda e: e.activation(out=cd[:], in_=gt[:], func=AF.Exp), [gt], [cd])
                S.op("dve", lambda e: e.tensor_tensor(out=ekt[:], in0=gt[:], in1=gg[:], op=ALU.subtract), [gt, gg], [ekt])
                S.op("act", lambda e: e.activation(out=ekt[:], in_=ekt[:], func=AF.Exp), [ekt], [ekt])
                S.op("dve", lambda e: e.tensor_tensor(out=beg[:], in0=nbeta[:], in1=eg[:], op=ALU.mult), [nbeta, eg], [beg])
                S.op("dve", lambda e: e.tensor_scalar_mul(out=nbeta[:], in0=nbeta[:], scalar1=-1.0), [nbeta], [nbeta])
                qT = [C.sb(f"gqT{i}", [64, 4, 128], BF16, es=pes) for i in range(2)]
                kT = [C.sb(f"gkT{i}", [64, 4, 128], BF16, es=pes) for i in range(2)]
                kv = [C.sb(f"gkv{i}", [128, 512], BF16, es=pes) for i in range(2)]
                Rm = C.sb("Rm", [128, 4, 128], es=pes)
                Ds = C.sb("Ds", [128, 4, 128], es=pes)
                DT = C.sb("DT", [128, 4, 128], es=pes)
                X = [C.sb(f"X{i}", [128, 4, 2, 128], es=pes) for i in range(2)]
                P = C.sb("P", [128, 4, 128], es=pes)
                ident_r = ident_f
                Pb = C.sb("Pb", [128, 4, 128], BF16, es=pes)
                qkT = C.sb("qkT", [128, 4, 128], BF16, es=pes)
                kg = C.sb("kg", [128, 256], BF16, es=pes)
                vb = C.sb("vb", [128, 256], BF16, es=pes)
                ktl = C.sb("ktl", [128, 256], BF16, es=pes)
                wT = C.sb("wT", [64, 4, 128], BF16, es=pes)
                ub = C.sb("ub", [128, 256], es=pes)
                u = C.sb("u", [128, 256], BF16, es=pes)
                Sf = C.sb("Sf", [64, 256], es=pes)
                Sb16 = C.sb("Sb16", [64, 256], BF16, es=pes)
                Oacc = C.sb("Oacc", [128, NT, 256], es=pes)
                ocr = C.sb("ocr", [128, 256], es=pes)
                gsq = C.sb("gosq", [128, 256], es=pes)
                gss = C.sb("goss", [128, 4], es=pes)
                gsg = C.sb("gosg", [128, 256], es=pes)
                ggate = [C.sb(f"ggate{i}", [128, 256], BF16, es=pes) for i in range(2)]
                gy = [C.sb(f"gy{i}", [128, 256], BF16, es=pes) for i in range(2)]
                seq = {0: [NLT, NLT + 1] + list(range(NLT)), 1: [NLT + 1, NLT] + list(range(NLT - 1, -1, -1))}
                step = 0
                for r in range(2 if GD >= 3 else 0):
                    S.op("pool", lambda e: e.memset(Sf[:], 0.0), [], [Sf])
                    S.op("pool", lambda e: e.memset(Sb16[:], 0.0), [], [Sb16])
                    for c in seq[r]:
                        i = step % 2
                        step += 1
                        rc = slice(r * 4, r * 4 + 4)
                        S.dma("sp", lambda e, i=i, c=c: e.dma_start(
                            out=qT[i][:], in_=GFM.t[0:2, :, c * 128:(c + 1) * 128].rearrange("g (h d) t -> d (g h) t", h=2)), [], [qT[i]])
                        S.dma("sp", lambda e, i=i, c=c: e.dma_start(
                            out=kT[i][:], in_=GFM.t[2:4, :, c * 128:(c + 1) * 128].rearrange("g (h d) t -> d (g h) t", h=2)), [], [kT[i]])
                        S.dma("act", lambda e, i=i, c=c: e.dma_start(out=kv[i][:], in_=GTM.t[c * 128:(c + 1) * 128, :]), [], [kv[i]])
                        pKK = C.nextps()
                        pQK = C.nextps()
                        for h in range(4):
                            S.op("pe", lambda e, pKK=pKK, h=h, i=i: e.matmul(pKK[:, h * 128:(h + 1) * 128], lhsT=kT[i][:, h, :], rhs=kT[i][:, h, :],
                                                                              start=True, stop=True), [kT[i]], [pKK])
                        for h in range(4):
                            S.op("pe", lambda e, pQK=pQK, h=h, i=i: e.matmul(pQK[:, h * 128:(h + 1) * 128], lhsT=kT[i][:, h, :], rhs=qT[i][:, h, :],
                                                                              start=True, stop=True), [kT[i], qT[i]], [pQK])
                        for h in range(4):
                            S.op("dve", lambda e, h=h, c=c, r=r: e.tensor_scalar(
                                out=Rm[:, h, :], in0=tri[:, r, :], scalar1=la[:, c, r * 4 + h:r * 4 + h + 1], scalar2=None, op0=ALU.mult),
                                [tri, la], [Rm])
                        pD = C.nextps()
                        pDT = C.nextps()
                        for (pp, mk) in ((pD, mD), (pDT, mDT)):
                            S.op("pe", lambda e, pp=pp, mk=mk, r=r: e.matmul(pp[:, :], lhsT=ident_f[:], rhs=mk[:, r, :, :].rearrange("p h t -> p (h t)"),
                                                                             start=True, stop=False), [ident_f, mk], [pp])
                            for h in range(4):
                                S.op("pe", lambda e, pp=pp, h=h: e.matmul(pp[:, h * 128:(h + 1) * 128], lhsT=onesf[:], rhs=Rm[:, h, :],
                                                                          start=False, stop=False), [onesf, Rm], [pp])
                                S.op("pe", lambda e, pp=pp, h=h: e.matmul(pp[:, h * 128:(h + 1) * 128], lhsT=Rm[:, h, :], rhs=nonesf[:],
                                                                          start=False, stop=(h == 3)), [nonesf, Rm], [pp])
                        S.op("act", lambda e, pD=pD: e.activation(out=Ds[:].rearrange("p h t -> p (h t)"), in_=pD[:, :], func=AF.Exp, scale=-1.0),
                             [pD], [Ds])
                        S.op("act", lambda e, pDT=pDT: e.activation(out=DT[:].rearrange("p h t -> p (h t)"), in_=pDT[:, :], func=AF.Exp), [pDT], [DT])
                        if GD < 4:
                            continue
                        Xc = X[0]
                        S.op("dve", lambda e, pKK=pKK, Xc=Xc: e.tensor_tensor(out=Xc[:, :, 0, :], in0=pKK[:, :].rearrange("p (h t) -> p h t", h=4), in1=Ds[:], op=ALU.mult),
                             [pKK, Ds], [Xc])
                        S.op("dve", lambda e, Xc=Xc, c=c, rc=rc: e.tensor_tensor(
                            out=Xc[:, :, 0, :], in0=Xc[:, :, 0, :], in1=nbeta[:, c, rc].rearrange("p (h o) -> p h o", o=1).broadcast_to([128, 4, 128]), op=ALU.mult),
                            [Xc, nbeta], [Xc])
                        S.op("dve", lambda e, pQK=pQK: e.tensor_tensor(out=qkT[:], in0=pQK[:, :].rearrange("p (h t) -> p h t", h=4), in1=DT[:], op=ALU.mult),
                             [pQK, DT], [qkT])
                        pZ = C.nextps()
                        pZr = pZ.t[:, :]
                        for h in range(4):
                            S.op("pe", lambda e, pZr=pZr, h=h, Xc=Xc: e.transpose(pZr[:, h * 128:(h + 1) * 128], Xc[:, h, 0, :], ident_r[:]), [Xc, ident_r], [pZ])
                        S.op("act", lambda e, pZ=pZ, Xc=Xc: e.copy(out=Xc[:, :, 1, :], in_=pZ[:, :].rearrange("p (h t) -> p h t", h=4)), [pZ], [Xc])
                        S.op("dve", lambda e, Xc=Xc: e.tensor_tensor(out=P[:], in0=Xc[:, :, 1, :], in1=ident_f[:].rearrange("p (o t) -> p o t", o=1).broadcast_to([128, 4, 128]), op=ALU.add),
                             [Xc, ident_f], [P])
                        for lev in range(6):
                            Xo = X[lev % 2]
                            Xn = X[(lev + 1) % 2]
                            last = lev == 5
                            pxa = C.nextps()
                            pxb = C.nextps()
                            for h in range(4):
                                pp = pxa if h < 2 else pxb
                                o = (h % 2) * 256
                                S.op("pe", lambda e, pp=pp, o=o, h=h, Xo=Xo: e.matmul(pp[:, o:o + 128], lhsT=Xo[:, h, 1, :], rhs=Xo[:, h, 0, :], start=True, stop=True),
                                     [Xo], [pp])
                                if not last:
                                    S.op("pe", lambda e, pp=pp, o=o, h=h, Xo=Xo: e.matmul(pp[:, o + 128:o + 256], lhsT=Xo[:, h, 0, :], rhs=Xo[:, h, 1, :], start=True, stop=True),
                                         [Xo], [pp])
                            S.op("act", lambda e, pxa=pxa, Xn=Xn: e.copy(out=Xn[:, 0:2, :, :].rearrange("p h x t -> p (h x t)"), in_=pxa[:, :]), [pxa], [Xn])
                            S.op("dve", lambda e, pxb=pxb, Xn=Xn: e.tensor_copy(out=Xn[:, 2:4, :, :].rearrange("p h x t -> p (h x t)"), in_=pxb[:, :]), [pxb], [Xn])
                            pP = C.nextps()
                            for h in range(4):
                                S.op("pe", lambda e, pP=pP, h=h, Xn=Xn: e.matmul(pP[:, h * 128:(h + 1) * 128], lhsT=Xn[:, h, 0, :], rhs=P[:, h, :], start=True, stop=True),
                                     [Xn, P], [pP])
                            S.op("dve", lambda e, pP=pP: e.tensor_tensor(out=P[:].rearrange("p h t -> p (h t)"), in0=P[:].rearrange("p h t -> p (h t)"), in1=pP[:, :], op=ALU.add),
                                 [pP, P], [P])
                        if GD < 5:
                            continue
                        S.op("act", lambda e: e.copy(out=Pb[:], in_=P[:]), [P], [Pb])
                        S.op("dve", lambda e, i=i, c=c, rc=rc: e.tensor_tensor(
                            out=kg[:].rearrange("p (h d) -> p h d", h=4), in0=kv[i][:, 0:256].rearrange("p (h d) -> p h d", h=4),
                            in1=beg[:, c, rc].rearrange("p (h o) -> p h o", o=1).broadcast_to([128, 4, 64]), op=ALU.mult), [kv[i], beg], [kg])
                        S.op("dve", lambda e, i=i, c=c, rc=rc: e.tensor_tensor(
                            out=vb[:].rearrange("p (h d) -> p h d", h=4), in0=kv[i][:, 256:512].rearrange("p (h d) -> p h d", h=4),
                            in1=nbeta[:, c, rc].rearrange("p (h o) -> p h o", o=1).broadcast_to([128, 4, 64]), op=ALU.mult), [kv[i], nbeta], [vb])
                        S.op("dve", lambda e, i=i, c=c, rc=rc: e.tensor_tensor(
                            out=ktl[:].rearrange("p (h d) -> p h d", h=4), in0=kv[i][:, 0:256].rearrange("p (h d) -> p h d", h=4),
                            in1=ekt[:, c, rc].rearrange("p (h o) -> p h o", o=1).broadcast_to([128, 4, 64]), op=ALU.mult), [kv[i], ekt], [ktl])
                        pw = C.nextps()
                        pu = C.nextps()
                        for h in range(4):
                            S.op("pe", lambda e, pw=pw, h=h: e.matmul(pw[0:64, h * 128:(h + 1) * 128], lhsT=kg[:, h * 64:(h + 1) * 64], rhs=Pb[:, h, :], start=True, stop=True),
                                 [kg, Pb], [pw])
                            S.op("pe", lambda e, pu=pu, h=h: e.matmul(pu[:, h * 64:(h + 1) * 64], lhsT=Pb[:, h, :], rhs=vb[:, h * 64:(h + 1) * 64], start=True, stop=True),
                                 [vb, Pb], [pu])
                        S.op("dve", lambda e, pw=pw: e.tensor_copy(out=wT[:].rearrange("p h t -> p (h t)"), in_=pw[0:64, :]), [pw], [wT])
                        S.op("dve", lambda e, pu=pu: e.tensor_scalar_mul(out=ub[:], in0=pu[:, 0:256], scalar1=-1.0), [pu], [ub])
                        pws = C.nextps()
                        for h in range(4):
                            S.op("pe", lambda e, pws=pws, h=h: e.matmul(pws[:, h * 64:(h + 1) * 64], lhsT=wT[:, h, :], rhs=Sb16[:, h * 64:(h + 1) * 64], start=True, stop=True),
                                 [wT, Sb16], [pws])
                        S.op("dve", lambda e, pws=pws: e.tensor_tensor(out=u[:], in0=ub[:], in1=pws[:, 0:256], op=ALU.subtract), [ub, pws], [u])
                        pcr = C.nextps()
                        pin = C.nextps()
                        for h in range(4):
                            S.op("pe", lambda e, pcr=pcr, h=h, i=i: e.matmul(pcr[:, h * 64:(h + 1) * 64], lhsT=qT[i][:, h, :], rhs=Sb16[:, h * 64:(h + 1) * 64], start=True, stop=True),
                                 [qT[i], Sb16], [pcr])
                            S.op("pe", lambda e, pin=pin, h=h: e.matmul(pin[:, h * 64:(h + 1) * 64], lhsT=qkT[:, h, :], rhs=u[:, h * 64:(h + 1) * 64], start=True, stop=True),
                                 [qkT, u], [pin])
                        pS = C.nextps()
                        for h in range(4):
                            S.op("pe", lambda e, pS=pS, h=h: e.matmul(pS[0:64, h * 64:(h + 1) * 64], lhsT=ktl[:, h * 64:(h + 1) * 64], rhs=u[:, h * 64:(h + 1) * 64], start=True, stop=True),
                                 [ktl, u], [pS])
                        S.op("dve", lambda e, pcr=pcr, c=c, rc=rc: e.tensor_tensor(
                            out=ocr[:].rearrange("p (h d) -> p h d", h=4), in0=pcr[:, 0:256].rearrange("p (h d) -> p h d", h=4),
                            in1=eg[:, c, rc].rearrange("p (h o) -> p h o", o=1).broadcast_to([128, 4, 64]), op=ALU.mult), [pcr, eg], [ocr])
                        if r == 0:
                            S.op("dve", lambda e, pin=pin, c=c: e.tensor_tensor(out=Oacc[:, c, :], in0=ocr[:], in1=pin[:, 0:256], op=ALU.add), [ocr, pin], [Oacc.sub(c)])
                        else:
                            S.op("dve", lambda e, pin=pin: e.tensor_tensor(out=ocr[:], in0=ocr[:], in1=pin[:, 0:256], op=ALU.add), [ocr, pin], [ocr])
                            S.op("dve", lambda e, c=c: e.tensor_tensor(out=ocr[:], in0=ocr[:], in1=Oacc[:, c, :], op=ALU.add), [ocr, Oacc.sub(c)], [ocr])
                        S.op("dve", lambda e, c=c, rc=rc: e.tensor_tensor(
                            out=Sf[:].rearrange("p (h d) -> p h d", h=4), in0=Sf[:].rearrange("p (h d) -> p h d", h=4),
                            in1=cd[0:64, c, rc].rearrange("p (h o) -> p h o", o=1).broadcast_to([64, 4, 64]), op=ALU.mult), [Sf, cd], [Sf])
                        S.op("dve", lambda e, pS=pS: e.tensor_tensor(out=Sf[:], in0=Sf[:], in1=pS[0:64, 0:256], op=ALU.add), [Sf, pS], [Sf])
                        S.op("act", lambda e: e.copy(out=Sb16[:], in_=Sf[:]), [Sf], [Sb16])
                        if r == 1 and (full_ctx or c < NLT):
                            gt_ = ggate[i]
                            S.dma("act", lambda e, gt_=gt_, c=c: e.dma_start(out=gt_[:], in_=TMC.t[c * 128:(c + 1) * 128, :]), [], [gt_])
                            S.op("act", lambda e: e.activation(out=gsq[:], in_=ocr[:], func=AF.Square), [ocr], [gsq])
                            S.op("dve", lambda e: e.tensor_reduce(out=gss[:], in_=gsq[:].rearrange("p (h d) -> p h d", h=4), axis=AX.X, op=ALU.add), [gsq], [gss])
                            S.op("dve", lambda e: e.tensor_scalar(out=gss[:], in0=gss[:], scalar1=1.0 / 64, scalar2=1e-6, op0=ALU.mult, op1=ALU.add), [gss], [gss])
                            S.op("act", lambda e: e.activation(out=gss[:], in_=gss[:], func=AF.Sqrt), [gss], [gss])
                            S.op("dve", lambda e: e.reciprocal(out=gss[:], in_=gss[:]), [gss], [gss])
                            S.op("act", lambda e, gt_=gt_: e.activation(out=gsg[:], in_=gt_[:], func=AF.Silu), [gt_], [gsg])
                            S.op("dve", lambda e: e.tensor_tensor(
                                out=ocr[:].rearrange("p (h d) -> p h d", h=4), in0=ocr[:].rearrange("p (h d) -> p h d", h=4),
                                in1=gss[:].rearrange("p (h o) -> p h o", o=1).broadcast_to([128, 4, 64]), op=ALU.mult), [ocr, gss], [ocr])
                            S.op("dve", lambda e: e.tensor_tensor(
                                out=ocr[:].rearrange("p (h d) -> p h d", h=4), in0=ocr[:].rearrange("p (h d) -> p h d", h=4),
                                in1=gnb[:].rearrange("p (o d) -> p o d", o=1).broadcast_to([128, 4, 64]), op=ALU.mult), [ocr, gnb], [ocr])
                            yy = gy[i]
                            S.op("dve", lambda e, yy=yy: e.tensor_tensor(out=yy[:], in0=ocr[:], in1=gsg[:], op=ALU.mult), [ocr, gsg], [yy])
                            S.dma("sp", lambda e, yy=yy, c=c: e.dma_start(out=Y.t[c * 128:(c + 1) * 128, 512:768], in_=yy[:]), [yy], [Y.sub(c, 2)])
                S.barrier()
                S.emit()

        def layer_norm_tile(z, sqt, st, lnG, lnB, gi, outt, eng2="dve"):
            S.op("dve", lambda e: e.tensor_reduce(out=st[:, 0:1], in_=z[:], axis=AX.X, op=ALU.add), [z], [st])
            S.op("dve", lambda e: e.tensor_tensor(out=sqt[:], in0=z[:], in1=z[:], op=ALU.mult), [z], [sqt])
            S.op("dve", lambda e: e.tensor_reduce(out=st[:, 1:2], in_=sqt[:], axis=AX.X, op=ALU.add), [sqt], [st])
            S.op("dve", lambda e: e.tensor_scalar_mul(out=st[:, 0:2], in0=st[:, 0:2], scalar1=1.0 / D), [st], [st])
            S.op("dve", lambda e: e.tensor_tensor(out=st[:, 2:3], in0=st[:, 0:1], in1=st[:, 0:1], op=ALU.mult), [st], [st])
            S.op("dve", lambda e: e.tensor_tensor(out=st[:, 2:3], in0=st[:, 1:2], in1=st[:, 2:3], op=ALU.subtract), [st], [st])
            S.op("dve", lambda e: e.tensor_scalar_add(out=st[:, 2:3], in0=st[:, 2:3], scalar1=LN_EPS), [st], [st])
            S.op("act", lambda e: e.activation(out=st[:, 2:3], in_=st[:, 2:3], func=AF.Sqrt), [st], [st])
            S.op("dve", lambda e: e.reciprocal(out=st[:, 2:3], in_=st[:, 2:3]), [st], [st])
            S.op("dve", lambda e: e.tensor_scalar(out=sqt[:], in0=z[:], scalar1=st[:, 0:1], scalar2=st[:, 2:3], op0=ALU.subtract, op1=ALU.mult),
                 [z, st], [sqt])
            S.op("pool", lambda e: e.tensor_tensor(out=sqt[:], in0=sqt[:], in1=lnG[:, gi, :], op=ALU.mult), [sqt, lnG], [sqt])
            S.op("pool", lambda e: e.tensor_tensor(out=outt[:], in0=sqt[:], in1=lnB[:, gi, :], op=ALU.add), [sqt, lnB], [outt])

        def phase_E(l, full_ctx):
            src = h0 if l == 0 else H.t
            with ExitStack() as pes:
                wo = C.sb("wo", [128, 8, D], BF16, es=pes)
                lnG = C.sb("lnG", [128, 2, D], es=pes)
                lnB = C.sb("lnB", [128, 2, D], es=pes)
                rw = C.sb("rw", [128, 8, 128], es=pes)
                LG = C.sb("LG", [16, T], es=pes)
                yts = [C.sb(f"yt{i}", [128, D], BF16, es=pes) for i in range(2)]
                hts = [C.sb(f"eht{i}", [128, D], es=pes) for i in range(2)]
                YT = C.sb("YT", [128, 8, 128], BF16, es=pes)
                z = C.sb("z", [128, D], es=pes)
                sqt = C.sb("sqt", [128, D], es=pes)
                st = C.sb("st", [128, 4], es=pes)
                hn = [C.sb(f"hn{i}", [128, D], es=pes) for i in range(2)]
                u2b = [C.sb(f"u2b{i}", [128, D], BF16, es=pes) for i in range(2)]
                u2t = C.sb("u2t", [128, D], es=pes)
                u2T = C.sb("u2T", [128, 8, 128], es=pes)
                for k in range(8):
                    S.dma("pool", lambda e, k=k: e.dma_start(out=wo[:, k, :], in_=w_out[l, k * 128:(k + 1) * 128, :]), [], [wo])
                S.dma("sp", lambda e: e.dma_start(out=lnG[:].rearrange("p g d -> p (g d)"), in_=ln_g[l:l + 1, :].broadcast_to([128, 2 * D])), [], [lnG])
                S.dma("sp", lambda e: e.dma_start(out=lnB[:].rearrange("p g d -> p (g d)"), in_=ln_b[l:l + 1, :].broadcast_to([128, 2 * D])), [], [lnB])
                S.dma("sp", lambda e: e.dma_start(out=rw[:], in_=router_wp[l, :, :].rearrange("(k p) e -> p k e", p=128)), [], [rw])
                tiles = list(range(NT)) if full_ctx else list(range(NLT))
                for ti, t in enumerate(tiles):
                    lc = 0 if t < NLT else 1
                    yt = yts[ti % 2]
                    ht = hts[ti % 2]
                    S.dma("sp", lambda e, yt=yt, t=t: e.dma_start(out=yt[:], in_=Y.t[t * 128:(t + 1) * 128, :]), [], [yt])
                    S.dma("act", lambda e, ht=ht, t=t: e.dma_start(out=ht[:], in_=src[t * 128:(t + 1) * 128, :]), ["H/%d" % t], [ht])
                    ps = C.nextps()
                    pv = ps.t[:, :].bitcast(BF16)
                    for k in range(8):
                        S.op("pe", lambda e, pv=pv, yt=yt, k=k: e.transpose(pv[:, k * 128:(k + 1) * 128], yt[:, k * 128:(k + 1) * 128], ident_b[:]),
                             [yt, ident_b], [ps])
                    S.op("act", lambda e, pv=pv: e.copy(out=YT[:].rearrange("p k t -> p (k t)"), in_=pv[:, :]), [ps], [YT])
                    for half in range(2):
                        pm = C.nextps()
                        for k in range(8):
                            S.op("pe", lambda e, pm=pm, k=k, half=half: e.matmul(pm[:, :], lhsT=YT[:, k, :], rhs=wo[:, k, half * 512:(half + 1) * 512],
                                                                                 start=(k == 0), stop=(k == 7)), [YT, wo], [pm])
                        S.op("dve", lambda e, pm=pm, half=half, lc=lc: e.tensor_tensor(
                            out=z[:, half * 512:(half + 1) * 512], in0=pm[:, :], in1=gateB[:, 0, lc, half * 512:(half + 1) * 512], op=ALU.mult),
                            [pm, gateB], [z])
                    S.op("dve", lambda e, ht=ht: e.scalar_tensor_tensor(out=z[:], in0=ht[:], scalar=ALPHA, in1=z[:], op0=ALU.mult, op1=ALU.add),
                         [ht, z], [z])
                    h_ = hn[ti % 2]
                    layer_norm_tile(z, sqt, st, lnG, lnB, 0, h_)
                    S.dma("sp", lambda e, h_=h_, t=t: e.dma_start(out=H.t[t * 128:(t + 1) * 128, :], in_=h_[:]), [h_], ["H/%d" % t])
                    ub_ = u2b[ti % 2]
                    S.op("pool", lambda e, h_=h_, lc=lc: e.tensor_tensor(out=u2t[:], in0=h_[:], in1=gateB[:, 3, lc, :], op=ALU.mult), [h_, gateB], [u2t])
                    S.op("pool", lambda e, ub_=ub_, lc=lc: e.tensor_tensor(out=ub_[:], in0=u2t[:], in1=gateB[:, 2, lc, :], op=ALU.add), [u2t, gateB], [ub_])
                    S.dma("act", lambda e, ub_=ub_, t=t: e.dma_start(out=U2.t[t * 128:(t + 1) * 128, :], in_=ub_[:]), [ub_], [U2.sub(t)])
                    for half in range(2):
                        pt = C.nextps()
                        for j in range(4):
                            k = half * 4 + j
                            S.op("pe", lambda e, pt=pt, h_=h_, j=j, k=k: e.transpose(pt[:, j * 128:(j + 1) * 128], h_[:, k * 128:(k + 1) * 128], ident_f[:]),
                                 [h_, ident_f], [pt])
                        for j in range(4):
                            k = half * 4 + j
                            S.op("act", lambda e, pt=pt, j=j, k=k, lc=lc: e.activation(
                                out=u2T[:, k, :], in_=pt[:, j * 128:(j + 1) * 128], func=AF.Identity,
                                bias=modT[:, k, 2, lc:lc + 1], scale=modT[:, k, 3, lc:lc + 1]), [pt, modT], [u2T])
                    pl_ = C.nextps()
                    for k in range(8):
                        S.op("pe", lambda e, pl_=pl_, k=k: e.matmul(pl_[0:16, 0:128], lhsT=rw[:, k, 0:16], rhs=u2T[:, k, :], start=(k == 0), stop=(k == 7)),
                             [rw, u2T], [pl_])
                    S.op("dve", lambda e, pl_=pl_, t=t: e.tensor_copy(out=LG[:, t * 128:(t + 1) * 128], in_=pl_[0:16, 0:128]), [pl_], [LG])
                S.dma("sp", lambda e: e.dma_start(out=LGD.t[:, :], in_=LG[:]), [LG], [LGD])
                S.barrier()
                S.emit()

        def moe_segs(full_ctx):
            segs = [(0, NLT, NLAT // 8, 0, 0)]
            if full_ctx:
                segs.append((NLT, 2, 32, NLAT // 8, 512))
            return segs

        def phase_F(l, full_ctx):
            segs = moe_segs(full_ctx)
            NTOT = sum(sg[2] for sg in segs)
            with ExitStack() as pes:
                LG = C.sb("fLG", [16, T], es=pes)
                aff = C.sb("aff", [16, T], es=pes)
                work = C.sb("work", [16, NLAT], es=pes)
                m8 = C.sb("m8", [16, 8], es=pes)
                key = C.sb("key", [16, T], es=pes)
                ones16 = C.sb("ones16", [16, 16], es=pes)
                onesr = C.sb("onesr", [16, NLAT], es=pes)
                id16p = C.sb("id16p", [16, 128], es=pes)
                keyTok = C.sb("keyTok", [128, NT, 16], es=pes)
                iot = C.sb("iot", [128, 512], es=pes)
                S.dma("sp", lambda e: e.dma_start(out=LG[:], in_=LGD.t[:, :]), [], [LG])
                S.dma("sp", lambda e: e.dma_start(out=iot[:], in_=iota512[:, :]), [], [iot])
                S.op("pool", lambda e: e.memset(ones16[:], 1.0), [], [ones16])
                S.op("pool", lambda e: e.memset(onesr[:], 1.0), [], [onesr])
                S.op("pool", lambda e: e.memset(id16p[:], 0.0), [], [id16p])
                S.op("dve", lambda e: e.tensor_copy(out=id16p[:, 0:16], in_=ident_f[0:16, 0:16]), [id16p, ident_f], [id16p])
                S.op("act", lambda e: e.activation(out=aff[:], in_=LG[:], func=AF.Exp), [LG], [aff])
                for c0 in range(0, T, 512):
                    cw = min(512, T - c0)
                    ps = C.nextps()
                    S.op("pe", lambda e, ps=ps, c0=c0, cw=cw: e.matmul(ps[0:16, 0:cw], lhsT=ones16[:], rhs=aff[:, c0:c0 + cw], start=True, stop=True),
                         [ones16, aff], [ps])
                    S.op("dve", lambda e, ps=ps, c0=c0, cw=cw: e.reciprocal(out=LG[:, c0:c0 + cw], in_=ps[0:16, 0:cw]), [ps], [LG])
                S.op("dve", lambda e: e.tensor_tensor(out=aff[:], in0=aff[:], in1=LG[:], op=ALU.mult), [aff, LG], [aff])
                for (t0, ntl, cap, c0, r0) in segs:
                    n_ = ntl * 128
                    a0 = t0 * 128
                    S.op("dve", lambda e, a0=a0, n_=n_: e.tensor_copy(out=work[:, 0:n_], in_=aff[:, a0:a0 + n_]), [aff], [work])
                    for rnd in range(cap // 8):
                        S.op("dve", lambda e, n_=n_: e.max(out=m8[:], in_=work[:, 0:n_]), [work], [m8])
                        S.op("dve", lambda e, n_=n_: e.match_replace(out=work[:, 0:n_], in_to_replace=m8[:], in_values=work[:, 0:n_], imm_value=-1.0),
                             [work, m8], [work])
                    S.op("dve", lambda e, n_=n_: e.tensor_single_scalar(out=work[:, 0:n_], in_=work[:, 0:n_], scalar=0.0, op=ALU.is_lt), [work], [work])
                    S.op("dve", lambda e, a0=a0, n_=n_: e.tensor_tensor_scan(out=key[:, a0:a0 + n_], data0=onesr[:, 0:n_], data1=work[:, 0:n_], initial=0.0,
                                                                            op0=ALU.mult, op1=ALU.add), [work, onesr], [key])
                    S.op("dve", lambda e, a0=a0, n_=n_: e.tensor_tensor(out=key[:, a0:a0 + n_], in0=key[:, a0:a0 + n_], in1=work[:, 0:n_], op=ALU.mult), [key, work], [key])
                    S.op("dve", lambda e, a0=a0, n_=n_: e.tensor_scalar_add(out=key[:, a0:a0 + n_], in0=key[:, a0:a0 + n_], scalar1=-1.0), [key], [key])
                S.dma("sp", lambda e: e.dma_start(out=KEYD.t[:, :], in_=key[:]), [key], [KEYD])
                S.dma("sp", lambda e: e.dma_start(out=AFFD.t[:, :], in_=aff[:]), [aff], [AFFD])
                tl_all = [t for (t0, ntl, cap, c0, r0) in segs for t in range(t0, t0 + ntl)]
                for q0 in range(0, len(tl_all), 4):
                    grp = tl_all[q0:q0 + 4]
                    ps = C.nextps()
                    for qi, t in enumerate(grp):
                        S.op("pe", lambda e, ps=ps, qi=qi, t=t: e.matmul(ps[:, qi * 128:(qi + 1) * 128], lhsT=key[:, t * 128:(t + 1) * 128], rhs=id16p[:],
                                                                          start=True, stop=True), [key, id16p], [ps])
                    for qi, t in enumerate(grp):
                        S.op("dve", lambda e, ps=ps, qi=qi, t=t: e.tensor_copy(out=keyTok[:, t, :], in_=ps[:, qi * 128:qi * 128 + 16]), [ps], [keyTok])
                S.dma("sp", lambda e: e.dma_start(out=KTD.t[:, :], in_=keyTok[:].rearrange("p n c -> p (n c)")), [keyTok], [KTD])
                S.barrier()
                S.emit()
            with ExitStack() as pes:
                keyTok = C.sb("keyTok2", [128, NT, 16], es=pes)
                iot = C.sb("iot2", [128, 512], es=pes)
                Se = C.sb("Se", [128, NT, 512], BF16, es=pes)
                xinT = C.sb("xinT", [128, 8, NTOT], BF16, es=pes)
                hT = C.sb("hT", [128, 16, NTOT], BF16, es=pes)
                yacc = C.sb("yacc", [128, 5, D], es=pes)
                yw = C.sb("yw", [128, 5, D], BF16, es=pes)
                u2s = [C.sb(f"u2s{i}", [128, D], BF16, es=pes) for i in range(4)]
                wg = [C.sb(f"wg{i}", [128, 8, 512], BF16, es=pes) for i in range(2)]
                wu = [C.sb(f"wu{i}", [128, 8, 512], BF16, es=pes) for i in range(2)]
                wd = [C.sb(f"wd{i}", [128, 4, D], BF16, es=pes) for i in range(2)]
                sg_ = [C.sb(f"sgt{i}", [128, 512], es=pes) for i in range(2)]
                S.dma("sp", lambda e: e.dma_start(out=keyTok[:].rearrange("p n c -> p (n c)"), in_=KTD.t[:, :]), [], [keyTok])
                S.dma("sp", lambda e: e.dma_start(out=iot[:], in_=iota512[:, :]), [], [iot])
                C.rot = [4, 5, 6, 7]
                wcnt = 0
                ucnt = 0
                for ex in range(16):
                    for (t0, ntl, cap, c0, r0) in segs:
                        for t in range(t0, t0 + ntl):
                            S.op("dve", lambda e, t=t, cap=cap, ex=ex: e.tensor_scalar(
                                out=Se[:, t, 0:cap], in0=iot[:, 0:cap], scalar1=keyTok[:, t, ex:ex + 1], scalar2=None, op0=ALU.is_equal),
                                [iot, keyTok], [Se.sub(t)])
                        for kh in range(2):
                            for ti, t in enumerate(range(t0, t0 + ntl)):
                                ut = u2s[ucnt % 4]
                                S.dma("sp" if ucnt % 2 == 0 else "act", lambda e, ut=ut, t=t: e.dma_start(out=ut[:], in_=U2.t[t * 128:(t + 1) * 128, :]), [], [ut])
                                ucnt += 1
                                for kk in range(4):
                                    k = kh * 4 + kk
                                    S.op("pe", lambda e, ut=ut, t=t, k=k, kk=kk, cap=cap, ti=ti, ntl=ntl: e.matmul(
                                        C.ps[kk][:, 0:cap], lhsT=ut[:, k * 128:(k + 1) * 128], rhs=Se[:, t, 0:cap],
                                        start=(ti == 0), stop=(ti == ntl - 1)), [ut, Se.sub(t)], [C.ps[kk]])
                            for kk in range(4):
                                k = kh * 4 + kk
                                S.op("act", lambda e, k=k, kk=kk, c0=c0, cap=cap: e.copy(out=xinT[:, k, c0:c0 + cap], in_=C.ps[kk][:, 0:cap]),
                                     [C.ps[kk]], [xinT])
                    jts = []
                    for (t0, ntl, cap, c0, r0) in segs:
                        for j0 in range(0, cap, 128):
                            jts.append((r0 + j0, min(128, cap - j0), c0 + j0))
                    for fg in range(4):
                        g_, u_, d_ = wg[wcnt % 2], wu[wcnt % 2], wd[wcnt % 2]
                        wcnt += 1
                        for k in range(8):
                            S.dma("pool", lambda e, g_=g_, ex=ex, fg=fg, k=k: e.dma_start(
                                out=g_[:, k, :], in_=w_gate[l, ex, k * 128:(k + 1) * 128, fg * 512:(fg + 1) * 512]), [], [g_])
                            S.dma("pool", lambda e, u_=u_, ex=ex, fg=fg, k=k: e.dma_start(
                                out=u_[:, k, :], in_=w_up[l, ex, k * 128:(k + 1) * 128, fg * 512:(fg + 1) * 512]), [], [u_])
                        for cc in range(4):
                            S.dma("pool", lambda e, d_=d_, ex=ex, fg=fg, cc=cc: e.dma_start(
                                out=d_[:, cc, :], in_=w_down[l, ex, fg * 512 + cc * 128:fg * 512 + (cc + 1) * 128, :]), [], [d_])
                        for fc in range(4):
                            f = fg * 4 + fc
                            for (t0, ntl, cap, c0, r0) in segs:
                                pg = C.nextps()
                                pu_ = C.nextps()
                                for k in range(8):
                                    S.op("pe", lambda e, pg=pg, g_=g_, k=k, fc=fc, c0=c0, cap=cap: e.matmul(
                                        pg[:, 0:cap], lhsT=g_[:, k, fc * 128:(fc + 1) * 128], rhs=xinT[:, k, c0:c0 + cap], start=(k == 0), stop=(k == 7)),
                                        [g_, xinT], [pg])
                                for k in range(8):
                                    S.op("pe", lambda e, pu_=pu_, u_=u_, k=k, fc=fc, c0=c0, cap=cap: e.matmul(
                                        pu_[:, 0:cap], lhsT=u_[:, k, fc * 128:(fc + 1) * 128], rhs=xinT[:, k, c0:c0 + cap], start=(k == 0), stop=(k == 7)),
                                        [u_, xinT], [pu_])
                                sgt = sg_[f % 2]
                                S.op("act", lambda e, pg=pg, sgt=sgt, cap=cap: e.activation(out=sgt[:, 0:cap], in_=pg[:, 0:cap], func=AF.Silu), [pg], [sgt])
                                S.op("dve", lambda e, pu_=pu_, sgt=sgt, f=f, c0=c0, cap=cap: e.tensor_tensor(
                                    out=hT[:, f, c0:c0 + cap], in0=pu_[:, 0:cap], in1=sgt[:, 0:cap], op=ALU.mult), [pu_, sgt], [hT])
                        for ji, (ro, rows, co) in enumerate(jts):
                            for half in range(2):
                                py = C.nextps()
                                for fc in range(4):
                                    S.op("pe", lambda e, py=py, d_=d_, fc=fc, fg=fg, co=co, rows=rows, half=half: e.matmul(
                                        py[0:rows, :], lhsT=hT[:, fg * 4 + fc, co:co + rows], rhs=d_[:, fc, half * 512:(half + 1) * 512],
                                        start=(fc == 0), stop=(fc == 3)), [hT, d_], [py])
                                if fg == 0:
                                    S.op("dve", lambda e, py=py, ji=ji, rows=rows, half=half: e.tensor_copy(
                                        out=yacc[0:rows, ji, half * 512:(half + 1) * 512], in_=py[0:rows, :]), [py], [yacc])
                                elif fg < 3:
                                    S.op("dve", lambda e, py=py, ji=ji, rows=rows, half=half: e.tensor_tensor(
                                        out=yacc[0:rows, ji, half * 512:(half + 1) * 512], in0=yacc[0:rows, ji, half * 512:(half + 1) * 512],
                                        in1=py[0:rows, :], op=ALU.add), [py, yacc], [yacc])
                                else:
                                    S.op("dve", lambda e, py=py, ji=ji, rows=rows, half=half: e.tensor_tensor(
                                        out=yw[0:rows, ji, half * 512:(half + 1) * 512], in0=yacc[0:rows, ji, half * 512:(half + 1) * 512],
                                        in1=py[0:rows, :], op=ALU.add), [py, yacc], [yw])
                    for ji, (ro, rows, co) in enumerate(jts):
                        S.dma("act", lambda e, ji=ji, ro=ro, rows=rows, ex=ex: e.dma_start(out=YE.t[ex, ro:ro + rows, :], in_=yw[0:rows, ji, :]),
                              [yw], [YE.sub(ex, ji)])
                C.rot = None
                S.barrier()
                S.emit()

        def phase_G(l, full_ctx, last):
            segs = moe_segs(full_ctx)
            with ExitStack() as pes:
                key = C.sb("gkey", [16, T], es=pes)
                aff = C.sb("gaff", [16, T], es=pes)
                selE = C.sb("selE", [16, 16, 128], es=pes)
                jcol = C.sb("jcol", [128, 4], es=pes)
                lnG = C.sb("glnG", [128, 2, D], es=pes)
                lnB = C.sb("glnB", [128, 2, D], es=pes)
                acc = C.sb("acc", [128, 8, D], es=pes)
                yes = [C.sb(f"ye{i}", [128, 4, D], BF16, es=pes) for i in range(2)]
                Ws = [C.sb(f"W{i}", [128, 4, 512], BF16, es=pes) for i in range(2)]
                affB = [C.sb(f"affB{i}", [128, 512], es=pes) for i in range(2)]
                hts = [C.sb(f"ght{i}", [128, D], es=pes) for i in range(2)]
                z = C.sb("gz", [128, D], es=pes)
                sqt = C.sb("gsqt", [128, D], es=pes)
                st = C.sb("gst", [128, 4], es=pes)
                hn = [C.sb(f"ghn{i}", [128, D], es=pes) for i in range(2)]
                S.dma("sp", lambda e: e.dma_start(out=key[:], in_=KEYD.t[:, :]), [], [key])
                S.dma("sp", lambda e: e.dma_start(out=aff[:], in_=AFFD.t[:, :]), [], [aff])
                S.dma("sp", lambda e: e.dma_start(out=selE[:], in_=selE_in[:, :, :]), [], [selE])
                S.dma("sp", lambda e: e.dma_start(out=jcol[:], in_=jcol_in[:, :]), [], [jcol])
                S.dma("sp", lambda e: e.dma_start(out=lnG[:].rearrange("p g d -> p (g d)"), in_=ln_g[l:l + 1, :].broadcast_to([128, 2 * D])), [], [lnG])
                S.dma("sp", lambda e: e.dma_start(out=lnB[:].rearrange("p g d -> p (g d)"), in_=ln_b[l:l + 1, :].broadcast_to([128, 2 * D])), [], [lnB])
                cnt = 0
                for (t0, ntl, cap, c0, r0) in segs:
                    lc = 0 if t0 < NLT else 1
                    njt = (cap + 127) // 128
                    for gq in range(t0, t0 + ntl, 8):
                        gtiles = list(range(gq, min(gq + 8, t0 + ntl)))
                        items = [(ex, q0) for ex in range(16) for q0 in range(0, len(gtiles), 4)]
                        fr = {}

                        def front(ii, gtiles=gtiles, njt=njt, cap=cap, r0=r0):
                            ex, q0 = items[ii]
                            ye = yes[ex % 2]
                            if q0 == 0:
                                for jt in range(njt):
                                    rows = min(128, cap - jt * 128)
                                    S.dma("sp", lambda e, ye=ye, jt=jt, rows=rows, ex=ex, r0=r0: e.dma_start(
                                        out=ye[0:rows, jt, :], in_=YE.t[ex, r0 + jt * 128:r0 + jt * 128 + rows, :]), [ye], [ye])
                            ch = gtiles[q0:q0 + 4]
                            a0 = ch[0] * 128
                            cw = len(ch) * 128
                            pk = C.ps[(ii % 2) * 2]
                            pa = C.ps[(ii % 2) * 2 + 1]
                            S.op("pe", lambda e, pk=pk, ex=ex, a0=a0, cw=cw: e.matmul(pk[:, 0:cw], lhsT=selE[:, ex, :], rhs=key[:, a0:a0 + cw], start=True, stop=True),
                                 [selE, key], [pk])
                            S.op("pe", lambda e, pa=pa, ex=ex, a0=a0, cw=cw: e.matmul(pa[:, 0:cw], lhsT=selE[:, ex, :], rhs=aff[:, a0:a0 + cw], start=True, stop=True),
                                 [selE, aff], [pa])
                            ab = affB[ii % 2]
                            W = Ws[ii % 2]
                            S.op("act", lambda e, pa=pa, ab=ab, cw=cw: e.copy(out=ab[:, 0:cw], in_=pa[:, 0:cw]), [pa], [ab])
                            for jt in range(njt):
                                S.op("dve", lambda e, pk=pk, ab=ab, W=W, jt=jt, cw=cw: e.scalar_tensor_tensor(
                                    out=W[:, jt, 0:cw], in0=pk[:, 0:cw], scalar=jcol[:, jt:jt + 1], in1=ab[:, 0:cw], op0=ALU.is_equal, op1=ALU.mult),
                                    [pk, ab, jcol], [W])
                            fr[ii] = (ch, W, ye)

                        def back(ii, gq=gq, njt=njt, cap=cap):
                            ex, q0 = items[ii]
                            ch, W, ye = fr.pop(ii)
                            for qi, t in enumerate(ch):
                                ti = t - gq
                                for half in range(2):
                                    py = C.nextps()
                                    for jt in range(njt):
                                        rows = min(128, cap - jt * 128)
                                        S.op("pe", lambda e, py=py, W=W, ye=ye, jt=jt, rows=rows, qi=qi, half=half, njt=njt: e.matmul(
                                            py[:, :], lhsT=W[0:rows, jt, qi * 128:(qi + 1) * 128], rhs=ye[0:rows, jt, half * 512:(half + 1) * 512],
                                            start=(jt == 0), stop=(jt == njt - 1)), [W, ye], [py])
                                    if ex == 0:
                                        S.op("dve", lambda e, py=py, ti=ti, half=half: e.tensor_copy(out=acc[:, ti, half * 512:(half + 1) * 512], in_=py[:, :]),
                                             [py], [acc.sub(ti)])
                                    else:
                                        S.op("dve", lambda e, py=py, ti=ti, half=half: e.tensor_tensor(
                                            out=acc[:, ti, half * 512:(half + 1) * 512], in0=acc[:, ti, half * 512:(half + 1) * 512], in1=py[:, :], op=ALU.add),
                                            [py, acc.sub(ti)], [acc.sub(ti)])

                        C.rot = [4, 5, 6, 7]
                        for ii in range(len(items) + 1):
                            if ii < len(items):
                                front(ii)
                            if ii >= 1:
                                back(ii - 1)
                        C.rot = None
                        for t in gtiles:
                            ti = t - gq
                            ht = hts[t % 2]
                            S.dma("act", lambda e, ht=ht, t=t: e.dma_start(out=ht[:], in_=H.t[t * 128:(t + 1) * 128, :]), ["H/%d" % t], [ht])
                            S.op("pool", lambda e, ti=ti, lc=lc: e.tensor_tensor(out=z[:], in0=acc[:, ti, :], in1=gateB[:, 1, lc, :], op=ALU.mult),
                                 [acc.sub(ti), gateB], [z])
                            S.op("dve", lambda e, ht=ht: e.scalar_tensor_tensor(out=z[:], in0=ht[:], scalar=ALPHA, in1=z[:], op0=ALU.mult, op1=ALU.add),
                                 [ht, z], [z])
                            h_ = hn[t % 2]
                            layer_norm_tile(z, sqt, st, lnG, lnB, 1, h_)
                            if last:
                                S.dma("sp", lambda e, h_=h_, t=t: e.dma_start(out=out[t * 128:(t + 1) * 128, :], in_=h_[:]), [h_], ["out/%d" % t])
                            else:
                                S.dma("sp", lambda e, h_=h_, t=t: e.dma_start(out=H.t[t * 128:(t + 1) * 128, :], in_=h_[:]), [h_], ["H/%d" % t])
                S.barrier()
                S.emit()

        for l in range(NL):
            full_ctx = l < DEPTH - 1
            if "A" in phases.split(","):
                phase_A(l)
            if "BC" in phases.split(","):
                phase_BC(l)
            if "ret" in phases.split(","):
                phase_ret(l, full_ctx)
            if "diff" in phases.split(","):
                phase_diff(l, full_ctx)
            if "swa" in phases.split(","):
                phase_swa(l, full_ctx)
            if "gdn" in phases.split(","):
                phase_gdn(l, full_ctx)
            if "E" in phases.split(","):
                phase_E(l, full_ctx)
            if "F" in phases.split(","):
                phase_F(l, full_ctx)
            if "G" in phases.split(","):
                phase_G(l, full_ctx, last=(l == NL - 1 and "keepH" not in dbg))
    dbg_t["ninst"] = S.ninst
    return nc, dbg_t


def prep_shared(inputs, NLT, NL):
    sh = {}
    ada_b = np.asarray(inputs["ada_b"], np.float32)[:NL]
    sh["ada_w"] = np.ascontiguousarray(np.asarray(inputs["ada_w"], np.float32)[:NL])
    sh["ada_b"] = np.ascontiguousarray(ada_b)
    sh["ada_bT"] = np.ascontiguousarray(ada_b.reshape(NL, 48, 128).transpose(0, 2, 1))
    w_in = np.asarray(inputs["w_in"], np.float32)
    sh["w_in"] = np.stack([build_w_in_ext(w_in[l]) for l in range(NL)])
    sh["w_out"] = np.ascontiguousarray(np.asarray(inputs["w_out"], np.float32)[:NL])
    sh["rope"] = rope_tables(NLT)
    sh["ret_decay"] = np.ascontiguousarray(np.asarray(inputs["ret_decay"], np.float32)[:NL].reshape(NL, 8))
    sh["ln_g"] = np.ascontiguousarray(np.asarray(inputs["ln_g"], np.float32)[:NL].reshape(NL, 2 * D))
    sh["ln_b"] = np.ascontiguousarray(np.asarray(inputs["ln_b"], np.float32)[:NL].reshape(NL, 2 * D))
    sh["diff_lambda"] = np.ascontiguousarray(np.asarray(inputs["diff_lambda"], np.float32)[:NL].reshape(NL, 128))
    sh["diff_norm"] = np.ascontiguousarray(np.asarray(inputs["diff_norm"], np.float32)[:NL])
    sh["swa_sink"] = np.ascontiguousarray(np.asarray(inputs["swa_sink"], np.float32)[:NL])
    rw = np.asarray(inputs["router_w"], np.float32)[:NL]
    rwp = np.zeros((NL, D, 128), np.float32)
    rwp[:, :, :16] = rw
    sh["router_wp"] = rwp
    sh["w_gate"] = np.ascontiguousarray(np.asarray(inputs["w_gate"], np.float32)[:NL])
    sh["w_up"] = np.ascontiguousarray(np.asarray(inputs["w_up"], np.float32)[:NL])
    sh["w_down"] = np.ascontiguousarray(np.asarray(inputs["w_down"], np.float32)[:NL])
    sh["iota512"] = np.ascontiguousarray(np.broadcast_to(np.arange(512, dtype=np.float32)[None, :], (128, 512)))
    sh["jcol_in"] = np.ascontiguousarray(np.arange(128, dtype=np.float32)[:, None] + 128.0 * np.arange(4, dtype=np.float32)[None, :])
    se = np.zeros((16, 16, 128), np.float32)
    for e_ in range(16):
        se[e_, e_, :] = 1.0
    sh["selE_in"] = se
    gc = np.asarray(inputs["gdn_conv"], np.float32)[:NL]
    sh["gdn_convT"] = np.ascontiguousarray(gc.reshape(NL, 3, 6, 128).transpose(0, 3, 2, 1).reshape(NL, 128, 18))
    sh["gdn_a_log"] = np.ascontiguousarray(np.asarray(inputs["gdn_a_log"], np.float32)[:NL].reshape(NL, 8))
    sh["gdn_dt_bias"] = np.ascontiguousarray(np.asarray(inputs["gdn_dt_bias"], np.float32)[:NL].reshape(NL, 8))
    sh["gdn_norm"] = np.ascontiguousarray(np.asarray(inputs["gdn_norm"], np.float32)[:NL])
    i = np.arange(128, dtype=np.float32)
    sh["cmask"] = np.ascontiguousarray(i[None, :] - i[:, None])
    return sh


def prep_core(inputs, b, NLT):
    m = {}
    x = np.asarray(inputs["x"], np.float32)[b, :NLT * 128]
    ctx = np.asarray(inputs["ctx"], np.float32)[b]
    m["h0"] = np.ascontiguousarray(np.concatenate([x, ctx], 0))
    c = np.asarray(inputs["c"], np.float32)[b]
    cx = np.asarray(inputs["c_ctx"], np.float32)
    m["cc"] = np.ascontiguousarray(np.stack([c, cx], -1).reshape(8, 128, 2).transpose(1, 0, 2))
    return m


def kernel(**inputs):
    NLT, NL = 32, 4
    nc, _ = build(NLT, NL)
    sh = prep_shared(inputs, NLT, NL)
    in_maps = []
    for b in range(8):
        m = dict(sh)
        m.update(prep_core(inputs, b, NLT))
        in_maps.append(m)
    res = run_bass_kernel_spmd(nc, in_maps, core_ids=list(range(8)))
    return np.stack([np.asarray(r["out"], np.float32) for r in res.results], 0)
```
